# Optimizing a Trainium2 kernel written in Bass

```python
import math
import jax
import jax.numpy as jnp
from jax import lax
import numpy as np

D_MODEL = 2048
BATCH = 2
SEQ = 4096
DEPTH = 4

GRID_W = 64
CTX_LEN = 256
DN_WIDTH = D_MODEL // 2
DN_HEAD_DIM = 128
DN_HEADS = DN_WIDTH // DN_HEAD_DIM
RW_WIDTH = D_MODEL - DN_WIDTH
RW_HEAD_DIM = 64
RW_HEADS = RW_WIDTH // RW_HEAD_DIM
CONV_K = 7
CHUNK = 64
W_LORA = 64
A_LORA = 64
G_LORA = 160
FFN_HIDDEN = -(-8 * D_MODEL // (3 * 256)) * 256
P_DN = 4 * DN_WIDTH + 4 * DN_HEADS
P_RW = 3 * RW_WIDTH + 2 * W_LORA + 2 * A_LORA + G_LORA
P_IN = P_DN + P_RW
NORM_EPS = 1e-6
RW_LN_EPS = 64e-5

kernel_name = 'hybrid_deltanet_rwkv7_flow_block'


def rms_norm(x, gain):
    xf = x.astype(jnp.float32)
    y = xf * lax.rsqrt(jnp.mean(xf * xf, axis=-1, keepdims=True) + NORM_EPS)
    return y.astype(x.dtype) * gain


def modulate(x, shift, scale):
    return x * (1.0 + scale) + shift


def l2_normalize(x):
    return x * lax.rsqrt(jnp.sum(x * x, axis=-1, keepdims=True) + NORM_EPS)


def to_col_major(a, rows):
    b, t, ch = a.shape
    return a.reshape(b, rows, GRID_W, ch).transpose(0, 2, 1, 3).reshape(b, t, ch)


def to_row_major(a, rows):
    b, t, ch = a.shape
    return a.reshape(b, GRID_W, rows, ch).transpose(0, 2, 1, 3).reshape(b, t, ch)


def centred_conv(a, w):
    ch = a.shape[-1]
    return lax.conv_general_dilated(
        a, w.astype(a.dtype)[:, None, :], window_strides=(1,),
        padding=[(CONV_K // 2, CONV_K // 2)],
        dimension_numbers=('NWC', 'WIO', 'NWC'), feature_group_count=ch)


def dn_heads(a):
    b, t, _ = a.shape
    return a.reshape(b, t, DN_HEADS, DN_HEAD_DIM).transpose(0, 2, 1, 3)


def gated_delta_rule_chunked(q, k, v, g, beta, state):
    b, nh, t_len, _ = q.shape
    dv = v.shape[-1]
    n = t_len // CHUNK

    def chunks(a):
        return a.reshape(b, nh, n, CHUNK, *a.shape[3:])

    q, k, v, g, beta = (chunks(a) for a in (q, k, v, g, beta))
    gcum = jnp.cumsum(g, axis=-1)
    idx = jnp.arange(CHUNK)
    incl = idx[:, None] >= idx[None, :]
    strict = idx[:, None] > idx[None, :]
    diff = gcum[..., :, None] - gcum[..., None, :]
    decay = jnp.where(incl, jnp.exp(jnp.where(incl, diff, 0.0)), 0.0)
    k_beta = k * beta[..., None]
    a_mat = jnp.where(strict, jnp.einsum('bhnid,bhnjd->bhnij', k_beta, k) * decay, 0.0)
    rhs = jnp.concatenate([v * beta[..., None], k_beta * jnp.exp(gcum)[..., None]], axis=-1)
    sol = lax.linalg.triangular_solve(a_mat, rhs, left_side=True, lower=True, unit_diagonal=True)
    u, w = sol[..., :dv], sol[..., dv:]
    qk = jnp.einsum('bhnid,bhnjd->bhnij', q, k) * decay
    q_dec = q * jnp.exp(gcum)[..., None]
    k_dec = k * jnp.exp(gcum[..., -1:] - gcum)[..., None]
    chunk_decay = jnp.exp(gcum[..., -1])

    def step(s, xs):
        u_n, w_n, qk_n, qd_n, kd_n, cd_n = xs
        v_new = u_n - jnp.einsum('bhck,bhkv->bhcv', w_n, s)
        o_n = jnp.einsum('bhck,bhkv->bhcv', qd_n, s) + jnp.einsum('bhij,bhjv->bhiv', qk_n, v_new)
        s = s * cd_n[..., None, None] + jnp.einsum('bhck,bhcv->bhkv', kd_n, v_new)
        return s, o_n

    xs = tuple(jnp.moveaxis(a, 2, 0) for a in (u, w, qk, q_dec, k_dec, chunk_decay))
    state, o = lax.scan(step, state, xs)
    o = jnp.moveaxis(o, 0, 2).reshape(b, nh, t_len, dv)
    return o, state


def _dn_prepare(p, conv_w, a_log, dt_bias):
    b, t, _ = p.shape
    qkv = jax.nn.silu(centred_conv(p[..., :3 * DN_WIDTH], conv_w))
    q, k, v = jnp.split(qkv, 3, axis=-1)
    q = l2_normalize(dn_heads(q)) * DN_HEAD_DIM ** -0.5
    k = l2_normalize(dn_heads(k))
    v = dn_heads(v)
    z = p[..., 3 * DN_WIDTH:4 * DN_WIDTH]
    gates = p[..., 4 * DN_WIDTH:].reshape(b, t, 4, DN_HEADS).transpose(2, 0, 3, 1)
    beta = jax.nn.sigmoid(gates[:2])
    g = -jnp.exp(a_log)[:, None, :, None] * jax.nn.softplus(gates[2:] + dt_bias[:, None, :, None])
    return q, k, v, z, beta, g


def _dn_output(o, z, out_g):
    b, _, t, _ = o.shape
    o = o.transpose(0, 2, 1, 3)
    o = o * lax.rsqrt(jnp.mean(o * o, axis=-1, keepdims=True) + NORM_EPS) * out_g
    o = o * jax.nn.silu(z.reshape(b, t, DN_HEADS, DN_HEAD_DIM))
    return o.reshape(b, t, DN_WIDTH)


def gated_deltanet_mixer(p_ctx, p_lat, conv_w, a_log, dt_bias, out_g):
    p_ctx = p_ctx.astype(jnp.float32)
    p_lat = p_lat.astype(jnp.float32)
    conv_w = conv_w.astype(jnp.float32)
    streams = (_dn_prepare(p_ctx, conv_w, a_log, dt_bias), _dn_prepare(p_lat, conv_w, a_log, dt_bias))
    b = p_ctx.shape[0]
    outs = [0.0, 0.0]
    for d in range(2):
        flip = (lambda a: jnp.flip(a, axis=2)) if d == 1 else (lambda a: a)
        s = jnp.zeros((b, DN_HEADS, DN_HEAD_DIM, DN_HEAD_DIM), jnp.float32)
        for i, (q, k, v, _, beta, g) in enumerate(streams):
            o, s = gated_delta_rule_chunked(flip(q), flip(k), flip(v), flip(g[d]), flip(beta[d]), s)
            outs[i] = outs[i] + flip(o)
    return (_dn_output(outs[0], streams[0][3], out_g), _dn_output(outs[1], streams[1][3], out_g))


def rw_heads(a):
    return a.reshape(*a.shape[:-1], RW_HEADS, RW_HEAD_DIM)


def quad_shift(p, rows):
    b, t, ch = p.shape
    gq = p.reshape(b, rows, GRID_W, ch // 4, 4)
    left = jnp.pad(gq[..., 0], ((0, 0), (0, 0), (1, 0), (0, 0)))[:, :, :-1]
    right = jnp.pad(gq[..., 1], ((0, 0), (0, 0), (0, 1), (0, 0)))[:, :, 1:]
    up = jnp.pad(gq[..., 2], ((0, 0), (1, 0), (0, 0), (0, 0)))[:, :-1]
    down = jnp.pad(gq[..., 3], ((0, 0), (0, 1), (0, 0), (0, 0)))[:, 1:]
    return jnp.stack([left, right, up, down], axis=-1).reshape(b, t, ch)


def bi_shift(p):
    b, t, ch = p.shape
    gq = p.reshape(b, t, ch // 2, 2)
    prev = jnp.pad(gq[..., 0], ((0, 0), (1, 0), (0, 0)))[:, :-1]
    nxt = jnp.pad(gq[..., 1], ((0, 0), (0, 1), (0, 0)))[:, 1:]
    return jnp.stack([prev, nxt], axis=-1).reshape(b, t, ch)


def rwkv7_scan(r, w, k, v, a, bvec, s0, reverse):
    def step(s, inp):
        r_t, w_t, k_t, v_t, a_t, b_t = inp
        sa = jnp.einsum('bhvk,bhk->bhv', s, a_t)
        s = s * w_t[:, :, None, :] + sa[..., None] * b_t[:, :, None, :] + v_t[..., None] * k_t[:, :, None, :]
        return s, jnp.einsum('bhvk,bhk->bhv', s, r_t)

    xs = tuple(jnp.moveaxis(a_, 1, 0) for a_ in (r, w, k, v, a, bvec))
    s, y = lax.scan(step, s0, xs, reverse=reverse)
    return jnp.moveaxis(y, 0, 1), s


def _rw_prepare(p, g_up, k_k):
    b, t, _ = p.shape
    r = p[..., :RW_WIDTH]
    k = p[..., RW_WIDTH:2 * RW_WIDTH]
    v = p[..., 2 * RW_WIDTH:3 * RW_WIDTH]
    off = 3 * RW_WIDTH
    wd = p[..., off:off + 2 * W_LORA].reshape(b, t, 2, W_LORA)
    off = off + 2 * W_LORA
    ad = p[..., off:off + 2 * A_LORA].reshape(b, t, 2, A_LORA)
    gate = jax.nn.sigmoid(p[..., off + 2 * A_LORA:]) @ g_up
    kk = l2_normalize(rw_heads(k * k_k))
    return r, k, v, wd, ad, gate, kk


def _rw_direction(inp, d, s0, w0, w_up, a0, a_up, k_a, r_k):
    r, k, v, wd, ad, _, kk = inp
    log_w = -jax.nn.softplus(-(w0[d] + jnp.tanh(wd[:, :, d]) @ w_up[d])) - 0.5
    decay = jnp.exp(-jnp.exp(log_w))
    iclr = jax.nn.sigmoid(a0[d] + ad[:, :, d] @ a_up[d])
    k_d = rw_heads(k * (1.0 + (iclr - 1.0) * k_a))
    r_h, v_h = rw_heads(r), rw_heads(v)
    y, s = rwkv7_scan(r_h, rw_heads(decay), k_d, v_h, -kk, kk * rw_heads(iclr), s0, reverse=(d == 1))
    bonus = jnp.sum(r_h * k_d * r_k, axis=-1, keepdims=True) * v_h
    return y, bonus, s


def rw_group_norm(y, ln_w, ln_b):
    mean = jnp.mean(y, axis=-1, keepdims=True)
    var = jnp.mean(jnp.square(y - mean), axis=-1, keepdims=True)
    yn = (y - mean) * lax.rsqrt(var + RW_LN_EPS)
    return yn.reshape(*y.shape[:-2], RW_WIDTH) * ln_w + ln_b


def rwkv7_mixer(p_ctx, p_lat, rows, mu, w0, w_up, a0, a_up, g_up, k_k, k_a, r_k, ln_w, ln_b):
    p_ctx = p_ctx.astype(jnp.float32)
    p_lat = p_lat.astype(jnp.float32)
    p_ctx = p_ctx + mu * (bi_shift(p_ctx) - p_ctx)
    p_lat = p_lat + mu * (quad_shift(p_lat, rows) - p_lat)
    streams = (_rw_prepare(p_ctx, g_up, k_k), _rw_prepare(p_lat, g_up, k_k))
    b = p_ctx.shape[0]
    ys = [0.0, 0.0]
    bonuses = [0.0, 0.0]
    for d in range(2):
        s = jnp.zeros((b, RW_HEADS, RW_HEAD_DIM, RW_HEAD_DIM), jnp.float32)
        for i, inp in enumerate(streams):
            y, bonus, s = _rw_direction(inp, d, s, w0, w_up, a0, a_up, k_a, r_k)
            ys[i] = ys[i] + y
            bonuses[i] = bonuses[i] + bonus
    outs = []
    for i, inp in enumerate(streams):
        bonus_flat = bonuses[i].reshape(*bonuses[i].shape[:-2], RW_WIDTH)
        outs.append((rw_group_norm(ys[i], ln_w, ln_b) + bonus_flat) * inp[5])
    return outs[0], outs[1]


def mixer_residual(h, o, gate, w_o, g_post):
    return h + gate * rms_norm(o.astype(h.dtype) @ w_o, g_post)


def ffn_residual(h, shift, scale, gate, g_pre, g_post, w_i, w_o):
    u = modulate(rms_norm(h, g_pre), shift, scale)
    gt, up = jnp.split(u @ w_i, 2, axis=-1)
    return h + gate * rms_norm((jax.nn.silu(gt) * up) @ w_o, g_post)


def setup_inputs(seed: int = 0) -> dict:
    key = jax.random.key(seed)
    ks = iter(jax.random.split(key, 32))
    L = DEPTH

    def nrm(shape, scale):
        return jax.random.normal(next(ks), shape, jnp.float32) * scale

    def uni(shape, lo, hi):
        return jax.random.uniform(next(ks), shape, jnp.float32, lo, hi)

    dt = jnp.exp(uni((L, 2, DN_HEADS), math.log(1e-3), math.log(1e-1)))
    return {
        'x': nrm((BATCH, SEQ, D_MODEL), 1.0),
        'c': nrm((BATCH, D_MODEL), 1.0),
        'ctx': nrm((BATCH, CTX_LEN, D_MODEL), 1.0),
        'c_ctx': nrm((D_MODEL,), 1.0),
        'w_mod': nrm((L, D_MODEL, 6 * D_MODEL), 0.5 * D_MODEL ** -0.5),
        'b_mod': nrm((L, 6 * D_MODEL), 0.02),
        'mix_pre_g': 1.0 + nrm((L, D_MODEL), 0.1),
        'mix_post_g': 1.0 + nrm((L, D_MODEL), 0.1),
        'ffn_pre_g': 1.0 + nrm((L, D_MODEL), 0.1),
        'ffn_post_g': 1.0 + nrm((L, D_MODEL), 0.1),
        'w_in': nrm((L, D_MODEL, P_IN), D_MODEL ** -0.5),
        'dn_conv': nrm((L, CONV_K, 3 * DN_WIDTH), CONV_K ** -0.5),
        'dn_a_log': jnp.log(uni((L, 2, DN_HEADS), 1.0, 16.0)),
        'dn_dt_bias': dt + jnp.log(-jnp.expm1(-dt)),
        'dn_out_g': 1.0 + nrm((L, DN_HEAD_DIM), 0.1),
        'rw_mu': uni((L, P_RW), 0.0, 1.0),
        'rw_w0': uni((L, 2, RW_WIDTH), -6.0, 1.0),
        'rw_w_up': nrm((L, 2, W_LORA, RW_WIDTH), 0.5 * W_LORA ** -0.5),
        'rw_a0': nrm((L, 2, RW_WIDTH), 0.1),
        'rw_a_up': nrm((L, 2, A_LORA, RW_WIDTH), 0.5 * A_LORA ** -0.5),
        'rw_g_up': nrm((L, G_LORA, RW_WIDTH), G_LORA ** -0.5),
        'rw_k_k': 0.85 + nrm((L, RW_WIDTH), 0.1),
        'rw_k_a': 1.0 + nrm((L, RW_WIDTH), 0.1),
        'rw_r_k': nrm((L, RW_HEADS, RW_HEAD_DIM), 0.1),
        'rw_ln_w': 1.0 + nrm((L, RW_WIDTH), 0.1),
        'rw_ln_b': nrm((L, RW_WIDTH), 0.02),
        'w_out': nrm((L, D_MODEL, D_MODEL), D_MODEL ** -0.5),
        'w_ffn_in': nrm((L, D_MODEL, 2 * FFN_HIDDEN), D_MODEL ** -0.5),
        'w_ffn_out': nrm((L, FFN_HIDDEN, D_MODEL), FFN_HIDDEN ** -0.5),
    }


def reference(x, c, ctx, c_ctx, w_mod, b_mod, mix_pre_g, mix_post_g, ffn_pre_g, ffn_post_g,
              w_in, dn_conv, dn_a_log, dn_dt_bias, dn_out_g, rw_mu, rw_w0, rw_w_up, rw_a0,
              rw_a_up, rw_g_up, rw_k_k, rw_k_a, rw_r_k, rw_ln_w, rw_ln_b, w_out, w_ffn_in,
              w_ffn_out):
    rows = x.shape[1] // GRID_W
    n_ctx = ctx.shape[1]
    silu_c = jax.nn.silu(c)[:, None, :]
    silu_cc = jax.nn.silu(c_ctx)
    h, h_ctx = x, ctx
    for l in range(DEPTH):
        last = l == DEPTH - 1
        mod_lat = jnp.split(silu_c @ w_mod[l] + b_mod[l], 6, axis=-1)
        mod_ctx = jnp.split(silu_cc @ w_mod[l] + b_mod[l], 6, axis=-1)
        u_lat = modulate(rms_norm(h, mix_pre_g[l]), mod_lat[0], mod_lat[1])
        u_ctx = modulate(rms_norm(h_ctx, mix_pre_g[l]), mod_ctx[0], mod_ctx[1])
        p = jnp.concatenate([u_ctx, u_lat], axis=1) @ w_in[l]
        p_ctx, p_lat = p[:, :n_ctx], p[:, n_ctx:]
        dn_ctx, dn_lat = gated_deltanet_mixer(
            p_ctx[..., :P_DN], to_col_major(p_lat[..., :P_DN], rows),
            dn_conv[l], dn_a_log[l], dn_dt_bias[l], dn_out_g[l])
        dn_lat = to_row_major(dn_lat, rows)
        rw_ctx, rw_lat = rwkv7_mixer(
            p_ctx[..., P_DN:], p_lat[..., P_DN:], rows, rw_mu[l], rw_w0[l], rw_w_up[l],
            rw_a0[l], rw_a_up[l], rw_g_up[l], rw_k_k[l], rw_k_a[l], rw_r_k[l], rw_ln_w[l], rw_ln_b[l])
        h = mixer_residual(h, jnp.concatenate([dn_lat, rw_lat], axis=-1), mod_lat[2], w_out[l], mix_post_g[l])
        h = ffn_residual(h, mod_lat[3], mod_lat[4], mod_lat[5], ffn_pre_g[l], ffn_post_g[l],
                         w_ffn_in[l], w_ffn_out[l])
        if not last:
            h_ctx = mixer_residual(h_ctx, jnp.concatenate([dn_ctx, rw_ctx], axis=-1), mod_ctx[2],
                                   w_out[l], mix_post_g[l])
            h_ctx = ffn_residual(h_ctx, mod_ctx[3], mod_ctx[4], mod_ctx[5], ffn_pre_g[l],
                                 ffn_post_g[l], w_ffn_in[l], w_ffn_out[l])
    return h
```

```python
import numpy as np
from contextlib import ExitStack
import concourse.bass as bass
import concourse.mybir as mybir
from concourse.bass_utils import run_bass_kernel_spmd

F32 = mybir.dt.float32
BF16 = mybir.dt.bfloat16
AF = mybir.ActivationFunctionType
ALU = mybir.AluOpType
AX = mybir.AxisListType

ENGS = ['tensor', 'vector', 'scalar', 'gpsimd', 'sync']
NDS = 12


class Buf:
    __slots__ = ('w', 'r', 'name')

    def __init__(self, name=''):
        self.w = None
        self.r = []
        self.name = name


class View:
    __slots__ = ('ap', 'bufs')

    def __init__(self, ap, bufs):
        self.ap = ap
        self.bufs = bufs

    def __getitem__(self, idx):
        return View(self.ap[idx], self.bufs)

    def rr(self, s, **kw):
        return View(self.ap.rearrange(s, **kw), self.bufs)

    def bc(self, shape):
        return View(self.ap.broadcast_to(shape), self.bufs)

    def un(self, axis):
        return View(self.ap.unsqueeze(axis), self.bufs)

    def tr(self, perm):
        return View(self.ap.transpose(perm), self.bufs)

    def cast(self, dt):
        return View(self.ap.bitcast(dt), self.bufs)


class Tile:
    def __init__(self, handle, name='', keys=None):
        self.h = handle
        self.name = name
        if keys is None:
            self.bufs = {None: Buf(name)}
        else:
            self.bufs = {k: Buf(f'{name}.{k}') for k in keys}

    def __getitem__(self, idx):
        return View(self.h[idx], list(self.bufs.values()))

    def k(self, key, idx=None):
        b = [self.bufs[key]]
        if idx is None:
            return View(self.h[:], b)
        return View(self.h[idx], b)


class Sched:
    def __init__(self, nc, es):
        self.nc = nc
        self.es = es
        self.q = {e: [] for e in ENGS}
        self.sems = {}
        for e in ENGS:
            self.sems[e] = es.enter_context(nc.semaphore(f'cs_{e}'))
        self.cnt = {e: 0 for e in ENGS}
        self.waited = {e: {} for e in ENGS}
        self.dslots = {}
        self.dval = {}
        self.dnext = {}
        for e in ['sync', 'gpsimd', 'scalar']:
            self.dslots[e] = []
            for i in range(NDS):
                key = ('d', e, i)
                self.sems[key] = es.enter_context(nc.semaphore(f'ds_{e}{i}'))
                self.dval[key] = 0
                self.dslots[e].append(key)
            self.dnext[e] = 0
        self.same_engine_sync = {'tensor': False, 'vector': True, 'scalar': True,
                                 'gpsimd': True, 'sync': False}
        self.ninstr = 0

    def _uniq(self, name):
        self._u = getattr(self, '_u', 0) + 1
        return f'{name}_{self._u}'

    def sb(self, name, shape, dt=F32, keys=None):
        h = self.es.enter_context(self.nc.sbuf_tensor(self._uniq('sb_' + name), list(shape), dt))
        return Tile(h, name, keys)

    def ps(self, name, shape, dt=F32, keys=None):
        h = self.es.enter_context(self.nc.psum_tensor(self._uniq('ps_' + name), list(shape), dt))
        return Tile(h, name, keys)

    def dram(self, name, shape, dt=F32, kind="Internal", keys=None):
        h = self.nc.dram_tensor(name, list(shape), dt, kind=kind)
        return Tile(h, name, keys)

    def _waits(self, e, deps):
        need = {}
        for (key, val) in deps:
            if key == e and not self.same_engine_sync[e]:
                continue
            if self.waited[e].get(key, 0) >= val:
                continue
            if need.get(key, 0) < val:
                need[key] = val
        for key, val in need.items():
            self.waited[e][key] = val
        return list(need.items())

    def _deps(self, reads, writes):
        deps = []
        for v in reads:
            for b in v.bufs:
                if b.w is not None:
                    deps.append(b.w)
        for v in writes:
            for b in v.bufs:
                if b.w is not None:
                    deps.append(b.w)
                deps.extend(b.r)
        return deps

    def _commit(self, tok, reads, writes):
        for v in reads:
            for b in v.bufs:
                b.r.append(tok)
                if len(b.r) > 64:
                    m = {}
                    for k_, v_ in b.r:
                        if m.get(k_, 0) < v_:
                            m[k_] = v_
                    b.r = list(m.items())
        for v in writes:
            for b in v.bufs:
                b.w = tok
                b.r = []

    def op(self, e, fn, reads=(), writes=()):
        deps = self._deps(reads, writes)
        waits = self._waits(e, deps)
        self.cnt[e] += 1
        tok = (e, self.cnt[e])
        self.q[e].append((waits, fn, e, 1))
        self._commit(tok, reads, writes)
        self.ninstr += 1
        return tok

    def dma(self, e, fn, reads=(), writes=()):
        i = self.dnext[e]
        self.dnext[e] = (i + 1) % NDS
        key = self.dslots[e][i]
        deps = self._deps(reads, writes)
        if self.dval[key] > 0:
            deps.append((key, self.dval[key]))
        waits = self._waits(e, deps)
        self.dval[key] += 16
        tok = (key, self.dval[key])
        self.q[e].append((waits, fn, key, 16))
        self._commit(tok, reads, writes)
        self.ninstr += 1
        return tok

    def finish(self):
        waits = []
        for key, val in self.dval.items():
            if val > 0:
                waits.append((key, val))
        for e in ENGS:
            if e != 'sync' and self.cnt[e] > 0:
                waits.append((e, self.cnt[e]))
        self.q['sync'].append((waits, None, None, 0))

    def emit(self):
        nc = self.nc
        sems = self.sems
        q = self.q
        with nc.Block() as block:
            def run(eng, items):
                for waits, fn, skey, inc in items:
                    for key, val in waits:
                        eng.wait_ge(sems[key], val)
                    if fn is not None:
                        ins = fn(eng)
                        ins.then_inc(sems[skey], inc)

            @block.sync
            def _(eng):
                run(eng, q['sync'])

            @block.tensor
            def _(eng):
                run(eng, q['tensor'])

            @block.vector
            def _(eng):
                run(eng, q['vector'])

            @block.scalar
            def _(eng):
                run(eng, q['scalar'])

            @block.gpsimd
            def _(eng):
                run(eng, q['gpsimd'])

    def mm(self, out, lhsT, rhs, start=True, stop=True):
        rd = [lhsT, rhs] + ([] if start else [out])
        return self.op('tensor', lambda e: e.matmul(out.ap, lhsT.ap, rhs.ap, start=start, stop=stop),
                       rd, [out])

    def transpose(self, out, in_, ident):
        return self.op('tensor', lambda e: e.transpose(out.ap, in_.ap, ident.ap), [in_, ident], [out])

    def act(self, out, in_, func, bias=None, scale=None, eng='scalar'):
        rd = [in_]
        kw = {}
        if bias is not None:
            if isinstance(bias, View):
                rd.append(bias)
                kw['bias'] = bias.ap
            else:
                kw['bias'] = bias
        if scale is not None:
            if isinstance(scale, View):
                rd.append(scale)
                kw['scale'] = scale.ap
            else:
                kw['scale'] = scale
        return self.op('scalar', lambda e: e.activation(out.ap, in_.ap, func, **kw), rd, [out])

    def tt(self, out, in0, in1, op, eng='vector'):
        return self.op(eng, lambda e: e.tensor_tensor(out.ap, in0.ap, in1.ap, op), [in0, in1], [out])

    def ts(self, out, in0, s1, op0, s2=None, op1=None, eng='vector'):
        rd = [in0]
        a1 = s1
        if isinstance(s1, View):
            rd.append(s1)
            a1 = s1.ap
        a2 = s2
        if isinstance(s2, View):
            rd.append(s2)
            a2 = s2.ap
        if op1 is None:
            return self.op(eng, lambda e: e.tensor_scalar(out.ap, in0.ap, a1, None, op0), rd, [out])
        return self.op(eng, lambda e: e.tensor_scalar(out.ap, in0.ap, a1, a2, op0, op1), rd, [out])

    def stt(self, out, in0, scalar, in1, op0, op1, eng='vector'):
        rd = [in0, in1]
        a = scalar
        if isinstance(scalar, View):
            rd.append(scalar)
            a = scalar.ap
        return self.op(eng, lambda e: e.scalar_tensor_tensor(out.ap, in0.ap, a, in1.ap, op0, op1), rd, [out])

    def copy(self, out, in_, eng='vector'):
        if eng == 'scalar':
            return self.op(eng, lambda e: e.copy(out.ap, in_.ap), [in_], [out])
        return self.op(eng, lambda e: e.tensor_copy(out.ap, in_.ap), [in_], [out])

    def memset(self, out, val, eng='vector'):
        return self.op(eng, lambda e: e.memset(out.ap, val), [], [out])

    def recip(self, out, in_):
        return self.op('vector', lambda e: e.reciprocal(out.ap, in_.ap), [in_], [out])

    def scan(self, out, d0, d1, init, op0, op1):
        rd = [d0, d1]
        a = init
        if isinstance(init, View):
            rd.append(init)
            a = init.ap
        return self.op('vector', lambda e: e.tensor_tensor_scan(out.ap, d0.ap, d1.ap, a, op0, op1), rd, [out])

    def load(self, out, in_, eng='sync', **kw):
        return self.dma(eng, lambda e: e.dma_start(out=out.ap, in_=in_.ap, **kw), [in_], [out])


D = 2048
KC = 16
SEQ = 4096
CTX = 256
TT = SEQ + CTX
NTC = 1088
HALF = 544
FH = 5632
HC = 44
EPS = 1e-6


def new_nc():
    return bass.Bass("TRN2", target_bir_lowering=False)


MN = 1536


def build_M():
    nc = new_nc()
    ccT_d = nc.dram_tensor("ccT", [128, KC, 3], F32, kind="ExternalInput")
    wm_d = nc.dram_tensor("wm", [4, 128, KC, MN], F32, kind="ExternalInput")
    bm_d = nc.dram_tensor("bm", [3, 4, MN], F32, kind="ExternalInput")
    out_d = nc.dram_tensor("mo", [3, 4, MN], F32, kind="ExternalOutput")
    with ExitStack() as es:
        S = Sched(nc, es)
        ccT = Tile(ccT_d); wm = Tile(wm_d); bm = Tile(bm_d); out = Tile(out_d)
        cc = S.sb('cc', [128, KC, 3])
        sc = S.sb('sc', [128, KC, 3])
        bmt = S.sb('bmt', [3, 4, MN])
        res = S.sb('res', [3, 4, MN])
        wts = [S.sb(f'w{i}', [128, 4, 512]) for i in range(3)]
        pss = [S.ps(f'ps{i}', [128, 512]) for i in range(2)]
        S.load(cc[:], ccT[:])
        S.load(bmt[:], bm[:])
        S.act(sc[:], cc[:], AF.Silu)
        it = 0
        for l in range(4):
            for nt in range(MN // 512):
                ps = pss[(l * 3 + nt) % 2]
                for kq in range(KC // 4):
                    w = wts[it % 3]
                    it += 1
                    S.load(w[:], wm[l, :, kq * 4:(kq + 1) * 4, nt * 512:(nt + 1) * 512])
                    for k4 in range(4):
                        kc = kq * 4 + k4
                        S.mm(ps[0:3, :], sc[:, kc, :], w[:, k4, :], start=(kc == 0), stop=(kc == KC - 1))
                S.tt(res[:, l, nt * 512:(nt + 1) * 512], ps[0:3, :], bmt[:, l, nt * 512:(nt + 1) * 512], ALU.add)
        S.load(out[:], res[:])
        S.finish()
        S.emit()
    return nc


C_VECS = ['mix_post_g', 'gmix_lat', 'gmix_ctx', 'ffn_pre_g', 'fsh_lat', 'fsh_ctx', 'fsc_lat', 'fsc_ctx',
          'ffn_post_g', 'gffn_lat', 'gffn_ctx']
NPC = len(C_VECS) * KC


def build_C():
    nc = new_nc()
    hT_d = nc.dram_tensor("hT", [D, NTC], F32, kind="ExternalInput")
    oT_d = nc.dram_tensor("oT", [D, NTC], F32, kind="ExternalInput")
    wout_d = nc.dram_tensor("wout", [KC, 128, KC, 128], F32, kind="ExternalInput")
    wfi_d = nc.dram_tensor("wfi", [2 * HC, 128, KC, 128], F32, kind="ExternalInput")
    wfo_d = nc.dram_tensor("wfo", [KC, 128, HC, 128], F32, kind="ExternalInput")
    prm_d = nc.dram_tensor("prm", [128, NPC], F32, kind="ExternalInput")
    hn_d = nc.dram_tensor("hn", [D, NTC], F32, kind="ExternalOutput")
    with ExitStack() as es:
        S = Sched(nc, es)
        hT = Tile(hT_d); oT = Tile(oT_d); wout = Tile(wout_d); wfi = Tile(wfi_d); wfo = Tile(wfo_d)
        prm_dr = Tile(prm_d); hn = Tile(hn_d)
        prm = S.sb('prm', [128, NPC])
        S.load(prm[:], prm_dr[:])

        def pv(name):
            i = C_VECS.index(name)
            return prm[:, i * KC:(i + 1) * KC]
        ones = S.sb('ones', [128, 128])
        S.memset(ones[:], 1.0)
        der = S.sb('der', [128, 6, KC])
        S.tt(der[:, 0, :], pv('mix_post_g'), pv('gmix_lat'), ALU.mult)
        S.tt(der[:, 1, :], pv('mix_post_g'), pv('gmix_ctx'), ALU.mult)
        S.tt(der[:, 2, :], pv('ffn_post_g'), pv('gffn_lat'), ALU.mult)
        S.tt(der[:, 3, :], pv('ffn_post_g'), pv('gffn_ctx'), ALU.mult)
        S.stt(der[:, 4, :], pv('fsc_lat'), 1.0, pv('ffn_pre_g'), ALU.add, ALU.mult)
        S.stt(der[:, 5, :], pv('fsc_ctx'), 1.0, pv('ffn_pre_g'), ALU.add, ALU.mult)
        pg = {'lat': der[:, 0, :], 'ctx': der[:, 1, :]}
        fg = {'lat': der[:, 2, :], 'ctx': der[:, 3, :]}
        gs = {'lat': der[:, 4, :], 'ctx': der[:, 5, :]}
        fsh = {'lat': pv('fsh_lat'), 'ctx': pv('fsh_ctx')}

        H = S.sb('H', [128, KC, HALF])
        M = S.sb('M', [128, KC, HALF])
        OB = S.sb('OB', [128, KC, HALF], BF16)
        A = S.sb('A', [128, HC, HALF], BF16)
        rstd = S.sb('rstd', [128, HALF])
        sqt = [S.sb(f'sqt{i}', [128, 272]) for i in range(2)]
        t2 = [S.sb(f't2{i}', [128, HALF]) for i in range(2)]
        wA = [S.sb(f'wA{i}', [128, KC, 128], BF16) for i in range(4)]
        wB = [S.sb(f'wB{i}', [128, HC, 128], BF16) for i in range(2)]
        PS = [S.ps(f'ps{i}', [128, 512]) for i in range(8)]
        psi = [0]

        def nps():
            p = PS[psi[0] % 6]
            psi[0] += 1
            return p
        pstat = [PS[6], PS[7]]
        NT = [(0, 272), (272, 544)]
        hTv = hT[:].rr('(kc p) t -> p kc t', p=128)
        oTv = oT[:].rr('(kc p) t -> p kc t', p=128)
        hnv = hn[:].rr('(kc p) t -> p kc t', p=128)
        wai = [0]

        def stats(X, scale):
            for ni, (a, b) in enumerate(NT):
                ps = pstat[ni]
                for kc in range(KC):
                    sq = sqt[kc % 2]
                    S.act(sq[:, 0:b - a], X[:, kc, a:b], AF.Square)
                    S.mm(ps[:, 0:b - a], ones[:], sq[:, 0:b - a], start=(kc == 0), stop=(kc == KC - 1))
                S.act(rstd[:, a:b], ps[:, 0:b - a], AF.Sqrt, bias=EPS, scale=1.0 / D)
            S.recip(rstd[:], rstd[:])

        def residual(X, gvec, segs):
            for kc in range(KC):
                S.tt(X[:, kc, :], X[:, kc, :], rstd[:], ALU.mult)
                for (a, b, w) in segs:
                    S.stt(H[:, kc, a:b], X[:, kc, a:b], gvec[w][:, kc:kc + 1], H[:, kc, a:b], ALU.mult, ALU.add)

        for hf in range(2):
            t0 = hf * HALF
            segs = [(0, 64, 'ctx'), (64, HALF, 'lat')] if hf == 0 else [(0, HALF, 'lat')]
            S.load(H[:], hTv[:, :, t0:t0 + HALF])
            S.load(OB[:], oTv[:, :, t0:t0 + HALF], eng='gpsimd')
            for oc in range(KC):
                w = wA[wai[0] % 4]
                wai[0] += 1
                S.load(w[:], wout[oc], eng='gpsimd')
                for (a, b) in NT:
                    ps = nps()
                    for kc in range(KC):
                        S.mm(ps[:, 0:b - a], w[:, kc, :], OB[:, kc, a:b], start=(kc == 0), stop=(kc == KC - 1))
                    S.copy(M[:, oc, a:b], ps[:, 0:b - a], eng='scalar')
            stats(M, 1.0)
            residual(M, pg, segs)
            stats(H, 1.0)
            for kc in range(KC):
                t = t2[kc % 2]
                S.tt(t[:], H[:, kc, :], rstd[:], ALU.mult)
                for (a, b, w_) in segs:
                    S.ts(OB[:, kc, a:b], t[:, a:b], gs[w_][:, kc:kc + 1], ALU.mult, fsh[w_][:, kc:kc + 1], ALU.add)
            for hc in range(HC):
                wg = wA[wai[0] % 4]
                wai[0] += 1
                wu = wA[wai[0] % 4]
                wai[0] += 1
                S.load(wg[:], wfi[hc], eng='gpsimd')
                S.load(wu[:], wfi[HC + hc], eng='gpsimd')
                for ni, (a, b) in enumerate(NT):
                    pg_ = nps()
                    pu_ = nps()
                    for kc in range(KC):
                        S.mm(pg_[:, 0:b - a], wg[:, kc, :], OB[:, kc, a:b], start=(kc == 0), stop=(kc == KC - 1))
                    for kc in range(KC):
                        S.mm(pu_[:, 0:b - a], wu[:, kc, :], OB[:, kc, a:b], start=(kc == 0), stop=(kc == KC - 1))
                    sg = sqt[ni]
                    S.act(sg[:, 0:b - a], pg_[:, 0:b - a], AF.Silu)
                    S.tt(A[:, hc, a:b], sg[:, 0:b - a], pu_[:, 0:b - a], ALU.mult)
            for oc in range(KC):
                w = wB[oc % 2]
                S.load(w[:], wfo[oc], eng='gpsimd')
                for (a, b) in NT:
                    ps = nps()
                    for hc in range(HC):
                        S.mm(ps[:, 0:b - a], w[:, hc, :], A[:, hc, a:b], start=(hc == 0), stop=(hc == HC - 1))
                    S.copy(M[:, oc, a:b], ps[:, 0:b - a], eng='scalar')
            stats(M, 1.0)
            residual(M, fg, segs)
            S.load(hnv[:, :, t0:t0 + HALF], H[:])
        S.finish()
        S.emit()
        print('C ninstr', S.ninstr)
    return nc

NCOL = 2216
C_Q, C_K, C_V, C_Z, C_G = 0, 256, 512, 768, 1024
C_RR, C_RK, C_RV, C_WD, C_AD, C_GD = 1032, 1288, 1544, 1800, 1928, 2056
PROJ_CHUNKS = [(i * 128, 128) for i in range(8)] + [(1024, 8)] + [(1032 + i * 128, 128) for i in range(9)] + [(2184, 32)]
RW_TILES = [(C_RR, 128), (C_RR + 128, 128), (C_RK, 128), (C_RK + 128, 128), (C_RV, 128), (C_RV + 128, 128),
            (C_WD, 128), (C_AD, 128), (C_GD, 128), (C_GD + 128, 32)]

A_PRM = [('mix_pre_g', 16), ('msh_lat', 16), ('msc_lat', 16), ('msh_ctx', 16), ('msc_ctx', 16),
         ('cw', 42), ('alog', 4), ('dtb', 4), ('outg', 1), ('mu', 10), ('w0', 4), ('a0', 4),
         ('kk', 2), ('ka', 2), ('rk', 2), ('lnw', 2), ('lnb', 2)]
A_OFF = {}
_o = 0
for _n, _w in A_PRM:
    A_OFF[_n] = (_o, _w)
    _o += _w
NPA = _o
CST_OFF = {'ident': (0, 128), 'LS': (128, 64), 'US': (192, 64), 'LI': (256, 64), 'UI': (320, 64),
           'blk': (384, 128), 'reset': (512, 512), 'm4': (1024, 4), 'm2': (1028, 2)}
NCST = 1030
NEGBIG = -60000.0
RW_LN_EPS_ = 64e-5


def make_cst():
    c = np.zeros((128, NCST), np.float32)
    c[:, 0:128] = np.eye(128)
    r = np.arange(64)[:, None]
    q = np.arange(64)[None, :]
    c[0:64, 128:192] = (r > q)
    c[0:64, 192:256] = (r < q)
    c[0:64, 256:320] = (r >= q)
    c[0:64, 320:384] = (r <= q)
    blk = np.zeros((128, 128), np.float32)
    blk[0:64, 0:64] = 1
    blk[64:, 64:] = 1
    c[:, 384:512] = blk
    rs = np.ones(512, np.float32)
    rs[::64] = 0
    c[:, 512:1024] = rs[None]
    p = np.arange(128)
    for ct in range(4):
        c[:, 1024 + ct] = (p % 4 == ct)
    for ct in range(2):
        c[:, 1028 + ct] = (p % 2 == ct)
    return c


class _Stop(Exception):
    pass


class Scope:
    stopped = False

    def __enter__(self):
        self.es = ExitStack()
        return self.es

    def __exit__(self, t, v, tb):
        self.es.close()
        if t is not None and issubclass(t, _Stop):
            Scope.stopped = True
            return True
        return False


def build_AB(stages=('A', 'DN', 'RW')):
    import os as _os2
    STOP = int(_os2.environ.get('DN_STOP', '0'))

    def chk(k):
        if STOP == k:
            raise _Stop()
    nc = new_nc()
    hT_d = nc.dram_tensor("hT", [D, TT], F32, kind="ExternalInput")
    win_d = nc.dram_tensor("win", [128, KC, NCOL], F32, kind="ExternalInput")
    prm_d = nc.dram_tensor("prm", [128, NPA], F32, kind="ExternalInput")
    wup_d = nc.dram_tensor("wup", [64, 2, 256], F32, kind="ExternalInput")
    aup_d = nc.dram_tensor("aup", [64, 2, 256], F32, kind="ExternalInput")
    gup_d = nc.dram_tensor("gup", [160, 256], F32, kind="ExternalInput")
    cst_d = nc.dram_tensor("cst", [128, NCST], F32, kind="ExternalInput")
    oT_d = nc.dram_tensor("oT", [512, TT], F32, kind="ExternalOutput")
    import os as _os
    pT_d = nc.dram_tensor("pT", [NCOL, TT], F32, kind=("ExternalOutput" if _os.environ.get("DBG_PT") else "Internal"))
    with ExitStack() as es0:
        S = Sched(nc, es0)
        hT = Tile(hT_d); win = Tile(win_d); prm_dr = Tile(prm_d); wup_dr = Tile(wup_d); aup_dr = Tile(aup_d)
        gup_dr = Tile(gup_d); cst_dr = Tile(cst_d); oT = Tile(oT_d)
        pT = Tile(pT_d, 'pT', keys=list(range(len(PROJ_CHUNKS))))

        def pTrows(r0, n):
            for ci, (c0, w) in enumerate(PROJ_CHUNKS):
                if c0 <= r0 and r0 + n <= c0 + w:
                    return View(pT.h[r0:r0 + n, :], [pT.bufs[ci]])
            raise ValueError((r0, n))

        prm = S.sb('prm', [128, NPA])
        cst = S.sb('cst', [128, NCST])
        S.load(prm[:], prm_dr[:])
        S.load(cst[:], cst_dr[:])

        def pv(name, j=None, n=1, np_=128):
            o, w = A_OFF[name]
            if j is None:
                return prm[0:np_, o:o + w]
            return prm[0:np_, o + j:o + j + n]

        def cv(name, np_=64):
            o, w = CST_OFF[name]
            return cst[0:np_, o:o + w]
        ones = S.sb('ones', [128, 128])
        S.memset(ones[:], 1.0)
        ident = cv('ident', 128)
        id64 = cst[0:64, 0:64]
        PS = [S.ps(f'b{i}', [128, 512]) for i in range(8)]
        psi = [0]

        def nb():
            p = PS[psi[0] % 7]
            psi[0] += 1
            return p

        def barrier():
            toks = [(e, S.cnt[e]) for e in ENGS if S.cnt[e] > 0 and e != 'sync']
            for key, val in S.dval.items():
                if val > 0:
                    toks.append((key, val))
            for e in ['tensor', 'vector', 'scalar', 'gpsimd', 'sync']:
                w = S._waits(e, toks)
                if w:
                    S.q[e].append((w, None, None, 0))

        SEGS = [(0, 256)] + [(256 + 512 * i, 512) for i in range(8)]
        CSEGS = [(0, 4)] + [(4 + 8 * i, 8) for i in range(8)]

        if 'A' in stages:
            with ExitStack() as es:
                S.es = es
                W = S.sb('W', [128, KC, NCOL], BF16)
                for kq in range(4):
                    S.load(W[:, kq * 4:(kq + 1) * 4, :], win[:, kq * 4:(kq + 1) * 4, :], eng='gpsimd')
                der = S.sb('derA', [128, 2, KC])
                S.stt(der[:, 0, :], pv('msc_lat'), 1.0, pv('mix_pre_g'), ALU.add, ALU.mult)
                S.stt(der[:, 1, :], pv('msc_ctx'), 1.0, pv('mix_pre_g'), ALU.add, ALU.mult)
                Hs = S.sb('Hseg', [128, KC, 512])
                U = S.sb('Useg', [128, KC, 512], BF16)
                sq = [S.sb(f'sqA{i}', [128, 512]) for i in range(2)]
                rstd = S.sb('rstdA', [128, 512])
                stg = [S.sb(f'stg{i}', [128, 512]) for i in range(4)]
                hTv = hT[:].rr('(kc p) t -> p kc t', p=128)
                si = 0
                for (t0, n) in SEGS:
                    isctx = (t0 == 0)
                    gsv = der[:, 1, :] if isctx else der[:, 0, :]
                    shv = pv('msh_ctx') if isctx else pv('msh_lat')
                    S.load(Hs[:, :, 0:n], hTv[:, :, t0:t0 + n])
                    ps = nb()
                    for kc in range(KC):
                        s_ = sq[kc % 2]
                        S.act(s_[:, 0:n], Hs[:, kc, 0:n], AF.Square)
                        S.mm(ps[:, 0:n], ones[:], s_[:, 0:n], start=(kc == 0), stop=(kc == KC - 1))
                    S.act(rstd[:, 0:n], ps[:, 0:n], AF.Sqrt, bias=EPS, scale=1.0 / D)
                    S.recip(rstd[:, 0:n], rstd[:, 0:n])
                    for kc in range(KC):
                        s_ = sq[kc % 2]
                        S.tt(s_[:, 0:n], Hs[:, kc, 0:n], rstd[:, 0:n], ALU.mult)
                        S.ts(U[:, kc, 0:n], s_[:, 0:n], gsv[:, kc:kc + 1], ALU.mult, shv[:, kc:kc + 1], ALU.add)
                    for ci, (c0, w) in enumerate(PROJ_CHUNKS):
                        ps = nb()
                        for kc in range(KC):
                            S.mm(ps[0:w, 0:n], W[:, kc, c0:c0 + w], U[:, kc, 0:n], start=(kc == 0), stop=(kc == KC - 1))
                        st = stg[si % 4]
                        si += 1
                        if si % 2 == 0:
                            S.copy(st[0:w, 0:n], ps[0:w, 0:n], eng='scalar')
                        else:
                            S.copy(st[0:w, 0:n], ps[0:w, 0:n], eng='vector')
                        S.load(View(pT.h[c0:c0 + w, t0:t0 + n], [pT.bufs[ci]]), st[0:w, 0:n])
                barrier()
            S.es = es0

        cder = S.sb('cder', [64, 5, 64])
        S.ts(cder[:, 0, :], cv('US'), NEGBIG, ALU.mult)
        S.ts(cder[:, 1, :], cv('LS'), NEGBIG, ALU.mult)
        S.ts(cder[:, 2, :], cv('UI'), -1.0, ALU.mult)
        S.ts(cder[:, 3, :], cv('LI'), -1.0, ALU.mult)
        S.stt(cder[:, 4, :], cv('LS'), -1.0, cv('US'), ALU.mult, ALU.subtract)
        negones = S.sb('negones', [64, 64])
        S.memset(negones[:], -1.0)
        NEGM = [cder[:, 0, :], cder[:, 1, :]]
        NEGMT = [cder[:, 1, :], cder[:, 0, :]]
        TRI = [cv('UI'), cv('LI')]
        NEGTRI = [cder[:, 2, :], cder[:, 3, :]]
        NOFFD = cder[:, 4, :]
        STRICT = [cv('LS'), cv('US')]
        STRICTT = [cv('US'), cv('LS')]
        INCLT = [cv('UI'), cv('LI')]

        def b3(v, nchk, axis, w=64):
            np_ = v.ap.shape[0]
            return v.un(axis).bc([np_, nchk, w])


        class Pool:
            def __init__(self, banks):
                self.b = banks
                self.i = 0

            def nb(self):
                p = self.b[self.i % len(self.b)]
                self.i += 1
                return p

        def drive(gens):
            alive = list(gens)
            while alive:
                for g in list(alive):
                    try:
                        next(g)
                    except StopIteration:
                        alive.remove(g)

        def inverse_g(pool, X, XT, Ybuf, YTbuf, TTm, nchk):
            n = nchk * 64
            S.tt(TTm[:, 0:n].rr('p (c i) -> p c i', i=64), XT[:, 0:n].rr('p (c i) -> p c i', i=64),
                 b3(id64, nchk, 1), ALU.add)
            Y, YT = X, XT
            for k in range(1, 6):
                pY = pool.nb()
                for ci in range(nchk):
                    cs = slice(ci * 64, ci * 64 + 64)
                    S.mm(pY[0:64, cs], YT[:, cs], Y[:, cs])
                if k < 5:
                    pYT = pool.nb()
                    for ci in range(nchk):
                        cs = slice(ci * 64, ci * 64 + 64)
                        S.mm(pYT[0:64, cs], Y[:, cs], YT[:, cs])
                Yn = Ybuf[k % 2]
                S.copy(Yn[:, 0:n], pY[0:64, 0:n], eng='scalar')
                if k < 5:
                    YTn = YTbuf[k % 2]
                    S.copy(YTn[:, 0:n], pYT[0:64, 0:n], eng='vector')
                yield
                pT_ = pool.nb()
                for ci in range(nchk):
                    cs = slice(ci * 64, ci * 64 + 64)
                    S.mm(pT_[0:64, cs], Yn[:, cs], TTm[:, cs])
                S.tt(TTm[:, 0:n], TTm[:, 0:n], pT_[0:64, 0:n], ALU.add)
                Y = Yn
                if k < 5:
                    YT = YTn
                yield

        NSEG = 17
        CSEG4 = [(4 * i, 4) for i in range(NSEG)]
        SEG4 = [(256 * i, 256) for i in range(NSEG)]
        ORDER = {0: list(range(NSEG)), 1: [0] + list(range(NSEG - 1, 0, -1))}
        misc = Pool([PS[6], PS[7]])

        if 'DN' in stages:
            for hh in range(2):
                with ExitStack() as es:
                    S.es = es
                    QT = S.sb('QT', [128, 68, 64])
                    KT = S.sb('KT', [128, 68, 64])
                    VT = S.sb('VT', [128, 68, 64])
                    OT = S.sb('OT', [128, 68, 64], keys=list(range(NSEG)))
                    sq = S.sb('sqD', [128, 512])
                    rs = S.sb('rsD', [128, 512])
                    BTA = S.sb('BTA', [64, 2, 68])
                    G = S.sb('Gg', [64, 2, 68])
                    with ExitStack() as es2:
                        S.es = es2
                        RAW = S.sb('RAW', [128, TT])

                        def conv(r0, dst, cwj):
                            cw = pv('cw', cwj * 7, 7)
                            S.load(RAW[:], pTrows(r0, 128))
                            x = RAW[:, 0:256]
                            o = dst[:, 0:4, :].rr('p c i -> p (c i)')
                            S.ts(o, x, cw[:, 3:4], ALU.mult)
                            for k in range(7):
                                off = k - 3
                                if off == 0:
                                    continue
                                lo = max(0, -off)
                                hi = 256 - max(0, off)
                                S.stt(o[:, lo:hi], x[:, lo + off:hi + off], cw[:, k:k + 1], o[:, lo:hi], ALU.mult, ALU.add)
                            xl = RAW[:, 256:TT].rr('p (r c) -> p c r', c=64)
                            ol = dst[:, 4:68, :]
                            S.ts(ol, xl, cw[:, 3:4], ALU.mult)
                            for k in range(7):
                                off = k - 3
                                if off == 0:
                                    continue
                                c_ = cw[:, k:k + 1]
                                if off > 0:
                                    S.stt(ol[:, :, 0:64 - off], xl[:, :, off:64], c_, ol[:, :, 0:64 - off], ALU.mult, ALU.add)
                                    S.stt(ol[:, 0:63, 64 - off:64], xl[:, 1:64, 0:off], c_, ol[:, 0:63, 64 - off:64], ALU.mult, ALU.add)
                                else:
                                    o_ = -off
                                    S.stt(ol[:, :, o_:64], xl[:, :, 0:64 - o_], c_, ol[:, :, o_:64], ALU.mult, ALU.add)
                                    S.stt(ol[:, 1:64, 0:o_], xl[:, 0:63, 64 - o_:64], c_, ol[:, 1:64, 0:o_], ALU.mult, ALU.add)
                            df = dst[:].rr('p c i -> p (c i)')
                            S.act(df, df, AF.Silu)

                        def l2norm(X, mul):
                            Xf = X[:].rr('p c i -> p (c i)')
                            for (a, n) in SEGS:
                                S.act(sq[:, 0:n], Xf[:, a:a + n], AF.Square)
                                ps = misc.nb()
                                S.mm(ps[:, 0:n], ones[:], sq[:, 0:n])
                                S.act(rs[:, 0:n], ps[:, 0:n], AF.Sqrt, bias=EPS * mul, scale=float(mul))
                                S.recip(rs[:, 0:n], rs[:, 0:n])
                                S.tt(Xf[:, a:a + n], Xf[:, a:a + n], rs[:, 0:n], ALU.mult)

                        conv(C_Q + hh * 128, QT, 0 * 2 + hh)
                        l2norm(QT, 128.0)
                        conv(C_K + hh * 128, KT, 1 * 2 + hh)
                        l2norm(KT, 1.0)
                        conv(C_V + hh * 128, VT, 2 * 2 + hh)
                        GT = S.sb('GT', [64, 4, 68])
                        for j in range(4):
                            row = C_G + hh * 4 + j
                            gr = pTrows(row, 1)
                            S.load(GT[:, j, 4:68], View(gr.ap[:, 256:TT].rearrange('o (r c) -> (o r) c', c=64), gr.bufs))
                            S.load(GT[:, j, 0:4], View(gr.ap[:, 0:256].rearrange('o (c i) -> (o i) c', i=64), gr.bufs),
                                   allow_slow_non_contiguous=True)
                        gt1 = S.sb('gt1', [64, 2, 68])
                        gt2 = S.sb('gt2', [64, 2, 68])
                        gt3 = S.sb('gt3', [64, 2, 68])
                        nea = S.sb('nea', [64, 4])
                        S.act(nea[:], pv('alog', np_=64), AF.Exp)
                        S.ts(nea[:], nea[:], -1.0, ALU.mult)
                        S.act(BTA[:], GT[:, 0:2, :], AF.Sigmoid)
                        for d in range(2):
                            S.ts(gt1[:, d, :], GT[:, 2 + d, :], pv('dtb', hh * 2 + d, np_=64), ALU.add)
                        S.stt(gt2[:], gt1[:], -1.0, gt1[:], ALU.mult, ALU.max)
                        S.act(gt2[:], gt2[:], AF.Exp, scale=-1.0)
                        S.act(gt2[:], gt2[:], AF.Ln, bias=1.0)
                        S.ts(gt3[:], gt1[:], 0.0, ALU.max)
                        S.tt(gt3[:], gt3[:], gt2[:], ALU.add)
                        for d in range(2):
                            S.ts(G[:, d, :], gt3[:, d, :], nea[:, hh * 2 + d:hh * 2 + d + 1], ALU.mult)
                        barrier()
                    S.es = es
                    S.memset(OT[:], 0.0)
                    QTf = QT[:].rr('p c i -> p (c i)')

                    def mk_dset(d):
                        t = {}
                        for nm in ('GC', 'KDS', 'EGC', 'BEG', 't68'):
                            t[nm] = S.sb(f'{nm}{d}', [64, 68])
                        t['CD'] = S.sb(f'CD{d}', [128, 68])
                        for nm in ('GU', 'NBS', 'BI', 'NBST', 'Dm', 'DTm', 'X', 'XT', 'TTm', 'MT'):
                            t[nm] = S.sb(f'{nm}{d}', [64, 256])
                        t['Yb'] = [S.sb(f'Yb{d}{i}', [64, 256]) for i in range(2)]
                        t['YTb'] = [S.sb(f'YTb{d}{i}', [64, 256]) for i in range(2)]
                        for nm in ('EGB', 'QD', 'P1T'):
                            t[nm] = S.sb(f'{nm}{d}', [128, 256])
                        for nm in ('VB', 'KBG', 'KDC', 'P2'):
                            t[nm] = S.sb(f'{nm}{d}', [64, 4, 128])
                        t['Um'] = S.sb(f'Um{d}', [64, 128])
                        t['Hst'] = S.sb(f'Hst{d}', [128, 128])
                        return t

                    def dn_dir(d, t, pool):
                        GC, KDS, EGC, BEG, t68, CD = t['GC'], t['KDS'], t['EGC'], t['BEG'], t['t68'], t['CD']
                        GU, NBS, BI, NBST, Dm, DTm, X, XT, TTm, MT = (t[k_] for k_ in ('GU', 'NBS', 'BI', 'NBST', 'Dm', 'DTm', 'X', 'XT', 'TTm', 'MT'))
                        EGB, QD, P1T, VB, KBG, KDC, P2, Um, Hst = (t[k_] for k_ in ('EGB', 'QD', 'P1T', 'VB', 'KBG', 'KDC', 'P2', 'Um', 'Hst'))
                        gT = G[:, d, :]
                        bT = BTA[:, d, :]
                        ps = pool.nb()
                        S.mm(ps[0:64, 0:68], TRI[d], gT)
                        S.copy(GC[:], ps[0:64, 0:68])
                        ps2 = pool.nb()
                        S.mm(ps2[:, 0:68], ones[0:64, :], gT)
                        S.act(CD[:], ps2[:, 0:68], AF.Exp)
                        S.tt(t68[:], ps2[0:64, 0:68], GC[:], ALU.subtract)
                        S.act(KDS[:], t68[:], AF.Exp)
                        S.act(EGC[:], GC[:], AF.Exp)
                        S.tt(BEG[:], bT, EGC[:], ALU.mult)
                        S.memset(Hst[:], 0.0)
                        yield
                        for sidx in ORDER[d]:
                            c0, nchk = CSEG4[sidx]
                            n = nchk * 64
                            col0 = c0 * 64
                            gTs = G[:, d, c0:c0 + nchk]
                            bTs = BTA[:, d, c0:c0 + nchk]

                            def v3(tl):
                                return tl[:, 0:n].rr('p (c i) -> p c i', i=64)
                            S.tt(v3(GU), b3(gTs, nchk, 2), b3(TRI[d], nchk, 1), ALU.mult)
                            S.tt(v3(NBS), b3(bTs, nchk, 2), b3(NOFFD, nchk, 1), ALU.mult)
                            S.tt(v3(BI), b3(bTs, nchk, 2), b3(id64, nchk, 1), ALU.mult)
                            yield
                            pa = pool.nb()
                            S.mm(pa[:, 0:n], ones[0:64, :], GU[:, 0:n])
                            S.act(EGB[:, 0:n], pa[:, 0:n], AF.Exp)
                            S.tt(QD[:, 0:n], QTf[:, col0:col0 + n], EGB[:, 0:n], ALU.mult)
                            pb = pool.nb()
                            S.mm(pb[0:64, 0:n], TRI[d], b3(gTs, nchk, 2), start=True, stop=False)
                            S.mm(pb[0:64, 0:n], negones[:], GU[:, 0:n], start=False, stop=False)
                            S.mm(pb[0:64, 0:n], id64, b3(NEGM[d], nchk, 1), start=False, stop=True)
                            S.act(Dm[:, 0:n], pb[0:64, 0:n], AF.Exp)
                            yield
                            pc = pool.nb()
                            S.mm(pc[0:64, 0:n], ones[0:64, 0:64], GU[:, 0:n], start=True, stop=False)
                            S.mm(pc[0:64, 0:n], NEGTRI[d], b3(gTs, nchk, 2), start=False, stop=False)
                            S.mm(pc[0:64, 0:n], id64, b3(NEGMT[d], nchk, 1), start=False, stop=True)
                            S.act(DTm[:, 0:n], pc[0:64, 0:n], AF.Exp)
                            pe_ = pool.nb()
                            S.mm(pe_[0:64, 0:n], ones[0:64, 0:64], BI[:, 0:n])
                            S.tt(v3(NBST), pe_[0:64, 0:n].rr('p (c i) -> p c i', i=64), b3(NOFFD, nchk, 1), ALU.mult)
                            yield
                            pd = pool.nb()
                            for ci in range(nchk):
                                c = c0 + ci
                                S.mm(pd[0:64, ci * 64:ci * 64 + 64], KT[:, c, :], KT[:, c, :])
                            S.tt(X[:, 0:n], pd[0:64, 0:n], Dm[:, 0:n], ALU.mult)
                            S.tt(X[:, 0:n], X[:, 0:n], NBS[:, 0:n], ALU.mult)
                            S.tt(XT[:, 0:n], pd[0:64, 0:n], DTm[:, 0:n], ALU.mult)
                            S.tt(XT[:, 0:n], XT[:, 0:n], NBST[:, 0:n], ALU.mult)
                            yield
                            yield from inverse_g(pool, X, XT, t['Yb'], t['YTb'], TTm, nchk)
                            pv_ = pool.nb()
                            pk_ = pool.nb()
                            for cj in range(4):
                                c = c0 + cj
                                S.transpose(pv_[0:64, cj * 128:cj * 128 + 128], VT[:, c, :], ident)
                                S.transpose(pk_[0:64, cj * 128:cj * 128 + 128], KT[:, c, :], ident)
                            pv3 = pv_[0:64, :].rr('p (c k) -> p c k', k=128)
                            pk3 = pk_[0:64, :].rr('p (c k) -> p c k', k=128)
                            S.tt(VB[:], pv3, b3(BTA[:, d, c0:c0 + 4], 4, 2, 128), ALU.mult)
                            S.tt(KBG[:], pk3, b3(BEG[:, c0:c0 + 4], 4, 2, 128), ALU.mult)
                            S.tt(KDC[:], pk3, b3(KDS[:, c0:c0 + 4], 4, 2, 128), ALU.mult)
                            yield
                            pu = pool.nb()
                            for ci in range(4):
                                S.mm(pu[0:64, ci * 128:ci * 128 + 128], TTm[:, ci * 64:ci * 64 + 64], VB[:, ci, :])
                            S.copy(P2[:], pu[0:64, :].rr('p (c k) -> p c k', k=128), eng='scalar')
                            pw = pool.nb()
                            for ci in range(nchk):
                                S.mm(pw[:, ci * 64:ci * 64 + 64], KBG[:, ci, :], TTm[:, ci * 64:ci * 64 + 64])
                            S.act(P1T[:, 0:n], pw[:, 0:n], AF.Copy, scale=-1.0)
                            pm = pool.nb()
                            for ci in range(nchk):
                                c = c0 + ci
                                S.mm(pm[0:64, ci * 64:ci * 64 + 64], KT[:, c, :], QT[:, c, :])
                            S.tt(MT[:, 0:n], pm[0:64, 0:n], DTm[:, 0:n], ALU.mult)
                            yield
                            corder = list(range(nchk)) if d == 0 else list(range(nchk - 1, -1, -1))
                            for ci in corder:
                                c = c0 + ci
                                cs = slice(ci * 64, ci * 64 + 64)
                                p1 = pool.nb()
                                S.mm(p1[0:64, 0:128], P1T[:, cs], Hst[:])
                                S.tt(Um[:], p1[0:64, 0:128], P2[:, ci, :], ALU.add)
                                yield
                                p2 = pool.nb()
                                S.mm(p2[:, 0:64], Hst[:], QD[:, cs], start=True, stop=False)
                                S.mm(p2[:, 0:64], Um[:], MT[:, cs], start=False, stop=True)
                                ov = OT.k(sidx, (slice(None), c, slice(None)))
                                S.tt(ov, p2[:, 0:64], ov, ALU.add)
                                p3 = pool.nb()
                                S.mm(p3[:, 0:128], KDC[:, ci, :], Um[:])
                                S.stt(Hst[:], Hst[:], CD[:, c:c + 1], p3[:, 0:128], ALU.mult, ALU.add)
                                yield

                    dsets = [mk_dset(0), mk_dset(1)]
                    drive([dn_dir(0, dsets[0], Pool(PS[0:3])), dn_dir(1, dsets[1], Pool(PS[3:6]))])
                    OTf = OT[:].rr('p c i -> p (c i)')
                    for (a, n) in SEGS:
                        S.act(sq[:, 0:n], OTf[:, a:a + n], AF.Square)
                        ps = misc.nb()
                        S.mm(ps[:, 0:n], ones[:], sq[:, 0:n])
                        S.act(rs[:, 0:n], ps[:, 0:n], AF.Sqrt, bias=EPS, scale=1.0 / 128)
                        S.recip(rs[:, 0:n], rs[:, 0:n])
                        S.tt(OTf[:, a:a + n], OTf[:, a:a + n], rs[:, 0:n], ALU.mult)
                    ZR = QT[:].rr('p c i -> p (c i)')
                    S.load(ZR, pTrows(C_Z + hh * 128, 128))
                    S.act(ZR, ZR, AF.Silu)
                    S.ts(OTf, OTf, pv('outg', 0), ALU.mult)
                    RES = VT[:].rr('p c i -> p (c i)')
                    S.tt(RES[:, 0:256], OTf[:, 0:256], ZR[:, 0:256], ALU.mult)
                    S.tt(RES[:, 256:TT].rr('p (r c) -> p c r', c=64), OT[:, 4:68, :],
                         ZR[:, 256:TT].rr('p (r c) -> p c r', c=64), ALU.mult)
                    S.load(oT[hh * 128:(hh + 1) * 128, :], RES)
                    barrier()
                S.es = es0

        if 'RW' in stages:
            with ExitStack() as es:
                S.es = es
                RAWs = [S.sb(f'RAWr{i}', [128, TT]) for i in range(2)]
                OUTs = [S.sb(f'OUTr{i}', [128, TT]) for i in range(2)]
                c0m = S.sb('c0m', [128, 10])
                m4mu = S.sb('m4mu', [128, 10, 4])
                m2mu = S.sb('m2mu', [128, 10, 2])
                S.ts(c0m[:], pv('mu'), -1.0, ALU.mult, 1.0, ALU.add)
                S.tt(m4mu[:], pv('mu').un(2).bc([128, 10, 4]), cv('m4', 128).un(1).bc([128, 10, 4]), ALU.mult)
                S.tt(m2mu[:], pv('mu').un(2).bc([128, 10, 2]), cv('m2', 128).un(1).bc([128, 10, 2]), ALU.mult)
                for ti, (r0, np_) in enumerate(RW_TILES):
                    RAW = RAWs[ti % 2]
                    OUT = OUTs[ti % 2]
                    ve = 'vector'
                    S.load(RAW[0:np_, :], pTrows(r0, np_))
                    x = RAW[0:np_, 0:256]
                    o = OUT[0:np_, 0:256]
                    S.ts(o, x, c0m[0:np_, ti:ti + 1], ALU.mult, eng=ve)
                    S.stt(o[:, 1:256], x[:, 0:255], m2mu[0:np_, ti, 0:1], o[:, 1:256], ALU.mult, ALU.add, eng=ve)
                    S.stt(o[:, 0:255], x[:, 1:256], m2mu[0:np_, ti, 1:2], o[:, 0:255], ALU.mult, ALU.add, eng=ve)
                    xl = RAW[0:np_, 256:TT].rr('p (r c) -> p r c', c=64)
                    ol = OUT[0:np_, 256:TT].rr('p (r c) -> p r c', c=64)
                    S.ts(ol, xl, c0m[0:np_, ti:ti + 1], ALU.mult, eng=ve)
                    S.stt(ol[:, :, 1:64], xl[:, :, 0:63], m4mu[0:np_, ti, 0:1], ol[:, :, 1:64], ALU.mult, ALU.add, eng=ve)
                    S.stt(ol[:, :, 0:63], xl[:, :, 1:64], m4mu[0:np_, ti, 1:2], ol[:, :, 0:63], ALU.mult, ALU.add, eng=ve)
                    S.stt(ol[:, 1:64, :], xl[:, 0:63, :], m4mu[0:np_, ti, 2:3], ol[:, 1:64, :], ALU.mult, ALU.add, eng=ve)
                    S.stt(ol[:, 0:63, :], xl[:, 1:64, :], m4mu[0:np_, ti, 3:4], ol[:, 0:63, :], ALU.mult, ALU.add, eng=ve)
                    if r0 == C_WD:
                        S.act(OUT[0:np_, :], OUT[0:np_, :], AF.Tanh)
                    if r0 >= C_GD:
                        S.act(OUT[0:np_, :], OUT[0:np_, :], AF.Sigmoid)
                    S.load(pTrows(r0, np_), OUT[0:np_, :])
                barrier()
            S.es = es0
            with ExitStack() as es:
                S.es = es
                WUP = S.sb('WUP', [64, 2, 256]); AUP = S.sb('AUP', [64, 2, 256])
                GUP0 = S.sb('GUP0', [128, 256]); GUP1 = S.sb('GUP1', [32, 256])
                S.load(WUP[:], wup_dr[:]); S.load(AUP[:], aup_dr[:])
                S.load(GUP0[:], gup_dr[0:128, :]); S.load(GUP1[:], gup_dr[128:160, :])
                omka = S.sb('omka', [128, 2])
                S.ts(omka[:], pv('ka'), -1.0, ALU.mult, 1.0, ALU.add)
                YT = S.sb('YTr', [128, TT], keys=list(range(NSEG)))
                BON = S.sb('BON', [128, TT], keys=list(range(NSEG)))
                GD0 = S.sb('GD0', [128, 256]); GD1 = S.sb('GD1', [32, 256])
                RESET = cv('reset', 128)
                BLK = cv('blk', 128)
                NQ = 256

                def mk_rset(d):
                    t = {}
                    names = ('R', 'Kp', 'V', 'LD', 'IC', 'KK', 'KD', 'PRE', 'LIN', 'LEX', 'EIN', 'EEX', 'ENI', 'ETL',
                             'RT', 'AT', 'KIC', 'BT_', 'KT_', 'BH', 'KH', 'P1T')
                    for nm in names:
                        t[nm] = S.sb(f'r{nm}{d}', [128, NQ])
                    t['WD'] = S.sb(f'rWD{d}', [64, NQ]); t['AD'] = S.sb(f'rAD{d}', [64, NQ])
                    for nm in ('ATK', 'BHK', 'KHK', 'VTK', 'P2'):
                        t[nm] = S.sb(f'r{nm}{d}', [64, 4, 128])
                    t['WC'] = S.sb(f'rWC{d}', [128, 4])
                    t['hd'] = []
                    for e in range(2):
                        t['hd'].append(dict(
                            X=S.sb(f'rX{d}{e}', [64, NQ]), XT=S.sb(f'rXT{d}{e}', [64, NQ]), AkT=S.sb(f'rAk{d}{e}', [64, NQ]),
                            MrbT=S.sb(f'rMb{d}{e}', [64, NQ]), MrkT=S.sb(f'rMk{d}{e}', [64, NQ]),
                            Yb=[S.sb(f'rY{d}{e}{i}', [64, NQ]) for i in range(2)],
                            YTb=[S.sb(f'rYT{d}{e}{i}', [64, NQ]) for i in range(2)],
                            TTm=S.sb(f'rTT{d}{e}', [64, NQ]), ZS=S.sb(f'rZS{d}{e}', [64, 4, 64])))
                    t['Um'] = S.sb(f'rUm{d}', [64, 128])
                    t['Hst'] = [S.sb(f'rHst{d}{i}', [128, 128]) for i in range(2)]
                    return t

                def rw_dir(hp, d, t, pool, pP1):
                    hc0 = hp * 128
                    (R, Kp, V, LD, IC, KK, KD, PRE, LIN, LEX, EIN, EEX, ENI, ETL, RT, AT, KIC, BT_, KT_, BH, KH, P1T) = (
                        t[k_] for k_ in ('R', 'Kp', 'V', 'LD', 'IC', 'KK', 'KD', 'PRE', 'LIN', 'LEX', 'EIN', 'EEX', 'ENI', 'ETL',
                                         'RT', 'AT', 'KIC', 'BT_', 'KT_', 'BH', 'KH', 'P1T'))
                    KX, SQ, RS, TMP, T1, T2 = PRE, LIN, LEX, EIN, EEX, ENI
                    WD, AD, ATK, BHK, KHK, VTK, P2, WC, hd, Um = (t[k_] for k_ in ('WD', 'AD', 'ATK', 'BHK', 'KHK', 'VTK', 'P2', 'WC', 'hd', 'Um'))
                    Hst = t['Hst'][hp]
                    S.memset(Hst[:], 0.0)
                    w0c = pv('w0', d * 2 + hp)
                    a0c = pv('a0', d * 2 + hp)
                    for sidx in ORDER[d]:
                        t0, n = SEG4[sidx]
                        nchk = 4

                        def v3(tl, np_=128):
                            return tl[0:np_, 0:n].rr('p (c i) -> p c i', i=64)
                        cols = slice(t0, t0 + n)
                        S.load(R[:, 0:n], View(pT.h[C_RR + hc0:C_RR + hc0 + 128, cols], [pT.bufs[9 + hp]]))
                        S.load(Kp[:, 0:n], View(pT.h[C_RK + hc0:C_RK + hc0 + 128, cols], [pT.bufs[11 + hp]]))
                        S.load(V[:, 0:n], View(pT.h[C_RV + hc0:C_RV + hc0 + 128, cols], [pT.bufs[13 + hp]]))
                        S.load(WD[:, 0:n], View(pT.h[C_WD + d * 64:C_WD + d * 64 + 64, cols], [pT.bufs[15]]))
                        S.load(AD[:, 0:n], View(pT.h[C_AD + d * 64:C_AD + d * 64 + 64, cols], [pT.bufs[16]]))
                        yield
                        p1 = pool.nb()
                        S.mm(p1[:, 0:n], WUP[:, d, hc0:hc0 + 128], WD[:, 0:n])
                        S.act(LD[:, 0:n], p1[:, 0:n], AF.Sigmoid, bias=w0c)
                        S.ts(LD[:, 0:n], LD[:, 0:n], -0.6065306597126334, ALU.mult, eng='gpsimd')
                        p2 = pool.nb()
                        S.mm(p2[:, 0:n], AUP[:, d, hc0:hc0 + 128], AD[:, 0:n])
                        S.act(IC[:, 0:n], p2[:, 0:n], AF.Sigmoid, bias=a0c)
                        S.ts(KX[:, 0:n], Kp[:, 0:n], pv('kk', hp), ALU.mult)
                        S.act(SQ[:, 0:n], KX[:, 0:n], AF.Square)
                        yield
                        p3 = pool.nb()
                        S.mm(p3[:, 0:n], BLK, SQ[:, 0:n])
                        S.act(RS[:, 0:n], p3[:, 0:n], AF.Sqrt, bias=EPS, scale=1.0)
                        S.recip(RS[:, 0:n], RS[:, 0:n])
                        S.tt(KK[:, 0:n], KX[:, 0:n], RS[:, 0:n], ALU.mult)
                        S.ts(TMP[:, 0:n], IC[:, 0:n], pv('ka', hp), ALU.mult, omka[:, hp:hp + 1], ALU.add)
                        S.tt(KD[:, 0:n], Kp[:, 0:n], TMP[:, 0:n], ALU.mult)
                        S.tt(T1[:, 0:n], R[:, 0:n], KD[:, 0:n], ALU.mult)
                        S.ts(T1[:, 0:n], T1[:, 0:n], pv('rk', hp), ALU.mult)
                        yield
                        p4 = pool.nb()
                        S.mm(p4[:, 0:n], BLK, T1[:, 0:n])
                        S.tt(T2[:, 0:n], p4[:, 0:n], V[:, 0:n], ALU.mult)
                        bv = BON.k(sidx, (slice(None), cols))
                        S.tt(bv, bv, T2[:, 0:n], ALU.add)
                        S.scan(PRE[:, 0:n], RESET[:, 0:n], LD[:, 0:n], 0.0, ALU.mult, ALU.add)
                        TOT = v3(PRE)[:, :, 63:64]
                        if d == 0:
                            LINv = PRE
                        else:
                            S.tt(v3(LIN), TOT.bc([128, nchk, 64]), v3(PRE), ALU.subtract)
                            S.tt(LIN[:, 0:n], LIN[:, 0:n], LD[:, 0:n], ALU.add)
                            LINv = LIN
                        S.tt(LEX[:, 0:n], LINv[:, 0:n], LD[:, 0:n], ALU.subtract, eng='gpsimd')
                        S.act(WC[:, 0:nchk], PRE[:, 0:n].rr('p (c i) -> p c i', i=64)[:, :, 63], AF.Exp)
                        S.tt(v3(ETL), TOT.bc([128, nchk, 64]), v3(LINv), ALU.subtract)
                        yield
                        S.act(ETL[:, 0:n], ETL[:, 0:n], AF.Exp)
                        S.act(EEX[:, 0:n], LEX[:, 0:n], AF.Exp)
                        S.act(ENI[:, 0:n], LINv[:, 0:n], AF.Exp, scale=-1.0)
                        S.act(EIN[:, 0:n], LINv[:, 0:n], AF.Exp)
                        S.tt(RT[:, 0:n], R[:, 0:n], EIN[:, 0:n], ALU.mult, eng='gpsimd')
                        S.stt(AT[:, 0:n], KK[:, 0:n], -1.0, EEX[:, 0:n], ALU.mult, ALU.mult)
                        S.tt(KIC[:, 0:n], KK[:, 0:n], IC[:, 0:n], ALU.mult, eng='gpsimd')
                        yield
                        S.tt(BT_[:, 0:n], KIC[:, 0:n], ENI[:, 0:n], ALU.mult)
                        S.tt(KT_[:, 0:n], KD[:, 0:n], ENI[:, 0:n], ALU.mult, eng='gpsimd')
                        S.tt(BH[:, 0:n], KIC[:, 0:n], ETL[:, 0:n], ALU.mult)
                        S.tt(KH[:, 0:n], KD[:, 0:n], ETL[:, 0:n], ALU.mult, eng='gpsimd')
                        yield
                        for qi, (src, dst) in enumerate(((AT, ATK), (BH, BHK), (KH, KHK), (V, VTK))):
                            pt_ = pool.nb()
                            for cj in range(4):
                                S.transpose(pt_[0:64, cj * 128:cj * 128 + 128], src[:, cj * 64:cj * 64 + 64], ident)
                            S.copy(dst[:], pt_[0:64, :].rr('p (c k) -> p c k', k=128),
                                   eng='scalar' if qi % 2 == 0 else 'vector')
                            if qi % 2 == 1:
                                yield
                        for e in range(2):
                            h_ = hd[e]
                            pe = slice(64 * e, 64 * e + 64)
                            for (lh, rh, dst, msk) in ((AT, BT_, h_['X'], STRICT[d]), (BT_, AT, h_['XT'], STRICTT[d]),
                                                      (KT_, AT, h_['AkT'], STRICTT[d]), (BT_, RT, h_['MrbT'], INCLT[d]),
                                                      (KT_, RT, h_['MrkT'], INCLT[d])):
                                pq = pool.nb()
                                for ci in range(nchk):
                                    cs = slice(ci * 64, ci * 64 + 64)
                                    S.mm(pq[0:64, cs], lh[pe, cs], rh[pe, cs])
                                S.tt(v3(dst, 64), pq[0:64, 0:n].rr('p (c i) -> p c i', i=64), b3(msk, nchk, 1), ALU.mult)
                            yield
                            yield from inverse_g(pool, h_['X'], h_['XT'], h_['Yb'], h_['YTb'], h_['TTm'], nchk)
                            TTm = h_['TTm']
                            for ci in range(nchk):
                                cs = slice(ci * 64, ci * 64 + 64)
                                S.mm(pP1[pe, cs], ATK[:, ci, pe], TTm[:, cs])
                            pz = pool.nb()
                            for ci in range(nchk):
                                cs = slice(ci * 64, ci * 64 + 64)
                                S.mm(pz[0:64, cs], h_['AkT'][:, cs], VTK[:, ci, pe])
                            S.copy(h_['ZS'][:], pz[0:64, 0:n].rr('p (c v) -> p c v', v=64), eng='scalar')
                            yield
                            pp2 = pool.nb()
                            for ci in range(nchk):
                                cs = slice(ci * 64, ci * 64 + 64)
                                S.mm(pp2[0:64, cs], TTm[:, cs], h_['ZS'][:, ci, :])
                            S.copy(P2[:, :, pe], pp2[0:64, 0:n].rr('p (c v) -> p c v', v=64), eng='vector')
                            yield
                        S.copy(P1T[:, 0:n], pP1[:, 0:n], eng='scalar')
                        yield
                        corder = list(range(nchk)) if d == 0 else list(range(nchk - 1, -1, -1))
                        for ci in corder:
                            cs = slice(ci * 64, ci * 64 + 64)
                            gcol = slice(t0 + ci * 64, t0 + ci * 64 + 64)
                            q1 = pool.nb()
                            S.mm(q1[0:64, 0:128], P1T[:, cs], Hst[:])
                            S.tt(Um[:], q1[0:64, 0:128], P2[:, ci, :], ALU.add)
                            yield
                            q2 = pool.nb()
                            S.mm(q2[:, 0:64], Hst[:], RT[:, cs], start=True, stop=False)
                            for e in range(2):
                                pe = slice(64 * e, 64 * e + 64)
                                S.mm(q2[pe, 0:64], Um[:, pe], hd[e]['MrbT'][:, cs], start=False, stop=False)
                                S.mm(q2[pe, 0:64], VTK[:, ci, pe], hd[e]['MrkT'][:, cs], start=False, stop=True)
                            yv = YT.k(sidx, (slice(None), gcol))
                            S.tt(yv, q2[:, 0:64], yv, ALU.add)
                            q3 = pool.nb()
                            S.mm(q3[:, 0:128], BHK[:, ci, :], Um[:], start=True, stop=False)
                            S.mm(q3[:, 0:128], KHK[:, ci, :], VTK[:, ci, :], start=False, stop=True)
                            for e in range(2):
                                pe = slice(64 * e, 64 * e + 64)
                                S.stt(Hst[pe, pe], Hst[pe, pe], WC[pe, ci:ci + 1], q3[pe, pe], ALU.mult, ALU.add)
                            yield

                rsets = [mk_rset(0), mk_rset(1)]
                R_ = rsets[0]['R']; Kp_ = rsets[0]['Kp']; V_ = rsets[0]['V']
                fpool = Pool(PS[0:6])
                for hp in range(2):
                    hc0 = hp * 128
                    S.memset(YT[:], 0.0)
                    S.memset(BON[:], 0.0)
                    drive([rw_dir(hp, 0, rsets[0], Pool(PS[0:3]), PS[6]), rw_dir(hp, 1, rsets[1], Pool(PS[3:6]), PS[7])])
                    for sidx, (t0, n) in enumerate(SEG4):
                        cols = slice(t0, t0 + n)
                        S.load(GD0[:, 0:n], View(pT.h[C_GD:C_GD + 128, cols], [pT.bufs[17]]))
                        S.load(GD1[:, 0:n], View(pT.h[C_GD + 128:C_GD + 160, cols], [pT.bufs[18]]))
                        yv = YT.k(sidx, (slice(None), cols))
                        f1 = fpool.nb()
                        S.mm(f1[:, 0:n], BLK, yv)
                        S.stt(R_[:, 0:n], f1[:, 0:n], -1.0 / 64, yv, ALU.mult, ALU.add)
                        S.act(Kp_[:, 0:n], R_[:, 0:n], AF.Square)
                        f2 = fpool.nb()
                        S.mm(f2[:, 0:n], BLK, Kp_[:, 0:n])
                        S.act(V_[:, 0:n], f2[:, 0:n], AF.Sqrt, bias=RW_LN_EPS_, scale=1.0 / 64)
                        S.recip(V_[:, 0:n], V_[:, 0:n])
                        S.tt(R_[:, 0:n], R_[:, 0:n], V_[:, 0:n], ALU.mult)
                        S.ts(R_[:, 0:n], R_[:, 0:n], pv('lnw', hp), ALU.mult, pv('lnb', hp), ALU.add)
                        S.tt(R_[:, 0:n], R_[:, 0:n], BON.k(sidx, (slice(None), cols)), ALU.add)
                        f3 = fpool.nb()
                        S.mm(f3[:, 0:n], GUP0[:, hc0:hc0 + 128], GD0[:, 0:n], start=True, stop=False)
                        S.mm(f3[:, 0:n], GUP1[:, hc0:hc0 + 128], GD1[:, 0:n], start=False, stop=True)
                        S.tt(yv, R_[:, 0:n], f3[:, 0:n], ALU.mult)
                    S.load(oT[256 + hc0:256 + hc0 + 128, :], YT[:])
                barrier()
            S.es = es0
        S.finish()
        S.emit()
        print('AB ninstr', S.ninstr, {e: S.cnt[e] for e in ENGS})
    return nc

DEPTH = 4
P_DN_ = 4128


def fm(v):
    return np.ascontiguousarray(np.asarray(v, np.float32).reshape(KC, 128).T)


def lay_w(w):
    K, N = w.shape
    return np.ascontiguousarray(w.reshape(K // 128, 128, N // 128, 128).transpose(2, 1, 0, 3))


def ab_cols(g):
    cols = []
    for t in range(4):
        cols += list(range(t * 1024 + 256 * g, t * 1024 + 256 * g + 256))
    for hh in range(2):
        for j in range(4):
            cols.append(4096 + j * 8 + 2 * g + hh)
    for t in range(3):
        cols += list(range(P_DN_ + t * 1024 + 256 * g, P_DN_ + t * 1024 + 256 * g + 256))
    cols += list(range(P_DN_ + 3072, P_DN_ + 3488))
    return np.array(cols)


def rep(x):
    return np.full((128,), x, np.float32)


def ab_prm(I, l, g, modv):
    P = np.zeros((128, NPA), np.float32)

    def put(name, j, col):
        o, w = A_OFF[name]
        P[:len(col), o + j] = col
    for name in ('mix_pre_g',):
        o, w = A_OFF[name]
        P[:, o:o + 16] = fm(I['mix_pre_g'][l])
    for name in ('msh_lat', 'msc_lat', 'msh_ctx', 'msc_ctx'):
        o, w = A_OFF[name]
        P[:, o:o + 16] = fm(modv[name])
    for t in range(3):
        for hh in range(2):
            ch0 = t * 1024 + (2 * g + hh) * 128
            for k in range(7):
                put('cw', (t * 2 + hh) * 7 + k, I['dn_conv'][l][k, ch0:ch0 + 128])
    for hh in range(2):
        for d in range(2):
            put('alog', hh * 2 + d, rep(I['dn_a_log'][l][d, 2 * g + hh]))
            put('dtb', hh * 2 + d, rep(I['dn_dt_bias'][l][d, 2 * g + hh]))
    put('outg', 0, I['dn_out_g'][l])
    mu = I['rw_mu'][l]
    ch0s = [256 * g, 256 * g + 128, 1024 + 256 * g, 1024 + 256 * g + 128, 2048 + 256 * g, 2048 + 256 * g + 128,
            3072, 3200, 3328, 3456]
    for ti, c0 in enumerate(ch0s):
        n = 32 if ti == 9 else 128
        put('mu', ti, mu[c0:c0 + n])
    for hp in range(2):
        c0 = 256 * g + 128 * hp
        for d in range(2):
            put('w0', d * 2 + hp, I['rw_w0'][l][d, c0:c0 + 128])
            put('a0', d * 2 + hp, I['rw_a0'][l][d, c0:c0 + 128])
        put('kk', hp, I['rw_k_k'][l][c0:c0 + 128])
        put('ka', hp, I['rw_k_a'][l][c0:c0 + 128])
        put('rk', hp, I['rw_r_k'][l].reshape(-1)[c0:c0 + 128])
        put('lnw', hp, I['rw_ln_w'][l][c0:c0 + 128])
        put('lnb', hp, I['rw_ln_b'][l][c0:c0 + 128])
    return P


def ab_inputs(I, l, b, g, hT_b, modv, cstv):
    cols = ab_cols(g)
    w = I['w_in'][l][:, cols]
    win = np.ascontiguousarray(w.reshape(KC, 128, NCOL).transpose(1, 0, 2))
    c0 = 256 * g
    return {
        'hT': hT_b, 'win': win, 'prm': ab_prm(I, l, g, modv),
        'wup': np.ascontiguousarray(I['rw_w_up'][l][:, :, c0:c0 + 256].transpose(1, 0, 2)),
        'aup': np.ascontiguousarray(I['rw_a_up'][l][:, :, c0:c0 + 256].transpose(1, 0, 2)),
        'gup': np.ascontiguousarray(I['rw_g_up'][l][:, c0:c0 + 256]),
        'cst': cstv,
    }


_NC_CACHE = {}


def get_nc(name):
    if name not in _NC_CACHE:
        _NC_CACHE[name] = {'M': build_M, 'AB': build_AB, 'C': build_C}[name]()
    return _NC_CACHE[name]


def kernel(**I):
    I = {k: np.asarray(v, np.float32) for k, v in I.items()}
    B = 2
    cc = np.stack([I['c'][0], I['c'][1], I['c_ctx']])
    ccT = np.ascontiguousarray(cc.reshape(3, KC, 128).transpose(2, 1, 0))
    in_maps = []
    for c in range(8):
        wm = I['w_mod'][:, :, c * MN:(c + 1) * MN]
        in_maps.append({'ccT': ccT,
                        'wm': np.ascontiguousarray(wm.reshape(DEPTH, KC, 128, MN).transpose(0, 2, 1, 3)),
                        'bm': np.ascontiguousarray(np.broadcast_to(I['b_mod'][None, :, c * MN:(c + 1) * MN], (3, DEPTH, MN)))})
    res = run_bass_kernel_spmd(get_nc('M'), in_maps, core_ids=list(range(8)))
    mod = np.concatenate([r['mo'] for r in res.results], axis=2)
    cstv = make_cst()
    hT = [np.ascontiguousarray(np.concatenate([I['ctx'][b], I['x'][b]], axis=0).T) for b in range(B)]
    for l in range(DEPTH):
        def mv(row, i):
            return mod[row, l, i * D:(i + 1) * D]
        in_maps = []
        for c in range(8):
            b, g = c // 4, c % 4
            modv = {'msh_lat': mv(b, 0), 'msc_lat': mv(b, 1), 'msh_ctx': mv(2, 0), 'msc_ctx': mv(2, 1)}
            in_maps.append(ab_inputs(I, l, b, g, hT[b], modv, cstv))
        res = run_bass_kernel_spmd(get_nc('AB'), in_maps, core_ids=list(range(8)))
        oT = []
        for b in range(B):
            parts = [res.results[b * 4 + g]['oT'] for g in range(4)]
            dn = np.concatenate([p[0:256] for p in parts], axis=0)
            rw = np.concatenate([p[256:512] for p in parts], axis=0)
            oT.append(np.concatenate([dn, rw], axis=0))
        wout = lay_w(I['w_out'][l])
        wfi = lay_w(I['w_ffn_in'][l])
        wfo = lay_w(I['w_ffn_out'][l])
        in_maps = []

        def tok(a, j):
            return np.ascontiguousarray(np.concatenate([a[:, 64 * j:64 * j + 64], a[:, 256 + 1024 * j:256 + 1024 * j + 1024]], axis=1))
        for c in range(8):
            b, j = c // 4, c % 4
            vec = {'mix_post_g': I['mix_post_g'][l], 'gmix_lat': mv(b, 2), 'gmix_ctx': mv(2, 2),
                   'ffn_pre_g': I['ffn_pre_g'][l], 'fsh_lat': mv(b, 3), 'fsh_ctx': mv(2, 3),
                   'fsc_lat': mv(b, 4), 'fsc_ctx': mv(2, 4), 'ffn_post_g': I['ffn_post_g'][l],
                   'gffn_lat': mv(b, 5), 'gffn_ctx': mv(2, 5)}
            prm = np.concatenate([fm(vec[k]) for k in C_VECS], axis=1)
            in_maps.append({'hT': tok(hT[b], j), 'oT': tok(oT[b], j), 'wout': wout, 'wfi': wfi, 'wfo': wfo, 'prm': prm})
        res = run_bass_kernel_spmd(get_nc('C'), in_maps, core_ids=list(range(8)))
        for b in range(B):
            hn = np.empty_like(hT[b])
            for j in range(4):
                r = res.results[b * 4 + j]['hn']
                hn[:, 64 * j:64 * j + 64] = r[:, 0:64]
                hn[:, 256 + 1024 * j:256 + 1024 * j + 1024] = r[:, 64:]
            hT[b] = hn
    out = np.stack([np.ascontiguousarray(hT[b][:, 256:].T) for b in range(B)])
    return out.astype(np.float32)
```

```python
import numpy as np
from contextlib import ExitStack
import concourse.bass as bass
import concourse.mybir as mybir
from concourse.bass_utils import run_bass_kernel_spmd

F32 = mybir.dt.float32
BF16 = mybir.dt.bfloat16
AF = mybir.ActivationFunctionType
ALU = mybir.AluOpType
AX = mybir.AxisListType

ENGS = ['tensor', 'vector', 'scalar', 'gpsimd', 'sync']
NDS = 12


class Buf:
    __slots__ = ('w', 'r', 'name')

    def __init__(self, name=''):
        self.w = None
        self.r = []
        self.name = name


class View:
    __slots__ = ('ap', 'bufs')

    def __init__(self, ap, bufs):
        self.ap = ap
        self.bufs = bufs

    def __getitem__(self, idx):
        return View(self.ap[idx], self.bufs)

    def rr(self, s, **kw):
        return View(self.ap.rearrange(s, **kw), self.bufs)

    def bc(self, shape):
        return View(self.ap.broadcast_to(shape), self.bufs)

    def un(self, axis):
        return View(self.ap.unsqueeze(axis), self.bufs)

    def tr(self, perm):
        return View(self.ap.transpose(perm), self.bufs)

    def cast(self, dt):
        return View(self.ap.bitcast(dt), self.bufs)


class Tile:
    def __init__(self, handle, name='', keys=None):
        self.h = handle
        self.name = name
        if keys is None:
            self.bufs = {None: Buf(name)}
        else:
            self.bufs = {k: Buf(f'{name}.{k}') for k in keys}

    def __getitem__(self, idx):
        return View(self.h[idx], list(self.bufs.values()))

    def k(self, key, idx=None):
        b = [self.bufs[key]]
        if idx is None:
            return View(self.h[:], b)
        return View(self.h[idx], b)


class Sched:
    def __init__(self, nc, es):
        self.nc = nc
        self.es = es
        self.q = {e: [] for e in ENGS}
        self.sems = {}
        for e in ENGS:
            self.sems[e] = es.enter_context(nc.semaphore(f'cs_{e}'))
        self.cnt = {e: 0 for e in ENGS}
        self.waited = {e: {} for e in ENGS}
        self.dslots = {}
        self.dval = {}
        self.dnext = {}
        for e in ['sync', 'gpsimd', 'scalar']:
            self.dslots[e] = []
            for i in range(NDS):
                key = ('d', e, i)
                self.sems[key] = es.enter_context(nc.semaphore(f'ds_{e}{i}'))
                self.dval[key] = 0
                self.dslots[e].append(key)
            self.dnext[e] = 0
        self.same_engine_sync = {'tensor': False, 'vector': True, 'scalar': True,
                                 'gpsimd': True, 'sync': False}
        self.ninstr = 0

    def _uniq(self, name):
        self._u = getattr(self, '_u', 0) + 1
        return f'{name}_{self._u}'

    def sb(self, name, shape, dt=F32, keys=None):
        h = self.es.enter_context(self.nc.sbuf_tensor(self._uniq('sb_' + name), list(shape), dt))
        return Tile(h, name, keys)

    def ps(self, name, shape, dt=F32, keys=None):
        h = self.es.enter_context(self.nc.psum_tensor(self._uniq('ps_' + name), list(shape), dt))
        return Tile(h, name, keys)

    def dram(self, name, shape, dt=F32, kind="Internal", keys=None):
        h = self.nc.dram_tensor(name, list(shape), dt, kind=kind)
        return Tile(h, name, keys)

    def _waits(self, e, deps):
        need = {}
        for (key, val) in deps:
            if key == e and not self.same_engine_sync[e]:
                continue
            if self.waited[e].get(key, 0) >= val:
                continue
            if need.get(key, 0) < val:
                need[key] = val
        for key, val in need.items():
            self.waited[e][key] = val
        return list(need.items())

    def _deps(self, reads, writes):
        deps = []
        for v in reads:
            for b in v.bufs:
                if b.w is not None:
                    deps.append(b.w)
        for v in writes:
            for b in v.bufs:
                if b.w is not None:
                    deps.append(b.w)
                deps.extend(b.r)
        return deps

    def _commit(self, tok, reads, writes):
        for v in reads:
            for b in v.bufs:
                b.r.append(tok)
                if len(b.r) > 64:
                    m = {}
                    for k_, v_ in b.r:
                        if m.get(k_, 0) < v_:
                            m[k_] = v_
                    b.r = list(m.items())
        for v in writes:
            for b in v.bufs:
                b.w = tok
                b.r = []

    def op(self, e, fn, reads=(), writes=()):
        deps = self._deps(reads, writes)
        waits = self._waits(e, deps)
        self.cnt[e] += 1
        tok = (e, self.cnt[e])
        self.q[e].append((waits, fn, e, 1))
        self._commit(tok, reads, writes)
        self.ninstr += 1
        return tok

    def dma(self, e, fn, reads=(), writes=()):
        i = self.dnext[e]
        self.dnext[e] = (i + 1) % NDS
        key = self.dslots[e][i]
        deps = self._deps(reads, writes)
        if self.dval[key] > 0:
            deps.append((key, self.dval[key]))
        waits = self._waits(e, deps)
        self.dval[key] += 16
        tok = (key, self.dval[key])
        self.q[e].append((waits, fn, key, 16))
        self._commit(tok, reads, writes)
        self.ninstr += 1
        return tok

    def finish(self):
        waits = []
        for key, val in self.dval.items():
            if val > 0:
                waits.append((key, val))
        for e in ENGS:
            if e != 'sync' and self.cnt[e] > 0:
                waits.append((e, self.cnt[e]))
        self.q['sync'].append((waits, None, None, 0))

    def emit(self):
        nc = self.nc
        sems = self.sems
        q = self.q
        with nc.Block() as block:
            def run(eng, items):
                for waits, fn, skey, inc in items:
                    for key, val in waits:
                        eng.wait_ge(sems[key], val)
                    if fn is not None:
                        ins = fn(eng)
                        ins.then_inc(sems[skey], inc)

            @block.sync
            def _(eng):
                run(eng, q['sync'])

            @block.tensor
            def _(eng):
                run(eng, q['tensor'])

            @block.vector
            def _(eng):
                run(eng, q['vector'])

            @block.scalar
            def _(eng):
                run(eng, q['scalar'])

            @block.gpsimd
            def _(eng):
                run(eng, q['gpsimd'])

    def mm(self, out, lhsT, rhs, start=True, stop=True):
        rd = [lhsT, rhs] + ([] if start else [out])
        return self.op('tensor', lambda e: e.matmul(out.ap, lhsT.ap, rhs.ap, start=start, stop=stop),
                       rd, [out])

    def transpose(self, out, in_, ident):
        return self.op('tensor', lambda e: e.transpose(out.ap, in_.ap, ident.ap), [in_, ident], [out])

    def act(self, out, in_, func, bias=None, scale=None, eng='scalar'):
        rd = [in_]
        kw = {}
        if bias is not None:
            if isinstance(bias, View):
                rd.append(bias)
                kw['bias'] = bias.ap
            else:
                kw['bias'] = bias
        if scale is not None:
            if isinstance(scale, View):
                rd.append(scale)
                kw['scale'] = scale.ap
            else:
                kw['scale'] = scale
        return self.op('scalar', lambda e: e.activation(out.ap, in_.ap, func, **kw), rd, [out])

    def tt(self, out, in0, in1, op, eng='vector'):
        return self.op(eng, lambda e: e.tensor_tensor(out.ap, in0.ap, in1.ap, op), [in0, in1], [out])

    def ts(self, out, in0, s1, op0, s2=None, op1=None, eng='vector'):
        rd = [in0]
        a1 = s1
        if isinstance(s1, View):
            rd.append(s1)
            a1 = s1.ap
        a2 = s2
        if isinstance(s2, View):
            rd.append(s2)
            a2 = s2.ap
        if op1 is None:
            return self.op(eng, lambda e: e.tensor_scalar(out.ap, in0.ap, a1, None, op0), rd, [out])
        return self.op(eng, lambda e: e.tensor_scalar(out.ap, in0.ap, a1, a2, op0, op1), rd, [out])

    def stt(self, out, in0, scalar, in1, op0, op1, eng='vector'):
        rd = [in0, in1]
        a = scalar
        if isinstance(scalar, View):
            rd.append(scalar)
            a = scalar.ap
        return self.op(eng, lambda e: e.scalar_tensor_tensor(out.ap, in0.ap, a, in1.ap, op0, op1), rd, [out])

    def copy(self, out, in_, eng='vector'):
        if eng == 'scalar':
            return self.op(eng, lambda e: e.copy(out.ap, in_.ap), [in_], [out])
        return self.op(eng, lambda e: e.tensor_copy(out.ap, in_.ap), [in_], [out])

    def memset(self, out, val, eng='vector'):
        return self.op(eng, lambda e: e.memset(out.ap, val), [], [out])

    def recip(self, out, in_):
        return self.op('vector', lambda e: e.reciprocal(out.ap, in_.ap), [in_], [out])

    def scan(self, out, d0, d1, init, op0, op1):
        rd = [d0, d1]
        a = init
        if isinstance(init, View):
            rd.append(init)
            a = init.ap
        return self.op('vector', lambda e: e.tensor_tensor_scan(out.ap, d0.ap, d1.ap, a, op0, op1), rd, [out])

    def load(self, out, in_, eng='sync', **kw):
        return self.dma(eng, lambda e: e.dma_start(out=out.ap, in_=in_.ap, **kw), [in_], [out])


D = 2048
KC = 16
SEQ = 4096
CTX = 256
TT = SEQ + CTX
NTC = 1088
HALF = 544
FH = 5632
HC = 44
EPS = 1e-6


def new_nc():
    return bass.Bass("TRN2", target_bir_lowering=False)


MN = 1536


def build_M():
    nc = new_nc()
    ccT_d = nc.dram_tensor("ccT", [128, KC, 3], F32, kind="ExternalInput")
    wm_d = nc.dram_tensor("wm", [4, 128, KC, MN], F32, kind="ExternalInput")
    bm_d = nc.dram_tensor("bm", [3, 4, MN], F32, kind="ExternalInput")
    out_d = nc.dram_tensor("mo", [3, 4, MN], F32, kind="ExternalOutput")
    with ExitStack() as es:
        S = Sched(nc, es)
        ccT = Tile(ccT_d); wm = Tile(wm_d); bm = Tile(bm_d); out = Tile(out_d)
        cc = S.sb('cc', [128, KC, 3])
        sc = S.sb('sc', [128, KC, 3])
        bmt = S.sb('bmt', [3, 4, MN])
        res = S.sb('res', [3, 4, MN])
        wts = [S.sb(f'w{i}', [128, 4, 512]) for i in range(3)]
        pss = [S.ps(f'ps{i}', [128, 512]) for i in range(2)]
        S.load(cc[:], ccT[:])
        S.load(bmt[:], bm[:])
        S.act(sc[:], cc[:], AF.Silu)
        it = 0
        for l in range(4):
            for nt in range(MN // 512):
                ps = pss[(l * 3 + nt) % 2]
                for kq in range(KC // 4):
                    w = wts[it % 3]
                    it += 1
                    S.load(w[:], wm[l, :, kq * 4:(kq + 1) * 4, nt * 512:(nt + 1) * 512])
                    for k4 in range(4):
                        kc = kq * 4 + k4
                        S.mm(ps[0:3, :], sc[:, kc, :], w[:, k4, :], start=(kc == 0), stop=(kc == KC - 1))
                S.tt(res[:, l, nt * 512:(nt + 1) * 512], ps[0:3, :], bmt[:, l, nt * 512:(nt + 1) * 512], ALU.add)
        S.load(out[:], res[:])
        S.finish()
        S.emit()
    return nc


C_VECS = ['mix_post_g', 'gmix_lat', 'gmix_ctx', 'ffn_pre_g', 'fsh_lat', 'fsh_ctx', 'fsc_lat', 'fsc_ctx',
          'ffn_post_g', 'gffn_lat', 'gffn_ctx']
NPC = len(C_VECS) * KC


def build_C():
    nc = new_nc()
    hT_d = nc.dram_tensor("hT", [D, NTC], F32, kind="ExternalInput")
    oT_d = nc.dram_tensor("oT", [D, NTC], F32, kind="ExternalInput")
    wout_d = nc.dram_tensor("wout", [KC, 128, KC, 128], F32, kind="ExternalInput")
    wfi_d = nc.dram_tensor("wfi", [2 * HC, 128, KC, 128], F32, kind="ExternalInput")
    wfo_d = nc.dram_tensor("wfo", [KC, 128, HC, 128], F32, kind="ExternalInput")
    prm_d = nc.dram_tensor("prm", [128, NPC], F32, kind="ExternalInput")
    hn_d = nc.dram_tensor("hn", [D, NTC], F32, kind="ExternalOutput")
    with ExitStack() as es:
        S = Sched(nc, es)
        hT = Tile(hT_d); oT = Tile(oT_d); wout = Tile(wout_d); wfi = Tile(wfi_d); wfo = Tile(wfo_d)
        prm_dr = Tile(prm_d); hn = Tile(hn_d)
        prm = S.sb('prm', [128, NPC])
        S.load(prm[:], prm_dr[:])

        def pv(name):
            i = C_VECS.index(name)
            return prm[:, i * KC:(i + 1) * KC]
        ones = S.sb('ones', [128, 128])
        S.memset(ones[:], 1.0)
        der = S.sb('der', [128, 6, KC])
        S.tt(der[:, 0, :], pv('mix_post_g'), pv('gmix_lat'), ALU.mult)
        S.tt(der[:, 1, :], pv('mix_post_g'), pv('gmix_ctx'), ALU.mult)
        S.tt(der[:, 2, :], pv('ffn_post_g'), pv('gffn_lat'), ALU.mult)
        S.tt(der[:, 3, :], pv('ffn_post_g'), pv('gffn_ctx'), ALU.mult)
        S.stt(der[:, 4, :], pv('fsc_lat'), 1.0, pv('ffn_pre_g'), ALU.add, ALU.mult)
        S.stt(der[:, 5, :], pv('fsc_ctx'), 1.0, pv('ffn_pre_g'), ALU.add, ALU.mult)
        pg = {'lat': der[:, 0, :], 'ctx': der[:, 1, :]}
        fg = {'lat': der[:, 2, :], 'ctx': der[:, 3, :]}
        gs = {'lat': der[:, 4, :], 'ctx': der[:, 5, :]}
        fsh = {'lat': pv('fsh_lat'), 'ctx': pv('fsh_ctx')}

        H = S.sb('H', [128, KC, HALF])
        M = S.sb('M', [128, KC, HALF])
        OB = S.sb('OB', [128, KC, HALF], BF16)
        A = S.sb('A', [128, HC, HALF], BF16)
        rstd = S.sb('rstd', [128, HALF])
        sqt = [S.sb(f'sqt{i}', [128, 272]) for i in range(2)]
        t2 = [S.sb(f't2{i}', [128, HALF]) for i in range(2)]
        wA = [S.sb(f'wA{i}', [128, KC, 128], BF16) for i in range(4)]
        wB = [S.sb(f'wB{i}', [128, HC, 128], BF16) for i in range(2)]
        PS = [S.ps(f'ps{i}', [128, 512]) for i in range(8)]
        psi = [0]

        def nps():
            p = PS[psi[0] % 6]
            psi[0] += 1
            return p
        pstat = [PS[6], PS[7]]
        NT = [(0, 272), (272, 544)]
        hTv = hT[:].rr('(kc p) t -> p kc t', p=128)
        oTv = oT[:].rr('(kc p) t -> p kc t', p=128)
        hnv = hn[:].rr('(kc p) t -> p kc t', p=128)
        wai = [0]

        def stats(X, scale):
            for ni, (a, b) in enumerate(NT):
                ps = pstat[ni]
                for kc in range(KC):
                    sq = sqt[kc % 2]
                    S.act(sq[:, 0:b - a], X[:, kc, a:b], AF.Square)
                    S.mm(ps[:, 0:b - a], ones[:], sq[:, 0:b - a], start=(kc == 0), stop=(kc == KC - 1))
                S.act(rstd[:, a:b], ps[:, 0:b - a], AF.Sqrt, bias=EPS, scale=1.0 / D)
            S.recip(rstd[:], rstd[:])

        def residual(X, gvec, segs):
            for kc in range(KC):
                S.tt(X[:, kc, :], X[:, kc, :], rstd[:], ALU.mult)
                for (a, b, w) in segs:
                    S.stt(H[:, kc, a:b], X[:, kc, a:b], gvec[w][:, kc:kc + 1], H[:, kc, a:b], ALU.mult, ALU.add)

        for hf in range(2):
            t0 = hf * HALF
            segs = [(0, 64, 'ctx'), (64, HALF, 'lat')] if hf == 0 else [(0, HALF, 'lat')]
            S.load(H[:], hTv[:, :, t0:t0 + HALF])
            S.load(OB[:], oTv[:, :, t0:t0 + HALF], eng='gpsimd')
            for oc in range(KC):
                w = wA[wai[0] % 4]
                wai[0] += 1
                S.load(w[:], wout[oc], eng='gpsimd')
                for (a, b) in NT:
                    ps = nps()
                    for kc in range(KC):
                        S.mm(ps[:, 0:b - a], w[:, kc, :], OB[:, kc, a:b], start=(kc == 0), stop=(kc == KC - 1))
                    S.copy(M[:, oc, a:b], ps[:, 0:b - a], eng='scalar')
            stats(M, 1.0)
            residual(M, pg, segs)
            stats(H, 1.0)
            for kc in range(KC):
                t = t2[kc % 2]
                S.tt(t[:], H[:, kc, :], rstd[:], ALU.mult)
                for (a, b, w_) in segs:
                    S.ts(OB[:, kc, a:b], t[:, a:b], gs[w_][:, kc:kc + 1], ALU.mult, fsh[w_][:, kc:kc + 1], ALU.add)
            for hc in range(HC):
                wg = wA[wai[0] % 4]
                wai[0] += 1
                wu = wA[wai[0] % 4]
                wai[0] += 1
                S.load(wg[:], wfi[hc], eng='gpsimd')
                S.load(wu[:], wfi[HC + hc], eng='gpsimd')
                for ni, (a, b) in enumerate(NT):
                    pg_ = nps()
                    pu_ = nps()
                    for kc in range(KC):
                        S.mm(pg_[:, 0:b - a], wg[:, kc, :], OB[:, kc, a:b], start=(kc == 0), stop=(kc == KC - 1))
                    for kc in range(KC):
                        S.mm(pu_[:, 0:b - a], wu[:, kc, :], OB[:, kc, a:b], start=(kc == 0), stop=(kc == KC - 1))
                    sg = sqt[ni]
                    S.act(sg[:, 0:b - a], pg_[:, 0:b - a], AF.Silu)
                    S.tt(A[:, hc, a:b], sg[:, 0:b - a], pu_[:, 0:b - a], ALU.mult)
            for oc in range(KC):
                w = wB[oc % 2]
                S.load(w[:], wfo[oc], eng='gpsimd')
                for (a, b) in NT:
                    ps = nps()
                    for hc in range(HC):
                        S.mm(ps[:, 0:b - a], w[:, hc, :], A[:, hc, a:b], start=(hc == 0), stop=(hc == HC - 1))
                    S.copy(M[:, oc, a:b], ps[:, 0:b - a], eng='scalar')
            stats(M, 1.0)
            residual(M, fg, segs)
            S.load(hnv[:, :, t0:t0 + HALF], H[:])
        S.finish()
        S.emit()
        print('C ninstr', S.ninstr)
    return nc

NCOL = 2216
C_Q, C_K, C_V, C_Z, C_G = 0, 256, 512, 768, 1024
C_RR, C_RK, C_RV, C_WD, C_AD, C_GD = 1032, 1288, 1544, 1800, 1928, 2056
PROJ_CHUNKS = [(i * 128, 128) for i in range(8)] + [(1024, 8)] + [(1032 + i * 128, 128) for i in range(9)] + [(2184, 32)]
RW_TILES = [(C_RR, 128), (C_RR + 128, 128), (C_RK, 128), (C_RK + 128, 128), (C_RV, 128), (C_RV + 128, 128),
            (C_WD, 128), (C_AD, 128), (C_GD, 128), (C_GD + 128, 32)]

A_PRM = [('mix_pre_g', 16), ('msh_lat', 16), ('msc_lat', 16), ('msh_ctx', 16), ('msc_ctx', 16),
         ('cw', 42), ('alog', 4), ('dtb', 4), ('outg', 1), ('mu', 10), ('w0', 4), ('a0', 4),
         ('kk', 2), ('ka', 2), ('rk', 2), ('lnw', 2), ('lnb', 2)]
A_OFF = {}
_o = 0
for _n, _w in A_PRM:
    A_OFF[_n] = (_o, _w)
    _o += _w
NPA = _o
CST_OFF = {'ident': (0, 128), 'LS': (128, 64), 'US': (192, 64), 'LI': (256, 64), 'UI': (320, 64),
           'blk': (384, 128), 'reset': (512, 512), 'm4': (1024, 4), 'm2': (1028, 2)}
NCST = 1030
NEGBIG = -60000.0
RW_LN_EPS_ = 64e-5


def make_cst():
    c = np.zeros((128, NCST), np.float32)
    c[:, 0:128] = np.eye(128)
    r = np.arange(64)[:, None]
    q = np.arange(64)[None, :]
    c[0:64, 128:192] = (r > q)
    c[0:64, 192:256] = (r < q)
    c[0:64, 256:320] = (r >= q)
    c[0:64, 320:384] = (r <= q)
    blk = np.zeros((128, 128), np.float32)
    blk[0:64, 0:64] = 1
    blk[64:, 64:] = 1
    c[:, 384:512] = blk
    rs = np.ones(512, np.float32)
    rs[::64] = 0
    c[:, 512:1024] = rs[None]
    p = np.arange(128)
    for ct in range(4):
        c[:, 1024 + ct] = (p % 4 == ct)
    for ct in range(2):
        c[:, 1028 + ct] = (p % 2 == ct)
    return c


class _Stop(Exception):
    pass


class Scope:
    stopped = False

    def __enter__(self):
        self.es = ExitStack()
        return self.es

    def __exit__(self, t, v, tb):
        self.es.close()
        if t is not None and issubclass(t, _Stop):
            Scope.stopped = True
            return True
        return False


def build_AB(stages=('A', 'DN', 'RW')):
    import os as _os2
    STOP = int(_os2.environ.get('DN_STOP', '0'))

    def chk(k):
        if STOP == k:
            raise _Stop()
    nc = new_nc()
    hT_d = nc.dram_tensor("hT", [D, TT], F32, kind="ExternalInput")
    win_d = nc.dram_tensor("win", [128, KC, NCOL], F32, kind="ExternalInput")
    prm_d = nc.dram_tensor("prm", [128, NPA], F32, kind="ExternalInput")
    wup_d = nc.dram_tensor("wup", [64, 2, 256], F32, kind="ExternalInput")
    aup_d = nc.dram_tensor("aup", [64, 2, 256], F32, kind="ExternalInput")
    gup_d = nc.dram_tensor("gup", [160, 256], F32, kind="ExternalInput")
    cst_d = nc.dram_tensor("cst", [128, NCST], F32, kind="ExternalInput")
    oT_d = nc.dram_tensor("oT", [512, TT], F32, kind="ExternalOutput")
    import os as _os
    pT_d = nc.dram_tensor("pT", [NCOL, TT], F32, kind=("ExternalOutput" if _os.environ.get("DBG_PT") else "Internal"))
    with ExitStack() as es0:
        S = Sched(nc, es0)
        hT = Tile(hT_d); win = Tile(win_d); prm_dr = Tile(prm_d); wup_dr = Tile(wup_d); aup_dr = Tile(aup_d)
        gup_dr = Tile(gup_d); cst_dr = Tile(cst_d); oT = Tile(oT_d)
        pT = Tile(pT_d, 'pT', keys=list(range(len(PROJ_CHUNKS))))

        def pTrows(r0, n):
            for ci, (c0, w) in enumerate(PROJ_CHUNKS):
                if c0 <= r0 and r0 + n <= c0 + w:
                    return View(pT.h[r0:r0 + n, :], [pT.bufs[ci]])
            raise ValueError((r0, n))

        prm = S.sb('prm', [128, NPA])
        cst = S.sb('cst', [128, NCST])
        S.load(prm[:], prm_dr[:])
        S.load(cst[:], cst_dr[:])

        def pv(name, j=None, n=1, np_=128):
            o, w = A_OFF[name]
            if j is None:
                return prm[0:np_, o:o + w]
            return prm[0:np_, o + j:o + j + n]

        def cv(name, np_=64):
            o, w = CST_OFF[name]
            return cst[0:np_, o:o + w]
        ones = S.sb('ones', [128, 128])
        S.memset(ones[:], 1.0)
        RDT = mybir.dt.float32r
        identR = S.sb('identR', [128, 128], RDT)
        ident = cv('ident', 128)
        S.copy(identR[:], ident)
        id64R = identR[0:64, 0:64]
        id64 = cst[0:64, 0:64]
        PS = [S.ps(f'b{i}', [128, 512]) for i in range(8)]
        psi = [0]

        def nb():
            p = PS[psi[0] % 7]
            psi[0] += 1
            return p

        def barrier():
            toks = [(e, S.cnt[e]) for e in ENGS if S.cnt[e] > 0 and e != 'sync']
            for key, val in S.dval.items():
                if val > 0:
                    toks.append((key, val))
            for e in ['tensor', 'vector', 'scalar', 'gpsimd', 'sync']:
                w = S._waits(e, toks)
                if w:
                    S.q[e].append((w, None, None, 0))

        SEGS = [(0, 256)] + [(256 + 512 * i, 512) for i in range(8)]
        CSEGS = [(0, 4)] + [(4 + 8 * i, 8) for i in range(8)]

        if 'A' in stages:
            with ExitStack() as es:
                S.es = es
                W = S.sb('W', [128, KC, NCOL], BF16)
                for kq in range(4):
                    S.load(W[:, kq * 4:(kq + 1) * 4, :], win[:, kq * 4:(kq + 1) * 4, :], eng='gpsimd')
                der = S.sb('derA', [128, 2, KC])
                S.stt(der[:, 0, :], pv('msc_lat'), 1.0, pv('mix_pre_g'), ALU.add, ALU.mult)
                S.stt(der[:, 1, :], pv('msc_ctx'), 1.0, pv('mix_pre_g'), ALU.add, ALU.mult)
                Hs = S.sb('Hseg', [128, KC, 512])
                U = S.sb('Useg', [128, KC, 512], BF16)
                sq = [S.sb(f'sqA{i}', [128, 512]) for i in range(2)]
                rstd = S.sb('rstdA', [128, 512])
                stg = [S.sb(f'stg{i}', [128, 512]) for i in range(4)]
                hTv = hT[:].rr('(kc p) t -> p kc t', p=128)
                si = 0
                for (t0, n) in SEGS:
                    isctx = (t0 == 0)
                    gsv = der[:, 1, :] if isctx else der[:, 0, :]
                    shv = pv('msh_ctx') if isctx else pv('msh_lat')
                    S.load(Hs[:, :, 0:n], hTv[:, :, t0:t0 + n])
                    ps = nb()
                    for kc in range(KC):
                        s_ = sq[kc % 2]
                        S.act(s_[:, 0:n], Hs[:, kc, 0:n], AF.Square)
                        S.mm(ps[:, 0:n], ones[:], s_[:, 0:n], start=(kc == 0), stop=(kc == KC - 1))
                    S.act(rstd[:, 0:n], ps[:, 0:n], AF.Sqrt, bias=EPS, scale=1.0 / D)
                    S.recip(rstd[:, 0:n], rstd[:, 0:n])
                    for kc in range(KC):
                        s_ = sq[kc % 2]
                        S.tt(s_[:, 0:n], Hs[:, kc, 0:n], rstd[:, 0:n], ALU.mult)
                        S.ts(U[:, kc, 0:n], s_[:, 0:n], gsv[:, kc:kc + 1], ALU.mult, shv[:, kc:kc + 1], ALU.add)
                    for ci, (c0, w) in enumerate(PROJ_CHUNKS):
                        ps = nb()
                        for kc in range(KC):
                            S.mm(ps[0:w, 0:n], W[:, kc, c0:c0 + w], U[:, kc, 0:n], start=(kc == 0), stop=(kc == KC - 1))
                        st = stg[si % 4]
                        si += 1
                        if si % 2 == 0:
                            S.copy(st[0:w, 0:n], ps[0:w, 0:n], eng='scalar')
                        else:
                            S.copy(st[0:w, 0:n], ps[0:w, 0:n], eng='vector')
                        S.load(View(pT.h[c0:c0 + w, t0:t0 + n], [pT.bufs[ci]]), st[0:w, 0:n])
                barrier()
            S.es = es0

        cder = S.sb('cder', [64, 5, 64])
        S.ts(cder[:, 0, :], cv('US'), NEGBIG, ALU.mult)
        S.ts(cder[:, 1, :], cv('LS'), NEGBIG, ALU.mult)
        S.ts(cder[:, 2, :], cv('UI'), -1.0, ALU.mult)
        S.ts(cder[:, 3, :], cv('LI'), -1.0, ALU.mult)
        S.stt(cder[:, 4, :], cv('LS'), -1.0, cv('US'), ALU.mult, ALU.subtract)
        negones = S.sb('negones', [64, 64])
        S.memset(negones[:], -1.0)
        NEGM = [cder[:, 0, :], cder[:, 1, :]]
        NEGMT = [cder[:, 1, :], cder[:, 0, :]]
        TRI = [cv('UI'), cv('LI')]
        NEGTRI = [cder[:, 2, :], cder[:, 3, :]]
        NOFFD = cder[:, 4, :]
        STRICT = [cv('LS'), cv('US')]
        STRICTT = [cv('US'), cv('LS')]
        INCLT = [cv('UI'), cv('LI')]

        def b3(v, nchk, axis, w=64):
            np_ = v.ap.shape[0]
            return v.un(axis).bc([np_, nchk, w])


        class Pool:
            def __init__(self, banks):
                self.b = banks
                self.i = 0

            def nb(self):
                p = self.b[self.i % len(self.b)]
                self.i += 1
                return p

        def drive(gens):
            alive = list(gens)
            while alive:
                for g in list(alive):
                    try:
                        next(g)
                    except StopIteration:
                        alive.remove(g)

        def inverse_g(pool, X, XT, Ybuf, YTbuf, TTm, nchk):
            n = nchk * 64
            S.tt(TTm[:, 0:n].rr('p (c i) -> p c i', i=64), XT[:, 0:n].rr('p (c i) -> p c i', i=64),
                 b3(id64, nchk, 1), ALU.add)
            Y, YT = X, XT
            for k in range(1, 6):
                pY = pool.nb()
                for ci in range(nchk):
                    cs = slice(ci * 64, ci * 64 + 64)
                    S.mm(pY[0:64, cs], YT[:, cs], Y[:, cs])
                if k < 5:
                    pYT = pool.nb()
                    for ci in range(nchk):
                        cs = slice(ci * 64, ci * 64 + 64)
                        S.mm(pYT[0:64, cs], Y[:, cs], YT[:, cs])
                Yn = Ybuf[k % 2]
                S.copy(Yn[:, 0:n], pY[0:64, 0:n], eng='scalar')
                if k < 5:
                    YTn = YTbuf[k % 2]
                    S.copy(YTn[:, 0:n], pYT[0:64, 0:n], eng='vector')
                yield
                pT_ = pool.nb()
                for ci in range(nchk):
                    cs = slice(ci * 64, ci * 64 + 64)
                    S.mm(pT_[0:64, cs], Yn[:, cs], TTm[:, cs])
                S.tt(TTm[:, 0:n], TTm[:, 0:n], pT_[0:64, 0:n], ALU.add)
                Y = Yn
                if k < 5:
                    YT = YTn
                yield

        NSEG = 17
        CSEG4 = [(4 * i, 4) for i in range(NSEG)]
        SEG4 = [(256 * i, 256) for i in range(NSEG)]
        ORDER = {0: list(range(NSEG)), 1: [0] + list(range(NSEG - 1, 0, -1))}
        misc = Pool([PS[6], PS[7]])

        if 'DN' in stages:
            for hh in range(2):
                with ExitStack() as es:
                    S.es = es
                    QT = S.sb('QT', [128, 68, 64], RDT)
                    KT = S.sb('KT', [128, 68, 64], RDT)
                    VT = S.sb('VT', [128, 68, 64], RDT)
                    RESt = S.sb('RESt', [128, TT])
                    ZRt = S.sb('ZRt', [128, TT])
                    OT = S.sb('OT', [128, 68, 64], keys=list(range(NSEG)))
                    sq = S.sb('sqD', [128, 512])
                    rs = S.sb('rsD', [128, 512])
                    BTA = S.sb('BTA', [64, 2, 68])
                    G = S.sb('Gg', [64, 2, 68])
                    with ExitStack() as es2:
                        S.es = es2
                        RAW = S.sb('RAW', [128, TT])
                        CV = S.sb('CV', [128, 68, 64])

                        def conv(r0, dst_final, cwj):
                            dst = CV
                            cw = pv('cw', cwj * 7, 7)
                            S.load(RAW[:], pTrows(r0, 128))
                            x = RAW[:, 0:256]
                            o = dst[:, 0:4, :].rr('p c i -> p (c i)')
                            S.ts(o, x, cw[:, 3:4], ALU.mult)
                            for k in range(7):
                                off = k - 3
                                if off == 0:
                                    continue
                                lo = max(0, -off)
                                hi = 256 - max(0, off)
                                S.stt(o[:, lo:hi], x[:, lo + off:hi + off], cw[:, k:k + 1], o[:, lo:hi], ALU.mult, ALU.add)
                            xl = RAW[:, 256:TT].rr('p (r c) -> p c r', c=64)
                            ol = dst[:, 4:68, :]
                            S.ts(ol, xl, cw[:, 3:4], ALU.mult)
                            for k in range(7):
                                off = k - 3
                                if off == 0:
                                    continue
                                c_ = cw[:, k:k + 1]
                                if off > 0:
                                    S.stt(ol[:, :, 0:64 - off], xl[:, :, off:64], c_, ol[:, :, 0:64 - off], ALU.mult, ALU.add)
                                    S.stt(ol[:, 0:63, 64 - off:64], xl[:, 1:64, 0:off], c_, ol[:, 0:63, 64 - off:64], ALU.mult, ALU.add)
                                else:
                                    o_ = -off
                                    S.stt(ol[:, :, o_:64], xl[:, :, 0:64 - o_], c_, ol[:, :, o_:64], ALU.mult, ALU.add)
                                    S.stt(ol[:, 1:64, 0:o_], xl[:, 0:63, 64 - o_:64], c_, ol[:, 1:64, 0:o_], ALU.mult, ALU.add)
                            S.act(dst_final[:].rr('p c i -> p (c i)'), dst[:].rr('p c i -> p (c i)'), AF.Silu)

                        def l2norm(X, mul):
                            Xf = X[:].rr('p c i -> p (c i)')
                            for (a, n) in SEGS:
                                S.act(sq[:, 0:n], Xf[:, a:a + n], AF.Square)
                                ps = misc.nb()
                                S.mm(ps[:, 0:n], ones[:], sq[:, 0:n])
                                S.act(rs[:, 0:n], ps[:, 0:n], AF.Sqrt, bias=EPS * mul, scale=float(mul))
                                S.recip(rs[:, 0:n], rs[:, 0:n])
                                S.tt(Xf[:, a:a + n], Xf[:, a:a + n], rs[:, 0:n], ALU.mult)

                        conv(C_Q + hh * 128, QT, 0 * 2 + hh)
                        l2norm(QT, 128.0)
                        conv(C_K + hh * 128, KT, 1 * 2 + hh)
                        l2norm(KT, 1.0)
                        conv(C_V + hh * 128, VT, 2 * 2 + hh)
                        GT = S.sb('GT', [64, 4, 68])
                        for j in range(4):
                            row = C_G + hh * 4 + j
                            gr = pTrows(row, 1)
                            S.load(GT[:, j, 4:68], View(gr.ap[:, 256:TT].rearrange('o (r c) -> (o r) c', c=64), gr.bufs))
                            S.load(GT[:, j, 0:4], View(gr.ap[:, 0:256].rearrange('o (c i) -> (o i) c', i=64), gr.bufs),
                                   allow_slow_non_contiguous=True)
                        gt1 = S.sb('gt1', [64, 2, 68])
                        gt2 = S.sb('gt2', [64, 2, 68])
                        gt3 = S.sb('gt3', [64, 2, 68])
                        nea = S.sb('nea', [64, 4])
                        S.act(nea[:], pv('alog', np_=64), AF.Exp)
                        S.ts(nea[:], nea[:], -1.0, ALU.mult)
                        S.act(BTA[:], GT[:, 0:2, :], AF.Sigmoid)
                        for d in range(2):
                            S.ts(gt1[:, d, :], GT[:, 2 + d, :], pv('dtb', hh * 2 + d, np_=64), ALU.add)
                        S.stt(gt2[:], gt1[:], -1.0, gt1[:], ALU.mult, ALU.max)
                        S.act(gt2[:], gt2[:], AF.Exp, scale=-1.0)
                        S.act(gt2[:], gt2[:], AF.Ln, bias=1.0)
                        S.ts(gt3[:], gt1[:], 0.0, ALU.max)
                        S.tt(gt3[:], gt3[:], gt2[:], ALU.add)
                        for d in range(2):
                            S.ts(G[:, d, :], gt3[:, d, :], nea[:, hh * 2 + d:hh * 2 + d + 1], ALU.mult)
                        barrier()
                    S.es = es
                    S.memset(OT[:], 0.0)
                    QTf = QT[:].rr('p c i -> p (c i)')

                    def mk_dset(d):
                        t = {}
                        for nm in ('GC', 'KDS', 'EGC', 'BEG', 't68'):
                            t[nm] = S.sb(f'{nm}{d}', [64, 68])
                        t['CD'] = S.sb(f'CD{d}', [128, 68])
                        for nm in ('GU', 'NBS', 'BI', 'NBST', 'Dm', 'DTm', 'X', 'XT', 'TTm', 'MT'):
                            t[nm] = S.sb(f'{nm}{d}', [64, 256], RDT if nm in ('X', 'XT', 'TTm', 'MT') else F32)
                        t['Yb'] = [S.sb(f'Yb{d}{i}', [64, 256], RDT) for i in range(2)]
                        t['YTb'] = [S.sb(f'YTb{d}{i}', [64, 256], RDT) for i in range(2)]
                        for nm in ('EGB', 'QD', 'P1T'):
                            t[nm] = S.sb(f'{nm}{d}', [128, 256], F32 if nm == 'EGB' else RDT)
                        for nm in ('VB', 'KBG', 'KDC', 'P2'):
                            t[nm] = S.sb(f'{nm}{d}', [64, 4, 128], F32 if nm == 'P2' else RDT)
                        t['Um'] = S.sb(f'Um{d}', [64, 128], RDT)
                        t['Hst'] = S.sb(f'Hst{d}', [128, 128], RDT)
                        return t

                    def dn_dir(d, t, pool):
                        GC, KDS, EGC, BEG, t68, CD = t['GC'], t['KDS'], t['EGC'], t['BEG'], t['t68'], t['CD']
                        GU, NBS, BI, NBST, Dm, DTm, X, XT, TTm, MT = (t[k_] for k_ in ('GU', 'NBS', 'BI', 'NBST', 'Dm', 'DTm', 'X', 'XT', 'TTm', 'MT'))
                        EGB, QD, P1T, VB, KBG, KDC, P2, Um, Hst = (t[k_] for k_ in ('EGB', 'QD', 'P1T', 'VB', 'KBG', 'KDC', 'P2', 'Um', 'Hst'))
                        gT = G[:, d, :]
                        bT = BTA[:, d, :]
                        ps = pool.nb()
                        S.mm(ps[0:64, 0:68], TRI[d], gT)
                        S.copy(GC[:], ps[0:64, 0:68])
                        ps2 = pool.nb()
                        S.mm(ps2[:, 0:68], ones[0:64, :], gT)
                        S.act(CD[:], ps2[:, 0:68], AF.Exp)
                        S.tt(t68[:], ps2[0:64, 0:68], GC[:], ALU.subtract)
                        S.act(KDS[:], t68[:], AF.Exp)
                        S.act(EGC[:], GC[:], AF.Exp)
                        S.tt(BEG[:], bT, EGC[:], ALU.mult)
                        S.memset(Hst[:].cast(F32), 0.0)
                        yield
                        for sidx in ORDER[d]:
                            c0, nchk = CSEG4[sidx]
                            n = nchk * 64
                            col0 = c0 * 64
                            gTs = G[:, d, c0:c0 + nchk]
                            bTs = BTA[:, d, c0:c0 + nchk]

                            def v3(tl):
                                return tl[:, 0:n].rr('p (c i) -> p c i', i=64)
                            S.tt(v3(GU), b3(gTs, nchk, 2), b3(TRI[d], nchk, 1), ALU.mult)
                            S.tt(v3(NBS), b3(bTs, nchk, 2), b3(NOFFD, nchk, 1), ALU.mult)
                            S.tt(v3(BI), b3(bTs, nchk, 2), b3(id64, nchk, 1), ALU.mult)
                            yield
                            pa = pool.nb()
                            S.mm(pa[:, 0:n], ones[0:64, :], GU[:, 0:n])
                            S.act(EGB[:, 0:n], pa[:, 0:n], AF.Exp)
                            S.tt(QD[:, 0:n], QTf[:, col0:col0 + n], EGB[:, 0:n], ALU.mult)
                            pb = pool.nb()
                            S.mm(pb[0:64, 0:n], TRI[d], b3(gTs, nchk, 2), start=True, stop=False)
                            S.mm(pb[0:64, 0:n], negones[:], GU[:, 0:n], start=False, stop=False)
                            S.mm(pb[0:64, 0:n], id64, b3(NEGM[d], nchk, 1), start=False, stop=True)
                            S.act(Dm[:, 0:n], pb[0:64, 0:n], AF.Exp)
                            yield
                            pc = pool.nb()
                            S.mm(pc[0:64, 0:n], ones[0:64, 0:64], GU[:, 0:n], start=True, stop=False)
                            S.mm(pc[0:64, 0:n], NEGTRI[d], b3(gTs, nchk, 2), start=False, stop=False)
                            S.mm(pc[0:64, 0:n], id64, b3(NEGMT[d], nchk, 1), start=False, stop=True)
                            S.act(DTm[:, 0:n], pc[0:64, 0:n], AF.Exp)
                            pe_ = pool.nb()
                            S.mm(pe_[0:64, 0:n], ones[0:64, 0:64], BI[:, 0:n])
                            S.tt(v3(NBST), pe_[0:64, 0:n].rr('p (c i) -> p c i', i=64), b3(NOFFD, nchk, 1), ALU.mult)
                            yield
                            pd = pool.nb()
                            for ci in range(nchk):
                                c = c0 + ci
                                S.mm(pd[0:64, ci * 64:ci * 64 + 64], KT[:, c, :], KT[:, c, :])
                            S.tt(X[:, 0:n], pd[0:64, 0:n], Dm[:, 0:n], ALU.mult)
                            S.tt(X[:, 0:n], X[:, 0:n], NBS[:, 0:n], ALU.mult)
                            S.tt(XT[:, 0:n], pd[0:64, 0:n], DTm[:, 0:n], ALU.mult)
                            S.tt(XT[:, 0:n], XT[:, 0:n], NBST[:, 0:n], ALU.mult)
                            yield
                            yield from inverse_g(pool, X, XT, t['Yb'], t['YTb'], TTm, nchk)
                            pv_ = pool.nb()
                            pk_ = pool.nb()
                            for cj in range(4):
                                c = c0 + cj
                                S.transpose(pv_[0:64, cj * 128:cj * 128 + 128].cast(RDT), VT[:, c, :], identR[:])
                                S.transpose(pk_[0:64, cj * 128:cj * 128 + 128].cast(RDT), KT[:, c, :], identR[:])
                            pv3 = pv_[0:64, :].rr('p (c k) -> p c k', k=128)
                            pk3 = pk_[0:64, :].rr('p (c k) -> p c k', k=128)
                            S.tt(VB[:], pv3, b3(BTA[:, d, c0:c0 + 4], 4, 2, 128), ALU.mult)
                            S.tt(KBG[:], pk3, b3(BEG[:, c0:c0 + 4], 4, 2, 128), ALU.mult)
                            S.tt(KDC[:], pk3, b3(KDS[:, c0:c0 + 4], 4, 2, 128), ALU.mult)
                            yield
                            pu = pool.nb()
                            for ci in range(4):
                                S.mm(pu[0:64, ci * 128:ci * 128 + 128], TTm[:, ci * 64:ci * 64 + 64], VB[:, ci, :])
                            S.copy(P2[:], pu[0:64, :].rr('p (c k) -> p c k', k=128), eng='scalar')
                            pw = pool.nb()
                            for ci in range(nchk):
                                S.mm(pw[:, ci * 64:ci * 64 + 64], KBG[:, ci, :], TTm[:, ci * 64:ci * 64 + 64])
                            S.act(P1T[:, 0:n], pw[:, 0:n], AF.Copy, scale=-1.0)
                            pm = pool.nb()
                            for ci in range(nchk):
                                c = c0 + ci
                                S.mm(pm[0:64, ci * 64:ci * 64 + 64], KT[:, c, :], QT[:, c, :])
                            S.tt(MT[:, 0:n], pm[0:64, 0:n], DTm[:, 0:n], ALU.mult)
                            yield
                            corder = list(range(nchk)) if d == 0 else list(range(nchk - 1, -1, -1))
                            for ci in corder:
                                c = c0 + ci
                                cs = slice(ci * 64, ci * 64 + 64)
                                p1 = pool.nb()
                                S.mm(p1[0:64, 0:128], P1T[:, cs], Hst[:])
                                S.tt(Um[:], p1[0:64, 0:128], P2[:, ci, :], ALU.add)
                                yield
                                p2 = pool.nb()
                                S.mm(p2[:, 0:64], Hst[:], QD[:, cs], start=True, stop=False)
                                S.mm(p2[:, 0:64], Um[:], MT[:, cs], start=False, stop=True)
                                ov = OT.k(sidx, (slice(None), c, slice(None)))
                                S.tt(ov, p2[:, 0:64], ov, ALU.add)
                                p3 = pool.nb()
                                S.mm(p3[:, 0:128], KDC[:, ci, :], Um[:])
                                S.stt(Hst[:], Hst[:], CD[:, c:c + 1], p3[:, 0:128], ALU.mult, ALU.add)
                                yield

                    dsets = [mk_dset(0), mk_dset(1)]
                    drive([dn_dir(0, dsets[0], Pool(PS[0:3])), dn_dir(1, dsets[1], Pool(PS[3:6]))])
                    OTf = OT[:].rr('p c i -> p (c i)')
                    for (a, n) in SEGS:
                        S.act(sq[:, 0:n], OTf[:, a:a + n], AF.Square)
                        ps = misc.nb()
                        S.mm(ps[:, 0:n], ones[:], sq[:, 0:n])
                        S.act(rs[:, 0:n], ps[:, 0:n], AF.Sqrt, bias=EPS, scale=1.0 / 128)
                        S.recip(rs[:, 0:n], rs[:, 0:n])
                        S.tt(OTf[:, a:a + n], OTf[:, a:a + n], rs[:, 0:n], ALU.mult)
                    ZR = ZRt[:]
                    S.load(ZR, pTrows(C_Z + hh * 128, 128))
                    S.act(ZR, ZR, AF.Silu)
                    S.ts(OTf, OTf, pv('outg', 0), ALU.mult)
                    RES = RESt[:]
                    S.tt(RES[:, 0:256], OTf[:, 0:256], ZR[:, 0:256], ALU.mult)
                    S.tt(RES[:, 256:TT].rr('p (r c) -> p c r', c=64), OT[:, 4:68, :],
                         ZR[:, 256:TT].rr('p (r c) -> p c r', c=64), ALU.mult)
                    S.load(oT[hh * 128:(hh + 1) * 128, :], RES)
                    barrier()
                S.es = es0

        if 'RW' in stages:
            with ExitStack() as es:
                S.es = es
                RAWs = [S.sb(f'RAWr{i}', [128, TT]) for i in range(2)]
                OUTs = [S.sb(f'OUTr{i}', [128, TT]) for i in range(2)]
                c0m = S.sb('c0m', [128, 10])
                m4mu = S.sb('m4mu', [128, 10, 4])
                m2mu = S.sb('m2mu', [128, 10, 2])
                S.ts(c0m[:], pv('mu'), -1.0, ALU.mult, 1.0, ALU.add)
                S.tt(m4mu[:], pv('mu').un(2).bc([128, 10, 4]), cv('m4', 128).un(1).bc([128, 10, 4]), ALU.mult)
                S.tt(m2mu[:], pv('mu').un(2).bc([128, 10, 2]), cv('m2', 128).un(1).bc([128, 10, 2]), ALU.mult)
                for ti, (r0, np_) in enumerate(RW_TILES):
                    RAW = RAWs[ti % 2]
                    OUT = OUTs[ti % 2]
                    ve = 'vector'
                    S.load(RAW[0:np_, :], pTrows(r0, np_))
                    x = RAW[0:np_, 0:256]
                    o = OUT[0:np_, 0:256]
                    S.ts(o, x, c0m[0:np_, ti:ti + 1], ALU.mult, eng=ve)
                    S.stt(o[:, 1:256], x[:, 0:255], m2mu[0:np_, ti, 0:1], o[:, 1:256], ALU.mult, ALU.add, eng=ve)
                    S.stt(o[:, 0:255], x[:, 1:256], m2mu[0:np_, ti, 1:2], o[:, 0:255], ALU.mult, ALU.add, eng=ve)
                    xl = RAW[0:np_, 256:TT].rr('p (r c) -> p r c', c=64)
                    ol = OUT[0:np_, 256:TT].rr('p (r c) -> p r c', c=64)
                    S.ts(ol, xl, c0m[0:np_, ti:ti + 1], ALU.mult, eng=ve)
                    S.stt(ol[:, :, 1:64], xl[:, :, 0:63], m4mu[0:np_, ti, 0:1], ol[:, :, 1:64], ALU.mult, ALU.add, eng=ve)
                    S.stt(ol[:, :, 0:63], xl[:, :, 1:64], m4mu[0:np_, ti, 1:2], ol[:, :, 0:63], ALU.mult, ALU.add, eng=ve)
                    S.stt(ol[:, 1:64, :], xl[:, 0:63, :], m4mu[0:np_, ti, 2:3], ol[:, 1:64, :], ALU.mult, ALU.add, eng=ve)
                    S.stt(ol[:, 0:63, :], xl[:, 1:64, :], m4mu[0:np_, ti, 3:4], ol[:, 0:63, :], ALU.mult, ALU.add, eng=ve)
                    if r0 == C_WD:
                        S.act(OUT[0:np_, :], OUT[0:np_, :], AF.Tanh)
                    if r0 >= C_GD:
                        S.act(OUT[0:np_, :], OUT[0:np_, :], AF.Sigmoid)
                    S.load(pTrows(r0, np_), OUT[0:np_, :])
                barrier()
            S.es = es0
            with ExitStack() as es:
                S.es = es
                WUP = S.sb('WUP', [64, 2, 256]); AUP = S.sb('AUP', [64, 2, 256])
                GUP0 = S.sb('GUP0', [128, 256]); GUP1 = S.sb('GUP1', [32, 256])
                S.load(WUP[:], wup_dr[:]); S.load(AUP[:], aup_dr[:])
                S.load(GUP0[:], gup_dr[0:128, :]); S.load(GUP1[:], gup_dr[128:160, :])
                omka = S.sb('omka', [128, 2])
                S.ts(omka[:], pv('ka'), -1.0, ALU.mult, 1.0, ALU.add)
                YT = S.sb('YTr', [128, TT], keys=list(range(NSEG)))
                BON = S.sb('BON', [128, TT], keys=list(range(NSEG)))
                GD0 = S.sb('GD0', [128, 256]); GD1 = S.sb('GD1', [32, 256])
                RESET = cv('reset', 128)
                BLK = cv('blk', 128)
                NQ = 256

                def mk_rset(d):
                    t = {}
                    names = ('R', 'Kp', 'V', 'LD', 'IC', 'KK', 'KD', 'PRE', 'LIN', 'LEX', 'EIN', 'EEX', 'ENI', 'ETL',
                             'RT', 'AT', 'KIC', 'BT_', 'KT_', 'BH', 'KH', 'P1T')
                    for nm in names:
                        t[nm] = S.sb(f'r{nm}{d}', [128, NQ], RDT if nm in ('RT', 'AT', 'BT_', 'KT_', 'BH', 'KH', 'P1T') else F32)
                    t['WD'] = S.sb(f'rWD{d}', [64, NQ]); t['AD'] = S.sb(f'rAD{d}', [64, NQ])
                    for nm in ('ATK', 'BHK', 'KHK', 'VTK', 'P2'):
                        t[nm] = S.sb(f'r{nm}{d}', [64, 4, 128], F32 if nm == 'P2' else RDT)
                    t['WC'] = S.sb(f'rWC{d}', [128, 4])
                    t['hd'] = []
                    for e in range(2):
                        t['hd'].append(dict(
                            X=S.sb(f'rX{d}{e}', [64, NQ], RDT), XT=S.sb(f'rXT{d}{e}', [64, NQ], RDT), AkT=S.sb(f'rAk{d}{e}', [64, NQ], RDT),
                            MrbT=S.sb(f'rMb{d}{e}', [64, NQ], RDT), MrkT=S.sb(f'rMk{d}{e}', [64, NQ], RDT),
                            Yb=[S.sb(f'rY{d}{e}{i}', [64, NQ], RDT) for i in range(2)],
                            YTb=[S.sb(f'rYT{d}{e}{i}', [64, NQ], RDT) for i in range(2)],
                            TTm=S.sb(f'rTT{d}{e}', [64, NQ], RDT), ZS=S.sb(f'rZS{d}{e}', [64, 4, 64], RDT)))
                    t['Um'] = S.sb(f'rUm{d}', [64, 128], RDT)
                    t['Hst'] = [S.sb(f'rHst{d}{i}', [128, 128], RDT) for i in range(2)]
                    return t

                def rw_dir(hp, d, t, pool, pP1):
                    hc0 = hp * 128
                    (R, Kp, V, LD, IC, KK, KD, PRE, LIN, LEX, EIN, EEX, ENI, ETL, RT, AT, KIC, BT_, KT_, BH, KH, P1T) = (
                        t[k_] for k_ in ('R', 'Kp', 'V', 'LD', 'IC', 'KK', 'KD', 'PRE', 'LIN', 'LEX', 'EIN', 'EEX', 'ENI', 'ETL',
                                         'RT', 'AT', 'KIC', 'BT_', 'KT_', 'BH', 'KH', 'P1T'))
                    KX, SQ, RS, TMP, T1, T2 = PRE, LIN, LEX, EIN, EEX, ENI
                    WD, AD, ATK, BHK, KHK, VTK, P2, WC, hd, Um = (t[k_] for k_ in ('WD', 'AD', 'ATK', 'BHK', 'KHK', 'VTK', 'P2', 'WC', 'hd', 'Um'))
                    Hst = t['Hst'][hp]
                    S.memset(Hst[:].cast(F32), 0.0)
                    w0c = pv('w0', d * 2 + hp)
                    a0c = pv('a0', d * 2 + hp)
                    for sidx in ORDER[d]:
                        t0, n = SEG4[sidx]
                        nchk = 4

                        def v3(tl, np_=128):
                            return tl[0:np_, 0:n].rr('p (c i) -> p c i', i=64)
                        cols = slice(t0, t0 + n)
                        S.load(R[:, 0:n], View(pT.h[C_RR + hc0:C_RR + hc0 + 128, cols], [pT.bufs[9 + hp]]))
                        S.load(Kp[:, 0:n], View(pT.h[C_RK + hc0:C_RK + hc0 + 128, cols], [pT.bufs[11 + hp]]))
                        S.load(V[:, 0:n], View(pT.h[C_RV + hc0:C_RV + hc0 + 128, cols], [pT.bufs[13 + hp]]))
                        S.load(WD[:, 0:n], View(pT.h[C_WD + d * 64:C_WD + d * 64 + 64, cols], [pT.bufs[15]]))
                        S.load(AD[:, 0:n], View(pT.h[C_AD + d * 64:C_AD + d * 64 + 64, cols], [pT.bufs[16]]))
                        yield
                        p1 = pool.nb()
                        S.mm(p1[:, 0:n], WUP[:, d, hc0:hc0 + 128], WD[:, 0:n])
                        S.act(LD[:, 0:n], p1[:, 0:n], AF.Sigmoid, bias=w0c)
                        S.ts(LD[:, 0:n], LD[:, 0:n], -0.6065306597126334, ALU.mult, eng='gpsimd')
                        p2 = pool.nb()
                        S.mm(p2[:, 0:n], AUP[:, d, hc0:hc0 + 128], AD[:, 0:n])
                        S.act(IC[:, 0:n], p2[:, 0:n], AF.Sigmoid, bias=a0c)
                        S.ts(KX[:, 0:n], Kp[:, 0:n], pv('kk', hp), ALU.mult)
                        S.act(SQ[:, 0:n], KX[:, 0:n], AF.Square)
                        yield
                        p3 = pool.nb()
                        S.mm(p3[:, 0:n], BLK, SQ[:, 0:n])
                        S.act(RS[:, 0:n], p3[:, 0:n], AF.Sqrt, bias=EPS, scale=1.0)
                        S.recip(RS[:, 0:n], RS[:, 0:n])
                        S.tt(KK[:, 0:n], KX[:, 0:n], RS[:, 0:n], ALU.mult)
                        S.ts(TMP[:, 0:n], IC[:, 0:n], pv('ka', hp), ALU.mult, omka[:, hp:hp + 1], ALU.add)
                        S.tt(KD[:, 0:n], Kp[:, 0:n], TMP[:, 0:n], ALU.mult)
                        S.tt(T1[:, 0:n], R[:, 0:n], KD[:, 0:n], ALU.mult)
                        S.ts(T1[:, 0:n], T1[:, 0:n], pv('rk', hp), ALU.mult)
                        yield
                        p4 = pool.nb()
                        S.mm(p4[:, 0:n], BLK, T1[:, 0:n])
                        S.tt(T2[:, 0:n], p4[:, 0:n], V[:, 0:n], ALU.mult)
                        bv = BON.k(sidx, (slice(None), cols))
                        S.tt(bv, bv, T2[:, 0:n], ALU.add)
                        S.scan(PRE[:, 0:n], RESET[:, 0:n], LD[:, 0:n], 0.0, ALU.mult, ALU.add)
                        TOT = v3(PRE)[:, :, 63:64]
                        if d == 0:
                            LINv = PRE
                        else:
                            S.tt(v3(LIN), TOT.bc([128, nchk, 64]), v3(PRE), ALU.subtract)
                            S.tt(LIN[:, 0:n], LIN[:, 0:n], LD[:, 0:n], ALU.add)
                            LINv = LIN
                        S.tt(LEX[:, 0:n], LINv[:, 0:n], LD[:, 0:n], ALU.subtract, eng='gpsimd')
                        S.act(WC[:, 0:nchk], PRE[:, 0:n].rr('p (c i) -> p c i', i=64)[:, :, 63], AF.Exp)
                        S.tt(v3(ETL), TOT.bc([128, nchk, 64]), v3(LINv), ALU.subtract)
                        yield
                        S.act(ETL[:, 0:n], ETL[:, 0:n], AF.Exp)
                        S.act(EEX[:, 0:n], LEX[:, 0:n], AF.Exp)
                        S.act(ENI[:, 0:n], LINv[:, 0:n], AF.Exp, scale=-1.0)
                        S.act(EIN[:, 0:n], LINv[:, 0:n], AF.Exp)
                        S.tt(RT[:, 0:n], R[:, 0:n], EIN[:, 0:n], ALU.mult, eng='gpsimd')
                        S.stt(AT[:, 0:n], KK[:, 0:n], -1.0, EEX[:, 0:n], ALU.mult, ALU.mult)
                        S.tt(KIC[:, 0:n], KK[:, 0:n], IC[:, 0:n], ALU.mult, eng='gpsimd')
                        yield
                        S.tt(BT_[:, 0:n], KIC[:, 0:n], ENI[:, 0:n], ALU.mult)
                        S.tt(KT_[:, 0:n], KD[:, 0:n], ENI[:, 0:n], ALU.mult, eng='gpsimd')
                        S.tt(BH[:, 0:n], KIC[:, 0:n], ETL[:, 0:n], ALU.mult)
                        S.tt(KH[:, 0:n], KD[:, 0:n], ETL[:, 0:n], ALU.mult, eng='gpsimd')
                        yield
                        for qi, (src, dst) in enumerate(((AT, ATK), (BH, BHK), (KH, KHK), (V, VTK))):
                            pt_ = pool.nb()
                            for cj in range(4):
                                if src is V:
                                    S.transpose(pt_[0:64, cj * 128:cj * 128 + 128], src[:, cj * 64:cj * 64 + 64], ident)
                                else:
                                    S.transpose(pt_[0:64, cj * 128:cj * 128 + 128].cast(RDT), src[:, cj * 64:cj * 64 + 64], identR[:])
                            S.copy(dst[:], pt_[0:64, :].rr('p (c k) -> p c k', k=128),
                                   eng='scalar' if qi % 2 == 0 else 'vector')
                            if qi % 2 == 1:
                                yield
                        for e in range(2):
                            h_ = hd[e]
                            pe = slice(64 * e, 64 * e + 64)
                            for (lh, rh, dst, msk) in ((AT, BT_, h_['X'], STRICT[d]), (BT_, AT, h_['XT'], STRICTT[d]),
                                                      (KT_, AT, h_['AkT'], STRICTT[d]), (BT_, RT, h_['MrbT'], INCLT[d]),
                                                      (KT_, RT, h_['MrkT'], INCLT[d])):
                                pq = pool.nb()
                                for ci in range(nchk):
                                    cs = slice(ci * 64, ci * 64 + 64)
                                    S.mm(pq[0:64, cs], lh[pe, cs], rh[pe, cs])
                                S.tt(v3(dst, 64), pq[0:64, 0:n].rr('p (c i) -> p c i', i=64), b3(msk, nchk, 1), ALU.mult)
                            yield
                            yield from inverse_g(pool, h_['X'], h_['XT'], h_['Yb'], h_['YTb'], h_['TTm'], nchk)
                            TTm = h_['TTm']
                            for ci in range(nchk):
                                cs = slice(ci * 64, ci * 64 + 64)
                                if e == 0:
                                    S.mm(pP1[pe, cs], ATK[:, ci, pe], TTm[:, cs])
                                else:
                                    S.mm(pP1[pe, cs], ATK[:, ci, pe].cast(F32), TTm[:, cs].cast(F32))
                            pz = pool.nb()
                            for ci in range(nchk):
                                cs = slice(ci * 64, ci * 64 + 64)
                                S.mm(pz[0:64, cs], h_['AkT'][:, cs], VTK[:, ci, pe])
                            S.copy(h_['ZS'][:], pz[0:64, 0:n].rr('p (c v) -> p c v', v=64), eng='scalar')
                            yield
                            pp2 = pool.nb()
                            for ci in range(nchk):
                                cs = slice(ci * 64, ci * 64 + 64)
                                S.mm(pp2[0:64, cs], TTm[:, cs], h_['ZS'][:, ci, :])
                            S.copy(P2[:, :, pe], pp2[0:64, 0:n].rr('p (c v) -> p c v', v=64), eng='vector')
                            yield
                        S.copy(P1T[:, 0:n], pP1[:, 0:n], eng='scalar')
                        yield
                        corder = list(range(nchk)) if d == 0 else list(range(nchk - 1, -1, -1))
                        for ci in corder:
                            cs = slice(ci * 64, ci * 64 + 64)
                            gcol = slice(t0 + ci * 64, t0 + ci * 64 + 64)
                            q1 = pool.nb()
                            S.mm(q1[0:64, 0:128], P1T[:, cs], Hst[:])
                            S.tt(Um[:], q1[0:64, 0:128], P2[:, ci, :], ALU.add)
                            yield
                            q2 = pool.nb()
                            S.mm(q2[:, 0:64], Hst[:], RT[:, cs], start=True, stop=False)
                            for e in range(2):
                                pe = slice(64 * e, 64 * e + 64)
                                if e == 0:
                                    S.mm(q2[pe, 0:64], Um[:, pe], hd[e]['MrbT'][:, cs], start=False, stop=False)
                                    S.mm(q2[pe, 0:64], VTK[:, ci, pe], hd[e]['MrkT'][:, cs], start=False, stop=True)
                                else:
                                    S.mm(q2[pe, 0:64], Um[:, pe].cast(F32), hd[e]['MrbT'][:, cs].cast(F32), start=False, stop=False)
                                    S.mm(q2[pe, 0:64], VTK[:, ci, pe].cast(F32), hd[e]['MrkT'][:, cs].cast(F32), start=False, stop=True)
                            yv = YT.k(sidx, (slice(None), gcol))
                            S.tt(yv, q2[:, 0:64], yv, ALU.add)
                            q3 = pool.nb()
                            S.mm(q3[:, 0:128], BHK[:, ci, :], Um[:], start=True, stop=False)
                            S.mm(q3[:, 0:128], KHK[:, ci, :], VTK[:, ci, :], start=False, stop=True)
                            for e in range(2):
                                pe = slice(64 * e, 64 * e + 64)
                                S.stt(Hst[pe, pe], Hst[pe, pe], WC[pe, ci:ci + 1], q3[pe, pe], ALU.mult, ALU.add)
                            yield

                rsets = [mk_rset(0), mk_rset(1)]
                R_ = rsets[0]['R']; Kp_ = rsets[0]['Kp']; V_ = rsets[0]['V']
                fpool = Pool(PS[0:6])
                for hp in range(2):
                    hc0 = hp * 128
                    S.memset(YT[:], 0.0)
                    S.memset(BON[:], 0.0)
                    drive([rw_dir(hp, 0, rsets[0], Pool(PS[0:3]), PS[6]), rw_dir(hp, 1, rsets[1], Pool(PS[3:6]), PS[7])])
                    for sidx, (t0, n) in enumerate(SEG4):
                        cols = slice(t0, t0 + n)
                        S.load(GD0[:, 0:n], View(pT.h[C_GD:C_GD + 128, cols], [pT.bufs[17]]))
                        S.load(GD1[:, 0:n], View(pT.h[C_GD + 128:C_GD + 160, cols], [pT.bufs[18]]))
                        yv = YT.k(sidx, (slice(None), cols))
                        f1 = fpool.nb()
                        S.mm(f1[:, 0:n], BLK, yv)
                        S.stt(R_[:, 0:n], f1[:, 0:n], -1.0 / 64, yv, ALU.mult, ALU.add)
                        S.act(Kp_[:, 0:n], R_[:, 0:n], AF.Square)
                        f2 = fpool.nb()
                        S.mm(f2[:, 0:n], BLK, Kp_[:, 0:n])
                        S.act(V_[:, 0:n], f2[:, 0:n], AF.Sqrt, bias=RW_LN_EPS_, scale=1.0 / 64)
                        S.recip(V_[:, 0:n], V_[:, 0:n])
                        S.tt(R_[:, 0:n], R_[:, 0:n], V_[:, 0:n], ALU.mult)
                        S.ts(R_[:, 0:n], R_[:, 0:n], pv('lnw', hp), ALU.mult, pv('lnb', hp), ALU.add)
                        S.tt(R_[:, 0:n], R_[:, 0:n], BON.k(sidx, (slice(None), cols)), ALU.add)
                        f3 = fpool.nb()
                        S.mm(f3[:, 0:n], GUP0[:, hc0:hc0 + 128], GD0[:, 0:n], start=True, stop=False)
                        S.mm(f3[:, 0:n], GUP1[:, hc0:hc0 + 128], GD1[:, 0:n], start=False, stop=True)
                        S.tt(yv, R_[:, 0:n], f3[:, 0:n], ALU.mult)
                    S.load(oT[256 + hc0:256 + hc0 + 128, :], YT[:])
                barrier()
            S.es = es0
        S.finish()
        S.emit()
        print('AB ninstr', S.ninstr, {e: S.cnt[e] for e in ENGS})
    return nc

DEPTH = 4
P_DN_ = 4128


def fm(v):
    return np.ascontiguousarray(np.asarray(v, np.float32).reshape(KC, 128).T)


def lay_w(w):
    K, N = w.shape
    return np.ascontiguousarray(w.reshape(K // 128, 128, N // 128, 128).transpose(2, 1, 0, 3))


def ab_cols(g):
    cols = []
    for t in range(4):
        cols += list(range(t * 1024 + 256 * g, t * 1024 + 256 * g + 256))
    for hh in range(2):
        for j in range(4):
            cols.append(4096 + j * 8 + 2 * g + hh)
    for t in range(3):
        cols += list(range(P_DN_ + t * 1024 + 256 * g, P_DN_ + t * 1024 + 256 * g + 256))
    cols += list(range(P_DN_ + 3072, P_DN_ + 3488))
    return np.array(cols)


def rep(x):
    return np.full((128,), x, np.float32)


def ab_prm(I, l, g, modv):
    P = np.zeros((128, NPA), np.float32)

    def put(name, j, col):
        o, w = A_OFF[name]
        P[:len(col), o + j] = col
    for name in ('mix_pre_g',):
        o, w = A_OFF[name]
        P[:, o:o + 16] = fm(I['mix_pre_g'][l])
    for name in ('msh_lat', 'msc_lat', 'msh_ctx', 'msc_ctx'):
        o, w = A_OFF[name]
        P[:, o:o + 16] = fm(modv[name])
    for t in range(3):
        for hh in range(2):
            ch0 = t * 1024 + (2 * g + hh) * 128
            for k in range(7):
                put('cw', (t * 2 + hh) * 7 + k, I['dn_conv'][l][k, ch0:ch0 + 128])
    for hh in range(2):
        for d in range(2):
            put('alog', hh * 2 + d, rep(I['dn_a_log'][l][d, 2 * g + hh]))
            put('dtb', hh * 2 + d, rep(I['dn_dt_bias'][l][d, 2 * g + hh]))
    put('outg', 0, I['dn_out_g'][l])
    mu = I['rw_mu'][l]
    ch0s = [256 * g, 256 * g + 128, 1024 + 256 * g, 1024 + 256 * g + 128, 2048 + 256 * g, 2048 + 256 * g + 128,
            3072, 3200, 3328, 3456]
    for ti, c0 in enumerate(ch0s):
        n = 32 if ti == 9 else 128
        put('mu', ti, mu[c0:c0 + n])
    for hp in range(2):
        c0 = 256 * g + 128 * hp
        for d in range(2):
            put('w0', d * 2 + hp, I['rw_w0'][l][d, c0:c0 + 128])
            put('a0', d * 2 + hp, I['rw_a0'][l][d, c0:c0 + 128])
        put('kk', hp, I['rw_k_k'][l][c0:c0 + 128])
        put('ka', hp, I['rw_k_a'][l][c0:c0 + 128])
        put('rk', hp, I['rw_r_k'][l].reshape(-1)[c0:c0 + 128])
        put('lnw', hp, I['rw_ln_w'][l][c0:c0 + 128])
        put('lnb', hp, I['rw_ln_b'][l][c0:c0 + 128])
    return P


def ab_inputs(I, l, b, g, hT_b, modv, cstv):
    cols = ab_cols(g)
    w = I['w_in'][l][:, cols]
    win = np.ascontiguousarray(w.reshape(KC, 128, NCOL).transpose(1, 0, 2))
    c0 = 256 * g
    return {
        'hT': hT_b, 'win': win, 'prm': ab_prm(I, l, g, modv),
        'wup': np.ascontiguousarray(I['rw_w_up'][l][:, :, c0:c0 + 256].transpose(1, 0, 2)),
        'aup': np.ascontiguousarray(I['rw_a_up'][l][:, :, c0:c0 + 256].transpose(1, 0, 2)),
        'gup': np.ascontiguousarray(I['rw_g_up'][l][:, c0:c0 + 256]),
        'cst': cstv,
    }


_NC_CACHE = {}


def get_nc(name):
    if name not in _NC_CACHE:
        _NC_CACHE[name] = {'M': build_M, 'AB': build_AB, 'C': build_C}[name]()
    return _NC_CACHE[name]


def kernel(**I):
    I = {k: np.asarray(v, np.float32) for k, v in I.items()}
    B = 2
    cc = np.stack([I['c'][0], I['c'][1], I['c_ctx']])
    ccT = np.ascontiguousarray(cc.reshape(3, KC, 128).transpose(2, 1, 0))
    in_maps = []
    for c in range(8):
        wm = I['w_mod'][:, :, c * MN:(c + 1) * MN]
        in_maps.append({'ccT': ccT,
                        'wm': np.ascontiguousarray(wm.reshape(DEPTH, KC, 128, MN).transpose(0, 2, 1, 3)),
                        'bm': np.ascontiguousarray(np.broadcast_to(I['b_mod'][None, :, c * MN:(c + 1) * MN], (3, DEPTH, MN)))})
    res = run_bass_kernel_spmd(get_nc('M'), in_maps, core_ids=list(range(8)))
    mod = np.concatenate([r['mo'] for r in res.results], axis=2)
    cstv = make_cst()
    hT = [np.ascontiguousarray(np.concatenate([I['ctx'][b], I['x'][b]], axis=0).T) for b in range(B)]
    for l in range(DEPTH):
        def mv(row, i):
            return mod[row, l, i * D:(i + 1) * D]
        in_maps = []
        for c in range(8):
            b, g = c // 4, c % 4
            modv = {'msh_lat': mv(b, 0), 'msc_lat': mv(b, 1), 'msh_ctx': mv(2, 0), 'msc_ctx': mv(2, 1)}
            in_maps.append(ab_inputs(I, l, b, g, hT[b], modv, cstv))
        res = run_bass_kernel_spmd(get_nc('AB'), in_maps, core_ids=list(range(8)))
        oT = []
        for b in range(B):
            parts = [res.results[b * 4 + g]['oT'] for g in range(4)]
            dn = np.concatenate([p[0:256] for p in parts], axis=0)
            rw = np.concatenate([p[256:512] for p in parts], axis=0)
            oT.append(np.concatenate([dn, rw], axis=0))
        wout = lay_w(I['w_out'][l])
        wfi = lay_w(I['w_ffn_in'][l])
        wfo = lay_w(I['w_ffn_out'][l])
        in_maps = []

        def tok(a, j):
            return np.ascontiguousarray(np.concatenate([a[:, 64 * j:64 * j + 64], a[:, 256 + 1024 * j:256 + 1024 * j + 1024]], axis=1))
        for c in range(8):
            b, j = c // 4, c % 4
            vec = {'mix_post_g': I['mix_post_g'][l], 'gmix_lat': mv(b, 2), 'gmix_ctx': mv(2, 2),
                   'ffn_pre_g': I['ffn_pre_g'][l], 'fsh_lat': mv(b, 3), 'fsh_ctx': mv(2, 3),
                   'fsc_lat': mv(b, 4), 'fsc_ctx': mv(2, 4), 'ffn_post_g': I['ffn_post_g'][l],
                   'gffn_lat': mv(b, 5), 'gffn_ctx': mv(2, 5)}
            prm = np.concatenate([fm(vec[k]) for k in C_VECS], axis=1)
            in_maps.append({'hT': tok(hT[b], j), 'oT': tok(oT[b], j), 'wout': wout, 'wfi': wfi, 'wfo': wfo, 'prm': prm})
        res = run_bass_kernel_spmd(get_nc('C'), in_maps, core_ids=list(range(8)))
        for b in range(B):
            hn = np.empty_like(hT[b])
            for j in range(4):
                r = res.results[b * 4 + j]['hn']
                hn[:, 64 * j:64 * j + 64] = r[:, 0:64]
                hn[:, 256 + 1024 * j:256 + 1024 * j + 1024] = r[:, 64:]
            hT[b] = hn
    out = np.stack([np.ascontiguousarray(hT[b][:, 256:].T) for b in range(B)])
    return out.astype(np.float32)
```

```python
import numpy as np
from contextlib import ExitStack
import concourse.bass as bass
import concourse.mybir as mybir
from concourse.bass_utils import run_bass_kernel_spmd

F32 = mybir.dt.float32
BF16 = mybir.dt.bfloat16
AF = mybir.ActivationFunctionType
ALU = mybir.AluOpType
AX = mybir.AxisListType

ENGS = ['tensor', 'vector', 'scalar', 'gpsimd', 'sync']
NDS = 12


class Buf:
    __slots__ = ('w', 'r', 'name')

    def __init__(self, name=''):
        self.w = None
        self.r = []
        self.name = name


class View:
    __slots__ = ('ap', 'bufs')

    def __init__(self, ap, bufs):
        self.ap = ap
        self.bufs = bufs

    def __getitem__(self, idx):
        return View(self.ap[idx], self.bufs)

    def rr(self, s, **kw):
        return View(self.ap.rearrange(s, **kw), self.bufs)

    def bc(self, shape):
        return View(self.ap.broadcast_to(shape), self.bufs)

    def un(self, axis):
        return View(self.ap.unsqueeze(axis), self.bufs)

    def tr(self, perm):
        return View(self.ap.transpose(perm), self.bufs)

    def cast(self, dt):
        return View(self.ap.bitcast(dt), self.bufs)


class Tile:
    def __init__(self, handle, name='', keys=None):
        self.h = handle
        self.name = name
        if keys is None:
            self.bufs = {None: Buf(name)}
        else:
            self.bufs = {k: Buf(f'{name}.{k}') for k in keys}

    def __getitem__(self, idx):
        return View(self.h[idx], list(self.bufs.values()))

    def k(self, key, idx=None):
        b = [self.bufs[key]]
        if idx is None:
            return View(self.h[:], b)
        return View(self.h[idx], b)


class Sched:
    def __init__(self, nc, es):
        self.nc = nc
        self.es = es
        self.q = {e: [] for e in ENGS}
        self.sems = {}
        for e in ENGS:
            self.sems[e] = es.enter_context(nc.semaphore(f'cs_{e}'))
        self.cnt = {e: 0 for e in ENGS}
        self.waited = {e: {} for e in ENGS}
        self.dslots = {}
        self.dval = {}
        self.dnext = {}
        for e in ['sync', 'gpsimd', 'scalar']:
            self.dslots[e] = []
            for i in range(NDS):
                key = ('d', e, i)
                self.sems[key] = es.enter_context(nc.semaphore(f'ds_{e}{i}'))
                self.dval[key] = 0
                self.dslots[e].append(key)
            self.dnext[e] = 0
        self.same_engine_sync = {'tensor': False, 'vector': True, 'scalar': True,
                                 'gpsimd': True, 'sync': False}
        self.ninstr = 0

    def _uniq(self, name):
        self._u = getattr(self, '_u', 0) + 1
        return f'{name}_{self._u}'

    def sb(self, name, shape, dt=F32, keys=None):
        h = self.es.enter_context(self.nc.sbuf_tensor(self._uniq('sb_' + name), list(shape), dt))
        return Tile(h, name, keys)

    def ps(self, name, shape, dt=F32, keys=None):
        h = self.es.enter_context(self.nc.psum_tensor(self._uniq('ps_' + name), list(shape), dt))
        return Tile(h, name, keys)

    def dram(self, name, shape, dt=F32, kind="Internal", keys=None):
        h = self.nc.dram_tensor(name, list(shape), dt, kind=kind)
        return Tile(h, name, keys)

    def _waits(self, e, deps):
        need = {}
        for (key, val) in deps:
            if key == e and not self.same_engine_sync[e]:
                continue
            if self.waited[e].get(key, 0) >= val:
                continue
            if need.get(key, 0) < val:
                need[key] = val
        for key, val in need.items():
            self.waited[e][key] = val
        return list(need.items())

    def _deps(self, reads, writes):
        deps = []
        for v in reads:
            for b in v.bufs:
                if b.w is not None:
                    deps.append(b.w)
        for v in writes:
            for b in v.bufs:
                if b.w is not None:
                    deps.append(b.w)
                deps.extend(b.r)
        return deps

    def _commit(self, tok, reads, writes):
        for v in reads:
            for b in v.bufs:
                b.r.append(tok)
                if len(b.r) > 64:
                    m = {}
                    for k_, v_ in b.r:
                        if m.get(k_, 0) < v_:
                            m[k_] = v_
                    b.r = list(m.items())
        for v in writes:
            for b in v.bufs:
                b.w = tok
                b.r = []

    def op(self, e, fn, reads=(), writes=()):
        deps = self._deps(reads, writes)
        waits = self._waits(e, deps)
        self.cnt[e] += 1
        tok = (e, self.cnt[e])
        self.q[e].append((waits, fn, e, 1))
        self._commit(tok, reads, writes)
        self.ninstr += 1
        return tok

    def dma(self, e, fn, reads=(), writes=()):
        i = self.dnext[e]
        self.dnext[e] = (i + 1) % NDS
        key = self.dslots[e][i]
        deps = self._deps(reads, writes)
        if self.dval[key] > 0:
            deps.append((key, self.dval[key]))
        waits = self._waits(e, deps)
        self.dval[key] += 16
        tok = (key, self.dval[key])
        self.q[e].append((waits, fn, key, 16))
        self._commit(tok, reads, writes)
        self.ninstr += 1
        return tok

    def finish(self):
        waits = []
        for key, val in self.dval.items():
            if val > 0:
                waits.append((key, val))
        for e in ENGS:
            if e != 'sync' and self.cnt[e] > 0:
                waits.append((e, self.cnt[e]))
        self.q['sync'].append((waits, None, None, 0))

    def emit(self):
        nc = self.nc
        sems = self.sems
        q = self.q
        with nc.Block() as block:
            def run(eng, items):
                for waits, fn, skey, inc in items:
                    for key, val in waits:
                        eng.wait_ge(sems[key], val)
                    if fn is not None:
                        ins = fn(eng)
                        ins.then_inc(sems[skey], inc)

            @block.sync
            def _(eng):
                run(eng, q['sync'])

            @block.tensor
            def _(eng):
                run(eng, q['tensor'])

            @block.vector
            def _(eng):
                run(eng, q['vector'])

            @block.scalar
            def _(eng):
                run(eng, q['scalar'])

            @block.gpsimd
            def _(eng):
                run(eng, q['gpsimd'])

    def mm(self, out, lhsT, rhs, start=True, stop=True):
        rd = [lhsT, rhs] + ([] if start else [out])
        return self.op('tensor', lambda e: e.matmul(out.ap, lhsT.ap, rhs.ap, start=start, stop=stop),
                       rd, [out])

    def transpose(self, out, in_, ident):
        return self.op('tensor', lambda e: e.transpose(out.ap, in_.ap, ident.ap), [in_, ident], [out])

    def act(self, out, in_, func, bias=None, scale=None, eng='scalar'):
        rd = [in_]
        kw = {}
        if bias is not None:
            if isinstance(bias, View):
                rd.append(bias)
                kw['bias'] = bias.ap
            else:
                kw['bias'] = bias
        if scale is not None:
            if isinstance(scale, View):
                rd.append(scale)
                kw['scale'] = scale.ap
            else:
                kw['scale'] = scale
        return self.op('scalar', lambda e: e.activation(out.ap, in_.ap, func, **kw), rd, [out])

    def tt(self, out, in0, in1, op, eng='vector'):
        return self.op(eng, lambda e: e.tensor_tensor(out.ap, in0.ap, in1.ap, op), [in0, in1], [out])

    def ts(self, out, in0, s1, op0, s2=None, op1=None, eng='vector'):
        rd = [in0]
        a1 = s1
        if isinstance(s1, View):
            rd.append(s1)
            a1 = s1.ap
        a2 = s2
        if isinstance(s2, View):
            rd.append(s2)
            a2 = s2.ap
        if op1 is None:
            return self.op(eng, lambda e: e.tensor_scalar(out.ap, in0.ap, a1, None, op0), rd, [out])
        return self.op(eng, lambda e: e.tensor_scalar(out.ap, in0.ap, a1, a2, op0, op1), rd, [out])

    def stt(self, out, in0, scalar, in1, op0, op1, eng='vector'):
        rd = [in0, in1]
        a = scalar
        if isinstance(scalar, View):
            rd.append(scalar)
            a = scalar.ap
        return self.op(eng, lambda e: e.scalar_tensor_tensor(out.ap, in0.ap, a, in1.ap, op0, op1), rd, [out])

    def copy(self, out, in_, eng='vector'):
        if eng == 'scalar':
            return self.op(eng, lambda e: e.copy(out.ap, in_.ap), [in_], [out])
        return self.op(eng, lambda e: e.tensor_copy(out.ap, in_.ap), [in_], [out])

    def memset(self, out, val, eng='vector'):
        return self.op(eng, lambda e: e.memset(out.ap, val), [], [out])

    def recip(self, out, in_):
        return self.op('vector', lambda e: e.reciprocal(out.ap, in_.ap), [in_], [out])

    def scan(self, out, d0, d1, init, op0, op1):
        rd = [d0, d1]
        a = init
        if isinstance(init, View):
            rd.append(init)
            a = init.ap
        return self.op('vector', lambda e: e.tensor_tensor_scan(out.ap, d0.ap, d1.ap, a, op0, op1), rd, [out])

    def load(self, out, in_, eng='sync', **kw):
        return self.dma(eng, lambda e: e.dma_start(out=out.ap, in_=in_.ap, **kw), [in_], [out])


D = 2048
KC = 16
SEQ = 4096
CTX = 256
TT = SEQ + CTX
NTC = 1088
HALF = 544
FH = 5632
HC = 44
EPS = 1e-6


def new_nc():
    return bass.Bass("TRN2", target_bir_lowering=False)


MN = 1536


def build_M():
    nc = new_nc()
    ccT_d = nc.dram_tensor("ccT", [128, KC, 3], F32, kind="ExternalInput")
    wm_d = nc.dram_tensor("wm", [4, 128, KC, MN], F32, kind="ExternalInput")
    bm_d = nc.dram_tensor("bm", [3, 4, MN], F32, kind="ExternalInput")
    out_d = nc.dram_tensor("mo", [3, 4, MN], F32, kind="ExternalOutput")
    with ExitStack() as es:
        S = Sched(nc, es)
        ccT = Tile(ccT_d); wm = Tile(wm_d); bm = Tile(bm_d); out = Tile(out_d)
        cc = S.sb('cc', [128, KC, 3])
        sc = S.sb('sc', [128, KC, 3])
        bmt = S.sb('bmt', [3, 4, MN])
        res = S.sb('res', [3, 4, MN])
        wts = [S.sb(f'w{i}', [128, 4, 512]) for i in range(3)]
        pss = [S.ps(f'ps{i}', [128, 512]) for i in range(2)]
        S.load(cc[:], ccT[:])
        S.load(bmt[:], bm[:])
        S.act(sc[:], cc[:], AF.Silu)
        it = 0
        for l in range(4):
            for nt in range(MN // 512):
                ps = pss[(l * 3 + nt) % 2]
                for kq in range(KC // 4):
                    w = wts[it % 3]
                    it += 1
                    S.load(w[:], wm[l, :, kq * 4:(kq + 1) * 4, nt * 512:(nt + 1) * 512])
                    for k4 in range(4):
                        kc = kq * 4 + k4
                        S.mm(ps[0:3, :], sc[:, kc, :], w[:, k4, :], start=(kc == 0), stop=(kc == KC - 1))
                S.tt(res[:, l, nt * 512:(nt + 1) * 512], ps[0:3, :], bmt[:, l, nt * 512:(nt + 1) * 512], ALU.add)
        S.load(out[:], res[:])
        S.finish()
        S.emit()
    return nc


C_VECS = ['mix_post_g', 'gmix_lat', 'gmix_ctx', 'ffn_pre_g', 'fsh_lat', 'fsh_ctx', 'fsc_lat', 'fsc_ctx',
          'ffn_post_g', 'gffn_lat', 'gffn_ctx']
NPC = len(C_VECS) * KC


def build_C():
    nc = new_nc()
    hT_d = nc.dram_tensor("hT", [D, NTC], F32, kind="ExternalInput")
    oT_d = nc.dram_tensor("oT", [D, NTC], F32, kind="ExternalInput")
    wout_d = nc.dram_tensor("wout", [KC, 128, KC, 128], F32, kind="ExternalInput")
    wfi_d = nc.dram_tensor("wfi", [2 * HC, 128, KC, 128], F32, kind="ExternalInput")
    wfo_d = nc.dram_tensor("wfo", [KC, 128, HC, 128], F32, kind="ExternalInput")
    prm_d = nc.dram_tensor("prm", [128, NPC], F32, kind="ExternalInput")
    hn_d = nc.dram_tensor("hn", [D, NTC], F32, kind="ExternalOutput")
    with ExitStack() as es:
        S = Sched(nc, es)
        hT = Tile(hT_d); oT = Tile(oT_d); wout = Tile(wout_d); wfi = Tile(wfi_d); wfo = Tile(wfo_d)
        prm_dr = Tile(prm_d); hn = Tile(hn_d)
        prm = S.sb('prm', [128, NPC])
        S.load(prm[:], prm_dr[:])

        def pv(name):
            i = C_VECS.index(name)
            return prm[:, i * KC:(i + 1) * KC]
        ones = S.sb('ones', [128, 128])
        S.memset(ones[:], 1.0)
        der = S.sb('der', [128, 6, KC])
        S.tt(der[:, 0, :], pv('mix_post_g'), pv('gmix_lat'), ALU.mult)
        S.tt(der[:, 1, :], pv('mix_post_g'), pv('gmix_ctx'), ALU.mult)
        S.tt(der[:, 2, :], pv('ffn_post_g'), pv('gffn_lat'), ALU.mult)
        S.tt(der[:, 3, :], pv('ffn_post_g'), pv('gffn_ctx'), ALU.mult)
        S.stt(der[:, 4, :], pv('fsc_lat'), 1.0, pv('ffn_pre_g'), ALU.add, ALU.mult)
        S.stt(der[:, 5, :], pv('fsc_ctx'), 1.0, pv('ffn_pre_g'), ALU.add, ALU.mult)
        pg = {'lat': der[:, 0, :], 'ctx': der[:, 1, :]}
        fg = {'lat': der[:, 2, :], 'ctx': der[:, 3, :]}
        gs = {'lat': der[:, 4, :], 'ctx': der[:, 5, :]}
        fsh = {'lat': pv('fsh_lat'), 'ctx': pv('fsh_ctx')}

        H = S.sb('H', [128, KC, HALF])
        M = S.sb('M', [128, KC, HALF])
        OB = S.sb('OB', [128, KC, HALF], BF16)
        A = S.sb('A', [128, HC, HALF], BF16)
        rstd = S.sb('rstd', [128, HALF])
        sqt = [S.sb(f'sqt{i}', [128, 272]) for i in range(2)]
        t2 = [S.sb(f't2{i}', [128, HALF]) for i in range(2)]
        wA = [S.sb(f'wA{i}', [128, KC, 128], BF16) for i in range(4)]
        wB = [S.sb(f'wB{i}', [128, HC, 128], BF16) for i in range(2)]
        PS = [S.ps(f'ps{i}', [128, 512]) for i in range(8)]
        psi = [0]

        def nps():
            p = PS[psi[0] % 6]
            psi[0] += 1
            return p
        pstat = [PS[6], PS[7]]
        NT = [(0, 272), (272, 544)]
        hTv = hT[:].rr('(kc p) t -> p kc t', p=128)
        oTv = oT[:].rr('(kc p) t -> p kc t', p=128)
        hnv = hn[:].rr('(kc p) t -> p kc t', p=128)
        wai = [0]

        def stats(X, scale):
            for ni, (a, b) in enumerate(NT):
                ps = pstat[ni]
                for kc in range(KC):
                    sq = sqt[kc % 2]
                    S.act(sq[:, 0:b - a], X[:, kc, a:b], AF.Square)
                    S.mm(ps[:, 0:b - a], ones[:], sq[:, 0:b - a], start=(kc == 0), stop=(kc == KC - 1))
                S.act(rstd[:, a:b], ps[:, 0:b - a], AF.Sqrt, bias=EPS, scale=1.0 / D)
            S.recip(rstd[:], rstd[:])

        def residual(X, gvec, segs):
            for kc in range(KC):
                S.tt(X[:, kc, :], X[:, kc, :], rstd[:], ALU.mult)
                for (a, b, w) in segs:
                    S.stt(H[:, kc, a:b], X[:, kc, a:b], gvec[w][:, kc:kc + 1], H[:, kc, a:b], ALU.mult, ALU.add)

        for hf in range(2):
            t0 = hf * HALF
            segs = [(0, 64, 'ctx'), (64, HALF, 'lat')] if hf == 0 else [(0, HALF, 'lat')]
            S.load(H[:], hTv[:, :, t0:t0 + HALF])
            S.load(OB[:], oTv[:, :, t0:t0 + HALF], eng='gpsimd')
            for oc in range(KC):
                w = wA[wai[0] % 4]
                wai[0] += 1
                S.load(w[:], wout[oc], eng='gpsimd')
                for (a, b) in NT:
                    ps = nps()
                    for kc in range(KC):
                        S.mm(ps[:, 0:b - a], w[:, kc, :], OB[:, kc, a:b], start=(kc == 0), stop=(kc == KC - 1))
                    S.copy(M[:, oc, a:b], ps[:, 0:b - a], eng='scalar')
            stats(M, 1.0)
            residual(M, pg, segs)
            stats(H, 1.0)
            for kc in range(KC):
                t = t2[kc % 2]
                S.tt(t[:], H[:, kc, :], rstd[:], ALU.mult)
                for (a, b, w_) in segs:
                    S.ts(OB[:, kc, a:b], t[:, a:b], gs[w_][:, kc:kc + 1], ALU.mult, fsh[w_][:, kc:kc + 1], ALU.add)
            for hc in range(HC):
                wg = wA[wai[0] % 4]
                wai[0] += 1
                wu = wA[wai[0] % 4]
                wai[0] += 1
                S.load(wg[:], wfi[hc], eng='gpsimd')
                S.load(wu[:], wfi[HC + hc], eng='gpsimd')
                for ni, (a, b) in enumerate(NT):
                    pg_ = nps()
                    pu_ = nps()
                    for kc in range(KC):
                        S.mm(pg_[:, 0:b - a], wg[:, kc, :], OB[:, kc, a:b], start=(kc == 0), stop=(kc == KC - 1))
                    for kc in range(KC):
                        S.mm(pu_[:, 0:b - a], wu[:, kc, :], OB[:, kc, a:b], start=(kc == 0), stop=(kc == KC - 1))
                    sg = sqt[ni]
                    S.act(sg[:, 0:b - a], pg_[:, 0:b - a], AF.Silu)
                    S.tt(A[:, hc, a:b], sg[:, 0:b - a], pu_[:, 0:b - a], ALU.mult)
            for oc in range(KC):
                w = wB[oc % 2]
                S.load(w[:], wfo[oc], eng='gpsimd')
                for (a, b) in NT:
                    ps = nps()
                    for hc in range(HC):
                        S.mm(ps[:, 0:b - a], w[:, hc, :], A[:, hc, a:b], start=(hc == 0), stop=(hc == HC - 1))
                    S.copy(M[:, oc, a:b], ps[:, 0:b - a], eng='scalar')
            stats(M, 1.0)
            residual(M, fg, segs)
            S.load(hnv[:, :, t0:t0 + HALF], H[:])
        S.finish()
        S.emit()
        print('C ninstr', S.ninstr)
    return nc

NCOL = 2216
C_Q, C_K, C_V, C_Z, C_G = 0, 256, 512, 768, 1024
C_RR, C_RK, C_RV, C_WD, C_AD, C_GD = 1032, 1288, 1544, 1800, 1928, 2056
PROJ_CHUNKS = [(i * 128, 128) for i in range(8)] + [(1024, 8)] + [(1032 + i * 128, 128) for i in range(9)] + [(2184, 32)]
RW_TILES = [(C_RR, 128), (C_RR + 128, 128), (C_RK, 128), (C_RK + 128, 128), (C_RV, 128), (C_RV + 128, 128),
            (C_WD, 128), (C_AD, 128), (C_GD, 128), (C_GD + 128, 32)]

A_PRM = [('mix_pre_g', 16), ('msh_lat', 16), ('msc_lat', 16), ('msh_ctx', 16), ('msc_ctx', 16),
         ('cw', 42), ('alog', 4), ('dtb', 4), ('outg', 1), ('mu', 10), ('w0', 4), ('a0', 4),
         ('kk', 2), ('ka', 2), ('rk', 2), ('lnw', 2), ('lnb', 2)]
A_OFF = {}
_o = 0
for _n, _w in A_PRM:
    A_OFF[_n] = (_o, _w)
    _o += _w
NPA = _o
CST_OFF = {'ident': (0, 128), 'LS': (128, 64), 'US': (192, 64), 'LI': (256, 64), 'UI': (320, 64),
           'blk': (384, 128), 'reset': (512, 512), 'm4': (1024, 4), 'm2': (1028, 2)}
NCST = 1030
NEGBIG = -60000.0
RW_LN_EPS_ = 64e-5


def make_cst():
    c = np.zeros((128, NCST), np.float32)
    c[:, 0:128] = np.eye(128)
    r = np.arange(64)[:, None]
    q = np.arange(64)[None, :]
    c[0:64, 128:192] = (r > q)
    c[0:64, 192:256] = (r < q)
    c[0:64, 256:320] = (r >= q)
    c[0:64, 320:384] = (r <= q)
    blk = np.zeros((128, 128), np.float32)
    blk[0:64, 0:64] = 1
    blk[64:, 64:] = 1
    c[:, 384:512] = blk
    rs = np.ones(512, np.float32)
    rs[::64] = 0
    c[:, 512:1024] = rs[None]
    p = np.arange(128)
    for ct in range(4):
        c[:, 1024 + ct] = (p % 4 == ct)
    for ct in range(2):
        c[:, 1028 + ct] = (p % 2 == ct)
    return c


class _Stop(Exception):
    pass


class Scope:
    stopped = False

    def __enter__(self):
        self.es = ExitStack()
        return self.es

    def __exit__(self, t, v, tb):
        self.es.close()
        if t is not None and issubclass(t, _Stop):
            Scope.stopped = True
            return True
        return False


def build_AB(stages=('A', 'DN', 'RW')):
    import os as _os2
    STOP = int(_os2.environ.get('DN_STOP', '0'))

    def chk(k):
        if STOP == k:
            raise _Stop()
    nc = new_nc()
    hT_d = nc.dram_tensor("hT", [D, TT], F32, kind="ExternalInput")
    win_d = nc.dram_tensor("win", [128, KC, NCOL], F32, kind="ExternalInput")
    prm_d = nc.dram_tensor("prm", [128, NPA], F32, kind="ExternalInput")
    wup_d = nc.dram_tensor("wup", [64, 2, 256], F32, kind="ExternalInput")
    aup_d = nc.dram_tensor("aup", [64, 2, 256], F32, kind="ExternalInput")
    gup_d = nc.dram_tensor("gup", [160, 256], F32, kind="ExternalInput")
    cst_d = nc.dram_tensor("cst", [128, NCST], F32, kind="ExternalInput")
    oT_d = nc.dram_tensor("oT", [512, TT], F32, kind="ExternalOutput")
    import os as _os
    pT_d = nc.dram_tensor("pT", [NCOL, TT], F32, kind=("ExternalOutput" if _os.environ.get("DBG_PT") else "Internal"))
    with ExitStack() as es0:
        S = Sched(nc, es0)
        hT = Tile(hT_d); win = Tile(win_d); prm_dr = Tile(prm_d); wup_dr = Tile(wup_d); aup_dr = Tile(aup_d)
        gup_dr = Tile(gup_d); cst_dr = Tile(cst_d); oT = Tile(oT_d)
        pT = Tile(pT_d, 'pT', keys=list(range(len(PROJ_CHUNKS))))

        def pTrows(r0, n):
            for ci, (c0, w) in enumerate(PROJ_CHUNKS):
                if c0 <= r0 and r0 + n <= c0 + w:
                    return View(pT.h[r0:r0 + n, :], [pT.bufs[ci]])
            raise ValueError((r0, n))

        prm = S.sb('prm', [128, NPA])
        cst = S.sb('cst', [128, NCST])
        S.load(prm[:], prm_dr[:])
        S.load(cst[:], cst_dr[:])

        def pv(name, j=None, n=1, np_=128):
            o, w = A_OFF[name]
            if j is None:
                return prm[0:np_, o:o + w]
            return prm[0:np_, o + j:o + j + n]

        def cv(name, np_=64):
            o, w = CST_OFF[name]
            return cst[0:np_, o:o + w]
        ones = S.sb('ones', [128, 128])
        S.memset(ones[:], 1.0)
        RDT = mybir.dt.float32r
        identR = S.sb('identR', [128, 128], RDT)
        ident = cv('ident', 128)
        S.copy(identR[:], ident)
        id64R = identR[0:64, 0:64]
        id64 = cst[0:64, 0:64]
        PS = [S.ps(f'b{i}', [128, 512]) for i in range(8)]
        psi = [0]

        def nb():
            p = PS[psi[0] % 7]
            psi[0] += 1
            return p

        def barrier():
            toks = [(e, S.cnt[e]) for e in ENGS if S.cnt[e] > 0 and e != 'sync']
            for key, val in S.dval.items():
                if val > 0:
                    toks.append((key, val))
            for e in ['tensor', 'vector', 'scalar', 'gpsimd', 'sync']:
                w = S._waits(e, toks)
                if w:
                    S.q[e].append((w, None, None, 0))

        SEGS = [(0, 256)] + [(256 + 512 * i, 512) for i in range(8)]
        CSEGS = [(0, 4)] + [(4 + 8 * i, 8) for i in range(8)]

        if 'A' in stages:
            with ExitStack() as es:
                S.es = es
                W = S.sb('W', [128, KC, NCOL], BF16)
                for kq in range(4):
                    S.load(W[:, kq * 4:(kq + 1) * 4, :], win[:, kq * 4:(kq + 1) * 4, :], eng='gpsimd')
                der = S.sb('derA', [128, 2, KC])
                S.stt(der[:, 0, :], pv('msc_lat'), 1.0, pv('mix_pre_g'), ALU.add, ALU.mult)
                S.stt(der[:, 1, :], pv('msc_ctx'), 1.0, pv('mix_pre_g'), ALU.add, ALU.mult)
                Hs = S.sb('Hseg', [128, KC, 512])
                U = S.sb('Useg', [128, KC, 512], BF16)
                sq = [S.sb(f'sqA{i}', [128, 512]) for i in range(2)]
                rstd = S.sb('rstdA', [128, 512])
                stg = [S.sb(f'stg{i}', [128, 512]) for i in range(4)]
                hTv = hT[:].rr('(kc p) t -> p kc t', p=128)
                si = 0
                for (t0, n) in SEGS:
                    isctx = (t0 == 0)
                    gsv = der[:, 1, :] if isctx else der[:, 0, :]
                    shv = pv('msh_ctx') if isctx else pv('msh_lat')
                    S.load(Hs[:, :, 0:n], hTv[:, :, t0:t0 + n])
                    ps = nb()
                    for kc in range(KC):
                        s_ = sq[kc % 2]
                        S.act(s_[:, 0:n], Hs[:, kc, 0:n], AF.Square)
                        S.mm(ps[:, 0:n], ones[:], s_[:, 0:n], start=(kc == 0), stop=(kc == KC - 1))
                    S.act(rstd[:, 0:n], ps[:, 0:n], AF.Sqrt, bias=EPS, scale=1.0 / D)
                    S.recip(rstd[:, 0:n], rstd[:, 0:n])
                    for kc in range(KC):
                        s_ = sq[kc % 2]
                        S.tt(s_[:, 0:n], Hs[:, kc, 0:n], rstd[:, 0:n], ALU.mult)
                        S.ts(U[:, kc, 0:n], s_[:, 0:n], gsv[:, kc:kc + 1], ALU.mult, shv[:, kc:kc + 1], ALU.add)
                    for ci, (c0, w) in enumerate(PROJ_CHUNKS):
                        ps = nb()
                        for kc in range(KC):
                            S.mm(ps[0:w, 0:n], W[:, kc, c0:c0 + w], U[:, kc, 0:n], start=(kc == 0), stop=(kc == KC - 1))
                        st = stg[si % 4]
                        si += 1
                        if si % 2 == 0:
                            S.copy(st[0:w, 0:n], ps[0:w, 0:n], eng='scalar')
                        else:
                            S.copy(st[0:w, 0:n], ps[0:w, 0:n], eng='vector')
                        S.load(View(pT.h[c0:c0 + w, t0:t0 + n], [pT.bufs[ci]]), st[0:w, 0:n])
                barrier()
            S.es = es0

        cder = S.sb('cder', [64, 5, 64])
        S.ts(cder[:, 0, :], cv('US'), NEGBIG, ALU.mult)
        S.ts(cder[:, 1, :], cv('LS'), NEGBIG, ALU.mult)
        S.ts(cder[:, 2, :], cv('UI'), -1.0, ALU.mult)
        S.ts(cder[:, 3, :], cv('LI'), -1.0, ALU.mult)
        S.stt(cder[:, 4, :], cv('LS'), -1.0, cv('US'), ALU.mult, ALU.subtract)
        negones = S.sb('negones', [64, 64])
        S.memset(negones[:], -1.0)
        NEGM = [cder[:, 0, :], cder[:, 1, :]]
        NEGMT = [cder[:, 1, :], cder[:, 0, :]]
        TRI = [cv('UI'), cv('LI')]
        NEGTRI = [cder[:, 2, :], cder[:, 3, :]]
        NOFFD = cder[:, 4, :]
        STRICT = [cv('LS'), cv('US')]
        STRICTT = [cv('US'), cv('LS')]
        INCLT = [cv('UI'), cv('LI')]

        def b3(v, nchk, axis, w=64):
            np_ = v.ap.shape[0]
            return v.un(axis).bc([np_, nchk, w])


        class Pool:
            def __init__(self, banks):
                self.b = banks
                self.i = 0

            def nb(self):
                p = self.b[self.i % len(self.b)]
                self.i += 1
                return p

        def zipgen(*gens):
            alive = [g for g in gens if g is not None]
            while alive:
                for g in list(alive):
                    try:
                        next(g)
                    except StopIteration:
                        alive.remove(g)
                yield

        def drive(gens):
            alive = list(gens)
            while alive:
                for g in list(alive):
                    try:
                        next(g)
                    except StopIteration:
                        alive.remove(g)

        def inverse_g(pool, X, XT, Ybuf, YTbuf, TTm, nchk):
            n = nchk * 64
            S.tt(TTm[:, 0:n].rr('p (c i) -> p c i', i=64), XT[:, 0:n].rr('p (c i) -> p c i', i=64),
                 b3(id64, nchk, 1), ALU.add)
            Y, YT = X, XT
            for k in range(1, 6):
                pY = pool.nb()
                for ci in range(nchk):
                    cs = slice(ci * 64, ci * 64 + 64)
                    S.mm(pY[0:64, cs], YT[:, cs], Y[:, cs])
                if k < 5:
                    pYT = pool.nb()
                    for ci in range(nchk):
                        cs = slice(ci * 64, ci * 64 + 64)
                        S.mm(pYT[0:64, cs], Y[:, cs], YT[:, cs])
                Yn = Ybuf[k % 2]
                S.copy(Yn[:, 0:n], pY[0:64, 0:n], eng='scalar')
                if k < 5:
                    YTn = YTbuf[k % 2]
                    S.copy(YTn[:, 0:n], pYT[0:64, 0:n], eng='vector')
                yield
                pT_ = pool.nb()
                for ci in range(nchk):
                    cs = slice(ci * 64, ci * 64 + 64)
                    S.mm(pT_[0:64, cs], Yn[:, cs], TTm[:, cs])
                S.tt(TTm[:, 0:n], TTm[:, 0:n], pT_[0:64, 0:n], ALU.add)
                Y = Yn
                if k < 5:
                    YT = YTn
                yield

        NSEG = 17
        CSEG4 = [(4 * i, 4) for i in range(NSEG)]
        SEG4 = [(256 * i, 256) for i in range(NSEG)]
        ORDER = {0: list(range(NSEG)), 1: [0] + list(range(NSEG - 1, 0, -1))}
        misc = Pool([PS[6], PS[7]])

        if 'DN' in stages:
            for hh in range(2):
                with ExitStack() as es:
                    S.es = es
                    QT = S.sb('QT', [128, 68, 64], RDT)
                    KT = S.sb('KT', [128, 68, 64], RDT)
                    VT = S.sb('VT', [128, 68, 64], RDT)
                    RESt = S.sb('RESt', [128, TT])
                    ZRt = S.sb('ZRt', [128, TT])
                    OT = S.sb('OT', [128, 68, 64], keys=list(range(NSEG)))
                    sq = S.sb('sqD', [128, 512])
                    rs = S.sb('rsD', [128, 512])
                    BTA = S.sb('BTA', [64, 2, 68])
                    G = S.sb('Gg', [64, 2, 68])
                    with ExitStack() as es2:
                        S.es = es2
                        RAW = S.sb('RAW', [128, TT])
                        CV = S.sb('CV', [128, 68, 64])

                        def conv(r0, dst_final, cwj):
                            dst = CV
                            cw = pv('cw', cwj * 7, 7)
                            S.load(RAW[:], pTrows(r0, 128))
                            x = RAW[:, 0:256]
                            o = dst[:, 0:4, :].rr('p c i -> p (c i)')
                            S.ts(o, x, cw[:, 3:4], ALU.mult)
                            for k in range(7):
                                off = k - 3
                                if off == 0:
                                    continue
                                lo = max(0, -off)
                                hi = 256 - max(0, off)
                                S.stt(o[:, lo:hi], x[:, lo + off:hi + off], cw[:, k:k + 1], o[:, lo:hi], ALU.mult, ALU.add)
                            xl = RAW[:, 256:TT].rr('p (r c) -> p c r', c=64)
                            ol = dst[:, 4:68, :]
                            S.ts(ol, xl, cw[:, 3:4], ALU.mult)
                            for k in range(7):
                                off = k - 3
                                if off == 0:
                                    continue
                                c_ = cw[:, k:k + 1]
                                if off > 0:
                                    S.stt(ol[:, :, 0:64 - off], xl[:, :, off:64], c_, ol[:, :, 0:64 - off], ALU.mult, ALU.add)
                                    S.stt(ol[:, 0:63, 64 - off:64], xl[:, 1:64, 0:off], c_, ol[:, 0:63, 64 - off:64], ALU.mult, ALU.add)
                                else:
                                    o_ = -off
                                    S.stt(ol[:, :, o_:64], xl[:, :, 0:64 - o_], c_, ol[:, :, o_:64], ALU.mult, ALU.add)
                                    S.stt(ol[:, 1:64, 0:o_], xl[:, 0:63, 64 - o_:64], c_, ol[:, 1:64, 0:o_], ALU.mult, ALU.add)
                            S.act(dst_final[:].rr('p c i -> p (c i)'), dst[:].rr('p c i -> p (c i)'), AF.Silu)

                        def l2norm(X, mul):
                            Xf = X[:].rr('p c i -> p (c i)')
                            for (a, n) in SEGS:
                                S.act(sq[:, 0:n], Xf[:, a:a + n], AF.Square)
                                ps = misc.nb()
                                S.mm(ps[:, 0:n], ones[:], sq[:, 0:n])
                                S.act(rs[:, 0:n], ps[:, 0:n], AF.Sqrt, bias=EPS * mul, scale=float(mul))
                                S.recip(rs[:, 0:n], rs[:, 0:n])
                                S.tt(Xf[:, a:a + n], Xf[:, a:a + n], rs[:, 0:n], ALU.mult)

                        conv(C_Q + hh * 128, QT, 0 * 2 + hh)
                        l2norm(QT, 128.0)
                        conv(C_K + hh * 128, KT, 1 * 2 + hh)
                        l2norm(KT, 1.0)
                        conv(C_V + hh * 128, VT, 2 * 2 + hh)
                        GT = S.sb('GT', [64, 4, 68])
                        for j in range(4):
                            row = C_G + hh * 4 + j
                            gr = pTrows(row, 1)
                            S.load(GT[:, j, 4:68], View(gr.ap[:, 256:TT].rearrange('o (r c) -> (o r) c', c=64), gr.bufs))
                            S.load(GT[:, j, 0:4], View(gr.ap[:, 0:256].rearrange('o (c i) -> (o i) c', i=64), gr.bufs),
                                   allow_slow_non_contiguous=True)
                        gt1 = S.sb('gt1', [64, 2, 68])
                        gt2 = S.sb('gt2', [64, 2, 68])
                        gt3 = S.sb('gt3', [64, 2, 68])
                        nea = S.sb('nea', [64, 4])
                        S.act(nea[:], pv('alog', np_=64), AF.Exp)
                        S.ts(nea[:], nea[:], -1.0, ALU.mult)
                        S.act(BTA[:], GT[:, 0:2, :], AF.Sigmoid)
                        for d in range(2):
                            S.ts(gt1[:, d, :], GT[:, 2 + d, :], pv('dtb', hh * 2 + d, np_=64), ALU.add)
                        S.stt(gt2[:], gt1[:], -1.0, gt1[:], ALU.mult, ALU.max)
                        S.act(gt2[:], gt2[:], AF.Exp, scale=-1.0)
                        S.act(gt2[:], gt2[:], AF.Ln, bias=1.0)
                        S.ts(gt3[:], gt1[:], 0.0, ALU.max)
                        S.tt(gt3[:], gt3[:], gt2[:], ALU.add)
                        for d in range(2):
                            S.ts(G[:, d, :], gt3[:, d, :], nea[:, hh * 2 + d:hh * 2 + d + 1], ALU.mult)
                        barrier()
                    S.es = es
                    S.memset(OT[:], 0.0)
                    QTf = QT[:].rr('p c i -> p (c i)')

                    def mk_dset(d):
                        t = {}
                        for nm in ('GC', 'KDS', 'EGC', 'BEG', 't68'):
                            t[nm] = S.sb(f'{nm}{d}', [64, 68])
                        t['CD'] = S.sb(f'CD{d}', [128, 68])
                        for nm in ('GU', 'NBS', 'BI', 'NBST', 'Dm', 'DTm', 'X', 'XT', 'TTm'):
                            t[nm] = S.sb(f'{nm}{d}', [64, 256], RDT if nm in ('X', 'XT', 'TTm') else F32)
                        t['Yb'] = [S.sb(f'Yb{d}{i}', [64, 256], RDT) for i in range(2)]
                        t['YTb'] = [S.sb(f'YTb{d}{i}', [64, 256], RDT) for i in range(2)]
                        t['EGB'] = S.sb(f'EGB{d}', [128, 256])
                        for nm in ('VB', 'KBG'):
                            t[nm] = S.sb(f'{nm}{d}', [64, 4, 128], RDT)
                        t['obs'] = []
                        for i_ in range(2):
                            t['obs'].append(dict(QD=S.sb(f'QD{d}{i_}', [128, 256], RDT), P1T=S.sb(f'P1T{d}{i_}', [128, 256], RDT),
                                                 P2=S.sb(f'P2{d}{i_}', [64, 4, 128]), MT=S.sb(f'MT{d}{i_}', [64, 256], RDT),
                                                 KDC=S.sb(f'KDC{d}{i_}', [64, 4, 128], RDT)))
                        t['Um'] = S.sb(f'Um{d}', [64, 128], RDT)
                        t['Hst'] = S.sb(f'Hst{d}', [128, 128], RDT)
                        return t

                    def dn_dir(d, t, pool):
                        GC, KDS, EGC, BEG, t68, CD = t['GC'], t['KDS'], t['EGC'], t['BEG'], t['t68'], t['CD']
                        GU, NBS, BI, NBST, Dm, DTm, X, XT, TTm = (t[k_] for k_ in ('GU', 'NBS', 'BI', 'NBST', 'Dm', 'DTm', 'X', 'XT', 'TTm'))
                        EGB, VB, KBG, Um, Hst = (t[k_] for k_ in ('EGB', 'VB', 'KBG', 'Um', 'Hst'))
                        gT = G[:, d, :]
                        bT = BTA[:, d, :]
                        ps = pool.nb()
                        S.mm(ps[0:64, 0:68], TRI[d], gT)
                        S.copy(GC[:], ps[0:64, 0:68])
                        ps2 = pool.nb()
                        S.mm(ps2[:, 0:68], ones[0:64, :], gT)
                        S.act(CD[:], ps2[:, 0:68], AF.Exp)
                        S.tt(t68[:], ps2[0:64, 0:68], GC[:], ALU.subtract)
                        S.act(KDS[:], t68[:], AF.Exp)
                        S.act(EGC[:], GC[:], AF.Exp)
                        S.tt(BEG[:], bT, EGC[:], ALU.mult)
                        S.memset(Hst[:].cast(F32), 0.0)
                        yield
                        nchk = 4
                        n = 256

                        def v3(tl):
                            return tl[:, 0:n].rr('p (c i) -> p c i', i=64)

                        def pre(sidx, ob):
                            QD, P1T, P2, MT, KDC = ob['QD'], ob['P1T'], ob['P2'], ob['MT'], ob['KDC']
                            c0 = 4 * sidx
                            col0 = c0 * 64
                            gTs = G[:, d, c0:c0 + nchk]
                            bTs = BTA[:, d, c0:c0 + nchk]
                            S.tt(v3(GU), b3(gTs, nchk, 2), b3(TRI[d], nchk, 1), ALU.mult)
                            S.tt(v3(NBS), b3(bTs, nchk, 2), b3(NOFFD, nchk, 1), ALU.mult)
                            S.tt(v3(BI), b3(bTs, nchk, 2), b3(id64, nchk, 1), ALU.mult)
                            yield
                            pa = pool.nb()
                            S.mm(pa[:, 0:n], ones[0:64, :], GU[:, 0:n])
                            S.act(EGB[:, 0:n], pa[:, 0:n], AF.Exp)
                            S.tt(QD[:, 0:n], QTf[:, col0:col0 + n], EGB[:, 0:n], ALU.mult)
                            pb = pool.nb()
                            S.mm(pb[0:64, 0:n], TRI[d], b3(gTs, nchk, 2), start=True, stop=False)
                            S.mm(pb[0:64, 0:n], negones[:], GU[:, 0:n], start=False, stop=False)
                            S.mm(pb[0:64, 0:n], id64, b3(NEGM[d], nchk, 1), start=False, stop=True)
                            S.act(Dm[:, 0:n], pb[0:64, 0:n], AF.Exp)
                            yield
                            pc = pool.nb()
                            S.mm(pc[0:64, 0:n], ones[0:64, 0:64], GU[:, 0:n], start=True, stop=False)
                            S.mm(pc[0:64, 0:n], NEGTRI[d], b3(gTs, nchk, 2), start=False, stop=False)
                            S.mm(pc[0:64, 0:n], id64, b3(NEGMT[d], nchk, 1), start=False, stop=True)
                            S.act(DTm[:, 0:n], pc[0:64, 0:n], AF.Exp)
                            pe_ = pool.nb()
                            S.mm(pe_[0:64, 0:n], ones[0:64, 0:64], BI[:, 0:n])
                            S.tt(v3(NBST), pe_[0:64, 0:n].rr('p (c i) -> p c i', i=64), b3(NOFFD, nchk, 1), ALU.mult)
                            yield
                            pd = pool.nb()
                            for ci in range(nchk):
                                c = c0 + ci
                                S.mm(pd[0:64, ci * 64:ci * 64 + 64], KT[:, c, :], KT[:, c, :])
                            S.tt(X[:, 0:n], pd[0:64, 0:n], Dm[:, 0:n], ALU.mult)
                            S.tt(X[:, 0:n], X[:, 0:n], NBS[:, 0:n], ALU.mult)
                            S.tt(XT[:, 0:n], pd[0:64, 0:n], DTm[:, 0:n], ALU.mult)
                            S.tt(XT[:, 0:n], XT[:, 0:n], NBST[:, 0:n], ALU.mult)
                            yield
                            yield from inverse_g(pool, X, XT, t['Yb'], t['YTb'], TTm, nchk)
                            pv_ = pool.nb()
                            pk_ = pool.nb()
                            for cj in range(4):
                                c = c0 + cj
                                S.transpose(pv_[0:64, cj * 128:cj * 128 + 128].cast(RDT), VT[:, c, :], identR[:])
                                S.transpose(pk_[0:64, cj * 128:cj * 128 + 128].cast(RDT), KT[:, c, :], identR[:])
                            pv3 = pv_[0:64, :].rr('p (c k) -> p c k', k=128)
                            pk3 = pk_[0:64, :].rr('p (c k) -> p c k', k=128)
                            S.tt(VB[:], pv3, b3(BTA[:, d, c0:c0 + 4], 4, 2, 128), ALU.mult)
                            S.tt(KBG[:], pk3, b3(BEG[:, c0:c0 + 4], 4, 2, 128), ALU.mult)
                            S.tt(KDC[:], pk3, b3(KDS[:, c0:c0 + 4], 4, 2, 128), ALU.mult)
                            yield
                            pu = pool.nb()
                            for ci in range(4):
                                S.mm(pu[0:64, ci * 128:ci * 128 + 128], TTm[:, ci * 64:ci * 64 + 64], VB[:, ci, :])
                            S.copy(P2[:], pu[0:64, :].rr('p (c k) -> p c k', k=128), eng='scalar')
                            pw = pool.nb()
                            for ci in range(nchk):
                                S.mm(pw[:, ci * 64:ci * 64 + 64], KBG[:, ci, :], TTm[:, ci * 64:ci * 64 + 64])
                            S.act(P1T[:, 0:n], pw[:, 0:n], AF.Copy, scale=-1.0)
                            yield
                            pm = pool.nb()
                            for ci in range(nchk):
                                c = c0 + ci
                                S.mm(pm[0:64, ci * 64:ci * 64 + 64], KT[:, c, :], QT[:, c, :])
                            S.tt(ob['MT'][:, 0:n], pm[0:64, 0:n], DTm[:, 0:n], ALU.mult)
                            yield

                        def chain(sidx, ob):
                            QD, P1T, P2, MT, KDC = ob['QD'], ob['P1T'], ob['P2'], ob['MT'], ob['KDC']
                            c0 = 4 * sidx
                            corder = list(range(nchk)) if d == 0 else list(range(nchk - 1, -1, -1))
                            for ci in corder:
                                c = c0 + ci
                                cs = slice(ci * 64, ci * 64 + 64)
                                p1 = pool.nb()
                                S.mm(p1[0:64, 0:128], P1T[:, cs], Hst[:])
                                S.tt(Um[:], p1[0:64, 0:128], P2[:, ci, :], ALU.add)
                                yield
                                p2 = pool.nb()
                                S.mm(p2[:, 0:64], Hst[:], QD[:, cs], start=True, stop=False)
                                S.mm(p2[:, 0:64], Um[:], MT[:, cs], start=False, stop=True)
                                ov = OT.k(sidx, (slice(None), c, slice(None)))
                                S.tt(ov, p2[:, 0:64], ov, ALU.add)
                                p3 = pool.nb()
                                S.mm(p3[:, 0:128], KDC[:, ci, :], Um[:])
                                S.stt(Hst[:], Hst[:], CD[:, c:c + 1], p3[:, 0:128], ALU.mult, ALU.add)
                                yield

                        order = ORDER[d]
                        obs = t['obs']
                        yield from pre(order[0], obs[0])
                        for i_, sidx in enumerate(order):
                            gl = [chain(sidx, obs[i_ % 2])]
                            if i_ + 1 < len(order):
                                gl.append(pre(order[i_ + 1], obs[(i_ + 1) % 2]))
                            yield from zipgen(*gl)

                    dsets = [mk_dset(0), mk_dset(1)]
                    drive([dn_dir(0, dsets[0], Pool(PS[0:4])), dn_dir(1, dsets[1], Pool(PS[4:8]))])
                    OTf = OT[:].rr('p c i -> p (c i)')
                    for (a, n) in SEGS:
                        S.act(sq[:, 0:n], OTf[:, a:a + n], AF.Square)
                        ps = misc.nb()
                        S.mm(ps[:, 0:n], ones[:], sq[:, 0:n])
                        S.act(rs[:, 0:n], ps[:, 0:n], AF.Sqrt, bias=EPS, scale=1.0 / 128)
                        S.recip(rs[:, 0:n], rs[:, 0:n])
                        S.tt(OTf[:, a:a + n], OTf[:, a:a + n], rs[:, 0:n], ALU.mult)
                    ZR = ZRt[:]
                    S.load(ZR, pTrows(C_Z + hh * 128, 128))
                    S.act(ZR, ZR, AF.Silu)
                    S.ts(OTf, OTf, pv('outg', 0), ALU.mult)
                    RES = RESt[:]
                    S.tt(RES[:, 0:256], OTf[:, 0:256], ZR[:, 0:256], ALU.mult)
                    S.tt(RES[:, 256:TT].rr('p (r c) -> p c r', c=64), OT[:, 4:68, :],
                         ZR[:, 256:TT].rr('p (r c) -> p c r', c=64), ALU.mult)
                    S.load(oT[hh * 128:(hh + 1) * 128, :], RES)
                    barrier()
                S.es = es0

        if 'RW' in stages:
            with ExitStack() as es:
                S.es = es
                RAWs = [S.sb(f'RAWr{i}', [128, TT]) for i in range(2)]
                OUTs = [S.sb(f'OUTr{i}', [128, TT]) for i in range(2)]
                c0m = S.sb('c0m', [128, 10])
                m4mu = S.sb('m4mu', [128, 10, 4])
                m2mu = S.sb('m2mu', [128, 10, 2])
                S.ts(c0m[:], pv('mu'), -1.0, ALU.mult, 1.0, ALU.add)
                S.tt(m4mu[:], pv('mu').un(2).bc([128, 10, 4]), cv('m4', 128).un(1).bc([128, 10, 4]), ALU.mult)
                S.tt(m2mu[:], pv('mu').un(2).bc([128, 10, 2]), cv('m2', 128).un(1).bc([128, 10, 2]), ALU.mult)
                for ti, (r0, np_) in enumerate(RW_TILES):
                    RAW = RAWs[ti % 2]
                    OUT = OUTs[ti % 2]
                    ve = 'vector'
                    S.load(RAW[0:np_, :], pTrows(r0, np_))
                    x = RAW[0:np_, 0:256]
                    o = OUT[0:np_, 0:256]
                    S.ts(o, x, c0m[0:np_, ti:ti + 1], ALU.mult, eng=ve)
                    S.stt(o[:, 1:256], x[:, 0:255], m2mu[0:np_, ti, 0:1], o[:, 1:256], ALU.mult, ALU.add, eng=ve)
                    S.stt(o[:, 0:255], x[:, 1:256], m2mu[0:np_, ti, 1:2], o[:, 0:255], ALU.mult, ALU.add, eng=ve)
                    xl = RAW[0:np_, 256:TT].rr('p (r c) -> p r c', c=64)
                    ol = OUT[0:np_, 256:TT].rr('p (r c) -> p r c', c=64)
                    S.ts(ol, xl, c0m[0:np_, ti:ti + 1], ALU.mult, eng=ve)
                    S.stt(ol[:, :, 1:64], xl[:, :, 0:63], m4mu[0:np_, ti, 0:1], ol[:, :, 1:64], ALU.mult, ALU.add, eng=ve)
                    S.stt(ol[:, :, 0:63], xl[:, :, 1:64], m4mu[0:np_, ti, 1:2], ol[:, :, 0:63], ALU.mult, ALU.add, eng=ve)
                    S.stt(ol[:, 1:64, :], xl[:, 0:63, :], m4mu[0:np_, ti, 2:3], ol[:, 1:64, :], ALU.mult, ALU.add, eng=ve)
                    S.stt(ol[:, 0:63, :], xl[:, 1:64, :], m4mu[0:np_, ti, 3:4], ol[:, 0:63, :], ALU.mult, ALU.add, eng=ve)
                    if r0 == C_WD:
                        S.act(OUT[0:np_, :], OUT[0:np_, :], AF.Tanh)
                    if r0 >= C_GD:
                        S.act(OUT[0:np_, :], OUT[0:np_, :], AF.Sigmoid)
                    S.load(pTrows(r0, np_), OUT[0:np_, :])
                barrier()
            S.es = es0
            with ExitStack() as es:
                S.es = es
                WUP = S.sb('WUP', [64, 2, 256]); AUP = S.sb('AUP', [64, 2, 256])
                GUP0 = S.sb('GUP0', [128, 256]); GUP1 = S.sb('GUP1', [32, 256])
                S.load(WUP[:], wup_dr[:]); S.load(AUP[:], aup_dr[:])
                S.load(GUP0[:], gup_dr[0:128, :]); S.load(GUP1[:], gup_dr[128:160, :])
                omka = S.sb('omka', [128, 2])
                S.ts(omka[:], pv('ka'), -1.0, ALU.mult, 1.0, ALU.add)
                YT = S.sb('YTr', [128, TT], keys=list(range(NSEG)))
                BON = S.sb('BON', [128, TT], keys=list(range(NSEG)))
                GD0 = S.sb('GD0', [128, 256]); GD1 = S.sb('GD1', [32, 256])
                RESET = cv('reset', 128)
                BLK = cv('blk', 128)
                NQ = 256

                def mk_rset(d):
                    t = {}
                    names = ('R', 'Kp', 'V', 'LD', 'IC', 'KK', 'KD', 'PRE', 'LIN', 'LEX', 'EIN', 'EEX', 'ENI', 'ETL',
                             'AT', 'KIC', 'BT_', 'KT_', 'BH', 'KH')
                    for nm in names:
                        t[nm] = S.sb(f'r{nm}{d}', [128, NQ], RDT if nm in ('AT', 'BT_', 'KT_', 'BH', 'KH') else F32)
                    t['WD'] = S.sb(f'rWD{d}', [64, NQ]); t['AD'] = S.sb(f'rAD{d}', [64, NQ])
                    t['ATK'] = S.sb(f'rATK{d}', [64, 4, 128], RDT)
                    t['obs'] = []
                    for i_ in range(2):
                        t['obs'].append(dict(
                            RT=S.sb(f'rRT{d}{i_}', [128, NQ], RDT), P1T=S.sb(f'rP1T{d}{i_}', [128, NQ], RDT),
                            P2=S.sb(f'rP2{d}{i_}', [64, 4, 128]), VTK=S.sb(f'rVTK{d}{i_}', [64, 4, 128], RDT),
                            BHK=S.sb(f'rBHK{d}{i_}', [64, 4, 128], RDT), KHK=S.sb(f'rKHK{d}{i_}', [64, 4, 128], RDT),
                            WC=S.sb(f'rWC{d}{i_}', [128, 4]),
                            MrbT=[S.sb(f'rMb{d}{i_}{e}', [64, NQ], RDT) for e in range(2)],
                            MrkT=[S.sb(f'rMk{d}{i_}{e}', [64, NQ], RDT) for e in range(2)]))
                    t['hd'] = []
                    for e in range(2):
                        t['hd'].append(dict(
                            X=S.sb(f'rX{d}{e}', [64, NQ], RDT), XT=S.sb(f'rXT{d}{e}', [64, NQ], RDT), AkT=S.sb(f'rAk{d}{e}', [64, NQ], RDT),
                            Yb=[S.sb(f'rY{d}{e}{i}', [64, NQ], RDT) for i in range(2)],
                            YTb=[S.sb(f'rYT{d}{e}{i}', [64, NQ], RDT) for i in range(2)],
                            TTm=S.sb(f'rTT{d}{e}', [64, NQ], RDT), ZS=S.sb(f'rZS{d}{e}', [64, 4, 64], RDT)))
                    t['Um'] = S.sb(f'rUm{d}', [64, 128], RDT)
                    t['Hst'] = [S.sb(f'rHst{d}{i}', [128, 128], RDT) for i in range(2)]
                    return t

                def rw_dir(hp, d, t, pool, pP1):
                    hc0 = hp * 128
                    (R, Kp, V, LD, IC, KK, KD, PRE, LIN, LEX, EIN, EEX, ENI, ETL, AT, KIC, BT_, KT_, BH, KH) = (
                        t[k_] for k_ in ('R', 'Kp', 'V', 'LD', 'IC', 'KK', 'KD', 'PRE', 'LIN', 'LEX', 'EIN', 'EEX', 'ENI', 'ETL',
                                         'AT', 'KIC', 'BT_', 'KT_', 'BH', 'KH'))
                    KX, SQ, RS, TMP, T1, T2 = PRE, LIN, LEX, EIN, EEX, ENI
                    WD, AD, ATK, hd, Um = (t[k_] for k_ in ('WD', 'AD', 'ATK', 'hd', 'Um'))
                    Hst = t['Hst'][hp]
                    S.memset(Hst[:].cast(F32), 0.0)
                    w0c = pv('w0', d * 2 + hp)
                    a0c = pv('a0', d * 2 + hp)
                    nchk = 4
                    n = 256

                    def v3(tl, np_=128):
                        return tl[0:np_, 0:n].rr('p (c i) -> p c i', i=64)

                    def pre(sidx, ob):
                        RT, P1T, P2, VTK, BHK, KHK, WC = (ob[k_] for k_ in ('RT', 'P1T', 'P2', 'VTK', 'BHK', 'KHK', 'WC'))
                        t0 = 256 * sidx
                        cols = slice(t0, t0 + n)
                        S.load(R[:, 0:n], View(pT.h[C_RR + hc0:C_RR + hc0 + 128, cols], [pT.bufs[9 + hp]]))
                        S.load(Kp[:, 0:n], View(pT.h[C_RK + hc0:C_RK + hc0 + 128, cols], [pT.bufs[11 + hp]]))
                        S.load(V[:, 0:n], View(pT.h[C_RV + hc0:C_RV + hc0 + 128, cols], [pT.bufs[13 + hp]]))
                        S.load(WD[:, 0:n], View(pT.h[C_WD + d * 64:C_WD + d * 64 + 64, cols], [pT.bufs[15]]))
                        S.load(AD[:, 0:n], View(pT.h[C_AD + d * 64:C_AD + d * 64 + 64, cols], [pT.bufs[16]]))
                        yield
                        p1 = pool.nb()
                        S.mm(p1[:, 0:n], WUP[:, d, hc0:hc0 + 128], WD[:, 0:n])
                        S.act(LD[:, 0:n], p1[:, 0:n], AF.Sigmoid, bias=w0c)
                        S.ts(LD[:, 0:n], LD[:, 0:n], -0.6065306597126334, ALU.mult, eng='gpsimd')
                        p2 = pool.nb()
                        S.mm(p2[:, 0:n], AUP[:, d, hc0:hc0 + 128], AD[:, 0:n])
                        S.act(IC[:, 0:n], p2[:, 0:n], AF.Sigmoid, bias=a0c)
                        S.ts(KX[:, 0:n], Kp[:, 0:n], pv('kk', hp), ALU.mult)
                        S.act(SQ[:, 0:n], KX[:, 0:n], AF.Square)
                        yield
                        p3 = pool.nb()
                        S.mm(p3[:, 0:n], BLK, SQ[:, 0:n])
                        S.act(RS[:, 0:n], p3[:, 0:n], AF.Sqrt, bias=EPS, scale=1.0)
                        S.recip(RS[:, 0:n], RS[:, 0:n])
                        S.tt(KK[:, 0:n], KX[:, 0:n], RS[:, 0:n], ALU.mult)
                        S.ts(TMP[:, 0:n], IC[:, 0:n], pv('ka', hp), ALU.mult, omka[:, hp:hp + 1], ALU.add)
                        S.tt(KD[:, 0:n], Kp[:, 0:n], TMP[:, 0:n], ALU.mult)
                        S.tt(T1[:, 0:n], R[:, 0:n], KD[:, 0:n], ALU.mult)
                        S.ts(T1[:, 0:n], T1[:, 0:n], pv('rk', hp), ALU.mult)
                        yield
                        p4 = pool.nb()
                        S.mm(p4[:, 0:n], BLK, T1[:, 0:n])
                        S.tt(T2[:, 0:n], p4[:, 0:n], V[:, 0:n], ALU.mult)
                        bv = BON.k(sidx, (slice(None), cols))
                        S.tt(bv, bv, T2[:, 0:n], ALU.add)
                        S.scan(PRE[:, 0:n], RESET[:, 0:n], LD[:, 0:n], 0.0, ALU.mult, ALU.add)
                        TOT = v3(PRE)[:, :, 63:64]
                        if d == 0:
                            LINv = PRE
                        else:
                            S.tt(v3(LIN), TOT.bc([128, nchk, 64]), v3(PRE), ALU.subtract)
                            S.tt(LIN[:, 0:n], LIN[:, 0:n], LD[:, 0:n], ALU.add)
                            LINv = LIN
                        S.tt(LEX[:, 0:n], LINv[:, 0:n], LD[:, 0:n], ALU.subtract, eng='gpsimd')
                        S.act(WC[:, 0:nchk], PRE[:, 0:n].rr('p (c i) -> p c i', i=64)[:, :, 63], AF.Exp)
                        S.tt(v3(ETL), TOT.bc([128, nchk, 64]), v3(LINv), ALU.subtract)
                        yield
                        S.act(ETL[:, 0:n], ETL[:, 0:n], AF.Exp)
                        S.act(EEX[:, 0:n], LEX[:, 0:n], AF.Exp)
                        S.act(ENI[:, 0:n], LINv[:, 0:n], AF.Exp, scale=-1.0)
                        S.act(EIN[:, 0:n], LINv[:, 0:n], AF.Exp)
                        S.tt(RT[:, 0:n], R[:, 0:n], EIN[:, 0:n], ALU.mult, eng='gpsimd')
                        S.stt(AT[:, 0:n], KK[:, 0:n], -1.0, EEX[:, 0:n], ALU.mult, ALU.mult)
                        S.tt(KIC[:, 0:n], KK[:, 0:n], IC[:, 0:n], ALU.mult, eng='gpsimd')
                        yield
                        S.tt(BT_[:, 0:n], KIC[:, 0:n], ENI[:, 0:n], ALU.mult)
                        S.tt(KT_[:, 0:n], KD[:, 0:n], ENI[:, 0:n], ALU.mult, eng='gpsimd')
                        S.tt(BH[:, 0:n], KIC[:, 0:n], ETL[:, 0:n], ALU.mult)
                        S.tt(KH[:, 0:n], KD[:, 0:n], ETL[:, 0:n], ALU.mult, eng='gpsimd')
                        yield
                        for qi, (src, dst) in enumerate(((AT, ATK), (BH, BHK), (KH, KHK), (V, VTK))):
                            pt_ = pool.nb()
                            for cj in range(4):
                                if src is V:
                                    S.transpose(pt_[0:64, cj * 128:cj * 128 + 128], src[:, cj * 64:cj * 64 + 64], ident)
                                else:
                                    S.transpose(pt_[0:64, cj * 128:cj * 128 + 128].cast(RDT), src[:, cj * 64:cj * 64 + 64], identR[:])
                            S.copy(dst[:], pt_[0:64, :].rr('p (c k) -> p c k', k=128),
                                   eng='scalar' if qi % 2 == 0 else 'vector')
                            if qi % 2 == 1:
                                yield

                        def head(e):
                            h_ = hd[e]
                            pe = slice(64 * e, 64 * e + 64)
                            for (lh, rh, dst, msk) in ((AT, BT_, h_['X'], STRICT[d]), (BT_, AT, h_['XT'], STRICTT[d]),
                                                      (KT_, AT, h_['AkT'], STRICTT[d]), (BT_, RT, ob['MrbT'][e], INCLT[d]),
                                                      (KT_, RT, ob['MrkT'][e], INCLT[d])):
                                pq = pool.nb()
                                for ci in range(nchk):
                                    cs = slice(ci * 64, ci * 64 + 64)
                                    S.mm(pq[0:64, cs], lh[pe, cs], rh[pe, cs])
                                S.tt(v3(dst, 64), pq[0:64, 0:n].rr('p (c i) -> p c i', i=64), b3(msk, nchk, 1), ALU.mult)
                                yield
                            yield from inverse_g(pool, h_['X'], h_['XT'], h_['Yb'], h_['YTb'], h_['TTm'], nchk)
                            TTm = h_['TTm']
                            for ci in range(nchk):
                                cs = slice(ci * 64, ci * 64 + 64)
                                if e == 0:
                                    S.mm(pP1[pe, cs], ATK[:, ci, pe], TTm[:, cs])
                                else:
                                    S.mm(pP1[pe, cs], ATK[:, ci, pe].cast(F32), TTm[:, cs].cast(F32))
                            pz = pool.nb()
                            for ci in range(nchk):
                                cs = slice(ci * 64, ci * 64 + 64)
                                S.mm(pz[0:64, cs], h_['AkT'][:, cs], VTK[:, ci, pe])
                            S.copy(h_['ZS'][:], pz[0:64, 0:n].rr('p (c v) -> p c v', v=64), eng='scalar')
                            yield
                            pp2 = pool.nb()
                            for ci in range(nchk):
                                cs = slice(ci * 64, ci * 64 + 64)
                                S.mm(pp2[0:64, cs], TTm[:, cs], h_['ZS'][:, ci, :])
                            S.copy(P2[:, :, pe], pp2[0:64, 0:n].rr('p (c v) -> p c v', v=64), eng='vector')
                            yield
                        yield from zipgen(head(0), head(1))
                        S.copy(P1T[:, 0:n], pP1[:, 0:n], eng='scalar')
                        yield

                    def chain(sidx, ob):
                        RT, P1T, P2, VTK, BHK, KHK, WC = (ob[k_] for k_ in ('RT', 'P1T', 'P2', 'VTK', 'BHK', 'KHK', 'WC'))
                        t0 = 256 * sidx
                        corder = list(range(nchk)) if d == 0 else list(range(nchk - 1, -1, -1))
                        for ci in corder:
                            cs = slice(ci * 64, ci * 64 + 64)
                            gcol = slice(t0 + ci * 64, t0 + ci * 64 + 64)
                            q1 = pool.nb()
                            S.mm(q1[0:64, 0:128], P1T[:, cs], Hst[:])
                            S.tt(Um[:], q1[0:64, 0:128], P2[:, ci, :], ALU.add)
                            yield
                            q2 = pool.nb()
                            S.mm(q2[:, 0:64], Hst[:], RT[:, cs], start=True, stop=False)
                            for e in range(2):
                                pe = slice(64 * e, 64 * e + 64)
                                if e == 0:
                                    S.mm(q2[pe, 0:64], Um[:, pe], ob['MrbT'][e][:, cs], start=False, stop=False)
                                    S.mm(q2[pe, 0:64], VTK[:, ci, pe], ob['MrkT'][e][:, cs], start=False, stop=True)
                                else:
                                    S.mm(q2[pe, 0:64], Um[:, pe].cast(F32), ob['MrbT'][e][:, cs].cast(F32), start=False, stop=False)
                                    S.mm(q2[pe, 0:64], VTK[:, ci, pe].cast(F32), ob['MrkT'][e][:, cs].cast(F32), start=False, stop=True)
                            yv = YT.k(sidx, (slice(None), gcol))
                            S.tt(yv, q2[:, 0:64], yv, ALU.add)
                            q3 = pool.nb()
                            S.mm(q3[:, 0:128], BHK[:, ci, :], Um[:], start=True, stop=False)
                            S.mm(q3[:, 0:128], KHK[:, ci, :], VTK[:, ci, :], start=False, stop=True)
                            for e in range(2):
                                pe = slice(64 * e, 64 * e + 64)
                                S.stt(Hst[pe, pe], Hst[pe, pe], WC[pe, ci:ci + 1], q3[pe, pe], ALU.mult, ALU.add)
                            yield

                    order = ORDER[d]
                    obs = t['obs']
                    yield from pre(order[0], obs[0])
                    for i_, sidx in enumerate(order):
                        gl = [chain(sidx, obs[i_ % 2])]
                        if i_ + 1 < len(order):
                            gl.append(pre(order[i_ + 1], obs[(i_ + 1) % 2]))
                        yield from zipgen(*gl)

                rsets = [mk_rset(0), mk_rset(1)]
                R_ = rsets[0]['R']; Kp_ = rsets[0]['Kp']; V_ = rsets[0]['V']
                fpool = Pool(PS[0:6])
                for hp in range(2):
                    hc0 = hp * 128
                    S.memset(YT[:], 0.0)
                    S.memset(BON[:], 0.0)
                    drive([rw_dir(hp, 0, rsets[0], Pool(PS[0:3]), PS[6]), rw_dir(hp, 1, rsets[1], Pool(PS[3:6]), PS[7])])
                    for sidx, (t0, n) in enumerate(SEG4):
                        cols = slice(t0, t0 + n)
                        S.load(GD0[:, 0:n], View(pT.h[C_GD:C_GD + 128, cols], [pT.bufs[17]]))
                        S.load(GD1[:, 0:n], View(pT.h[C_GD + 128:C_GD + 160, cols], [pT.bufs[18]]))
                        yv = YT.k(sidx, (slice(None), cols))
                        f1 = fpool.nb()
                        S.mm(f1[:, 0:n], BLK, yv)
                        S.stt(R_[:, 0:n], f1[:, 0:n], -1.0 / 64, yv, ALU.mult, ALU.add)
                        S.act(Kp_[:, 0:n], R_[:, 0:n], AF.Square)
                        f2 = fpool.nb()
                        S.mm(f2[:, 0:n], BLK, Kp_[:, 0:n])
                        S.act(V_[:, 0:n], f2[:, 0:n], AF.Sqrt, bias=RW_LN_EPS_, scale=1.0 / 64)
                        S.recip(V_[:, 0:n], V_[:, 0:n])
                        S.tt(R_[:, 0:n], R_[:, 0:n], V_[:, 0:n], ALU.mult)
                        S.ts(R_[:, 0:n], R_[:, 0:n], pv('lnw', hp), ALU.mult, pv('lnb', hp), ALU.add)
                        S.tt(R_[:, 0:n], R_[:, 0:n], BON.k(sidx, (slice(None), cols)), ALU.add)
                        f3 = fpool.nb()
                        S.mm(f3[:, 0:n], GUP0[:, hc0:hc0 + 128], GD0[:, 0:n], start=True, stop=False)
                        S.mm(f3[:, 0:n], GUP1[:, hc0:hc0 + 128], GD1[:, 0:n], start=False, stop=True)
                        S.tt(yv, R_[:, 0:n], f3[:, 0:n], ALU.mult)
                    S.load(oT[256 + hc0:256 + hc0 + 128, :], YT[:])
                barrier()
            S.es = es0
        S.finish()
        S.emit()
        print('AB ninstr', S.ninstr, {e: S.cnt[e] for e in ENGS})
    return nc

DEPTH = 4
P_DN_ = 4128


def fm(v):
    return np.ascontiguousarray(np.asarray(v, np.float32).reshape(KC, 128).T)


def lay_w(w):
    K, N = w.shape
    return np.ascontiguousarray(w.reshape(K // 128, 128, N // 128, 128).transpose(2, 1, 0, 3))


def ab_cols(g):
    cols = []
    for t in range(4):
        cols += list(range(t * 1024 + 256 * g, t * 1024 + 256 * g + 256))
    for hh in range(2):
        for j in range(4):
            cols.append(4096 + j * 8 + 2 * g + hh)
    for t in range(3):
        cols += list(range(P_DN_ + t * 1024 + 256 * g, P_DN_ + t * 1024 + 256 * g + 256))
    cols += list(range(P_DN_ + 3072, P_DN_ + 3488))
    return np.array(cols)


def rep(x):
    return np.full((128,), x, np.float32)


def ab_prm(I, l, g, modv):
    P = np.zeros((128, NPA), np.float32)

    def put(name, j, col):
        o, w = A_OFF[name]
        P[:len(col), o + j] = col
    for name in ('mix_pre_g',):
        o, w = A_OFF[name]
        P[:, o:o + 16] = fm(I['mix_pre_g'][l])
    for name in ('msh_lat', 'msc_lat', 'msh_ctx', 'msc_ctx'):
        o, w = A_OFF[name]
        P[:, o:o + 16] = fm(modv[name])
    for t in range(3):
        for hh in range(2):
            ch0 = t * 1024 + (2 * g + hh) * 128
            for k in range(7):
                put('cw', (t * 2 + hh) * 7 + k, I['dn_conv'][l][k, ch0:ch0 + 128])
    for hh in range(2):
        for d in range(2):
            put('alog', hh * 2 + d, rep(I['dn_a_log'][l][d, 2 * g + hh]))
            put('dtb', hh * 2 + d, rep(I['dn_dt_bias'][l][d, 2 * g + hh]))
    put('outg', 0, I['dn_out_g'][l])
    mu = I['rw_mu'][l]
    ch0s = [256 * g, 256 * g + 128, 1024 + 256 * g, 1024 + 256 * g + 128, 2048 + 256 * g, 2048 + 256 * g + 128,
            3072, 3200, 3328, 3456]
    for ti, c0 in enumerate(ch0s):
        n = 32 if ti == 9 else 128
        put('mu', ti, mu[c0:c0 + n])
    for hp in range(2):
        c0 = 256 * g + 128 * hp
        for d in range(2):
            put('w0', d * 2 + hp, I['rw_w0'][l][d, c0:c0 + 128])
            put('a0', d * 2 + hp, I['rw_a0'][l][d, c0:c0 + 128])
        put('kk', hp, I['rw_k_k'][l][c0:c0 + 128])
        put('ka', hp, I['rw_k_a'][l][c0:c0 + 128])
        put('rk', hp, I['rw_r_k'][l].reshape(-1)[c0:c0 + 128])
        put('lnw', hp, I['rw_ln_w'][l][c0:c0 + 128])
        put('lnb', hp, I['rw_ln_b'][l][c0:c0 + 128])
    return P


def ab_inputs(I, l, b, g, hT_b, modv, cstv):
    cols = ab_cols(g)
    w = I['w_in'][l][:, cols]
    win = np.ascontiguousarray(w.reshape(KC, 128, NCOL).transpose(1, 0, 2))
    c0 = 256 * g
    return {
        'hT': hT_b, 'win': win, 'prm': ab_prm(I, l, g, modv),
        'wup': np.ascontiguousarray(I['rw_w_up'][l][:, :, c0:c0 + 256].transpose(1, 0, 2)),
        'aup': np.ascontiguousarray(I['rw_a_up'][l][:, :, c0:c0 + 256].transpose(1, 0, 2)),
        'gup': np.ascontiguousarray(I['rw_g_up'][l][:, c0:c0 + 256]),
        'cst': cstv,
    }


_NC_CACHE = {}


def get_nc(name):
    if name not in _NC_CACHE:
        _NC_CACHE[name] = {'M': build_M, 'AB': build_AB, 'C': build_C}[name]()
    return _NC_CACHE[name]


def kernel(**I):
    I = {k: np.asarray(v, np.float32) for k, v in I.items()}
    B = 2
    cc = np.stack([I['c'][0], I['c'][1], I['c_ctx']])
    ccT = np.ascontiguousarray(cc.reshape(3, KC, 128).transpose(2, 1, 0))
    in_maps = []
    for c in range(8):
        wm = I['w_mod'][:, :, c * MN:(c + 1) * MN]
        in_maps.append({'ccT': ccT,
                        'wm': np.ascontiguousarray(wm.reshape(DEPTH, KC, 128, MN).transpose(0, 2, 1, 3)),
                        'bm': np.ascontiguousarray(np.broadcast_to(I['b_mod'][None, :, c * MN:(c + 1) * MN], (3, DEPTH, MN)))})
    res = run_bass_kernel_spmd(get_nc('M'), in_maps, core_ids=list(range(8)))
    mod = np.concatenate([r['mo'] for r in res.results], axis=2)
    cstv = make_cst()
    hT = [np.ascontiguousarray(np.concatenate([I['ctx'][b], I['x'][b]], axis=0).T) for b in range(B)]
    for l in range(DEPTH):
        def mv(row, i):
            return mod[row, l, i * D:(i + 1) * D]
        in_maps = []
        for c in range(8):
            b, g = c // 4, c % 4
            modv = {'msh_lat': mv(b, 0), 'msc_lat': mv(b, 1), 'msh_ctx': mv(2, 0), 'msc_ctx': mv(2, 1)}
            in_maps.append(ab_inputs(I, l, b, g, hT[b], modv, cstv))
        res = run_bass_kernel_spmd(get_nc('AB'), in_maps, core_ids=list(range(8)))
        oT = []
        for b in range(B):
            parts = [res.results[b * 4 + g]['oT'] for g in range(4)]
            dn = np.concatenate([p[0:256] for p in parts], axis=0)
            rw = np.concatenate([p[256:512] for p in parts], axis=0)
            oT.append(np.concatenate([dn, rw], axis=0))
        wout = lay_w(I['w_out'][l])
        wfi = lay_w(I['w_ffn_in'][l])
        wfo = lay_w(I['w_ffn_out'][l])
        in_maps = []

        def tok(a, j):
            return np.ascontiguousarray(np.concatenate([a[:, 64 * j:64 * j + 64], a[:, 256 + 1024 * j:256 + 1024 * j + 1024]], axis=1))
        for c in range(8):
            b, j = c // 4, c % 4
            vec = {'mix_post_g': I['mix_post_g'][l], 'gmix_lat': mv(b, 2), 'gmix_ctx': mv(2, 2),
                   'ffn_pre_g': I['ffn_pre_g'][l], 'fsh_lat': mv(b, 3), 'fsh_ctx': mv(2, 3),
                   'fsc_lat': mv(b, 4), 'fsc_ctx': mv(2, 4), 'ffn_post_g': I['ffn_post_g'][l],
                   'gffn_lat': mv(b, 5), 'gffn_ctx': mv(2, 5)}
            prm = np.concatenate([fm(vec[k]) for k in C_VECS], axis=1)
            in_maps.append({'hT': tok(hT[b], j), 'oT': tok(oT[b], j), 'wout': wout, 'wfi': wfi, 'wfo': wfo, 'prm': prm})
        res = run_bass_kernel_spmd(get_nc('C'), in_maps, core_ids=list(range(8)))
        for b in range(B):
            hn = np.empty_like(hT[b])
            for j in range(4):
                r = res.results[b * 4 + j]['hn']
                hn[:, 64 * j:64 * j + 64] = r[:, 0:64]
                hn[:, 256 + 1024 * j:256 + 1024 * j + 1024] = r[:, 64:]
            hT[b] = hn
    out = np.stack([np.ascontiguousarray(hT[b][:, 256:].T) for b in range(B)])
    return out.astype(np.float32)
```

```python
import numpy as np
from contextlib import ExitStack
import concourse.bass as bass
import concourse.mybir as mybir
from concourse.bass_utils import run_bass_kernel_spmd

F32 = mybir.dt.float32
BF16 = mybir.dt.bfloat16
AF = mybir.ActivationFunctionType
ALU = mybir.AluOpType
AX = mybir.AxisListType

ENGS = ['tensor', 'vector', 'scalar', 'gpsimd', 'sync']
NDS = 12


class Buf:
    __slots__ = ('w', 'r', 'name')

    def __init__(self, name=''):
        self.w = None
        self.r = []
        self.name = name


class View:
    __slots__ = ('ap', 'bufs')

    def __init__(self, ap, bufs):
        self.ap = ap
        self.bufs = bufs

    def __getitem__(self, idx):
        return View(self.ap[idx], self.bufs)

    def rr(self, s, **kw):
        return View(self.ap.rearrange(s, **kw), self.bufs)

    def bc(self, shape):
        return View(self.ap.broadcast_to(shape), self.bufs)

    def un(self, axis):
        return View(self.ap.unsqueeze(axis), self.bufs)

    def tr(self, perm):
        return View(self.ap.transpose(perm), self.bufs)

    def cast(self, dt):
        return View(self.ap.bitcast(dt), self.bufs)


class Tile:
    def __init__(self, handle, name='', keys=None):
        self.h = handle
        self.name = name
        if keys is None:
            self.bufs = {None: Buf(name)}
        else:
            self.bufs = {k: Buf(f'{name}.{k}') for k in keys}

    def __getitem__(self, idx):
        return View(self.h[idx], list(self.bufs.values()))

    def k(self, key, idx=None):
        b = [self.bufs[key]]
        if idx is None:
            return View(self.h[:], b)
        return View(self.h[idx], b)


class Sched:
    def __init__(self, nc, es):
        self.nc = nc
        self.es = es
        self.q = {e: [] for e in ENGS}
        self.sems = {}
        for e in ENGS:
            self.sems[e] = es.enter_context(nc.semaphore(f'cs_{e}'))
        self.cnt = {e: 0 for e in ENGS}
        self.waited = {e: {} for e in ENGS}
        self.dslots = {}
        self.dval = {}
        self.dnext = {}
        for e in ['sync', 'gpsimd', 'scalar']:
            self.dslots[e] = []
            for i in range(NDS):
                key = ('d', e, i)
                self.sems[key] = es.enter_context(nc.semaphore(f'ds_{e}{i}'))
                self.dval[key] = 0
                self.dslots[e].append(key)
            self.dnext[e] = 0
        self.same_engine_sync = {'tensor': False, 'vector': True, 'scalar': True,
                                 'gpsimd': True, 'sync': False}
        self.ninstr = 0

    def _uniq(self, name):
        self._u = getattr(self, '_u', 0) + 1
        return f'{name}_{self._u}'

    def sb(self, name, shape, dt=F32, keys=None):
        h = self.es.enter_context(self.nc.sbuf_tensor(self._uniq('sb_' + name), list(shape), dt))
        return Tile(h, name, keys)

    def ps(self, name, shape, dt=F32, keys=None):
        h = self.es.enter_context(self.nc.psum_tensor(self._uniq('ps_' + name), list(shape), dt))
        return Tile(h, name, keys)

    def dram(self, name, shape, dt=F32, kind="Internal", keys=None):
        h = self.nc.dram_tensor(name, list(shape), dt, kind=kind)
        return Tile(h, name, keys)

    def _waits(self, e, deps):
        need = {}
        for (key, val) in deps:
            if key == e and not self.same_engine_sync[e]:
                continue
            if self.waited[e].get(key, 0) >= val:
                continue
            if need.get(key, 0) < val:
                need[key] = val
        for key, val in need.items():
            self.waited[e][key] = val
        return list(need.items())

    def _deps(self, reads, writes):
        deps = []
        for v in reads:
            for b in v.bufs:
                if b.w is not None:
                    deps.append(b.w)
        for v in writes:
            for b in v.bufs:
                if b.w is not None:
                    deps.append(b.w)
                deps.extend(b.r)
        return deps

    def _commit(self, tok, reads, writes):
        for v in reads:
            for b in v.bufs:
                b.r.append(tok)
                if len(b.r) > 64:
                    m = {}
                    for k_, v_ in b.r:
                        if m.get(k_, 0) < v_:
                            m[k_] = v_
                    b.r = list(m.items())
        for v in writes:
            for b in v.bufs:
                b.w = tok
                b.r = []

    def op(self, e, fn, reads=(), writes=()):
        deps = self._deps(reads, writes)
        waits = self._waits(e, deps)
        self.cnt[e] += 1
        tok = (e, self.cnt[e])
        self.q[e].append((waits, fn, e, 1))
        self._commit(tok, reads, writes)
        self.ninstr += 1
        return tok

    def dma(self, e, fn, reads=(), writes=()):
        i = self.dnext[e]
        self.dnext[e] = (i + 1) % NDS
        key = self.dslots[e][i]
        deps = self._deps(reads, writes)
        if self.dval[key] > 0:
            deps.append((key, self.dval[key]))
        waits = self._waits(e, deps)
        self.dval[key] += 16
        tok = (key, self.dval[key])
        self.q[e].append((waits, fn, key, 16))
        self._commit(tok, reads, writes)
        self.ninstr += 1
        return tok

    def finish(self):
        waits = []
        for key, val in self.dval.items():
            if val > 0:
                waits.append((key, val))
        for e in ENGS:
            if e != 'sync' and self.cnt[e] > 0:
                waits.append((e, self.cnt[e]))
        self.q['sync'].append((waits, None, None, 0))

    def emit(self):
        nc = self.nc
        sems = self.sems
        q = self.q
        with nc.Block() as block:
            def run(eng, items):
                for waits, fn, skey, inc in items:
                    for key, val in waits:
                        eng.wait_ge(sems[key], val)
                    if fn is not None:
                        ins = fn(eng)
                        ins.then_inc(sems[skey], inc)

            @block.sync
            def _(eng):
                run(eng, q['sync'])

            @block.tensor
            def _(eng):
                run(eng, q['tensor'])

            @block.vector
            def _(eng):
                run(eng, q['vector'])

            @block.scalar
            def _(eng):
                run(eng, q['scalar'])

            @block.gpsimd
            def _(eng):
                run(eng, q['gpsimd'])

    def mm(self, out, lhsT, rhs, start=True, stop=True):
        rd = [lhsT, rhs] + ([] if start else [out])
        return self.op('tensor', lambda e: e.matmul(out.ap, lhsT.ap, rhs.ap, start=start, stop=stop),
                       rd, [out])

    def transpose(self, out, in_, ident):
        return self.op('tensor', lambda e: e.transpose(out.ap, in_.ap, ident.ap), [in_, ident], [out])

    def act(self, out, in_, func, bias=None, scale=None, eng='scalar'):
        rd = [in_]
        kw = {}
        if bias is not None:
            if isinstance(bias, View):
                rd.append(bias)
                kw['bias'] = bias.ap
            else:
                kw['bias'] = bias
        if scale is not None:
            if isinstance(scale, View):
                rd.append(scale)
                kw['scale'] = scale.ap
            else:
                kw['scale'] = scale
        return self.op('scalar', lambda e: e.activation(out.ap, in_.ap, func, **kw), rd, [out])

    def tt(self, out, in0, in1, op, eng='vector'):
        return self.op(eng, lambda e: e.tensor_tensor(out.ap, in0.ap, in1.ap, op), [in0, in1], [out])

    def ts(self, out, in0, s1, op0, s2=None, op1=None, eng='vector'):
        rd = [in0]
        a1 = s1
        if isinstance(s1, View):
            rd.append(s1)
            a1 = s1.ap
        a2 = s2
        if isinstance(s2, View):
            rd.append(s2)
            a2 = s2.ap
        if op1 is None:
            return self.op(eng, lambda e: e.tensor_scalar(out.ap, in0.ap, a1, None, op0), rd, [out])
        return self.op(eng, lambda e: e.tensor_scalar(out.ap, in0.ap, a1, a2, op0, op1), rd, [out])

    def stt(self, out, in0, scalar, in1, op0, op1, eng='vector'):
        rd = [in0, in1]
        a = scalar
        if isinstance(scalar, View):
            rd.append(scalar)
            a = scalar.ap
        return self.op(eng, lambda e: e.scalar_tensor_tensor(out.ap, in0.ap, a, in1.ap, op0, op1), rd, [out])

    def copy(self, out, in_, eng='vector'):
        if eng == 'scalar':
            return self.op(eng, lambda e: e.copy(out.ap, in_.ap), [in_], [out])
        return self.op(eng, lambda e: e.tensor_copy(out.ap, in_.ap), [in_], [out])

    def memset(self, out, val, eng='vector'):
        return self.op(eng, lambda e: e.memset(out.ap, val), [], [out])

    def recip(self, out, in_):
        return self.op('vector', lambda e: e.reciprocal(out.ap, in_.ap), [in_], [out])

    def scan(self, out, d0, d1, init, op0, op1):
        rd = [d0, d1]
        a = init
        if isinstance(init, View):
            rd.append(init)
            a = init.ap
        return self.op('vector', lambda e: e.tensor_tensor_scan(out.ap, d0.ap, d1.ap, a, op0, op1), rd, [out])

    def load(self, out, in_, eng='sync', **kw):
        return self.dma(eng, lambda e: e.dma_start(out=out.ap, in_=in_.ap, **kw), [in_], [out])


D = 2048
KC = 16
SEQ = 4096
CTX = 256
TT = SEQ + CTX
NTC = 1088
HALF = 544
FH = 5632
HC = 44
EPS = 1e-6


def new_nc():
    return bass.Bass("TRN2", target_bir_lowering=False)


MN = 1536


def build_M():
    nc = new_nc()
    ccT_d = nc.dram_tensor("ccT", [128, KC, 3], F32, kind="ExternalInput")
    wm_d = nc.dram_tensor("wm", [4, 128, KC, MN], F32, kind="ExternalInput")
    bm_d = nc.dram_tensor("bm", [3, 4, MN], F32, kind="ExternalInput")
    out_d = nc.dram_tensor("mo", [3, 4, MN], F32, kind="ExternalOutput")
    with ExitStack() as es:
        S = Sched(nc, es)
        ccT = Tile(ccT_d); wm = Tile(wm_d); bm = Tile(bm_d); out = Tile(out_d)
        cc = S.sb('cc', [128, KC, 3])
        sc = S.sb('sc', [128, KC, 3])
        bmt = S.sb('bmt', [3, 4, MN])
        res = S.sb('res', [3, 4, MN])
        wts = [S.sb(f'w{i}', [128, 4, 512]) for i in range(3)]
        pss = [S.ps(f'ps{i}', [128, 512]) for i in range(2)]
        S.load(cc[:], ccT[:])
        S.load(bmt[:], bm[:])
        S.act(sc[:], cc[:], AF.Silu)
        it = 0
        for l in range(4):
            for nt in range(MN // 512):
                ps = pss[(l * 3 + nt) % 2]
                for kq in range(KC // 4):
                    w = wts[it % 3]
                    it += 1
                    S.load(w[:], wm[l, :, kq * 4:(kq + 1) * 4, nt * 512:(nt + 1) * 512])
                    for k4 in range(4):
                        kc = kq * 4 + k4
                        S.mm(ps[0:3, :], sc[:, kc, :], w[:, k4, :], start=(kc == 0), stop=(kc == KC - 1))
                S.tt(res[:, l, nt * 512:(nt + 1) * 512], ps[0:3, :], bmt[:, l, nt * 512:(nt + 1) * 512], ALU.add)
        S.load(out[:], res[:])
        S.finish()
        S.emit()
    return nc


C_VECS = ['mix_post_g', 'gmix_lat', 'gmix_ctx', 'ffn_pre_g', 'fsh_lat', 'fsh_ctx', 'fsc_lat', 'fsc_ctx',
          'ffn_post_g', 'gffn_lat', 'gffn_ctx']
NPC = len(C_VECS) * KC


def build_C():
    nc = new_nc()
    hT_d = nc.dram_tensor("hT", [D, NTC], F32, kind="ExternalInput")
    oT_d = nc.dram_tensor("oT", [D, NTC], F32, kind="ExternalInput")
    wout_d = nc.dram_tensor("wout", [KC, 128, KC, 128], F32, kind="ExternalInput")
    wfi_d = nc.dram_tensor("wfi", [2 * HC, 128, KC, 128], F32, kind="ExternalInput")
    wfo_d = nc.dram_tensor("wfo", [KC, 128, HC, 128], F32, kind="ExternalInput")
    prm_d = nc.dram_tensor("prm", [128, NPC], F32, kind="ExternalInput")
    hn_d = nc.dram_tensor("hn", [D, NTC], F32, kind="ExternalOutput")
    with ExitStack() as es:
        S = Sched(nc, es)
        hT = Tile(hT_d); oT = Tile(oT_d); wout = Tile(wout_d); wfi = Tile(wfi_d); wfo = Tile(wfo_d)
        prm_dr = Tile(prm_d); hn = Tile(hn_d)
        prm = S.sb('prm', [128, NPC])
        S.load(prm[:], prm_dr[:])

        def pv(name):
            i = C_VECS.index(name)
            return prm[:, i * KC:(i + 1) * KC]
        ones = S.sb('ones', [128, 128])
        S.memset(ones[:], 1.0)
        der = S.sb('der', [128, 6, KC])
        S.tt(der[:, 0, :], pv('mix_post_g'), pv('gmix_lat'), ALU.mult)
        S.tt(der[:, 1, :], pv('mix_post_g'), pv('gmix_ctx'), ALU.mult)
        S.tt(der[:, 2, :], pv('ffn_post_g'), pv('gffn_lat'), ALU.mult)
        S.tt(der[:, 3, :], pv('ffn_post_g'), pv('gffn_ctx'), ALU.mult)
        S.stt(der[:, 4, :], pv('fsc_lat'), 1.0, pv('ffn_pre_g'), ALU.add, ALU.mult)
        S.stt(der[:, 5, :], pv('fsc_ctx'), 1.0, pv('ffn_pre_g'), ALU.add, ALU.mult)
        pg = {'lat': der[:, 0, :], 'ctx': der[:, 1, :]}
        fg = {'lat': der[:, 2, :], 'ctx': der[:, 3, :]}
        gs = {'lat': der[:, 4, :], 'ctx': der[:, 5, :]}
        fsh = {'lat': pv('fsh_lat'), 'ctx': pv('fsh_ctx')}

        H = S.sb('H', [128, KC, HALF])
        M = S.sb('M', [128, KC, HALF])
        OB = S.sb('OB', [128, KC, HALF], BF16)
        A = S.sb('A', [128, HC, HALF], BF16)
        rstd = S.sb('rstd', [128, HALF])
        sqt = [S.sb(f'sqt{i}', [128, 272]) for i in range(2)]
        t2 = [S.sb(f't2{i}', [128, HALF]) for i in range(2)]
        wA = [S.sb(f'wA{i}', [128, KC, 128], BF16) for i in range(4)]
        wB = [S.sb(f'wB{i}', [128, HC, 128], BF16) for i in range(2)]
        PS = [S.ps(f'ps{i}', [128, 512]) for i in range(8)]
        psi = [0]

        def nps():
            p = PS[psi[0] % 6]
            psi[0] += 1
            return p
        pstat = [PS[6], PS[7]]
        NT = [(0, 272), (272, 544)]
        hTv = hT[:].rr('(kc p) t -> p kc t', p=128)
        oTv = oT[:].rr('(kc p) t -> p kc t', p=128)
        hnv = hn[:].rr('(kc p) t -> p kc t', p=128)
        wai = [0]

        def stats(X, scale):
            for ni, (a, b) in enumerate(NT):
                ps = pstat[ni]
                for kc in range(KC):
                    sq = sqt[kc % 2]
                    S.act(sq[:, 0:b - a], X[:, kc, a:b], AF.Square)
                    S.mm(ps[:, 0:b - a], ones[:], sq[:, 0:b - a], start=(kc == 0), stop=(kc == KC - 1))
                S.act(rstd[:, a:b], ps[:, 0:b - a], AF.Sqrt, bias=EPS, scale=1.0 / D)
            S.recip(rstd[:], rstd[:])

        def residual(X, gvec, segs):
            for kc in range(KC):
                S.tt(X[:, kc, :], X[:, kc, :], rstd[:], ALU.mult)
                for (a, b, w) in segs:
                    S.stt(H[:, kc, a:b], X[:, kc, a:b], gvec[w][:, kc:kc + 1], H[:, kc, a:b], ALU.mult, ALU.add)

        for hf in range(2):
            t0 = hf * HALF
            segs = [(0, 64, 'ctx'), (64, HALF, 'lat')] if hf == 0 else [(0, HALF, 'lat')]
            S.load(H[:], hTv[:, :, t0:t0 + HALF])
            S.load(OB[:], oTv[:, :, t0:t0 + HALF], eng='gpsimd')
            for oc in range(KC):
                w = wA[wai[0] % 4]
                wai[0] += 1
                S.load(w[:], wout[oc], eng='gpsimd')
                for (a, b) in NT:
                    ps = nps()
                    for kc in range(KC):
                        S.mm(ps[:, 0:b - a], w[:, kc, :], OB[:, kc, a:b], start=(kc == 0), stop=(kc == KC - 1))
                    S.copy(M[:, oc, a:b], ps[:, 0:b - a], eng='scalar')
            stats(M, 1.0)
            residual(M, pg, segs)
            stats(H, 1.0)
            for kc in range(KC):
                t = t2[kc % 2]
                S.tt(t[:], H[:, kc, :], rstd[:], ALU.mult)
                for (a, b, w_) in segs:
                    S.ts(OB[:, kc, a:b], t[:, a:b], gs[w_][:, kc:kc + 1], ALU.mult, fsh[w_][:, kc:kc + 1], ALU.add)
            for hc in range(HC):
                wg = wA[wai[0] % 4]
                wai[0] += 1
                wu = wA[wai[0] % 4]
                wai[0] += 1
                S.load(wg[:], wfi[hc], eng='gpsimd')
                S.load(wu[:], wfi[HC + hc], eng='gpsimd')
                for ni, (a, b) in enumerate(NT):
                    pg_ = nps()
                    pu_ = nps()
                    for kc in range(KC):
                        S.mm(pg_[:, 0:b - a], wg[:, kc, :], OB[:, kc, a:b], start=(kc == 0), stop=(kc == KC - 1))
                    for kc in range(KC):
                        S.mm(pu_[:, 0:b - a], wu[:, kc, :], OB[:, kc, a:b], start=(kc == 0), stop=(kc == KC - 1))
                    sg = sqt[ni]
                    S.act(sg[:, 0:b - a], pg_[:, 0:b - a], AF.Silu)
                    S.tt(A[:, hc, a:b], sg[:, 0:b - a], pu_[:, 0:b - a], ALU.mult)
            for oc in range(KC):
                w = wB[oc % 2]
                S.load(w[:], wfo[oc], eng='gpsimd')
                for (a, b) in NT:
                    ps = nps()
                    for hc in range(HC):
                        S.mm(ps[:, 0:b - a], w[:, hc, :], A[:, hc, a:b], start=(hc == 0), stop=(hc == HC - 1))
                    S.copy(M[:, oc, a:b], ps[:, 0:b - a], eng='scalar')
            stats(M, 1.0)
            residual(M, fg, segs)
            S.load(hnv[:, :, t0:t0 + HALF], H[:])
        S.finish()
        S.emit()
        print('C ninstr', S.ninstr)
    return nc

NCOL = 2216
C_Q, C_K, C_V, C_Z, C_G = 0, 256, 512, 768, 1024
C_RR, C_RK, C_RV, C_WD, C_AD, C_GD = 1032, 1288, 1544, 1800, 1928, 2056
PROJ_CHUNKS = [(i * 128, 128) for i in range(8)] + [(1024, 8)] + [(1032 + i * 128, 128) for i in range(9)] + [(2184, 32)]
RW_TILES = [(C_RR, 128), (C_RR + 128, 128), (C_RK, 128), (C_RK + 128, 128), (C_RV, 128), (C_RV + 128, 128),
            (C_WD, 128), (C_AD, 128), (C_GD, 128), (C_GD + 128, 32)]

A_PRM = [('mix_pre_g', 16), ('msh_lat', 16), ('msc_lat', 16), ('msh_ctx', 16), ('msc_ctx', 16),
         ('cw', 42), ('alog', 4), ('dtb', 4), ('outg', 1), ('mu', 10), ('w0', 4), ('a0', 4),
         ('kk', 2), ('ka', 2), ('rk', 2), ('lnw', 2), ('lnb', 2)]
A_OFF = {}
_o = 0
for _n, _w in A_PRM:
    A_OFF[_n] = (_o, _w)
    _o += _w
NPA = _o
CST_OFF = {'ident': (0, 128), 'LS': (128, 64), 'US': (192, 64), 'LI': (256, 64), 'UI': (320, 64),
           'blk': (384, 128), 'reset': (512, 512), 'm4': (1024, 4), 'm2': (1028, 2)}
NCST = 1030
NEGBIG = -60000.0
RW_LN_EPS_ = 64e-5


def make_cst():
    c = np.zeros((128, NCST), np.float32)
    c[:, 0:128] = np.eye(128)
    r = np.arange(64)[:, None]
    q = np.arange(64)[None, :]
    c[0:64, 128:192] = (r > q)
    c[0:64, 192:256] = (r < q)
    c[0:64, 256:320] = (r >= q)
    c[0:64, 320:384] = (r <= q)
    blk = np.zeros((128, 128), np.float32)
    blk[0:64, 0:64] = 1
    blk[64:, 64:] = 1
    c[:, 384:512] = blk
    rs = np.ones(512, np.float32)
    rs[::64] = 0
    c[:, 512:1024] = rs[None]
    p = np.arange(128)
    for ct in range(4):
        c[:, 1024 + ct] = (p % 4 == ct)
    for ct in range(2):
        c[:, 1028 + ct] = (p % 2 == ct)
    return c


class _Stop(Exception):
    pass


class Scope:
    stopped = False

    def __enter__(self):
        self.es = ExitStack()
        return self.es

    def __exit__(self, t, v, tb):
        self.es.close()
        if t is not None and issubclass(t, _Stop):
            Scope.stopped = True
            return True
        return False


def build_AB(stages=('A', 'DN', 'RW')):
    import os as _os2
    STOP = int(_os2.environ.get('DN_STOP', '0'))

    def chk(k):
        if STOP == k:
            raise _Stop()
    nc = new_nc()
    hT_d = nc.dram_tensor("hT", [D, TT], F32, kind="ExternalInput")
    win_d = nc.dram_tensor("win", [128, KC, NCOL], F32, kind="ExternalInput")
    prm_d = nc.dram_tensor("prm", [128, NPA], F32, kind="ExternalInput")
    wup_d = nc.dram_tensor("wup", [64, 2, 256], F32, kind="ExternalInput")
    aup_d = nc.dram_tensor("aup", [64, 2, 256], F32, kind="ExternalInput")
    gup_d = nc.dram_tensor("gup", [160, 256], F32, kind="ExternalInput")
    cst_d = nc.dram_tensor("cst", [128, NCST], F32, kind="ExternalInput")
    oT_d = nc.dram_tensor("oT", [512, TT], F32, kind="ExternalOutput")
    import os as _os
    pT_d = nc.dram_tensor("pT", [NCOL, TT], F32, kind=("ExternalOutput" if _os.environ.get("DBG_PT") else "Internal"))
    with ExitStack() as es0:
        S = Sched(nc, es0)
        hT = Tile(hT_d); win = Tile(win_d); prm_dr = Tile(prm_d); wup_dr = Tile(wup_d); aup_dr = Tile(aup_d)
        gup_dr = Tile(gup_d); cst_dr = Tile(cst_d); oT = Tile(oT_d)
        pT = Tile(pT_d, 'pT', keys=list(range(len(PROJ_CHUNKS))))

        def pTrows(r0, n):
            for ci, (c0, w) in enumerate(PROJ_CHUNKS):
                if c0 <= r0 and r0 + n <= c0 + w:
                    return View(pT.h[r0:r0 + n, :], [pT.bufs[ci]])
            raise ValueError((r0, n))

        prm = S.sb('prm', [128, NPA])
        cst = S.sb('cst', [128, NCST])
        S.load(prm[:], prm_dr[:])
        S.load(cst[:], cst_dr[:])

        def pv(name, j=None, n=1, np_=128):
            o, w = A_OFF[name]
            if j is None:
                return prm[0:np_, o:o + w]
            return prm[0:np_, o + j:o + j + n]

        def cv(name, np_=64):
            o, w = CST_OFF[name]
            return cst[0:np_, o:o + w]
        ones = S.sb('ones', [128, 128])
        S.memset(ones[:], 1.0)
        RDT = mybir.dt.float32r
        identR = S.sb('identR', [128, 128], RDT)
        ident = cv('ident', 128)
        S.copy(identR[:], ident)
        id64R = identR[0:64, 0:64]
        id64 = cst[0:64, 0:64]
        PS = [S.ps(f'b{i}', [128, 512]) for i in range(8)]
        psi = [0]

        def nb():
            p = PS[psi[0] % 7]
            psi[0] += 1
            return p

        def barrier():
            toks = [(e, S.cnt[e]) for e in ENGS if S.cnt[e] > 0 and e != 'sync']
            for key, val in S.dval.items():
                if val > 0:
                    toks.append((key, val))
            for e in ['tensor', 'vector', 'scalar', 'gpsimd', 'sync']:
                w = S._waits(e, toks)
                if w:
                    S.q[e].append((w, None, None, 0))

        SEGS = [(0, 256)] + [(256 + 512 * i, 512) for i in range(8)]
        CSEGS = [(0, 4)] + [(4 + 8 * i, 8) for i in range(8)]

        if 'A' in stages:
            with ExitStack() as es:
                S.es = es
                W = S.sb('W', [128, KC, NCOL], BF16)
                for kq in range(4):
                    S.load(W[:, kq * 4:(kq + 1) * 4, :], win[:, kq * 4:(kq + 1) * 4, :], eng='gpsimd')
                der = S.sb('derA', [128, 2, KC])
                S.stt(der[:, 0, :], pv('msc_lat'), 1.0, pv('mix_pre_g'), ALU.add, ALU.mult)
                S.stt(der[:, 1, :], pv('msc_ctx'), 1.0, pv('mix_pre_g'), ALU.add, ALU.mult)
                Hs = S.sb('Hseg', [128, KC, 512])
                U = S.sb('Useg', [128, KC, 512], BF16)
                sq = [S.sb(f'sqA{i}', [128, 512]) for i in range(2)]
                rstd = S.sb('rstdA', [128, 512])
                stg = [S.sb(f'stg{i}', [128, 512]) for i in range(4)]
                hTv = hT[:].rr('(kc p) t -> p kc t', p=128)
                si = 0
                for (t0, n) in SEGS:
                    isctx = (t0 == 0)
                    gsv = der[:, 1, :] if isctx else der[:, 0, :]
                    shv = pv('msh_ctx') if isctx else pv('msh_lat')
                    S.load(Hs[:, :, 0:n], hTv[:, :, t0:t0 + n])
                    ps = nb()
                    for kc in range(KC):
                        s_ = sq[kc % 2]
                        S.act(s_[:, 0:n], Hs[:, kc, 0:n], AF.Square)
                        S.mm(ps[:, 0:n], ones[:], s_[:, 0:n], start=(kc == 0), stop=(kc == KC - 1))
                    S.act(rstd[:, 0:n], ps[:, 0:n], AF.Sqrt, bias=EPS, scale=1.0 / D)
                    S.recip(rstd[:, 0:n], rstd[:, 0:n])
                    for kc in range(KC):
                        s_ = sq[kc % 2]
                        S.tt(s_[:, 0:n], Hs[:, kc, 0:n], rstd[:, 0:n], ALU.mult)
                        S.ts(U[:, kc, 0:n], s_[:, 0:n], gsv[:, kc:kc + 1], ALU.mult, shv[:, kc:kc + 1], ALU.add)
                    for ci, (c0, w) in enumerate(PROJ_CHUNKS):
                        ps = nb()
                        for kc in range(KC):
                            S.mm(ps[0:w, 0:n], W[:, kc, c0:c0 + w], U[:, kc, 0:n], start=(kc == 0), stop=(kc == KC - 1))
                        st = stg[si % 4]
                        si += 1
                        if si % 2 == 0:
                            S.copy(st[0:w, 0:n], ps[0:w, 0:n], eng='scalar')
                        else:
                            S.copy(st[0:w, 0:n], ps[0:w, 0:n], eng='vector')
                        S.load(View(pT.h[c0:c0 + w, t0:t0 + n], [pT.bufs[ci]]), st[0:w, 0:n])
                barrier()
            S.es = es0

        cder = S.sb('cder', [64, 5, 64])
        S.ts(cder[:, 0, :], cv('US'), NEGBIG, ALU.mult)
        S.ts(cder[:, 1, :], cv('LS'), NEGBIG, ALU.mult)
        S.ts(cder[:, 2, :], cv('UI'), -1.0, ALU.mult)
        S.ts(cder[:, 3, :], cv('LI'), -1.0, ALU.mult)
        S.stt(cder[:, 4, :], cv('LS'), -1.0, cv('US'), ALU.mult, ALU.subtract)
        negones = S.sb('negones', [64, 64])
        S.memset(negones[:], -1.0)
        NEGM = [cder[:, 0, :], cder[:, 1, :]]
        NEGMT = [cder[:, 1, :], cder[:, 0, :]]
        TRI = [cv('UI'), cv('LI')]
        NEGTRI = [cder[:, 2, :], cder[:, 3, :]]
        NOFFD = cder[:, 4, :]
        STRICT = [cv('LS'), cv('US')]
        STRICTT = [cv('US'), cv('LS')]
        INCLT = [cv('UI'), cv('LI')]

        def b3(v, nchk, axis, w=64):
            np_ = v.ap.shape[0]
            return v.un(axis).bc([np_, nchk, w])


        class Pool:
            def __init__(self, banks):
                self.b = banks
                self.i = 0

            def nb(self):
                p = self.b[self.i % len(self.b)]
                self.i += 1
                return p

        def zipgen(*gens):
            alive = [g for g in gens if g is not None]
            while alive:
                for g in list(alive):
                    try:
                        next(g)
                    except StopIteration:
                        alive.remove(g)
                yield

        def drive(gens):
            alive = list(gens)
            while alive:
                for g in list(alive):
                    try:
                        next(g)
                    except StopIteration:
                        alive.remove(g)

        def inverse_g(pool, X, XT, Ybuf, YTbuf, TTm, nchk):
            n = nchk * 64
            S.tt(TTm[:, 0:n].rr('p (c i) -> p c i', i=64), XT[:, 0:n].rr('p (c i) -> p c i', i=64),
                 b3(id64, nchk, 1), ALU.add)
            Y, YT = X, XT
            for k in range(1, 6):
                pY = pool.nb()
                for ci in range(nchk):
                    cs = slice(ci * 64, ci * 64 + 64)
                    S.mm(pY[0:64, cs], YT[:, cs], Y[:, cs])
                if k < 5:
                    pYT = pool.nb()
                    for ci in range(nchk):
                        cs = slice(ci * 64, ci * 64 + 64)
                        S.mm(pYT[0:64, cs], Y[:, cs], YT[:, cs])
                Yn = Ybuf[k % 2]
                S.copy(Yn[:, 0:n], pY[0:64, 0:n], eng='scalar')
                if k < 5:
                    YTn = YTbuf[k % 2]
                    S.copy(YTn[:, 0:n], pYT[0:64, 0:n], eng='scalar')
                yield
                pT_ = pool.nb()
                for ci in range(nchk):
                    cs = slice(ci * 64, ci * 64 + 64)
                    S.mm(pT_[0:64, cs], Yn[:, cs], TTm[:, cs])
                S.tt(TTm[:, 0:n], TTm[:, 0:n], pT_[0:64, 0:n], ALU.add)
                Y = Yn
                if k < 5:
                    YT = YTn
                yield

        NSEG = 17
        CSEG4 = [(4 * i, 4) for i in range(NSEG)]
        SEG4 = [(256 * i, 256) for i in range(NSEG)]
        ORDER = {0: list(range(NSEG)), 1: [0] + list(range(NSEG - 1, 0, -1))}
        misc = Pool([PS[6], PS[7]])

        if 'DN' in stages:
            for hh in range(2):
                with ExitStack() as es:
                    S.es = es
                    QT = S.sb('QT', [128, 68, 64], RDT)
                    KT = S.sb('KT', [128, 68, 64], RDT)
                    VT = S.sb('VT', [128, 68, 64], RDT)
                    OT = S.sb('OT', [128, 68, 64], keys=list(range(NSEG)))
                    sq = S.sb('sqD', [128, 512])
                    rs = S.sb('rsD', [128, 512])
                    BTA = S.sb('BTA', [64, 2, 68])
                    G = S.sb('Gg', [64, 2, 68])
                    with ExitStack() as es2:
                        S.es = es2
                        RAW = S.sb('RAW', [128, TT])
                        CV = S.sb('CV', [128, 68, 64])

                        def conv(r0, dst_final, cwj):
                            dst = CV
                            cw = pv('cw', cwj * 7, 7)
                            S.load(RAW[:], pTrows(r0, 128))
                            x = RAW[:, 0:256]
                            o = dst[:, 0:4, :].rr('p c i -> p (c i)')
                            S.ts(o, x, cw[:, 3:4], ALU.mult)
                            for k in range(7):
                                off = k - 3
                                if off == 0:
                                    continue
                                lo = max(0, -off)
                                hi = 256 - max(0, off)
                                S.stt(o[:, lo:hi], x[:, lo + off:hi + off], cw[:, k:k + 1], o[:, lo:hi], ALU.mult, ALU.add)
                            xl = RAW[:, 256:TT].rr('p (r c) -> p c r', c=64)
                            ol = dst[:, 4:68, :]
                            S.ts(ol, xl, cw[:, 3:4], ALU.mult)
                            for k in range(7):
                                off = k - 3
                                if off == 0:
                                    continue
                                c_ = cw[:, k:k + 1]
                                if off > 0:
                                    S.stt(ol[:, :, 0:64 - off], xl[:, :, off:64], c_, ol[:, :, 0:64 - off], ALU.mult, ALU.add)
                                    S.stt(ol[:, 0:63, 64 - off:64], xl[:, 1:64, 0:off], c_, ol[:, 0:63, 64 - off:64], ALU.mult, ALU.add)
                                else:
                                    o_ = -off
                                    S.stt(ol[:, :, o_:64], xl[:, :, 0:64 - o_], c_, ol[:, :, o_:64], ALU.mult, ALU.add)
                                    S.stt(ol[:, 1:64, 0:o_], xl[:, 0:63, 64 - o_:64], c_, ol[:, 1:64, 0:o_], ALU.mult, ALU.add)
                            S.act(dst_final[:].rr('p c i -> p (c i)'), dst[:].rr('p c i -> p (c i)'), AF.Silu)

                        def l2norm(X, mul):
                            Xf = X[:].rr('p c i -> p (c i)')
                            for (a, n) in SEGS:
                                S.act(sq[:, 0:n], Xf[:, a:a + n], AF.Square)
                                ps = misc.nb()
                                S.mm(ps[:, 0:n], ones[:], sq[:, 0:n])
                                S.act(rs[:, 0:n], ps[:, 0:n], AF.Sqrt, bias=EPS * mul, scale=float(mul))
                                S.recip(rs[:, 0:n], rs[:, 0:n])
                                S.tt(Xf[:, a:a + n], Xf[:, a:a + n], rs[:, 0:n], ALU.mult)

                        conv(C_Q + hh * 128, QT, 0 * 2 + hh)
                        l2norm(QT, 128.0)
                        conv(C_K + hh * 128, KT, 1 * 2 + hh)
                        l2norm(KT, 1.0)
                        conv(C_V + hh * 128, VT, 2 * 2 + hh)
                        GT = S.sb('GT', [64, 4, 68])
                        for j in range(4):
                            row = C_G + hh * 4 + j
                            gr = pTrows(row, 1)
                            S.load(GT[:, j, 4:68], View(gr.ap[:, 256:TT].rearrange('o (r c) -> (o r) c', c=64), gr.bufs))
                            S.load(GT[:, j, 0:4], View(gr.ap[:, 0:256].rearrange('o (c i) -> (o i) c', i=64), gr.bufs),
                                   allow_slow_non_contiguous=True)
                        gt1 = S.sb('gt1', [64, 2, 68])
                        gt2 = S.sb('gt2', [64, 2, 68])
                        gt3 = S.sb('gt3', [64, 2, 68])
                        nea = S.sb('nea', [64, 4])
                        S.act(nea[:], pv('alog', np_=64), AF.Exp)
                        S.ts(nea[:], nea[:], -1.0, ALU.mult)
                        S.act(BTA[:], GT[:, 0:2, :], AF.Sigmoid)
                        for d in range(2):
                            S.ts(gt1[:, d, :], GT[:, 2 + d, :], pv('dtb', hh * 2 + d, np_=64), ALU.add)
                        S.stt(gt2[:], gt1[:], -1.0, gt1[:], ALU.mult, ALU.max)
                        S.act(gt2[:], gt2[:], AF.Exp, scale=-1.0)
                        S.act(gt2[:], gt2[:], AF.Ln, bias=1.0)
                        S.ts(gt3[:], gt1[:], 0.0, ALU.max)
                        S.tt(gt3[:], gt3[:], gt2[:], ALU.add)
                        for d in range(2):
                            S.ts(G[:, d, :], gt3[:, d, :], nea[:, hh * 2 + d:hh * 2 + d + 1], ALU.mult)
                        barrier()
                    S.es = es
                    S.memset(OT[:], 0.0)
                    QTf = QT[:].rr('p c i -> p (c i)')

                    def mk_dset(d):
                        t = {}
                        for nm in ('GC', 'KDS', 'EGC', 'BEG', 't68'):
                            t[nm] = S.sb(f'{nm}{d}', [64, 68])
                        t['CD'] = S.sb(f'CD{d}', [128, 68])
                        for nm in ('GU', 'NBS', 'BI', 'NBST', 'Dm', 'DTm', 'X', 'XT', 'TTm', 'GCB'):
                            t[nm] = S.sb(f'{nm}{d}', [64, 256], RDT if nm in ('X', 'XT', 'TTm') else F32)
                        t['Yb'] = [S.sb(f'Yb{d}{i}', [64, 256], RDT) for i in range(2)]
                        t['YTb'] = [S.sb(f'YTb{d}{i}', [64, 256], RDT) for i in range(2)]
                        t['EGB'] = S.sb(f'EGB{d}', [128, 256])
                        for nm in ('VB', 'KBG'):
                            t[nm] = S.sb(f'{nm}{d}', [64, 4, 128], RDT)
                        t['obs'] = []
                        for i_ in range(2):
                            t['obs'].append(dict(QD=S.sb(f'QD{d}{i_}', [128, 256], RDT), P1T=S.sb(f'P1T{d}{i_}', [128, 256], RDT),
                                                 P2=S.sb(f'P2{d}{i_}', [64, 4, 128]), MT=S.sb(f'MT{d}{i_}', [64, 256], RDT),
                                                 KDC=S.sb(f'KDC{d}{i_}', [64, 4, 128], RDT)))
                        t['Um'] = S.sb(f'Um{d}', [64, 128], RDT)
                        t['Hst'] = S.sb(f'Hst{d}', [128, 128], RDT)
                        return t

                    def dn_dir(d, t, pool):
                        GC, KDS, EGC, BEG, t68, CD = t['GC'], t['KDS'], t['EGC'], t['BEG'], t['t68'], t['CD']
                        GU, NBS, BI, NBST, Dm, DTm, X, XT, TTm, GCB = (t[k_] for k_ in ('GU', 'NBS', 'BI', 'NBST', 'Dm', 'DTm', 'X', 'XT', 'TTm', 'GCB'))
                        EGB, VB, KBG, Um, Hst = (t[k_] for k_ in ('EGB', 'VB', 'KBG', 'Um', 'Hst'))
                        gT = G[:, d, :]
                        bT = BTA[:, d, :]
                        ps = pool.nb()
                        S.mm(ps[0:64, 0:68], TRI[d], gT)
                        S.copy(GC[:], ps[0:64, 0:68])
                        ps2 = pool.nb()
                        S.mm(ps2[:, 0:68], ones[0:64, :], gT)
                        S.act(CD[:], ps2[:, 0:68], AF.Exp)
                        S.tt(t68[:], ps2[0:64, 0:68], GC[:], ALU.subtract)
                        S.act(KDS[:], t68[:], AF.Exp)
                        S.act(EGC[:], GC[:], AF.Exp)
                        S.tt(BEG[:], bT, EGC[:], ALU.mult)
                        S.memset(Hst[:].cast(F32), 0.0)
                        yield
                        nchk = 4
                        n = 256

                        def v3(tl):
                            return tl[:, 0:n].rr('p (c i) -> p c i', i=64)

                        def pre(sidx, ob):
                            QD, P1T, P2, MT, KDC = ob['QD'], ob['P1T'], ob['P2'], ob['MT'], ob['KDC']
                            c0 = 4 * sidx
                            col0 = c0 * 64
                            gTs = G[:, d, c0:c0 + nchk]
                            bTs = BTA[:, d, c0:c0 + nchk]
                            S.tt(v3(GU), b3(gTs, nchk, 2), b3(TRI[d], nchk, 1), ALU.mult)
                            S.tt(v3(NBS), b3(bTs, nchk, 2), b3(NOFFD, nchk, 1), ALU.mult)
                            S.tt(v3(BI), b3(bTs, nchk, 2), b3(id64, nchk, 1), ALU.mult)
                            yield
                            pa = pool.nb()
                            S.mm(pa[:, 0:n], ones[0:64, :], GU[:, 0:n])
                            S.act(EGB[:, 0:n], pa[:, 0:n], AF.Exp)
                            S.copy(GCB[:, 0:n], pa[0:64, 0:n], eng='scalar')
                            S.tt(QD[:, 0:n], QTf[:, col0:col0 + n], EGB[:, 0:n], ALU.mult)
                            gcs = GC[:, c0:c0 + nchk]
                            S.tt(v3(Dm), b3(gcs, nchk, 2), v3(GCB), ALU.subtract, eng='gpsimd')
                            S.tt(v3(Dm), v3(Dm), b3(NEGM[d], nchk, 1), ALU.add, eng='gpsimd')
                            S.act(Dm[:, 0:n], Dm[:, 0:n], AF.Exp)
                            yield
                            S.tt(v3(DTm), v3(GCB), b3(gcs, nchk, 2), ALU.subtract, eng='gpsimd')
                            S.tt(v3(DTm), v3(DTm), b3(NEGMT[d], nchk, 1), ALU.add, eng='gpsimd')
                            S.act(DTm[:, 0:n], DTm[:, 0:n], AF.Exp)
                            pe_ = pool.nb()
                            S.mm(pe_[0:64, 0:n], ones[0:64, 0:64], BI[:, 0:n])
                            S.tt(v3(NBST), pe_[0:64, 0:n].rr('p (c i) -> p c i', i=64), b3(NOFFD, nchk, 1), ALU.mult)
                            yield
                            pd = pool.nb()
                            for ci in range(nchk):
                                c = c0 + ci
                                S.mm(pd[0:64, ci * 64:ci * 64 + 64], KT[:, c, :], KT[:, c, :])
                            S.tt(NBS[:, 0:n], NBS[:, 0:n], Dm[:, 0:n], ALU.mult, eng='gpsimd')
                            S.tt(NBST[:, 0:n], NBST[:, 0:n], DTm[:, 0:n], ALU.mult, eng='gpsimd')
                            S.tt(X[:, 0:n], pd[0:64, 0:n], NBS[:, 0:n], ALU.mult)
                            S.tt(XT[:, 0:n], pd[0:64, 0:n], NBST[:, 0:n], ALU.mult)
                            yield
                            yield from inverse_g(pool, X, XT, t['Yb'], t['YTb'], TTm, nchk)
                            pv_ = pool.nb()
                            pk_ = pool.nb()
                            for cj in range(4):
                                c = c0 + cj
                                S.transpose(pv_[0:64, cj * 128:cj * 128 + 128].cast(RDT), VT[:, c, :], identR[:])
                                S.transpose(pk_[0:64, cj * 128:cj * 128 + 128].cast(RDT), KT[:, c, :], identR[:])
                            pv3 = pv_[0:64, :].rr('p (c k) -> p c k', k=128)
                            pk3 = pk_[0:64, :].rr('p (c k) -> p c k', k=128)
                            S.tt(VB[:], pv3, b3(BTA[:, d, c0:c0 + 4], 4, 2, 128), ALU.mult)
                            S.tt(KBG[:], pk3, b3(BEG[:, c0:c0 + 4], 4, 2, 128), ALU.mult)
                            S.tt(KDC[:], pk3, b3(KDS[:, c0:c0 + 4], 4, 2, 128), ALU.mult)
                            yield
                            pu = pool.nb()
                            for ci in range(4):
                                S.mm(pu[0:64, ci * 128:ci * 128 + 128], TTm[:, ci * 64:ci * 64 + 64], VB[:, ci, :])
                            S.copy(P2[:], pu[0:64, :].rr('p (c k) -> p c k', k=128), eng='scalar')
                            pw = pool.nb()
                            for ci in range(nchk):
                                S.mm(pw[:, ci * 64:ci * 64 + 64], KBG[:, ci, :], TTm[:, ci * 64:ci * 64 + 64])
                            S.act(P1T[:, 0:n], pw[:, 0:n], AF.Copy, scale=-1.0)
                            yield
                            pm = pool.nb()
                            for ci in range(nchk):
                                c = c0 + ci
                                S.mm(pm[0:64, ci * 64:ci * 64 + 64], KT[:, c, :], QT[:, c, :])
                            S.tt(ob['MT'][:, 0:n], pm[0:64, 0:n], DTm[:, 0:n], ALU.mult)
                            yield

                        def chain(sidx, ob):
                            QD, P1T, P2, MT, KDC = ob['QD'], ob['P1T'], ob['P2'], ob['MT'], ob['KDC']
                            c0 = 4 * sidx
                            corder = list(range(nchk)) if d == 0 else list(range(nchk - 1, -1, -1))
                            for ci in corder:
                                c = c0 + ci
                                cs = slice(ci * 64, ci * 64 + 64)
                                p1 = pool.nb()
                                S.mm(p1[0:64, 0:128], P1T[:, cs], Hst[:])
                                S.tt(Um[:], p1[0:64, 0:128], P2[:, ci, :], ALU.add)
                                yield
                                p2 = pool.nb()
                                S.mm(p2[:, 0:64], Hst[:], QD[:, cs], start=True, stop=False)
                                S.mm(p2[:, 0:64], Um[:], MT[:, cs], start=False, stop=True)
                                ov = OT.k(sidx, (slice(None), c, slice(None)))
                                S.tt(ov, p2[:, 0:64], ov, ALU.add)
                                p3 = pool.nb()
                                S.mm(p3[:, 0:128], KDC[:, ci, :], Um[:])
                                S.stt(Hst[:], Hst[:], CD[:, c:c + 1], p3[:, 0:128], ALU.mult, ALU.add)
                                yield

                        order = ORDER[d]
                        obs = t['obs']
                        yield from pre(order[0], obs[0])
                        for i_, sidx in enumerate(order):
                            gl = [chain(sidx, obs[i_ % 2])]
                            if i_ + 1 < len(order):
                                gl.append(pre(order[i_ + 1], obs[(i_ + 1) % 2]))
                            yield from zipgen(*gl)

                    dsets = [mk_dset(0), mk_dset(1)]
                    gens_ = [dn_dir(0, dsets[0], Pool(PS[0:4])), dn_dir(1, dsets[1], Pool(PS[4:8]))]
                    if hh == 0 and 'RW' in stages:
                        RAW1 = S.sb('RAW1', [128, TT])
                        OUT1 = S.sb('OUT1', [128, TT])
                        c0m = S.sb('c0m', [128, 10])
                        m4mu = S.sb('m4mu', [128, 10, 4])
                        m2mu = S.sb('m2mu', [128, 10, 2])

                        def rw1_gen():
                            S.ts(c0m[:], pv('mu'), -1.0, ALU.mult, 1.0, ALU.add)
                            S.tt(m4mu[:], pv('mu').un(2).bc([128, 10, 4]), cv('m4', 128).un(1).bc([128, 10, 4]), ALU.mult)
                            S.tt(m2mu[:], pv('mu').un(2).bc([128, 10, 2]), cv('m2', 128).un(1).bc([128, 10, 2]), ALU.mult)
                            yield
                            for ti, (r0, np_) in enumerate(RW_TILES):
                                RAW = RAW1
                                OUT = OUT1
                                S.load(RAW[0:np_, :], pTrows(r0, np_))
                                yield
                                x = RAW[0:np_, 0:256]
                                o = OUT[0:np_, 0:256]
                                S.ts(o, x, c0m[0:np_, ti:ti + 1], ALU.mult)
                                S.stt(o[:, 1:256], x[:, 0:255], m2mu[0:np_, ti, 0:1], o[:, 1:256], ALU.mult, ALU.add)
                                S.stt(o[:, 0:255], x[:, 1:256], m2mu[0:np_, ti, 1:2], o[:, 0:255], ALU.mult, ALU.add)
                                yield
                                xl = RAW[0:np_, 256:TT].rr('p (r c) -> p r c', c=64)
                                ol = OUT[0:np_, 256:TT].rr('p (r c) -> p r c', c=64)
                                S.ts(ol, xl, c0m[0:np_, ti:ti + 1], ALU.mult)
                                yield
                                S.stt(ol[:, :, 1:64], xl[:, :, 0:63], m4mu[0:np_, ti, 0:1], ol[:, :, 1:64], ALU.mult, ALU.add)
                                yield
                                S.stt(ol[:, :, 0:63], xl[:, :, 1:64], m4mu[0:np_, ti, 1:2], ol[:, :, 0:63], ALU.mult, ALU.add)
                                yield
                                S.stt(ol[:, 1:64, :], xl[:, 0:63, :], m4mu[0:np_, ti, 2:3], ol[:, 1:64, :], ALU.mult, ALU.add)
                                yield
                                S.stt(ol[:, 0:63, :], xl[:, 1:64, :], m4mu[0:np_, ti, 3:4], ol[:, 0:63, :], ALU.mult, ALU.add)
                                yield
                                if r0 == C_WD:
                                    S.act(OUT[0:np_, :], OUT[0:np_, :], AF.Tanh)
                                if r0 >= C_GD:
                                    S.act(OUT[0:np_, :], OUT[0:np_, :], AF.Sigmoid)
                                S.load(pTrows(r0, np_), OUT[0:np_, :])
                                yield
                        gens_.append(rw1_gen())
                    drive(gens_)
                    if hh == 0 and 'RW' in stages:
                        RESt, ZRt = OUT1, RAW1
                    else:
                        RESt = S.sb('RESt', [128, TT])
                        ZRt = S.sb('ZRt', [128, TT])
                    OTf = OT[:].rr('p c i -> p (c i)')
                    for (a, n) in SEGS:
                        S.act(sq[:, 0:n], OTf[:, a:a + n], AF.Square)
                        ps = misc.nb()
                        S.mm(ps[:, 0:n], ones[:], sq[:, 0:n])
                        S.act(rs[:, 0:n], ps[:, 0:n], AF.Sqrt, bias=EPS, scale=1.0 / 128)
                        S.recip(rs[:, 0:n], rs[:, 0:n])
                        S.tt(OTf[:, a:a + n], OTf[:, a:a + n], rs[:, 0:n], ALU.mult)
                    ZR = ZRt[:]
                    S.load(ZR, pTrows(C_Z + hh * 128, 128))
                    S.act(ZR, ZR, AF.Silu)
                    S.ts(OTf, OTf, pv('outg', 0), ALU.mult)
                    RES = RESt[:]
                    S.tt(RES[:, 0:256], OTf[:, 0:256], ZR[:, 0:256], ALU.mult)
                    S.tt(RES[:, 256:TT].rr('p (r c) -> p c r', c=64), OT[:, 4:68, :],
                         ZR[:, 256:TT].rr('p (r c) -> p c r', c=64), ALU.mult)
                    S.load(oT[hh * 128:(hh + 1) * 128, :], RES)
                    barrier()
                S.es = es0

        if 'RW' in stages:
            with ExitStack() as es:
                S.es = es
                RW_TILES_ = [] if 'DN' in stages else RW_TILES
                RAWs = [S.sb(f'RAWr{i}', [128, TT]) for i in range(2)]
                OUTs = [S.sb(f'OUTr{i}', [128, TT]) for i in range(2)]
                c0m = S.sb('c0m', [128, 10])
                m4mu = S.sb('m4mu', [128, 10, 4])
                m2mu = S.sb('m2mu', [128, 10, 2])
                S.ts(c0m[:], pv('mu'), -1.0, ALU.mult, 1.0, ALU.add)
                S.tt(m4mu[:], pv('mu').un(2).bc([128, 10, 4]), cv('m4', 128).un(1).bc([128, 10, 4]), ALU.mult)
                S.tt(m2mu[:], pv('mu').un(2).bc([128, 10, 2]), cv('m2', 128).un(1).bc([128, 10, 2]), ALU.mult)
                for ti, (r0, np_) in enumerate(RW_TILES_):
                    RAW = RAWs[ti % 2]
                    OUT = OUTs[ti % 2]
                    ve = 'vector'
                    S.load(RAW[0:np_, :], pTrows(r0, np_))
                    x = RAW[0:np_, 0:256]
                    o = OUT[0:np_, 0:256]
                    S.ts(o, x, c0m[0:np_, ti:ti + 1], ALU.mult, eng=ve)
                    S.stt(o[:, 1:256], x[:, 0:255], m2mu[0:np_, ti, 0:1], o[:, 1:256], ALU.mult, ALU.add, eng=ve)
                    S.stt(o[:, 0:255], x[:, 1:256], m2mu[0:np_, ti, 1:2], o[:, 0:255], ALU.mult, ALU.add, eng=ve)
                    xl = RAW[0:np_, 256:TT].rr('p (r c) -> p r c', c=64)
                    ol = OUT[0:np_, 256:TT].rr('p (r c) -> p r c', c=64)
                    S.ts(ol, xl, c0m[0:np_, ti:ti + 1], ALU.mult, eng=ve)
                    S.stt(ol[:, :, 1:64], xl[:, :, 0:63], m4mu[0:np_, ti, 0:1], ol[:, :, 1:64], ALU.mult, ALU.add, eng=ve)
                    S.stt(ol[:, :, 0:63], xl[:, :, 1:64], m4mu[0:np_, ti, 1:2], ol[:, :, 0:63], ALU.mult, ALU.add, eng=ve)
                    S.stt(ol[:, 1:64, :], xl[:, 0:63, :], m4mu[0:np_, ti, 2:3], ol[:, 1:64, :], ALU.mult, ALU.add, eng=ve)
                    S.stt(ol[:, 0:63, :], xl[:, 1:64, :], m4mu[0:np_, ti, 3:4], ol[:, 0:63, :], ALU.mult, ALU.add, eng=ve)
                    if r0 == C_WD:
                        S.act(OUT[0:np_, :], OUT[0:np_, :], AF.Tanh)
                    if r0 >= C_GD:
                        S.act(OUT[0:np_, :], OUT[0:np_, :], AF.Sigmoid)
                    S.load(pTrows(r0, np_), OUT[0:np_, :])
                barrier()
            S.es = es0
            with ExitStack() as es:
                S.es = es
                WUP = S.sb('WUP', [64, 2, 256]); AUP = S.sb('AUP', [64, 2, 256])
                GUP0 = S.sb('GUP0', [128, 256]); GUP1 = S.sb('GUP1', [32, 256])
                S.load(WUP[:], wup_dr[:]); S.load(AUP[:], aup_dr[:])
                S.load(GUP0[:], gup_dr[0:128, :]); S.load(GUP1[:], gup_dr[128:160, :])
                omka = S.sb('omka', [128, 2])
                S.ts(omka[:], pv('ka'), -1.0, ALU.mult, 1.0, ALU.add)
                YT = S.sb('YTr', [128, TT], keys=list(range(NSEG)))
                BON = S.sb('BON', [128, TT], keys=list(range(NSEG)))
                GD0 = S.sb('GD0', [128, 256]); GD1 = S.sb('GD1', [32, 256])
                RESET = cv('reset', 128)
                BLK = cv('blk', 128)
                NQ = 256

                def mk_rset(d):
                    t = {}
                    names = ('R', 'Kp', 'V', 'LD', 'IC', 'KK', 'KD', 'PRE', 'LIN', 'LEX', 'EIN', 'EEX', 'ENI', 'ETL',
                             'AT', 'KIC', 'BT_', 'KT_', 'BH', 'KH')
                    for nm in names:
                        t[nm] = S.sb(f'r{nm}{d}', [128, NQ], RDT if nm in ('AT', 'BT_', 'KT_', 'BH', 'KH') else F32)
                    t['WD'] = S.sb(f'rWD{d}', [64, NQ]); t['AD'] = S.sb(f'rAD{d}', [64, NQ])
                    t['ATK'] = S.sb(f'rATK{d}', [64, 4, 128], RDT)
                    t['obs'] = []
                    for i_ in range(2):
                        t['obs'].append(dict(
                            RT=S.sb(f'rRT{d}{i_}', [128, NQ], RDT), P1T=S.sb(f'rP1T{d}{i_}', [128, NQ], RDT),
                            P2=S.sb(f'rP2{d}{i_}', [64, 4, 128]), VTK=S.sb(f'rVTK{d}{i_}', [64, 4, 128], RDT),
                            BHK=S.sb(f'rBHK{d}{i_}', [64, 4, 128], RDT), KHK=S.sb(f'rKHK{d}{i_}', [64, 4, 128], RDT),
                            WC=S.sb(f'rWC{d}{i_}', [128, 4]),
                            MrbT=[S.sb(f'rMb{d}{i_}{e}', [64, NQ], RDT) for e in range(2)],
                            MrkT=[S.sb(f'rMk{d}{i_}{e}', [64, NQ], RDT) for e in range(2)]))
                    t['hd'] = []
                    for e in range(2):
                        t['hd'].append(dict(
                            X=S.sb(f'rX{d}{e}', [64, NQ], RDT), XT=S.sb(f'rXT{d}{e}', [64, NQ], RDT), AkT=S.sb(f'rAk{d}{e}', [64, NQ], RDT),
                            Yb=[S.sb(f'rY{d}{e}{i}', [64, NQ], RDT) for i in range(2)],
                            YTb=[S.sb(f'rYT{d}{e}{i}', [64, NQ], RDT) for i in range(2)],
                            TTm=S.sb(f'rTT{d}{e}', [64, NQ], RDT), ZS=S.sb(f'rZS{d}{e}', [64, 4, 64], RDT)))
                    t['Um'] = S.sb(f'rUm{d}', [64, 128], RDT)
                    t['Hst'] = [S.sb(f'rHst{d}{i}', [128, 128], RDT) for i in range(2)]
                    return t

                def rw_dir(hp, d, t, pool, pP1):
                    hc0 = hp * 128
                    (R, Kp, V, LD, IC, KK, KD, PRE, LIN, LEX, EIN, EEX, ENI, ETL, AT, KIC, BT_, KT_, BH, KH) = (
                        t[k_] for k_ in ('R', 'Kp', 'V', 'LD', 'IC', 'KK', 'KD', 'PRE', 'LIN', 'LEX', 'EIN', 'EEX', 'ENI', 'ETL',
                                         'AT', 'KIC', 'BT_', 'KT_', 'BH', 'KH'))
                    KX, SQ, RS, TMP, T1, T2 = PRE, LIN, LEX, EIN, EEX, ENI
                    WD, AD, ATK, hd, Um = (t[k_] for k_ in ('WD', 'AD', 'ATK', 'hd', 'Um'))
                    Hst = t['Hst'][hp]
                    S.memset(Hst[:].cast(F32), 0.0)
                    w0c = pv('w0', d * 2 + hp)
                    a0c = pv('a0', d * 2 + hp)
                    nchk = 4
                    n = 256

                    def v3(tl, np_=128):
                        return tl[0:np_, 0:n].rr('p (c i) -> p c i', i=64)

                    def pre(sidx, ob):
                        RT, P1T, P2, VTK, BHK, KHK, WC = (ob[k_] for k_ in ('RT', 'P1T', 'P2', 'VTK', 'BHK', 'KHK', 'WC'))
                        t0 = 256 * sidx
                        cols = slice(t0, t0 + n)
                        S.load(R[:, 0:n], View(pT.h[C_RR + hc0:C_RR + hc0 + 128, cols], [pT.bufs[9 + hp]]))
                        S.load(Kp[:, 0:n], View(pT.h[C_RK + hc0:C_RK + hc0 + 128, cols], [pT.bufs[11 + hp]]))
                        S.load(V[:, 0:n], View(pT.h[C_RV + hc0:C_RV + hc0 + 128, cols], [pT.bufs[13 + hp]]))
                        S.load(WD[:, 0:n], View(pT.h[C_WD + d * 64:C_WD + d * 64 + 64, cols], [pT.bufs[15]]))
                        S.load(AD[:, 0:n], View(pT.h[C_AD + d * 64:C_AD + d * 64 + 64, cols], [pT.bufs[16]]))
                        yield
                        p1 = pool.nb()
                        S.mm(p1[:, 0:n], WUP[:, d, hc0:hc0 + 128], WD[:, 0:n])
                        S.act(LD[:, 0:n], p1[:, 0:n], AF.Sigmoid, bias=w0c)
                        S.ts(LD[:, 0:n], LD[:, 0:n], -0.6065306597126334, ALU.mult, eng='gpsimd')
                        p2 = pool.nb()
                        S.mm(p2[:, 0:n], AUP[:, d, hc0:hc0 + 128], AD[:, 0:n])
                        S.act(IC[:, 0:n], p2[:, 0:n], AF.Sigmoid, bias=a0c)
                        S.ts(KX[:, 0:n], Kp[:, 0:n], pv('kk', hp), ALU.mult)
                        S.act(SQ[:, 0:n], KX[:, 0:n], AF.Square)
                        yield
                        p3 = pool.nb()
                        S.mm(p3[:, 0:n], BLK, SQ[:, 0:n])
                        S.act(RS[:, 0:n], p3[:, 0:n], AF.Sqrt, bias=EPS, scale=1.0)
                        S.recip(RS[:, 0:n], RS[:, 0:n])
                        S.tt(KK[:, 0:n], KX[:, 0:n], RS[:, 0:n], ALU.mult)
                        S.ts(TMP[:, 0:n], IC[:, 0:n], pv('ka', hp), ALU.mult, omka[:, hp:hp + 1], ALU.add)
                        S.tt(KD[:, 0:n], Kp[:, 0:n], TMP[:, 0:n], ALU.mult)
                        S.tt(T1[:, 0:n], R[:, 0:n], KD[:, 0:n], ALU.mult)
                        S.ts(T1[:, 0:n], T1[:, 0:n], pv('rk', hp), ALU.mult)
                        yield
                        p4 = pool.nb()
                        S.mm(p4[:, 0:n], BLK, T1[:, 0:n])
                        S.tt(T2[:, 0:n], p4[:, 0:n], V[:, 0:n], ALU.mult)
                        bv = BON.k(sidx, (slice(None), cols))
                        S.tt(bv, bv, T2[:, 0:n], ALU.add)
                        S.scan(PRE[:, 0:n], RESET[:, 0:n], LD[:, 0:n], 0.0, ALU.mult, ALU.add)
                        TOT = v3(PRE)[:, :, 63:64]
                        if d == 0:
                            LINv = PRE
                        else:
                            S.tt(v3(LIN), TOT.bc([128, nchk, 64]), v3(PRE), ALU.subtract)
                            S.tt(LIN[:, 0:n], LIN[:, 0:n], LD[:, 0:n], ALU.add)
                            LINv = LIN
                        S.tt(LEX[:, 0:n], LINv[:, 0:n], LD[:, 0:n], ALU.subtract, eng='gpsimd')
                        S.act(WC[:, 0:nchk], PRE[:, 0:n].rr('p (c i) -> p c i', i=64)[:, :, 63], AF.Exp)
                        S.tt(v3(ETL), TOT.bc([128, nchk, 64]), v3(LINv), ALU.subtract)
                        yield
                        S.act(ETL[:, 0:n], ETL[:, 0:n], AF.Exp)
                        S.act(EEX[:, 0:n], LEX[:, 0:n], AF.Exp)
                        S.act(ENI[:, 0:n], LINv[:, 0:n], AF.Exp, scale=-1.0)
                        S.act(EIN[:, 0:n], LINv[:, 0:n], AF.Exp)
                        S.tt(RT[:, 0:n], R[:, 0:n], EIN[:, 0:n], ALU.mult, eng='gpsimd')
                        S.stt(AT[:, 0:n], KK[:, 0:n], -1.0, EEX[:, 0:n], ALU.mult, ALU.mult)
                        S.tt(KIC[:, 0:n], KK[:, 0:n], IC[:, 0:n], ALU.mult, eng='gpsimd')
                        yield
                        S.tt(BT_[:, 0:n], KIC[:, 0:n], ENI[:, 0:n], ALU.mult)
                        S.tt(KT_[:, 0:n], KD[:, 0:n], ENI[:, 0:n], ALU.mult, eng='gpsimd')
                        S.tt(BH[:, 0:n], KIC[:, 0:n], ETL[:, 0:n], ALU.mult)
                        S.tt(KH[:, 0:n], KD[:, 0:n], ETL[:, 0:n], ALU.mult, eng='gpsimd')
                        yield
                        for qi, (src, dst) in enumerate(((AT, ATK), (BH, BHK), (KH, KHK), (V, VTK))):
                            pt_ = pool.nb()
                            for cj in range(4):
                                if src is V:
                                    S.transpose(pt_[0:64, cj * 128:cj * 128 + 128], src[:, cj * 64:cj * 64 + 64], ident)
                                else:
                                    S.transpose(pt_[0:64, cj * 128:cj * 128 + 128].cast(RDT), src[:, cj * 64:cj * 64 + 64], identR[:])
                            S.copy(dst[:], pt_[0:64, :].rr('p (c k) -> p c k', k=128),
                                   eng='scalar' if qi % 2 == 0 else 'vector')
                            if qi % 2 == 1:
                                yield

                        def head(e):
                            h_ = hd[e]
                            pe = slice(64 * e, 64 * e + 64)
                            for (lh, rh, dst, msk) in ((AT, BT_, h_['X'], STRICT[d]), (BT_, AT, h_['XT'], STRICTT[d]),
                                                      (KT_, AT, h_['AkT'], STRICTT[d]), (BT_, RT, ob['MrbT'][e], INCLT[d]),
                                                      (KT_, RT, ob['MrkT'][e], INCLT[d])):
                                pq = pool.nb()
                                for ci in range(nchk):
                                    cs = slice(ci * 64, ci * 64 + 64)
                                    S.mm(pq[0:64, cs], lh[pe, cs], rh[pe, cs])
                                S.tt(v3(dst, 64), pq[0:64, 0:n].rr('p (c i) -> p c i', i=64), b3(msk, nchk, 1), ALU.mult)
                                yield
                            yield from inverse_g(pool, h_['X'], h_['XT'], h_['Yb'], h_['YTb'], h_['TTm'], nchk)
                            TTm = h_['TTm']
                            for ci in range(nchk):
                                cs = slice(ci * 64, ci * 64 + 64)
                                if e == 0:
                                    S.mm(pP1[pe, cs], ATK[:, ci, pe], TTm[:, cs])
                                else:
                                    S.mm(pP1[pe, cs], ATK[:, ci, pe].cast(F32), TTm[:, cs].cast(F32))
                            pz = pool.nb()
                            for ci in range(nchk):
                                cs = slice(ci * 64, ci * 64 + 64)
                                S.mm(pz[0:64, cs], h_['AkT'][:, cs], VTK[:, ci, pe])
                            S.copy(h_['ZS'][:], pz[0:64, 0:n].rr('p (c v) -> p c v', v=64), eng='scalar')
                            yield
                            pp2 = pool.nb()
                            for ci in range(nchk):
                                cs = slice(ci * 64, ci * 64 + 64)
                                S.mm(pp2[0:64, cs], TTm[:, cs], h_['ZS'][:, ci, :])
                            S.copy(P2[:, :, pe], pp2[0:64, 0:n].rr('p (c v) -> p c v', v=64), eng='vector')
                            yield
                        yield from zipgen(head(0), head(1))
                        S.copy(P1T[:, 0:n], pP1[:, 0:n], eng='scalar')
                        yield

                    def chain(sidx, ob):
                        RT, P1T, P2, VTK, BHK, KHK, WC = (ob[k_] for k_ in ('RT', 'P1T', 'P2', 'VTK', 'BHK', 'KHK', 'WC'))
                        t0 = 256 * sidx
                        corder = list(range(nchk)) if d == 0 else list(range(nchk - 1, -1, -1))
                        for ci in corder:
                            cs = slice(ci * 64, ci * 64 + 64)
                            gcol = slice(t0 + ci * 64, t0 + ci * 64 + 64)
                            q1 = pool.nb()
                            S.mm(q1[0:64, 0:128], P1T[:, cs], Hst[:])
                            S.tt(Um[:], q1[0:64, 0:128], P2[:, ci, :], ALU.add)
                            yield
                            q2 = pool.nb()
                            S.mm(q2[:, 0:64], Hst[:], RT[:, cs], start=True, stop=False)
                            for e in range(2):
                                pe = slice(64 * e, 64 * e + 64)
                                if e == 0:
                                    S.mm(q2[pe, 0:64], Um[:, pe], ob['MrbT'][e][:, cs], start=False, stop=False)
                                    S.mm(q2[pe, 0:64], VTK[:, ci, pe], ob['MrkT'][e][:, cs], start=False, stop=True)
                                else:
                                    S.mm(q2[pe, 0:64], Um[:, pe].cast(F32), ob['MrbT'][e][:, cs].cast(F32), start=False, stop=False)
                                    S.mm(q2[pe, 0:64], VTK[:, ci, pe].cast(F32), ob['MrkT'][e][:, cs].cast(F32), start=False, stop=True)
                            yv = YT.k(sidx, (slice(None), gcol))
                            S.tt(yv, q2[:, 0:64], yv, ALU.add)
                            q3 = pool.nb()
                            S.mm(q3[:, 0:128], BHK[:, ci, :], Um[:], start=True, stop=False)
                            S.mm(q3[:, 0:128], KHK[:, ci, :], VTK[:, ci, :], start=False, stop=True)
                            for e in range(2):
                                pe = slice(64 * e, 64 * e + 64)
                                S.stt(Hst[pe, pe], Hst[pe, pe], WC[pe, ci:ci + 1], q3[pe, pe], ALU.mult, ALU.add)
                            yield

                    order = ORDER[d]
                    obs = t['obs']
                    yield from pre(order[0], obs[0])
                    for i_, sidx in enumerate(order):
                        gl = [chain(sidx, obs[i_ % 2])]
                        if i_ + 1 < len(order):
                            gl.append(pre(order[i_ + 1], obs[(i_ + 1) % 2]))
                        yield from zipgen(*gl)

                rsets = [mk_rset(0), mk_rset(1)]
                R_ = rsets[0]['R']; Kp_ = rsets[0]['Kp']; V_ = rsets[0]['V']
                fpool = Pool(PS[0:6])
                for hp in range(2):
                    hc0 = hp * 128
                    S.memset(YT[:], 0.0)
                    S.memset(BON[:], 0.0)
                    drive([rw_dir(hp, 0, rsets[0], Pool(PS[0:3]), PS[6]), rw_dir(hp, 1, rsets[1], Pool(PS[3:6]), PS[7])])
                    for sidx, (t0, n) in enumerate(SEG4):
                        cols = slice(t0, t0 + n)
                        S.load(GD0[:, 0:n], View(pT.h[C_GD:C_GD + 128, cols], [pT.bufs[17]]))
                        S.load(GD1[:, 0:n], View(pT.h[C_GD + 128:C_GD + 160, cols], [pT.bufs[18]]))
                        yv = YT.k(sidx, (slice(None), cols))
                        f1 = fpool.nb()
                        S.mm(f1[:, 0:n], BLK, yv)
                        S.stt(R_[:, 0:n], f1[:, 0:n], -1.0 / 64, yv, ALU.mult, ALU.add)
                        S.act(Kp_[:, 0:n], R_[:, 0:n], AF.Square)
                        f2 = fpool.nb()
                        S.mm(f2[:, 0:n], BLK, Kp_[:, 0:n])
                        S.act(V_[:, 0:n], f2[:, 0:n], AF.Sqrt, bias=RW_LN_EPS_, scale=1.0 / 64)
                        S.recip(V_[:, 0:n], V_[:, 0:n])
                        S.tt(R_[:, 0:n], R_[:, 0:n], V_[:, 0:n], ALU.mult)
                        S.ts(R_[:, 0:n], R_[:, 0:n], pv('lnw', hp), ALU.mult, pv('lnb', hp), ALU.add)
                        S.tt(R_[:, 0:n], R_[:, 0:n], BON.k(sidx, (slice(None), cols)), ALU.add)
                        f3 = fpool.nb()
                        S.mm(f3[:, 0:n], GUP0[:, hc0:hc0 + 128], GD0[:, 0:n], start=True, stop=False)
                        S.mm(f3[:, 0:n], GUP1[:, hc0:hc0 + 128], GD1[:, 0:n], start=False, stop=True)
                        S.tt(yv, R_[:, 0:n], f3[:, 0:n], ALU.mult)
                    S.load(oT[256 + hc0:256 + hc0 + 128, :], YT[:])
                barrier()
            S.es = es0
        S.finish()
        S.emit()
        print('AB ninstr', S.ninstr, {e: S.cnt[e] for e in ENGS})
    return nc

DEPTH = 4
P_DN_ = 4128


def fm(v):
    return np.ascontiguousarray(np.asarray(v, np.float32).reshape(KC, 128).T)


def lay_w(w):
    K, N = w.shape
    return np.ascontiguousarray(w.reshape(K // 128, 128, N // 128, 128).transpose(2, 1, 0, 3))


def ab_cols(g):
    cols = []
    for t in range(4):
        cols += list(range(t * 1024 + 256 * g, t * 1024 + 256 * g + 256))
    for hh in range(2):
        for j in range(4):
            cols.append(4096 + j * 8 + 2 * g + hh)
    for t in range(3):
        cols += list(range(P_DN_ + t * 1024 + 256 * g, P_DN_ + t * 1024 + 256 * g + 256))
    cols += list(range(P_DN_ + 3072, P_DN_ + 3488))
    return np.array(cols)


def rep(x):
    return np.full((128,), x, np.float32)


def ab_prm(I, l, g, modv):
    P = np.zeros((128, NPA), np.float32)

    def put(name, j, col):
        o, w = A_OFF[name]
        P[:len(col), o + j] = col
    for name in ('mix_pre_g',):
        o, w = A_OFF[name]
        P[:, o:o + 16] = fm(I['mix_pre_g'][l])
    for name in ('msh_lat', 'msc_lat', 'msh_ctx', 'msc_ctx'):
        o, w = A_OFF[name]
        P[:, o:o + 16] = fm(modv[name])
    for t in range(3):
        for hh in range(2):
            ch0 = t * 1024 + (2 * g + hh) * 128
            for k in range(7):
                put('cw', (t * 2 + hh) * 7 + k, I['dn_conv'][l][k, ch0:ch0 + 128])
    for hh in range(2):
        for d in range(2):
            put('alog', hh * 2 + d, rep(I['dn_a_log'][l][d, 2 * g + hh]))
            put('dtb', hh * 2 + d, rep(I['dn_dt_bias'][l][d, 2 * g + hh]))
    put('outg', 0, I['dn_out_g'][l])
    mu = I['rw_mu'][l]
    ch0s = [256 * g, 256 * g + 128, 1024 + 256 * g, 1024 + 256 * g + 128, 2048 + 256 * g, 2048 + 256 * g + 128,
            3072, 3200, 3328, 3456]
    for ti, c0 in enumerate(ch0s):
        n = 32 if ti == 9 else 128
        put('mu', ti, mu[c0:c0 + n])
    for hp in range(2):
        c0 = 256 * g + 128 * hp
        for d in range(2):
            put('w0', d * 2 + hp, I['rw_w0'][l][d, c0:c0 + 128])
            put('a0', d * 2 + hp, I['rw_a0'][l][d, c0:c0 + 128])
        put('kk', hp, I['rw_k_k'][l][c0:c0 + 128])
        put('ka', hp, I['rw_k_a'][l][c0:c0 + 128])
        put('rk', hp, I['rw_r_k'][l].reshape(-1)[c0:c0 + 128])
        put('lnw', hp, I['rw_ln_w'][l][c0:c0 + 128])
        put('lnb', hp, I['rw_ln_b'][l][c0:c0 + 128])
    return P


def ab_inputs(I, l, b, g, hT_b, modv, cstv):
    cols = ab_cols(g)
    w = I['w_in'][l][:, cols]
    win = np.ascontiguousarray(w.reshape(KC, 128, NCOL).transpose(1, 0, 2))
    c0 = 256 * g
    return {
        'hT': hT_b, 'win': win, 'prm': ab_prm(I, l, g, modv),
        'wup': np.ascontiguousarray(I['rw_w_up'][l][:, :, c0:c0 + 256].transpose(1, 0, 2)),
        'aup': np.ascontiguousarray(I['rw_a_up'][l][:, :, c0:c0 + 256].transpose(1, 0, 2)),
        'gup': np.ascontiguousarray(I['rw_g_up'][l][:, c0:c0 + 256]),
        'cst': cstv,
    }


_NC_CACHE = {}


def get_nc(name):
    if name not in _NC_CACHE:
        _NC_CACHE[name] = {'M': build_M, 'AB': build_AB, 'C': build_C}[name]()
    return _NC_CACHE[name]


def kernel(**I):
    I = {k: np.asarray(v, np.float32) for k, v in I.items()}
    B = 2
    cc = np.stack([I['c'][0], I['c'][1], I['c_ctx']])
    ccT = np.ascontiguousarray(cc.reshape(3, KC, 128).transpose(2, 1, 0))
    in_maps = []
    for c in range(8):
        wm = I['w_mod'][:, :, c * MN:(c + 1) * MN]
        in_maps.append({'ccT': ccT,
                        'wm': np.ascontiguousarray(wm.reshape(DEPTH, KC, 128, MN).transpose(0, 2, 1, 3)),
                        'bm': np.ascontiguousarray(np.broadcast_to(I['b_mod'][None, :, c * MN:(c + 1) * MN], (3, DEPTH, MN)))})
    res = run_bass_kernel_spmd(get_nc('M'), in_maps, core_ids=list(range(8)))
    mod = np.concatenate([r['mo'] for r in res.results], axis=2)
    cstv = make_cst()
    hT = [np.ascontiguousarray(np.concatenate([I['ctx'][b], I['x'][b]], axis=0).T) for b in range(B)]
    for l in range(DEPTH):
        def mv(row, i):
            return mod[row, l, i * D:(i + 1) * D]
        in_maps = []
        for c in range(8):
            b, g = c // 4, c % 4
            modv = {'msh_lat': mv(b, 0), 'msc_lat': mv(b, 1), 'msh_ctx': mv(2, 0), 'msc_ctx': mv(2, 1)}
            in_maps.append(ab_inputs(I, l, b, g, hT[b], modv, cstv))
        res = run_bass_kernel_spmd(get_nc('AB'), in_maps, core_ids=list(range(8)))
        oT = []
        for b in range(B):
            parts = [res.results[b * 4 + g]['oT'] for g in range(4)]
            dn = np.concatenate([p[0:256] for p in parts], axis=0)
            rw = np.concatenate([p[256:512] for p in parts], axis=0)
            oT.append(np.concatenate([dn, rw], axis=0))
        wout = lay_w(I['w_out'][l])
        wfi = lay_w(I['w_ffn_in'][l])
        wfo = lay_w(I['w_ffn_out'][l])
        in_maps = []

        def tok(a, j):
            return np.ascontiguousarray(np.concatenate([a[:, 64 * j:64 * j + 64], a[:, 256 + 1024 * j:256 + 1024 * j + 1024]], axis=1))
        for c in range(8):
            b, j = c // 4, c % 4
            vec = {'mix_post_g': I['mix_post_g'][l], 'gmix_lat': mv(b, 2), 'gmix_ctx': mv(2, 2),
                   'ffn_pre_g': I['ffn_pre_g'][l], 'fsh_lat': mv(b, 3), 'fsh_ctx': mv(2, 3),
                   'fsc_lat': mv(b, 4), 'fsc_ctx': mv(2, 4), 'ffn_post_g': I['ffn_post_g'][l],
                   'gffn_lat': mv(b, 5), 'gffn_ctx': mv(2, 5)}
            prm = np.concatenate([fm(vec[k]) for k in C_VECS], axis=1)
            in_maps.append({'hT': tok(hT[b], j), 'oT': tok(oT[b], j), 'wout': wout, 'wfi': wfi, 'wfo': wfo, 'prm': prm})
        res = run_bass_kernel_spmd(get_nc('C'), in_maps, core_ids=list(range(8)))
        for b in range(B):
            hn = np.empty_like(hT[b])
            for j in range(4):
                r = res.results[b * 4 + j]['hn']
                hn[:, 64 * j:64 * j + 64] = r[:, 0:64]
                hn[:, 256 + 1024 * j:256 + 1024 * j + 1024] = r[:, 64:]
            hT[b] = hn
    out = np.stack([np.ascontiguousarray(hT[b][:, 256:].T) for b in range(B)])
    return out.astype(np.float32)
```

```python
import numpy as np
from contextlib import ExitStack
import concourse.bass as bass
import concourse.mybir as mybir
from concourse.bass_utils import run_bass_kernel_spmd

F32 = mybir.dt.float32
BF16 = mybir.dt.bfloat16
AF = mybir.ActivationFunctionType
ALU = mybir.AluOpType
AX = mybir.AxisListType

ENGS = ['tensor', 'vector', 'scalar', 'gpsimd', 'sync']
NDS = 12


class Buf:
    __slots__ = ('w', 'r', 'name')

    def __init__(self, name=''):
        self.w = None
        self.r = []
        self.name = name


class View:
    __slots__ = ('ap', 'bufs')

    def __init__(self, ap, bufs):
        self.ap = ap
        self.bufs = bufs

    def __getitem__(self, idx):
        return View(self.ap[idx], self.bufs)

    def rr(self, s, **kw):
        return View(self.ap.rearrange(s, **kw), self.bufs)

    def bc(self, shape):
        return View(self.ap.broadcast_to(shape), self.bufs)

    def un(self, axis):
        return View(self.ap.unsqueeze(axis), self.bufs)

    def tr(self, perm):
        return View(self.ap.transpose(perm), self.bufs)

    def cast(self, dt):
        return View(self.ap.bitcast(dt), self.bufs)


class Tile:
    def __init__(self, handle, name='', keys=None):
        self.h = handle
        self.name = name
        if keys is None:
            self.bufs = {None: Buf(name)}
        else:
            self.bufs = {k: Buf(f'{name}.{k}') for k in keys}

    def __getitem__(self, idx):
        return View(self.h[idx], list(self.bufs.values()))

    def k(self, key, idx=None):
        b = [self.bufs[key]]
        if idx is None:
            return View(self.h[:], b)
        return View(self.h[idx], b)


class Sched:
    def __init__(self, nc, es):
        self.nc = nc
        self.es = es
        self.q = {e: [] for e in ENGS}
        self.sems = {}
        for e in ENGS:
            self.sems[e] = es.enter_context(nc.semaphore(f'cs_{e}'))
        self.cnt = {e: 0 for e in ENGS}
        self.waited = {e: {} for e in ENGS}
        self.dslots = {}
        self.dval = {}
        self.dnext = {}
        for e in ['sync', 'gpsimd', 'scalar']:
            self.dslots[e] = []
            for i in range(NDS):
                key = ('d', e, i)
                self.sems[key] = es.enter_context(nc.semaphore(f'ds_{e}{i}'))
                self.dval[key] = 0
                self.dslots[e].append(key)
            self.dnext[e] = 0
        self.same_engine_sync = {'tensor': False, 'vector': True, 'scalar': True,
                                 'gpsimd': True, 'sync': False}
        self.ninstr = 0

    def _uniq(self, name):
        self._u = getattr(self, '_u', 0) + 1
        return f'{name}_{self._u}'

    def sb(self, name, shape, dt=F32, keys=None):
        h = self.es.enter_context(self.nc.sbuf_tensor(self._uniq('sb_' + name), list(shape), dt))
        return Tile(h, name, keys)

    def ps(self, name, shape, dt=F32, keys=None):
        h = self.es.enter_context(self.nc.psum_tensor(self._uniq('ps_' + name), list(shape), dt))
        return Tile(h, name, keys)

    def dram(self, name, shape, dt=F32, kind="Internal", keys=None):
        h = self.nc.dram_tensor(name, list(shape), dt, kind=kind)
        return Tile(h, name, keys)

    def _waits(self, e, deps):
        need = {}
        for (key, val) in deps:
            if key == e and not self.same_engine_sync[e]:
                continue
            if self.waited[e].get(key, 0) >= val:
                continue
            if need.get(key, 0) < val:
                need[key] = val
        for key, val in need.items():
            self.waited[e][key] = val
        return list(need.items())

    def _deps(self, reads, writes):
        deps = []
        for v in reads:
            for b in v.bufs:
                if b.w is not None:
                    deps.append(b.w)
        for v in writes:
            for b in v.bufs:
                if b.w is not None:
                    deps.append(b.w)
                deps.extend(b.r)
        return deps

    def _commit(self, tok, reads, writes):
        for v in reads:
            for b in v.bufs:
                b.r.append(tok)
                if len(b.r) > 64:
                    m = {}
                    for k_, v_ in b.r:
                        if m.get(k_, 0) < v_:
                            m[k_] = v_
                    b.r = list(m.items())
        for v in writes:
            for b in v.bufs:
                b.w = tok
                b.r = []

    def op(self, e, fn, reads=(), writes=()):
        deps = self._deps(reads, writes)
        waits = self._waits(e, deps)
        self.cnt[e] += 1
        tok = (e, self.cnt[e])
        self.q[e].append((waits, fn, e, 1))
        self._commit(tok, reads, writes)
        self.ninstr += 1
        return tok

    def dma(self, e, fn, reads=(), writes=()):
        i = self.dnext[e]
        self.dnext[e] = (i + 1) % NDS
        key = self.dslots[e][i]
        deps = self._deps(reads, writes)
        if self.dval[key] > 0:
            deps.append((key, self.dval[key]))
        waits = self._waits(e, deps)
        self.dval[key] += 16
        tok = (key, self.dval[key])
        self.q[e].append((waits, fn, key, 16))
        self._commit(tok, reads, writes)
        self.ninstr += 1
        return tok

    def finish(self):
        waits = []
        for key, val in self.dval.items():
            if val > 0:
                waits.append((key, val))
        for e in ENGS:
            if e != 'sync' and self.cnt[e] > 0:
                waits.append((e, self.cnt[e]))
        self.q['sync'].append((waits, None, None, 0))

    def emit(self):
        nc = self.nc
        sems = self.sems
        q = self.q
        with nc.Block() as block:
            def run(eng, items):
                for waits, fn, skey, inc in items:
                    for key, val in waits:
                        eng.wait_ge(sems[key], val)
                    if fn is not None:
                        ins = fn(eng)
                        ins.then_inc(sems[skey], inc)

            @block.sync
            def _(eng):
                run(eng, q['sync'])

            @block.tensor
            def _(eng):
                run(eng, q['tensor'])

            @block.vector
            def _(eng):
                run(eng, q['vector'])

            @block.scalar
            def _(eng):
                run(eng, q['scalar'])

            @block.gpsimd
            def _(eng):
                run(eng, q['gpsimd'])

    def mm(self, out, lhsT, rhs, start=True, stop=True):
        rd = [lhsT, rhs] + ([] if start else [out])
        return self.op('tensor', lambda e: e.matmul(out.ap, lhsT.ap, rhs.ap, start=start, stop=stop),
                       rd, [out])

    def transpose(self, out, in_, ident):
        return self.op('tensor', lambda e: e.transpose(out.ap, in_.ap, ident.ap), [in_, ident], [out])

    def act(self, out, in_, func, bias=None, scale=None, eng='scalar'):
        rd = [in_]
        kw = {}
        if bias is not None:
            if isinstance(bias, View):
                rd.append(bias)
                kw['bias'] = bias.ap
            else:
                kw['bias'] = bias
        if scale is not None:
            if isinstance(scale, View):
                rd.append(scale)
                kw['scale'] = scale.ap
            else:
                kw['scale'] = scale
        return self.op('scalar', lambda e: e.activation(out.ap, in_.ap, func, **kw), rd, [out])

    def tt(self, out, in0, in1, op, eng='vector'):
        return self.op(eng, lambda e: e.tensor_tensor(out.ap, in0.ap, in1.ap, op), [in0, in1], [out])

    def ts(self, out, in0, s1, op0, s2=None, op1=None, eng='vector'):
        rd = [in0]
        a1 = s1
        if isinstance(s1, View):
            rd.append(s1)
            a1 = s1.ap
        a2 = s2
        if isinstance(s2, View):
            rd.append(s2)
            a2 = s2.ap
        if op1 is None:
            return self.op(eng, lambda e: e.tensor_scalar(out.ap, in0.ap, a1, None, op0), rd, [out])
        return self.op(eng, lambda e: e.tensor_scalar(out.ap, in0.ap, a1, a2, op0, op1), rd, [out])

    def stt(self, out, in0, scalar, in1, op0, op1, eng='vector'):
        rd = [in0, in1]
        a = scalar
        if isinstance(scalar, View):
            rd.append(scalar)
            a = scalar.ap
        return self.op(eng, lambda e: e.scalar_tensor_tensor(out.ap, in0.ap, a, in1.ap, op0, op1), rd, [out])

    def copy(self, out, in_, eng='vector'):
        if eng == 'scalar':
            return self.op(eng, lambda e: e.copy(out.ap, in_.ap), [in_], [out])
        return self.op(eng, lambda e: e.tensor_copy(out.ap, in_.ap), [in_], [out])

    def memset(self, out, val, eng='vector'):
        return self.op(eng, lambda e: e.memset(out.ap, val), [], [out])

    def recip(self, out, in_):
        return self.op('vector', lambda e: e.reciprocal(out.ap, in_.ap), [in_], [out])

    def scan(self, out, d0, d1, init, op0, op1):
        rd = [d0, d1]
        a = init
        if isinstance(init, View):
            rd.append(init)
            a = init.ap
        return self.op('vector', lambda e: e.tensor_tensor_scan(out.ap, d0.ap, d1.ap, a, op0, op1), rd, [out])

    def load(self, out, in_, eng='sync', **kw):
        return self.dma(eng, lambda e: e.dma_start(out=out.ap, in_=in_.ap, **kw), [in_], [out])


D = 2048
KC = 16
SEQ = 4096
CTX = 256
TT = SEQ + CTX
NTC = 1088
HALF = 544
FH = 5632
HC = 44
EPS = 1e-6


def new_nc():
    return bass.Bass("TRN2", target_bir_lowering=False)


MN = 1536


def build_M():
    nc = new_nc()
    ccT_d = nc.dram_tensor("ccT", [128, KC, 3], F32, kind="ExternalInput")
    wm_d = nc.dram_tensor("wm", [4, 128, KC, MN], F32, kind="ExternalInput")
    bm_d = nc.dram_tensor("bm", [3, 4, MN], F32, kind="ExternalInput")
    out_d = nc.dram_tensor("mo", [3, 4, MN], F32, kind="ExternalOutput")
    with ExitStack() as es:
        S = Sched(nc, es)
        ccT = Tile(ccT_d); wm = Tile(wm_d); bm = Tile(bm_d); out = Tile(out_d)
        cc = S.sb('cc', [128, KC, 3])
        sc = S.sb('sc', [128, KC, 3])
        bmt = S.sb('bmt', [3, 4, MN])
        res = S.sb('res', [3, 4, MN])
        wts = [S.sb(f'w{i}', [128, 4, 512]) for i in range(3)]
        pss = [S.ps(f'ps{i}', [128, 512]) for i in range(2)]
        S.load(cc[:], ccT[:])
        S.load(bmt[:], bm[:])
        S.act(sc[:], cc[:], AF.Silu)
        it = 0
        for l in range(4):
            for nt in range(MN // 512):
                ps = pss[(l * 3 + nt) % 2]
                for kq in range(KC // 4):
                    w = wts[it % 3]
                    it += 1
                    S.load(w[:], wm[l, :, kq * 4:(kq + 1) * 4, nt * 512:(nt + 1) * 512])
                    for k4 in range(4):
                        kc = kq * 4 + k4
                        S.mm(ps[0:3, :], sc[:, kc, :], w[:, k4, :], start=(kc == 0), stop=(kc == KC - 1))
                S.tt(res[:, l, nt * 512:(nt + 1) * 512], ps[0:3, :], bmt[:, l, nt * 512:(nt + 1) * 512], ALU.add)
        S.load(out[:], res[:])
        S.finish()
        S.emit()
    return nc


C_VECS = ['mix_post_g', 'gmix_lat', 'gmix_ctx', 'ffn_pre_g', 'fsh_lat', 'fsh_ctx', 'fsc_lat', 'fsc_ctx',
          'ffn_post_g', 'gffn_lat', 'gffn_ctx']
NPC = len(C_VECS) * KC


def build_C():
    nc = new_nc()
    hT_d = nc.dram_tensor("hT", [D, NTC], F32, kind="ExternalInput")
    oT_d = nc.dram_tensor("oT", [D, NTC], F32, kind="ExternalInput")
    wout_d = nc.dram_tensor("wout", [KC, 128, KC, 128], F32, kind="ExternalInput")
    wfi_d = nc.dram_tensor("wfi", [2 * HC, 128, KC, 128], F32, kind="ExternalInput")
    wfo_d = nc.dram_tensor("wfo", [KC, 128, HC, 128], F32, kind="ExternalInput")
    prm_d = nc.dram_tensor("prm", [128, NPC], F32, kind="ExternalInput")
    hn_d = nc.dram_tensor("hn", [D, NTC], F32, kind="ExternalOutput")
    with ExitStack() as es:
        S = Sched(nc, es)
        hT = Tile(hT_d); oT = Tile(oT_d); wout = Tile(wout_d); wfi = Tile(wfi_d); wfo = Tile(wfo_d)
        prm_dr = Tile(prm_d); hn = Tile(hn_d)
        prm = S.sb('prm', [128, NPC])
        S.load(prm[:], prm_dr[:])

        def pv(name):
            i = C_VECS.index(name)
            return prm[:, i * KC:(i + 1) * KC]
        ones = S.sb('ones', [128, 128])
        S.memset(ones[:], 1.0)
        der = S.sb('der', [128, 6, KC])
        S.tt(der[:, 0, :], pv('mix_post_g'), pv('gmix_lat'), ALU.mult)
        S.tt(der[:, 1, :], pv('mix_post_g'), pv('gmix_ctx'), ALU.mult)
        S.tt(der[:, 2, :], pv('ffn_post_g'), pv('gffn_lat'), ALU.mult)
        S.tt(der[:, 3, :], pv('ffn_post_g'), pv('gffn_ctx'), ALU.mult)
        S.stt(der[:, 4, :], pv('fsc_lat'), 1.0, pv('ffn_pre_g'), ALU.add, ALU.mult)
        S.stt(der[:, 5, :], pv('fsc_ctx'), 1.0, pv('ffn_pre_g'), ALU.add, ALU.mult)
        pg = {'lat': der[:, 0, :], 'ctx': der[:, 1, :]}
        fg = {'lat': der[:, 2, :], 'ctx': der[:, 3, :]}
        gs = {'lat': der[:, 4, :], 'ctx': der[:, 5, :]}
        fsh = {'lat': pv('fsh_lat'), 'ctx': pv('fsh_ctx')}

        H = S.sb('H', [128, KC, HALF])
        M = S.sb('M', [128, KC, HALF])
        OB = S.sb('OB', [128, KC, HALF], BF16)
        A = S.sb('A', [128, HC, HALF], BF16)
        rstd = S.sb('rstd', [128, HALF])
        sqt = [S.sb(f'sqt{i}', [128, 272]) for i in range(2)]
        t2 = [S.sb(f't2{i}', [128, HALF]) for i in range(2)]
        wA = [S.sb(f'wA{i}', [128, KC, 128], BF16) for i in range(4)]
        wB = [S.sb(f'wB{i}', [128, HC, 128], BF16) for i in range(2)]
        PS = [S.ps(f'ps{i}', [128, 512]) for i in range(8)]
        psi = [0]

        def nps():
            p = PS[psi[0] % 6]
            psi[0] += 1
            return p
        pstat = [PS[6], PS[7]]
        NT = [(0, 272), (272, 544)]
        hTv = hT[:].rr('(kc p) t -> p kc t', p=128)
        oTv = oT[:].rr('(kc p) t -> p kc t', p=128)
        hnv = hn[:].rr('(kc p) t -> p kc t', p=128)
        wai = [0]

        def stats(X, scale):
            for ni, (a, b) in enumerate(NT):
                ps = pstat[ni]
                for kc in range(KC):
                    sq = sqt[kc % 2]
                    S.act(sq[:, 0:b - a], X[:, kc, a:b], AF.Square)
                    S.mm(ps[:, 0:b - a], ones[:], sq[:, 0:b - a], start=(kc == 0), stop=(kc == KC - 1))
                S.act(rstd[:, a:b], ps[:, 0:b - a], AF.Sqrt, bias=EPS, scale=1.0 / D)
            S.recip(rstd[:], rstd[:])

        def residual(X, gvec, segs):
            for kc in range(KC):
                S.tt(X[:, kc, :], X[:, kc, :], rstd[:], ALU.mult)
                for (a, b, w) in segs:
                    S.stt(H[:, kc, a:b], X[:, kc, a:b], gvec[w][:, kc:kc + 1], H[:, kc, a:b], ALU.mult, ALU.add)

        for hf in range(2):
            t0 = hf * HALF
            segs = [(0, 64, 'ctx'), (64, HALF, 'lat')] if hf == 0 else [(0, HALF, 'lat')]
            S.load(H[:], hTv[:, :, t0:t0 + HALF])
            S.load(OB[:], oTv[:, :, t0:t0 + HALF], eng='gpsimd')
            for oc in range(KC):
                w = wA[wai[0] % 4]
                wai[0] += 1
                S.load(w[:], wout[oc], eng='gpsimd')
                for (a, b) in NT:
                    ps = nps()
                    for kc in range(KC):
                        S.mm(ps[:, 0:b - a], w[:, kc, :], OB[:, kc, a:b], start=(kc == 0), stop=(kc == KC - 1))
                    S.copy(M[:, oc, a:b], ps[:, 0:b - a], eng='scalar')
            stats(M, 1.0)
            residual(M, pg, segs)
            stats(H, 1.0)
            for kc in range(KC):
                t = t2[kc % 2]
                S.tt(t[:], H[:, kc, :], rstd[:], ALU.mult)
                for (a, b, w_) in segs:
                    S.ts(OB[:, kc, a:b], t[:, a:b], gs[w_][:, kc:kc + 1], ALU.mult, fsh[w_][:, kc:kc + 1], ALU.add)
            for hc in range(HC):
                wg = wA[wai[0] % 4]
                wai[0] += 1
                wu = wA[wai[0] % 4]
                wai[0] += 1
                S.load(wg[:], wfi[hc], eng='gpsimd')
                S.load(wu[:], wfi[HC + hc], eng='gpsimd')
                for ni, (a, b) in enumerate(NT):
                    pg_ = nps()
                    pu_ = nps()
                    for kc in range(KC):
                        S.mm(pg_[:, 0:b - a], wg[:, kc, :], OB[:, kc, a:b], start=(kc == 0), stop=(kc == KC - 1))
                    for kc in range(KC):
                        S.mm(pu_[:, 0:b - a], wu[:, kc, :], OB[:, kc, a:b], start=(kc == 0), stop=(kc == KC - 1))
                    sg = sqt[ni]
                    S.act(sg[:, 0:b - a], pg_[:, 0:b - a], AF.Silu)
                    S.tt(A[:, hc, a:b], sg[:, 0:b - a], pu_[:, 0:b - a], ALU.mult)
            for oc in range(KC):
                w = wB[oc % 2]
                S.load(w[:], wfo[oc], eng='gpsimd')
                for (a, b) in NT:
                    ps = nps()
                    for hc in range(HC):
                        S.mm(ps[:, 0:b - a], w[:, hc, :], A[:, hc, a:b], start=(hc == 0), stop=(hc == HC - 1))
                    S.copy(M[:, oc, a:b], ps[:, 0:b - a], eng='scalar')
            stats(M, 1.0)
            residual(M, fg, segs)
            S.load(hnv[:, :, t0:t0 + HALF], H[:])
        S.finish()
        S.emit()
        print('C ninstr', S.ninstr)
    return nc

NCOL = 2216
C_Q, C_K, C_V, C_Z, C_G = 0, 256, 512, 768, 1024
C_RR, C_RK, C_RV, C_WD, C_AD, C_GD = 1032, 1288, 1544, 1800, 1928, 2056
PROJ_CHUNKS = [(i * 128, 128) for i in range(8)] + [(1024, 8)] + [(1032 + i * 128, 128) for i in range(9)] + [(2184, 32)]
RW_TILES = [(C_RR, 128), (C_RR + 128, 128), (C_RK, 128), (C_RK + 128, 128), (C_RV, 128), (C_RV + 128, 128),
            (C_WD, 128), (C_AD, 128), (C_GD, 128), (C_GD + 128, 32)]

A_PRM = [('mix_pre_g', 16), ('msh_lat', 16), ('msc_lat', 16), ('msh_ctx', 16), ('msc_ctx', 16),
         ('cw', 42), ('alog', 4), ('dtb', 4), ('outg', 1), ('mu', 10), ('w0', 4), ('a0', 4),
         ('kk', 2), ('ka', 2), ('rk', 2), ('lnw', 2), ('lnb', 2)]
A_OFF = {}
_o = 0
for _n, _w in A_PRM:
    A_OFF[_n] = (_o, _w)
    _o += _w
NPA = _o
CST_OFF = {'ident': (0, 128), 'LS': (128, 64), 'US': (192, 64), 'LI': (256, 64), 'UI': (320, 64),
           'blk': (384, 128), 'reset': (512, 512), 'm4': (1024, 4), 'm2': (1028, 2)}
NCST = 1030
NEGBIG = -60000.0
RW_LN_EPS_ = 64e-5


def make_cst():
    c = np.zeros((128, NCST), np.float32)
    c[:, 0:128] = np.eye(128)
    r = np.arange(64)[:, None]
    q = np.arange(64)[None, :]
    c[0:64, 128:192] = (r > q)
    c[0:64, 192:256] = (r < q)
    c[0:64, 256:320] = (r >= q)
    c[0:64, 320:384] = (r <= q)
    blk = np.zeros((128, 128), np.float32)
    blk[0:64, 0:64] = 1
    blk[64:, 64:] = 1
    c[:, 384:512] = blk
    rs = np.ones(512, np.float32)
    rs[::64] = 0
    c[:, 512:1024] = rs[None]
    p = np.arange(128)
    for ct in range(4):
        c[:, 1024 + ct] = (p % 4 == ct)
    for ct in range(2):
        c[:, 1028 + ct] = (p % 2 == ct)
    return c


class _Stop(Exception):
    pass


class Scope:
    stopped = False

    def __enter__(self):
        self.es = ExitStack()
        return self.es

    def __exit__(self, t, v, tb):
        self.es.close()
        if t is not None and issubclass(t, _Stop):
            Scope.stopped = True
            return True
        return False


def build_AB(stages=('A', 'DN', 'RW')):
    import os as _os2
    STOP = int(_os2.environ.get('DN_STOP', '0'))

    def chk(k):
        if STOP == k:
            raise _Stop()
    nc = new_nc()
    hT_d = nc.dram_tensor("hT", [D, TT], F32, kind="ExternalInput")
    win_d = nc.dram_tensor("win", [128, KC, NCOL], F32, kind="ExternalInput")
    prm_d = nc.dram_tensor("prm", [128, NPA], F32, kind="ExternalInput")
    wup_d = nc.dram_tensor("wup", [64, 2, 256], F32, kind="ExternalInput")
    aup_d = nc.dram_tensor("aup", [64, 2, 256], F32, kind="ExternalInput")
    gup_d = nc.dram_tensor("gup", [160, 256], F32, kind="ExternalInput")
    cst_d = nc.dram_tensor("cst", [128, NCST], F32, kind="ExternalInput")
    oT_d = nc.dram_tensor("oT", [512, TT], F32, kind="ExternalOutput")
    import os as _os
    pT_d = nc.dram_tensor("pT", [NCOL, TT], F32, kind=("ExternalOutput" if _os.environ.get("DBG_PT") else "Internal"))
    with ExitStack() as es0:
        S = Sched(nc, es0)
        hT = Tile(hT_d); win = Tile(win_d); prm_dr = Tile(prm_d); wup_dr = Tile(wup_d); aup_dr = Tile(aup_d)
        gup_dr = Tile(gup_d); cst_dr = Tile(cst_d); oT = Tile(oT_d)
        pT = Tile(pT_d, 'pT', keys=list(range(len(PROJ_CHUNKS))))

        def pTrows(r0, n):
            for ci, (c0, w) in enumerate(PROJ_CHUNKS):
                if c0 <= r0 and r0 + n <= c0 + w:
                    return View(pT.h[r0:r0 + n, :], [pT.bufs[ci]])
            raise ValueError((r0, n))

        prm = S.sb('prm', [128, NPA])
        cst = S.sb('cst', [128, NCST])
        S.load(prm[:], prm_dr[:])
        S.load(cst[:], cst_dr[:])

        def pv(name, j=None, n=1, np_=128):
            o, w = A_OFF[name]
            if j is None:
                return prm[0:np_, o:o + w]
            return prm[0:np_, o + j:o + j + n]

        def cv(name, np_=64):
            o, w = CST_OFF[name]
            return cst[0:np_, o:o + w]
        ones = S.sb('ones', [128, 128])
        S.memset(ones[:], 1.0)
        RDT = mybir.dt.float32r
        identR = S.sb('identR', [128, 128], RDT)
        ident = cv('ident', 128)
        S.copy(identR[:], ident)
        id64R = identR[0:64, 0:64]
        id64 = cst[0:64, 0:64]
        PS = [S.ps(f'b{i}', [128, 512]) for i in range(8)]
        psi = [0]

        def nb():
            p = PS[psi[0] % 7]
            psi[0] += 1
            return p

        def barrier():
            toks = [(e, S.cnt[e]) for e in ENGS if S.cnt[e] > 0 and e != 'sync']
            for key, val in S.dval.items():
                if val > 0:
                    toks.append((key, val))
            for e in ['tensor', 'vector', 'scalar', 'gpsimd', 'sync']:
                w = S._waits(e, toks)
                if w:
                    S.q[e].append((w, None, None, 0))

        SEGS = [(0, 256)] + [(256 + 512 * i, 512) for i in range(8)]
        CSEGS = [(0, 4)] + [(4 + 8 * i, 8) for i in range(8)]

        if 'A' in stages:
            with ExitStack() as es:
                S.es = es
                W = S.sb('W', [128, KC, NCOL], BF16)
                for kq in range(4):
                    S.load(W[:, kq * 4:(kq + 1) * 4, :], win[:, kq * 4:(kq + 1) * 4, :], eng='gpsimd')
                der = S.sb('derA', [128, 2, KC])
                S.stt(der[:, 0, :], pv('msc_lat'), 1.0, pv('mix_pre_g'), ALU.add, ALU.mult)
                S.stt(der[:, 1, :], pv('msc_ctx'), 1.0, pv('mix_pre_g'), ALU.add, ALU.mult)
                Hs = S.sb('Hseg', [128, KC, 512])
                U = S.sb('Useg', [128, KC, 512], BF16)
                sq = [S.sb(f'sqA{i}', [128, 512]) for i in range(2)]
                rstd = S.sb('rstdA', [128, 512])
                stg = [S.sb(f'stg{i}', [128, 512]) for i in range(4)]
                hTv = hT[:].rr('(kc p) t -> p kc t', p=128)
                si = 0
                for (t0, n) in SEGS:
                    isctx = (t0 == 0)
                    gsv = der[:, 1, :] if isctx else der[:, 0, :]
                    shv = pv('msh_ctx') if isctx else pv('msh_lat')
                    S.load(Hs[:, :, 0:n], hTv[:, :, t0:t0 + n])
                    ps = nb()
                    for kc in range(KC):
                        s_ = sq[kc % 2]
                        S.act(s_[:, 0:n], Hs[:, kc, 0:n], AF.Square)
                        S.mm(ps[:, 0:n], ones[:], s_[:, 0:n], start=(kc == 0), stop=(kc == KC - 1))
                    S.act(rstd[:, 0:n], ps[:, 0:n], AF.Sqrt, bias=EPS, scale=1.0 / D)
                    S.recip(rstd[:, 0:n], rstd[:, 0:n])
                    for kc in range(KC):
                        s_ = sq[kc % 2]
                        S.tt(s_[:, 0:n], Hs[:, kc, 0:n], rstd[:, 0:n], ALU.mult)
                        S.ts(U[:, kc, 0:n], s_[:, 0:n], gsv[:, kc:kc + 1], ALU.mult, shv[:, kc:kc + 1], ALU.add)
                    for ci, (c0, w) in enumerate(PROJ_CHUNKS):
                        ps = nb()
                        for kc in range(KC):
                            S.mm(ps[0:w, 0:n], W[:, kc, c0:c0 + w], U[:, kc, 0:n], start=(kc == 0), stop=(kc == KC - 1))
                        st = stg[si % 4]
                        si += 1
                        if si % 2 == 0:
                            S.copy(st[0:w, 0:n], ps[0:w, 0:n], eng='scalar')
                        else:
                            S.copy(st[0:w, 0:n], ps[0:w, 0:n], eng='vector')
                        S.load(View(pT.h[c0:c0 + w, t0:t0 + n], [pT.bufs[ci]]), st[0:w, 0:n])
                barrier()
            S.es = es0

        cder = S.sb('cder', [64, 5, 64])
        S.ts(cder[:, 0, :], cv('US'), NEGBIG, ALU.mult)
        S.ts(cder[:, 1, :], cv('LS'), NEGBIG, ALU.mult)
        S.ts(cder[:, 2, :], cv('UI'), -1.0, ALU.mult)
        S.ts(cder[:, 3, :], cv('LI'), -1.0, ALU.mult)
        S.stt(cder[:, 4, :], cv('LS'), -1.0, cv('US'), ALU.mult, ALU.subtract)
        negones = S.sb('negones', [64, 64])
        S.memset(negones[:], -1.0)
        NEGM = [cder[:, 0, :], cder[:, 1, :]]
        NEGMT = [cder[:, 1, :], cder[:, 0, :]]
        TRI = [cv('UI'), cv('LI')]
        NEGTRI = [cder[:, 2, :], cder[:, 3, :]]
        NOFFD = cder[:, 4, :]
        STRICT = [cv('LS'), cv('US')]
        STRICTT = [cv('US'), cv('LS')]
        INCLT = [cv('UI'), cv('LI')]

        def b3(v, nchk, axis, w=64):
            np_ = v.ap.shape[0]
            return v.un(axis).bc([np_, nchk, w])


        class Pool:
            def __init__(self, banks):
                self.b = banks
                self.i = 0

            def nb(self):
                p = self.b[self.i % len(self.b)]
                self.i += 1
                return p

        def zipgen(*gens):
            alive = [g for g in gens if g is not None]
            while alive:
                for g in list(alive):
                    try:
                        next(g)
                    except StopIteration:
                        alive.remove(g)
                yield

        def drive(gens):
            alive = list(gens)
            while alive:
                for g in list(alive):
                    try:
                        next(g)
                    except StopIteration:
                        alive.remove(g)

        def inverse_g(pool, X, XT, Ybuf, YTbuf, TTm, nchk):
            n = nchk * 64
            S.tt(TTm[:, 0:n].rr('p (c i) -> p c i', i=64), XT[:, 0:n].rr('p (c i) -> p c i', i=64),
                 b3(id64, nchk, 1), ALU.add)
            Y, YT = X, XT
            for k in range(1, 6):
                pY = pool.nb()
                for ci in range(nchk):
                    cs = slice(ci * 64, ci * 64 + 64)
                    S.mm(pY[0:64, cs], YT[:, cs], Y[:, cs])
                if k < 5:
                    pYT = pool.nb()
                    for ci in range(nchk):
                        cs = slice(ci * 64, ci * 64 + 64)
                        S.mm(pYT[0:64, cs], Y[:, cs], YT[:, cs])
                Yn = Ybuf[k % 2]
                S.copy(Yn[:, 0:n], pY[0:64, 0:n], eng='scalar')
                if k < 5:
                    YTn = YTbuf[k % 2]
                    S.copy(YTn[:, 0:n], pYT[0:64, 0:n], eng='scalar')
                yield
                pT_ = pool.nb()
                for ci in range(nchk):
                    cs = slice(ci * 64, ci * 64 + 64)
                    S.mm(pT_[0:64, cs], Yn[:, cs], TTm[:, cs])
                S.tt(TTm[:, 0:n], TTm[:, 0:n], pT_[0:64, 0:n], ALU.add)
                Y = Yn
                if k < 5:
                    YT = YTn
                yield

        NSEG = 17
        CSEG4 = [(4 * i, 4) for i in range(NSEG)]
        SEG4 = [(256 * i, 256) for i in range(NSEG)]
        ORDER = {0: list(range(NSEG)), 1: [0] + list(range(NSEG - 1, 0, -1))}
        misc = Pool([PS[6], PS[7]])

        if 'DN' in stages:
            for hh in range(2):
                with ExitStack() as es:
                    S.es = es
                    QT = S.sb('QT', [128, 68, 64], RDT)
                    KT = S.sb('KT', [128, 68, 64], RDT)
                    VT = S.sb('VT', [128, 68, 64], RDT)
                    OT = S.sb('OT', [128, 68, 64], keys=list(range(NSEG)))
                    sq = S.sb('sqD', [128, 512])
                    rs = S.sb('rsD', [128, 512])
                    BTA = S.sb('BTA', [64, 2, 68])
                    G = S.sb('Gg', [64, 2, 68])
                    with ExitStack() as es2:
                        S.es = es2
                        RAW = S.sb('RAW', [128, TT])
                        CV = S.sb('CV', [128, 68, 64])

                        def conv(r0, dst_final, cwj):
                            dst = CV
                            cw = pv('cw', cwj * 7, 7)
                            S.load(RAW[:], pTrows(r0, 128))
                            x = RAW[:, 0:256]
                            o = dst[:, 0:4, :].rr('p c i -> p (c i)')
                            S.ts(o, x, cw[:, 3:4], ALU.mult)
                            for k in range(7):
                                off = k - 3
                                if off == 0:
                                    continue
                                lo = max(0, -off)
                                hi = 256 - max(0, off)
                                S.stt(o[:, lo:hi], x[:, lo + off:hi + off], cw[:, k:k + 1], o[:, lo:hi], ALU.mult, ALU.add)
                            xl = RAW[:, 256:TT].rr('p (r c) -> p c r', c=64)
                            ol = dst[:, 4:68, :]
                            S.ts(ol, xl, cw[:, 3:4], ALU.mult)
                            for k in range(7):
                                off = k - 3
                                if off == 0:
                                    continue
                                c_ = cw[:, k:k + 1]
                                if off > 0:
                                    S.stt(ol[:, :, 0:64 - off], xl[:, :, off:64], c_, ol[:, :, 0:64 - off], ALU.mult, ALU.add)
                                    S.stt(ol[:, 0:63, 64 - off:64], xl[:, 1:64, 0:off], c_, ol[:, 0:63, 64 - off:64], ALU.mult, ALU.add)
                                else:
                                    o_ = -off
                                    S.stt(ol[:, :, o_:64], xl[:, :, 0:64 - o_], c_, ol[:, :, o_:64], ALU.mult, ALU.add)
                                    S.stt(ol[:, 1:64, 0:o_], xl[:, 0:63, 64 - o_:64], c_, ol[:, 1:64, 0:o_], ALU.mult, ALU.add)
                            S.act(dst_final[:].rr('p c i -> p (c i)'), dst[:].rr('p c i -> p (c i)'), AF.Silu)

                        def l2norm(X, mul):
                            Xf = X[:].rr('p c i -> p (c i)')
                            for (a, n) in SEGS:
                                S.act(sq[:, 0:n], Xf[:, a:a + n], AF.Square)
                                ps = misc.nb()
                                S.mm(ps[:, 0:n], ones[:], sq[:, 0:n])
                                S.act(rs[:, 0:n], ps[:, 0:n], AF.Sqrt, bias=EPS * mul, scale=float(mul))
                                S.recip(rs[:, 0:n], rs[:, 0:n])
                                S.tt(Xf[:, a:a + n], Xf[:, a:a + n], rs[:, 0:n], ALU.mult)

                        conv(C_Q + hh * 128, QT, 0 * 2 + hh)
                        l2norm(QT, 128.0)
                        conv(C_K + hh * 128, KT, 1 * 2 + hh)
                        l2norm(KT, 1.0)
                        conv(C_V + hh * 128, VT, 2 * 2 + hh)
                        GT = S.sb('GT', [64, 4, 68])
                        for j in range(4):
                            row = C_G + hh * 4 + j
                            gr = pTrows(row, 1)
                            S.load(GT[:, j, 4:68], View(gr.ap[:, 256:TT].rearrange('o (r c) -> (o r) c', c=64), gr.bufs))
                            S.load(GT[:, j, 0:4], View(gr.ap[:, 0:256].rearrange('o (c i) -> (o i) c', i=64), gr.bufs),
                                   allow_slow_non_contiguous=True)
                        gt1 = S.sb('gt1', [64, 2, 68])
                        gt2 = S.sb('gt2', [64, 2, 68])
                        gt3 = S.sb('gt3', [64, 2, 68])
                        nea = S.sb('nea', [64, 4])
                        S.act(nea[:], pv('alog', np_=64), AF.Exp)
                        S.ts(nea[:], nea[:], -1.0, ALU.mult)
                        S.act(BTA[:], GT[:, 0:2, :], AF.Sigmoid)
                        for d in range(2):
                            S.ts(gt1[:, d, :], GT[:, 2 + d, :], pv('dtb', hh * 2 + d, np_=64), ALU.add)
                        S.stt(gt2[:], gt1[:], -1.0, gt1[:], ALU.mult, ALU.max)
                        S.act(gt2[:], gt2[:], AF.Exp, scale=-1.0)
                        S.act(gt2[:], gt2[:], AF.Ln, bias=1.0)
                        S.ts(gt3[:], gt1[:], 0.0, ALU.max)
                        S.tt(gt3[:], gt3[:], gt2[:], ALU.add)
                        for d in range(2):
                            S.ts(G[:, d, :], gt3[:, d, :], nea[:, hh * 2 + d:hh * 2 + d + 1], ALU.mult)
                        barrier()
                    S.es = es
                    S.memset(OT[:], 0.0)
                    QTf = QT[:].rr('p c i -> p (c i)')

                    def mk_dset(d):
                        t = {}
                        for nm in ('GC', 'KDS', 'EGC', 'BEG', 't68'):
                            t[nm] = S.sb(f'{nm}{d}', [64, 68])
                        t['CD'] = S.sb(f'CD{d}', [128, 68])
                        for nm in ('GU', 'NBS', 'BI', 'NBST', 'Dm', 'DTm', 'X', 'XT', 'TTm', 'GCB'):
                            t[nm] = S.sb(f'{nm}{d}', [64, 256], RDT if nm in ('X', 'XT', 'TTm') else F32)
                        t['Yb'] = [S.sb(f'Yb{d}{i}', [64, 256], RDT) for i in range(2)]
                        t['YTb'] = [S.sb(f'YTb{d}{i}', [64, 256], RDT) for i in range(2)]
                        t['EGB'] = S.sb(f'EGB{d}', [128, 256])
                        for nm in ('VB', 'KBG'):
                            t[nm] = S.sb(f'{nm}{d}', [64, 4, 128], RDT)
                        t['obs'] = []
                        for i_ in range(2):
                            t['obs'].append(dict(QD=S.sb(f'QD{d}{i_}', [128, 256], RDT), P1T=S.sb(f'P1T{d}{i_}', [128, 256], RDT),
                                                 P2=S.sb(f'P2{d}{i_}', [64, 4, 128]), MT=S.sb(f'MT{d}{i_}', [64, 256], RDT),
                                                 KDC=S.sb(f'KDC{d}{i_}', [64, 4, 128], RDT)))
                        t['Um'] = S.sb(f'Um{d}', [64, 128], RDT)
                        t['Hst'] = S.sb(f'Hst{d}', [128, 128], RDT)
                        return t

                    def dn_dir(d, t, pool):
                        GC, KDS, EGC, BEG, t68, CD = t['GC'], t['KDS'], t['EGC'], t['BEG'], t['t68'], t['CD']
                        GU, NBS, BI, NBST, Dm, DTm, X, XT, TTm, GCB = (t[k_] for k_ in ('GU', 'NBS', 'BI', 'NBST', 'Dm', 'DTm', 'X', 'XT', 'TTm', 'GCB'))
                        EGB, VB, KBG, Um, Hst = (t[k_] for k_ in ('EGB', 'VB', 'KBG', 'Um', 'Hst'))
                        gT = G[:, d, :]
                        bT = BTA[:, d, :]
                        ps = pool.nb()
                        S.mm(ps[0:64, 0:68], TRI[d], gT)
                        S.copy(GC[:], ps[0:64, 0:68])
                        ps2 = pool.nb()
                        S.mm(ps2[:, 0:68], ones[0:64, :], gT)
                        S.act(CD[:], ps2[:, 0:68], AF.Exp)
                        S.tt(t68[:], ps2[0:64, 0:68], GC[:], ALU.subtract)
                        S.act(KDS[:], t68[:], AF.Exp)
                        S.act(EGC[:], GC[:], AF.Exp)
                        S.tt(BEG[:], bT, EGC[:], ALU.mult)
                        S.memset(Hst[:].cast(F32), 0.0)
                        yield
                        nchk = 4
                        n = 256

                        def v3(tl):
                            return tl[:, 0:n].rr('p (c i) -> p c i', i=64)

                        def pre(sidx, ob):
                            QD, P1T, P2, MT, KDC = ob['QD'], ob['P1T'], ob['P2'], ob['MT'], ob['KDC']
                            c0 = 4 * sidx
                            col0 = c0 * 64
                            gTs = G[:, d, c0:c0 + nchk]
                            bTs = BTA[:, d, c0:c0 + nchk]
                            S.tt(v3(GU), b3(gTs, nchk, 2), b3(TRI[d], nchk, 1), ALU.mult)
                            S.tt(v3(NBS), b3(bTs, nchk, 2), b3(NOFFD, nchk, 1), ALU.mult)
                            S.tt(v3(BI), b3(bTs, nchk, 2), b3(id64, nchk, 1), ALU.mult)
                            yield
                            pa = pool.nb()
                            S.mm(pa[:, 0:n], ones[0:64, :], GU[:, 0:n])
                            S.act(EGB[:, 0:n], pa[:, 0:n], AF.Exp)
                            S.copy(GCB[:, 0:n], pa[0:64, 0:n], eng='scalar')
                            S.tt(QD[:, 0:n], QTf[:, col0:col0 + n], EGB[:, 0:n], ALU.mult)
                            gcs = GC[:, c0:c0 + nchk]
                            S.tt(v3(Dm), b3(gcs, nchk, 2), v3(GCB), ALU.subtract, eng='gpsimd')
                            S.tt(v3(Dm), v3(Dm), b3(NEGM[d], nchk, 1), ALU.add, eng='gpsimd')
                            S.act(Dm[:, 0:n], Dm[:, 0:n], AF.Exp)
                            yield
                            S.tt(v3(DTm), v3(GCB), b3(gcs, nchk, 2), ALU.subtract, eng='gpsimd')
                            S.tt(v3(DTm), v3(DTm), b3(NEGMT[d], nchk, 1), ALU.add, eng='gpsimd')
                            S.act(DTm[:, 0:n], DTm[:, 0:n], AF.Exp)
                            pe_ = pool.nb()
                            S.mm(pe_[0:64, 0:n], ones[0:64, 0:64], BI[:, 0:n])
                            S.tt(v3(NBST), pe_[0:64, 0:n].rr('p (c i) -> p c i', i=64), b3(NOFFD, nchk, 1), ALU.mult)
                            yield
                            pd = pool.nb()
                            for ci in range(nchk):
                                c = c0 + ci
                                S.mm(pd[0:64, ci * 64:ci * 64 + 64], KT[:, c, :], KT[:, c, :])
                            S.tt(NBS[:, 0:n], NBS[:, 0:n], Dm[:, 0:n], ALU.mult, eng='gpsimd')
                            S.tt(NBST[:, 0:n], NBST[:, 0:n], DTm[:, 0:n], ALU.mult, eng='gpsimd')
                            S.tt(X[:, 0:n], pd[0:64, 0:n], NBS[:, 0:n], ALU.mult)
                            S.tt(XT[:, 0:n], pd[0:64, 0:n], NBST[:, 0:n], ALU.mult)
                            yield
                            yield from inverse_g(pool, X, XT, t['Yb'], t['YTb'], TTm, nchk)
                            pv_ = pool.nb()
                            pk_ = pool.nb()
                            for cj in range(4):
                                c = c0 + cj
                                S.transpose(pv_[0:64, cj * 128:cj * 128 + 128].cast(RDT), VT[:, c, :], identR[:])
                                S.transpose(pk_[0:64, cj * 128:cj * 128 + 128].cast(RDT), KT[:, c, :], identR[:])
                            pv3 = pv_[0:64, :].rr('p (c k) -> p c k', k=128)
                            pk3 = pk_[0:64, :].rr('p (c k) -> p c k', k=128)
                            S.tt(VB[:], pv3, b3(BTA[:, d, c0:c0 + 4], 4, 2, 128), ALU.mult)
                            S.tt(KBG[:], pk3, b3(BEG[:, c0:c0 + 4], 4, 2, 128), ALU.mult)
                            S.tt(KDC[:], pk3, b3(KDS[:, c0:c0 + 4], 4, 2, 128), ALU.mult)
                            yield
                            pu = pool.nb()
                            for ci in range(4):
                                S.mm(pu[0:64, ci * 128:ci * 128 + 128], TTm[:, ci * 64:ci * 64 + 64], VB[:, ci, :])
                            S.copy(P2[:], pu[0:64, :].rr('p (c k) -> p c k', k=128), eng='scalar')
                            pw = pool.nb()
                            for ci in range(nchk):
                                S.mm(pw[:, ci * 64:ci * 64 + 64], KBG[:, ci, :], TTm[:, ci * 64:ci * 64 + 64])
                            S.act(P1T[:, 0:n], pw[:, 0:n], AF.Copy, scale=-1.0)
                            yield
                            pm = pool.nb()
                            for ci in range(nchk):
                                c = c0 + ci
                                S.mm(pm[0:64, ci * 64:ci * 64 + 64], KT[:, c, :], QT[:, c, :])
                            S.tt(ob['MT'][:, 0:n], pm[0:64, 0:n], DTm[:, 0:n], ALU.mult)
                            yield

                        def chain(sidx, ob):
                            QD, P1T, P2, MT, KDC = ob['QD'], ob['P1T'], ob['P2'], ob['MT'], ob['KDC']
                            c0 = 4 * sidx
                            corder = list(range(nchk)) if d == 0 else list(range(nchk - 1, -1, -1))
                            for ci in corder:
                                c = c0 + ci
                                cs = slice(ci * 64, ci * 64 + 64)
                                p1 = pool.nb()
                                S.mm(p1[0:64, 0:128], P1T[:, cs], Hst[:])
                                S.tt(Um[:], p1[0:64, 0:128], P2[:, ci, :], ALU.add)
                                yield
                                p2 = pool.nb()
                                S.mm(p2[:, 0:64], Hst[:], QD[:, cs], start=True, stop=False)
                                S.mm(p2[:, 0:64], Um[:], MT[:, cs], start=False, stop=True)
                                ov = OT.k(sidx, (slice(None), c, slice(None)))
                                S.tt(ov, p2[:, 0:64], ov, ALU.add)
                                p3 = pool.nb()
                                S.mm(p3[:, 0:128], KDC[:, ci, :], Um[:])
                                S.stt(Hst[:], Hst[:], CD[:, c:c + 1], p3[:, 0:128], ALU.mult, ALU.add)
                                yield

                        order = ORDER[d]
                        obs = t['obs']
                        yield from pre(order[0], obs[0])
                        for i_, sidx in enumerate(order):
                            gl = [chain(sidx, obs[i_ % 2])]
                            if i_ + 1 < len(order):
                                gl.append(pre(order[i_ + 1], obs[(i_ + 1) % 2]))
                            yield from zipgen(*gl)

                    dsets = [mk_dset(0), mk_dset(1)]
                    gens_ = [dn_dir(0, dsets[0], Pool(PS[0:4])), dn_dir(1, dsets[1], Pool(PS[4:8]))]
                    if hh == 0 and 'RW' in stages:
                        RAW1 = S.sb('RAW1', [128, TT])
                        OUT1 = S.sb('OUT1', [128, TT])
                        c0m = S.sb('c0m', [128, 10])
                        m4mu = S.sb('m4mu', [128, 10, 4])
                        m2mu = S.sb('m2mu', [128, 10, 2])

                        def rw1_gen():
                            S.ts(c0m[:], pv('mu'), -1.0, ALU.mult, 1.0, ALU.add)
                            S.tt(m4mu[:], pv('mu').un(2).bc([128, 10, 4]), cv('m4', 128).un(1).bc([128, 10, 4]), ALU.mult)
                            S.tt(m2mu[:], pv('mu').un(2).bc([128, 10, 2]), cv('m2', 128).un(1).bc([128, 10, 2]), ALU.mult)
                            yield
                            for ti, (r0, np_) in enumerate(RW_TILES):
                                RAW = RAW1
                                OUT = OUT1
                                S.load(RAW[0:np_, :], pTrows(r0, np_))
                                yield
                                x = RAW[0:np_, 0:256]
                                o = OUT[0:np_, 0:256]
                                S.ts(o, x, c0m[0:np_, ti:ti + 1], ALU.mult)
                                S.stt(o[:, 1:256], x[:, 0:255], m2mu[0:np_, ti, 0:1], o[:, 1:256], ALU.mult, ALU.add)
                                S.stt(o[:, 0:255], x[:, 1:256], m2mu[0:np_, ti, 1:2], o[:, 0:255], ALU.mult, ALU.add)
                                yield
                                xl = RAW[0:np_, 256:TT].rr('p (r c) -> p r c', c=64)
                                ol = OUT[0:np_, 256:TT].rr('p (r c) -> p r c', c=64)
                                S.ts(ol, xl, c0m[0:np_, ti:ti + 1], ALU.mult)
                                yield
                                S.stt(ol[:, :, 1:64], xl[:, :, 0:63], m4mu[0:np_, ti, 0:1], ol[:, :, 1:64], ALU.mult, ALU.add)
                                yield
                                S.stt(ol[:, :, 0:63], xl[:, :, 1:64], m4mu[0:np_, ti, 1:2], ol[:, :, 0:63], ALU.mult, ALU.add)
                                yield
                                S.stt(ol[:, 1:64, :], xl[:, 0:63, :], m4mu[0:np_, ti, 2:3], ol[:, 1:64, :], ALU.mult, ALU.add)
                                yield
                                S.stt(ol[:, 0:63, :], xl[:, 1:64, :], m4mu[0:np_, ti, 3:4], ol[:, 0:63, :], ALU.mult, ALU.add)
                                yield
                                if r0 == C_WD:
                                    S.act(OUT[0:np_, :], OUT[0:np_, :], AF.Tanh)
                                if r0 >= C_GD:
                                    S.act(OUT[0:np_, :], OUT[0:np_, :], AF.Sigmoid)
                                S.load(pTrows(r0, np_), OUT[0:np_, :])
                                yield
                        gens_.append(rw1_gen())
                    drive(gens_)
                    if hh == 0 and 'RW' in stages:
                        RESt, ZRt = OUT1, RAW1
                    else:
                        RESt = S.sb('RESt', [128, TT])
                        ZRt = S.sb('ZRt', [128, TT])
                    OTf = OT[:].rr('p c i -> p (c i)')
                    for (a, n) in SEGS:
                        S.act(sq[:, 0:n], OTf[:, a:a + n], AF.Square)
                        ps = misc.nb()
                        S.mm(ps[:, 0:n], ones[:], sq[:, 0:n])
                        S.act(rs[:, 0:n], ps[:, 0:n], AF.Sqrt, bias=EPS, scale=1.0 / 128)
                        S.recip(rs[:, 0:n], rs[:, 0:n])
                        S.tt(OTf[:, a:a + n], OTf[:, a:a + n], rs[:, 0:n], ALU.mult)
                    ZR = ZRt[:]
                    S.load(ZR, pTrows(C_Z + hh * 128, 128))
                    S.act(ZR, ZR, AF.Silu)
                    S.ts(OTf, OTf, pv('outg', 0), ALU.mult)
                    RES = RESt[:]
                    S.tt(RES[:, 0:256], OTf[:, 0:256], ZR[:, 0:256], ALU.mult)
                    S.tt(RES[:, 256:TT].rr('p (r c) -> p c r', c=64), OT[:, 4:68, :],
                         ZR[:, 256:TT].rr('p (r c) -> p c r', c=64), ALU.mult)
                    S.load(oT[hh * 128:(hh + 1) * 128, :], RES)
                    barrier()
                S.es = es0

        if 'RW' in stages:
            with ExitStack() as es:
                S.es = es
                RW_TILES_ = [] if 'DN' in stages else RW_TILES
                RAWs = [S.sb(f'RAWr{i}', [128, TT]) for i in range(2)]
                OUTs = [S.sb(f'OUTr{i}', [128, TT]) for i in range(2)]
                c0m = S.sb('c0m', [128, 10])
                m4mu = S.sb('m4mu', [128, 10, 4])
                m2mu = S.sb('m2mu', [128, 10, 2])
                S.ts(c0m[:], pv('mu'), -1.0, ALU.mult, 1.0, ALU.add)
                S.tt(m4mu[:], pv('mu').un(2).bc([128, 10, 4]), cv('m4', 128).un(1).bc([128, 10, 4]), ALU.mult)
                S.tt(m2mu[:], pv('mu').un(2).bc([128, 10, 2]), cv('m2', 128).un(1).bc([128, 10, 2]), ALU.mult)
                for ti, (r0, np_) in enumerate(RW_TILES_):
                    RAW = RAWs[ti % 2]
                    OUT = OUTs[ti % 2]
                    ve = 'vector'
                    S.load(RAW[0:np_, :], pTrows(r0, np_))
                    x = RAW[0:np_, 0:256]
                    o = OUT[0:np_, 0:256]
                    S.ts(o, x, c0m[0:np_, ti:ti + 1], ALU.mult, eng=ve)
                    S.stt(o[:, 1:256], x[:, 0:255], m2mu[0:np_, ti, 0:1], o[:, 1:256], ALU.mult, ALU.add, eng=ve)
                    S.stt(o[:, 0:255], x[:, 1:256], m2mu[0:np_, ti, 1:2], o[:, 0:255], ALU.mult, ALU.add, eng=ve)
                    xl = RAW[0:np_, 256:TT].rr('p (r c) -> p r c', c=64)
                    ol = OUT[0:np_, 256:TT].rr('p (r c) -> p r c', c=64)
                    S.ts(ol, xl, c0m[0:np_, ti:ti + 1], ALU.mult, eng=ve)
                    S.stt(ol[:, :, 1:64], xl[:, :, 0:63], m4mu[0:np_, ti, 0:1], ol[:, :, 1:64], ALU.mult, ALU.add, eng=ve)
                    S.stt(ol[:, :, 0:63], xl[:, :, 1:64], m4mu[0:np_, ti, 1:2], ol[:, :, 0:63], ALU.mult, ALU.add, eng=ve)
                    S.stt(ol[:, 1:64, :], xl[:, 0:63, :], m4mu[0:np_, ti, 2:3], ol[:, 1:64, :], ALU.mult, ALU.add, eng=ve)
                    S.stt(ol[:, 0:63, :], xl[:, 1:64, :], m4mu[0:np_, ti, 3:4], ol[:, 0:63, :], ALU.mult, ALU.add, eng=ve)
                    if r0 == C_WD:
                        S.act(OUT[0:np_, :], OUT[0:np_, :], AF.Tanh)
                    if r0 >= C_GD:
                        S.act(OUT[0:np_, :], OUT[0:np_, :], AF.Sigmoid)
                    S.load(pTrows(r0, np_), OUT[0:np_, :])
                barrier()
            S.es = es0
            with ExitStack() as es:
                S.es = es
                WUP = S.sb('WUP', [64, 2, 256]); AUP = S.sb('AUP', [64, 2, 256])
                GUP0 = S.sb('GUP0', [128, 256]); GUP1 = S.sb('GUP1', [32, 256])
                S.load(WUP[:], wup_dr[:]); S.load(AUP[:], aup_dr[:])
                S.load(GUP0[:], gup_dr[0:128, :]); S.load(GUP1[:], gup_dr[128:160, :])
                omka = S.sb('omka', [128, 2])
                S.ts(omka[:], pv('ka'), -1.0, ALU.mult, 1.0, ALU.add)
                YT = S.sb('YTr', [128, TT], keys=list(range(NSEG)))
                BON = S.sb('BON', [128, TT], keys=list(range(NSEG)))
                GD0 = S.sb('GD0', [128, 256]); GD1 = S.sb('GD1', [32, 256])
                RESET = cv('reset', 128)
                BLK = cv('blk', 128)
                NQ = 256

                def mk_rset(d):
                    t = {}
                    names = ('R', 'Kp', 'V', 'LD', 'IC', 'KK', 'KD', 'PRE', 'LIN', 'LEX', 'EIN', 'EEX', 'ENI', 'ETL',
                             'AT', 'KIC', 'BT_', 'KT_', 'BH', 'KH')
                    for nm in names:
                        t[nm] = S.sb(f'r{nm}{d}', [128, NQ], RDT if nm in ('AT', 'BT_', 'KT_', 'BH', 'KH') else F32)
                    t['WD'] = S.sb(f'rWD{d}', [64, NQ]); t['AD'] = S.sb(f'rAD{d}', [64, NQ])
                    t['ATK'] = S.sb(f'rATK{d}', [64, 4, 128], RDT)
                    t['obs'] = []
                    for i_ in range(2):
                        t['obs'].append(dict(
                            RT=S.sb(f'rRT{d}{i_}', [128, NQ], RDT), P1T=S.sb(f'rP1T{d}{i_}', [128, NQ], RDT),
                            P2=S.sb(f'rP2{d}{i_}', [64, 4, 128]), VTK=S.sb(f'rVTK{d}{i_}', [64, 4, 128], RDT),
                            BHK=S.sb(f'rBHK{d}{i_}', [64, 4, 128], RDT), KHK=S.sb(f'rKHK{d}{i_}', [64, 4, 128], RDT),
                            WC=S.sb(f'rWC{d}{i_}', [128, 4]),
                            MrbT=[S.sb(f'rMb{d}{i_}{e}', [64, NQ], RDT) for e in range(2)],
                            MrkT=[S.sb(f'rMk{d}{i_}{e}', [64, NQ], RDT) for e in range(2)]))
                    t['hd'] = []
                    for e in range(2):
                        t['hd'].append(dict(
                            X=S.sb(f'rX{d}{e}', [64, NQ], RDT), XT=S.sb(f'rXT{d}{e}', [64, NQ], RDT), AkT=S.sb(f'rAk{d}{e}', [64, NQ], RDT),
                            Yb=[S.sb(f'rY{d}{e}{i}', [64, NQ], RDT) for i in range(2)],
                            YTb=[S.sb(f'rYT{d}{e}{i}', [64, NQ], RDT) for i in range(2)],
                            TTm=S.sb(f'rTT{d}{e}', [64, NQ], RDT), ZS=S.sb(f'rZS{d}{e}', [64, 4, 64], RDT)))
                    t['Um'] = S.sb(f'rUm{d}', [64, 128], RDT)
                    t['Hst'] = [S.sb(f'rHst{d}{i}', [128, 128], RDT) for i in range(2)]
                    return t

                def rw_dir(hp, d, t, pool, pP1):
                    hc0 = hp * 128
                    (R, Kp, V, LD, IC, KK, KD, PRE, LIN, LEX, EIN, EEX, ENI, ETL, AT, KIC, BT_, KT_, BH, KH) = (
                        t[k_] for k_ in ('R', 'Kp', 'V', 'LD', 'IC', 'KK', 'KD', 'PRE', 'LIN', 'LEX', 'EIN', 'EEX', 'ENI', 'ETL',
                                         'AT', 'KIC', 'BT_', 'KT_', 'BH', 'KH'))
                    KX, SQ, RS, TMP, T1, T2 = PRE, LIN, LEX, EIN, EEX, ENI
                    WD, AD, ATK, hd, Um = (t[k_] for k_ in ('WD', 'AD', 'ATK', 'hd', 'Um'))
                    Hst = t['Hst'][hp]
                    S.memset(Hst[:].cast(F32), 0.0)
                    w0c = pv('w0', d * 2 + hp)
                    a0c = pv('a0', d * 2 + hp)
                    nchk = 4
                    n = 256

                    def v3(tl, np_=128):
                        return tl[0:np_, 0:n].rr('p (c i) -> p c i', i=64)

                    def pre(sidx, ob):
                        RT, P1T, P2, VTK, BHK, KHK, WC = (ob[k_] for k_ in ('RT', 'P1T', 'P2', 'VTK', 'BHK', 'KHK', 'WC'))
                        t0 = 256 * sidx
                        cols = slice(t0, t0 + n)
                        S.load(R[:, 0:n], View(pT.h[C_RR + hc0:C_RR + hc0 + 128, cols], [pT.bufs[9 + hp]]))
                        S.load(Kp[:, 0:n], View(pT.h[C_RK + hc0:C_RK + hc0 + 128, cols], [pT.bufs[11 + hp]]))
                        S.load(V[:, 0:n], View(pT.h[C_RV + hc0:C_RV + hc0 + 128, cols], [pT.bufs[13 + hp]]))
                        S.load(WD[:, 0:n], View(pT.h[C_WD + d * 64:C_WD + d * 64 + 64, cols], [pT.bufs[15]]))
                        S.load(AD[:, 0:n], View(pT.h[C_AD + d * 64:C_AD + d * 64 + 64, cols], [pT.bufs[16]]))
                        yield
                        p1 = pool.nb()
                        S.mm(p1[:, 0:n], WUP[:, d, hc0:hc0 + 128], WD[:, 0:n])
                        S.act(LD[:, 0:n], p1[:, 0:n], AF.Sigmoid, bias=w0c)
                        S.ts(LD[:, 0:n], LD[:, 0:n], -0.6065306597126334, ALU.mult, eng='gpsimd')
                        p2 = pool.nb()
                        S.mm(p2[:, 0:n], AUP[:, d, hc0:hc0 + 128], AD[:, 0:n])
                        S.act(IC[:, 0:n], p2[:, 0:n], AF.Sigmoid, bias=a0c)
                        S.ts(KX[:, 0:n], Kp[:, 0:n], pv('kk', hp), ALU.mult)
                        S.act(SQ[:, 0:n], KX[:, 0:n], AF.Square)
                        yield
                        p3 = pool.nb()
                        S.mm(p3[:, 0:n], BLK, SQ[:, 0:n])
                        S.act(RS[:, 0:n], p3[:, 0:n], AF.Sqrt, bias=EPS, scale=1.0)
                        S.recip(RS[:, 0:n], RS[:, 0:n])
                        S.tt(KK[:, 0:n], KX[:, 0:n], RS[:, 0:n], ALU.mult)
                        S.ts(TMP[:, 0:n], IC[:, 0:n], pv('ka', hp), ALU.mult, omka[:, hp:hp + 1], ALU.add)
                        S.tt(KD[:, 0:n], Kp[:, 0:n], TMP[:, 0:n], ALU.mult)
                        S.tt(T1[:, 0:n], R[:, 0:n], KD[:, 0:n], ALU.mult)
                        S.ts(T1[:, 0:n], T1[:, 0:n], pv('rk', hp), ALU.mult)
                        yield
                        p4 = pool.nb()
                        S.mm(p4[:, 0:n], BLK, T1[:, 0:n])
                        S.tt(T2[:, 0:n], p4[:, 0:n], V[:, 0:n], ALU.mult)
                        bv = BON.k(sidx, (slice(None), cols))
                        S.tt(bv, bv, T2[:, 0:n], ALU.add)
                        S.scan(PRE[:, 0:n], RESET[:, 0:n], LD[:, 0:n], 0.0, ALU.mult, ALU.add)
                        TOT = v3(PRE)[:, :, 63:64]
                        if d == 0:
                            LINv = PRE
                        else:
                            S.tt(v3(LIN), TOT.bc([128, nchk, 64]), v3(PRE), ALU.subtract)
                            S.tt(LIN[:, 0:n], LIN[:, 0:n], LD[:, 0:n], ALU.add)
                            LINv = LIN
                        S.tt(LEX[:, 0:n], LINv[:, 0:n], LD[:, 0:n], ALU.subtract, eng='gpsimd')
                        S.act(WC[:, 0:nchk], PRE[:, 0:n].rr('p (c i) -> p c i', i=64)[:, :, 63], AF.Exp)
                        S.tt(v3(ETL), TOT.bc([128, nchk, 64]), v3(LINv), ALU.subtract)
                        yield
                        S.act(ETL[:, 0:n], ETL[:, 0:n], AF.Exp)
                        S.act(EEX[:, 0:n], LEX[:, 0:n], AF.Exp)
                        S.act(ENI[:, 0:n], LINv[:, 0:n], AF.Exp, scale=-1.0)
                        S.act(EIN[:, 0:n], LINv[:, 0:n], AF.Exp)
                        S.tt(RT[:, 0:n], R[:, 0:n], EIN[:, 0:n], ALU.mult, eng='gpsimd')
                        S.stt(AT[:, 0:n], KK[:, 0:n], -1.0, EEX[:, 0:n], ALU.mult, ALU.mult)
                        S.tt(KIC[:, 0:n], KK[:, 0:n], IC[:, 0:n], ALU.mult, eng='gpsimd')
                        yield
                        S.tt(BT_[:, 0:n], KIC[:, 0:n], ENI[:, 0:n], ALU.mult)
                        S.tt(KT_[:, 0:n], KD[:, 0:n], ENI[:, 0:n], ALU.mult, eng='gpsimd')
                        S.tt(BH[:, 0:n], KIC[:, 0:n], ETL[:, 0:n], ALU.mult)
                        S.tt(KH[:, 0:n], KD[:, 0:n], ETL[:, 0:n], ALU.mult, eng='gpsimd')
                        yield
                        for qi, (src, dst) in enumerate(((AT, ATK), (BH, BHK), (KH, KHK), (V, VTK))):
                            pt_ = pool.nb()
                            for cj in range(4):
                                if src is V:
                                    S.transpose(pt_[0:64, cj * 128:cj * 128 + 128], src[:, cj * 64:cj * 64 + 64], ident)
                                else:
                                    S.transpose(pt_[0:64, cj * 128:cj * 128 + 128].cast(RDT), src[:, cj * 64:cj * 64 + 64], identR[:])
                            S.copy(dst[:], pt_[0:64, :].rr('p (c k) -> p c k', k=128),
                                   eng='scalar' if qi % 2 == 0 else 'vector')
                            if qi % 2 == 1:
                                yield

                        def head(e):
                            h_ = hd[e]
                            pe = slice(64 * e, 64 * e + 64)
                            for (lh, rh, dst, msk) in ((AT, BT_, h_['X'], STRICT[d]), (BT_, AT, h_['XT'], STRICTT[d]),
                                                      (KT_, AT, h_['AkT'], STRICTT[d]), (BT_, RT, ob['MrbT'][e], INCLT[d]),
                                                      (KT_, RT, ob['MrkT'][e], INCLT[d])):
                                pq = pool.nb()
                                for ci in range(nchk):
                                    cs = slice(ci * 64, ci * 64 + 64)
                                    S.mm(pq[0:64, cs], lh[pe, cs], rh[pe, cs])
                                S.tt(v3(dst, 64), pq[0:64, 0:n].rr('p (c i) -> p c i', i=64), b3(msk, nchk, 1), ALU.mult)
                                yield
                            yield from inverse_g(pool, h_['X'], h_['XT'], h_['Yb'], h_['YTb'], h_['TTm'], nchk)
                            TTm = h_['TTm']
                            for ci in range(nchk):
                                cs = slice(ci * 64, ci * 64 + 64)
                                if e == 0:
                                    S.mm(pP1[pe, cs], ATK[:, ci, pe], TTm[:, cs])
                                else:
                                    S.mm(pP1[pe, cs], ATK[:, ci, pe].cast(F32), TTm[:, cs].cast(F32))
                            pz = pool.nb()
                            for ci in range(nchk):
                                cs = slice(ci * 64, ci * 64 + 64)
                                S.mm(pz[0:64, cs], h_['AkT'][:, cs], VTK[:, ci, pe])
                            S.copy(h_['ZS'][:], pz[0:64, 0:n].rr('p (c v) -> p c v', v=64), eng='scalar')
                            yield
                            pp2 = pool.nb()
                            for ci in range(nchk):
                                cs = slice(ci * 64, ci * 64 + 64)
                                S.mm(pp2[0:64, cs], TTm[:, cs], h_['ZS'][:, ci, :])
                            S.copy(P2[:, :, pe], pp2[0:64, 0:n].rr('p (c v) -> p c v', v=64), eng='vector')
                            yield
                        yield from zipgen(head(0), head(1))
                        S.copy(P1T[:, 0:n], pP1[:, 0:n], eng='scalar')
                        yield

                    def chain(sidx, ob):
                        RT, P1T, P2, VTK, BHK, KHK, WC = (ob[k_] for k_ in ('RT', 'P1T', 'P2', 'VTK', 'BHK', 'KHK', 'WC'))
                        t0 = 256 * sidx
                        corder = list(range(nchk)) if d == 0 else list(range(nchk - 1, -1, -1))
                        for ci in corder:
                            cs = slice(ci * 64, ci * 64 + 64)
                            gcol = slice(t0 + ci * 64, t0 + ci * 64 + 64)
                            q1 = pool.nb()
                            S.mm(q1[0:64, 0:128], P1T[:, cs], Hst[:])
                            S.tt(Um[:], q1[0:64, 0:128], P2[:, ci, :], ALU.add)
                            yield
                            q2 = pool.nb()
                            S.mm(q2[:, 0:64], Hst[:], RT[:, cs], start=True, stop=False)
                            for e in range(2):
                                pe = slice(64 * e, 64 * e + 64)
                                if e == 0:
                                    S.mm(q2[pe, 0:64], Um[:, pe], ob['MrbT'][e][:, cs], start=False, stop=False)
                                    S.mm(q2[pe, 0:64], VTK[:, ci, pe], ob['MrkT'][e][:, cs], start=False, stop=True)
                                else:
                                    S.mm(q2[pe, 0:64], Um[:, pe].cast(F32), ob['MrbT'][e][:, cs].cast(F32), start=False, stop=False)
                                    S.mm(q2[pe, 0:64], VTK[:, ci, pe].cast(F32), ob['MrkT'][e][:, cs].cast(F32), start=False, stop=True)
                            yv = YT.k(sidx, (slice(None), gcol))
                            S.tt(yv, q2[:, 0:64], yv, ALU.add)
                            q3 = pool.nb()
                            S.mm(q3[:, 0:128], BHK[:, ci, :], Um[:], start=True, stop=False)
                            S.mm(q3[:, 0:128], KHK[:, ci, :], VTK[:, ci, :], start=False, stop=True)
                            for e in range(2):
                                pe = slice(64 * e, 64 * e + 64)
                                S.stt(Hst[pe, pe], Hst[pe, pe], WC[pe, ci:ci + 1], q3[pe, pe], ALU.mult, ALU.add)
                            yield

                    order = ORDER[d]
                    obs = t['obs']
                    yield from pre(order[0], obs[0])
                    for i_, sidx in enumerate(order):
                        gl = [chain(sidx, obs[i_ % 2])]
                        if i_ + 1 < len(order):
                            gl.append(pre(order[i_ + 1], obs[(i_ + 1) % 2]))
                        yield from zipgen(*gl)
                        done_[d].add(sidx)

                done_ = {0: set(), 1: set()}
                FR = S.sb('FRf', [128, 256]); FK = S.sb('FKf', [128, 256]); FV = S.sb('FVf', [128, 256])

                def fin_gen(hp):
                    hc0 = hp * 128
                    finpool = Pool([PS[2], PS[5]])
                    finished = set()
                    n = 256
                    while len(finished) < NSEG:
                        ready = [s_ for s_ in range(NSEG) if s_ not in finished and s_ in done_[0] and s_ in done_[1]]
                        if not ready:
                            yield
                            continue
                        sidx = ready[0]
                        t0 = 256 * sidx
                        cols = slice(t0, t0 + n)
                        S.load(GD0[:, 0:n], View(pT.h[C_GD:C_GD + 128, cols], [pT.bufs[17]]))
                        S.load(GD1[:, 0:n], View(pT.h[C_GD + 128:C_GD + 160, cols], [pT.bufs[18]]))
                        yv = YT.k(sidx, (slice(None), cols))
                        f1 = finpool.nb()
                        S.mm(f1[:, 0:n], BLK, yv)
                        S.stt(FR[:, 0:n], f1[:, 0:n], -1.0 / 64, yv, ALU.mult, ALU.add)
                        S.act(FK[:, 0:n], FR[:, 0:n], AF.Square)
                        yield
                        f2 = finpool.nb()
                        S.mm(f2[:, 0:n], BLK, FK[:, 0:n])
                        S.act(FV[:, 0:n], f2[:, 0:n], AF.Sqrt, bias=RW_LN_EPS_, scale=1.0 / 64)
                        S.recip(FV[:, 0:n], FV[:, 0:n])
                        S.tt(FR[:, 0:n], FR[:, 0:n], FV[:, 0:n], ALU.mult)
                        S.ts(FR[:, 0:n], FR[:, 0:n], pv('lnw', hp), ALU.mult, pv('lnb', hp), ALU.add)
                        S.tt(FR[:, 0:n], FR[:, 0:n], BON.k(sidx, (slice(None), cols)), ALU.add)
                        yield
                        f3 = finpool.nb()
                        S.mm(f3[:, 0:n], GUP0[:, hc0:hc0 + 128], GD0[:, 0:n], start=True, stop=False)
                        S.mm(f3[:, 0:n], GUP1[:, hc0:hc0 + 128], GD1[:, 0:n], start=False, stop=True)
                        S.tt(yv, FR[:, 0:n], f3[:, 0:n], ALU.mult)
                        finished.add(sidx)
                        yield

                rsets = [mk_rset(0), mk_rset(1)]
                R_ = rsets[0]['R']; Kp_ = rsets[0]['Kp']; V_ = rsets[0]['V']
                fpool = Pool(PS[0:6])
                for hp in range(2):
                    hc0 = hp * 128
                    S.memset(YT[:], 0.0)
                    S.memset(BON[:], 0.0)
                    done_[0].clear()
                    done_[1].clear()
                    drive([rw_dir(hp, 0, rsets[0], Pool(PS[0:3]), PS[6]), rw_dir(hp, 1, rsets[1], Pool(PS[3:6]), PS[7]), fin_gen(hp)])
                    S.load(oT[256 + hc0:256 + hc0 + 128, :], YT[:])
                barrier()
            S.es = es0
        S.finish()
        S.emit()
        print('AB ninstr', S.ninstr, {e: S.cnt[e] for e in ENGS})
    return nc

DEPTH = 4
P_DN_ = 4128


def fm(v):
    return np.ascontiguousarray(np.asarray(v, np.float32).reshape(KC, 128).T)


def lay_w(w):
    K, N = w.shape
    return np.ascontiguousarray(w.reshape(K // 128, 128, N // 128, 128).transpose(2, 1, 0, 3))


def ab_cols(g):
    cols = []
    for t in range(4):
        cols += list(range(t * 1024 + 256 * g, t * 1024 + 256 * g + 256))
    for hh in range(2):
        for j in range(4):
            cols.append(4096 + j * 8 + 2 * g + hh)
    for t in range(3):
        cols += list(range(P_DN_ + t * 1024 + 256 * g, P_DN_ + t * 1024 + 256 * g + 256))
    cols += list(range(P_DN_ + 3072, P_DN_ + 3488))
    return np.array(cols)


def rep(x):
    return np.full((128,), x, np.float32)


def ab_prm(I, l, g, modv):
    P = np.zeros((128, NPA), np.float32)

    def put(name, j, col):
        o, w = A_OFF[name]
        P[:len(col), o + j] = col
    for name in ('mix_pre_g',):
        o, w = A_OFF[name]
        P[:, o:o + 16] = fm(I['mix_pre_g'][l])
    for name in ('msh_lat', 'msc_lat', 'msh_ctx', 'msc_ctx'):
        o, w = A_OFF[name]
        P[:, o:o + 16] = fm(modv[name])
    for t in range(3):
        for hh in range(2):
            ch0 = t * 1024 + (2 * g + hh) * 128
            for k in range(7):
                put('cw', (t * 2 + hh) * 7 + k, I['dn_conv'][l][k, ch0:ch0 + 128])
    for hh in range(2):
        for d in range(2):
            put('alog', hh * 2 + d, rep(I['dn_a_log'][l][d, 2 * g + hh]))
            put('dtb', hh * 2 + d, rep(I['dn_dt_bias'][l][d, 2 * g + hh]))
    put('outg', 0, I['dn_out_g'][l])
    mu = I['rw_mu'][l]
    ch0s = [256 * g, 256 * g + 128, 1024 + 256 * g, 1024 + 256 * g + 128, 2048 + 256 * g, 2048 + 256 * g + 128,
            3072, 3200, 3328, 3456]
    for ti, c0 in enumerate(ch0s):
        n = 32 if ti == 9 else 128
        put('mu', ti, mu[c0:c0 + n])
    for hp in range(2):
        c0 = 256 * g + 128 * hp
        for d in range(2):
            put('w0', d * 2 + hp, I['rw_w0'][l][d, c0:c0 + 128])
            put('a0', d * 2 + hp, I['rw_a0'][l][d, c0:c0 + 128])
        put('kk', hp, I['rw_k_k'][l][c0:c0 + 128])
        put('ka', hp, I['rw_k_a'][l][c0:c0 + 128])
        put('rk', hp, I['rw_r_k'][l].reshape(-1)[c0:c0 + 128])
        put('lnw', hp, I['rw_ln_w'][l][c0:c0 + 128])
        put('lnb', hp, I['rw_ln_b'][l][c0:c0 + 128])
    return P


def ab_inputs(I, l, b, g, hT_b, modv, cstv):
    cols = ab_cols(g)
    w = I['w_in'][l][:, cols]
    win = np.ascontiguousarray(w.reshape(KC, 128, NCOL).transpose(1, 0, 2))
    c0 = 256 * g
    return {
        'hT': hT_b, 'win': win, 'prm': ab_prm(I, l, g, modv),
        'wup': np.ascontiguousarray(I['rw_w_up'][l][:, :, c0:c0 + 256].transpose(1, 0, 2)),
        'aup': np.ascontiguousarray(I['rw_a_up'][l][:, :, c0:c0 + 256].transpose(1, 0, 2)),
        'gup': np.ascontiguousarray(I['rw_g_up'][l][:, c0:c0 + 256]),
        'cst': cstv,
    }


_NC_CACHE = {}


def get_nc(name):
    if name not in _NC_CACHE:
        _NC_CACHE[name] = {'M': build_M, 'AB': build_AB, 'C': build_C}[name]()
    return _NC_CACHE[name]


def kernel(**I):
    I = {k: np.asarray(v, np.float32) for k, v in I.items()}
    B = 2
    cc = np.stack([I['c'][0], I['c'][1], I['c_ctx']])
    ccT = np.ascontiguousarray(cc.reshape(3, KC, 128).transpose(2, 1, 0))
    in_maps = []
    for c in range(8):
        wm = I['w_mod'][:, :, c * MN:(c + 1) * MN]
        in_maps.append({'ccT': ccT,
                        'wm': np.ascontiguousarray(wm.reshape(DEPTH, KC, 128, MN).transpose(0, 2, 1, 3)),
                        'bm': np.ascontiguousarray(np.broadcast_to(I['b_mod'][None, :, c * MN:(c + 1) * MN], (3, DEPTH, MN)))})
    res = run_bass_kernel_spmd(get_nc('M'), in_maps, core_ids=list(range(8)))
    mod = np.concatenate([r['mo'] for r in res.results], axis=2)
    cstv = make_cst()
    hT = [np.ascontiguousarray(np.concatenate([I['ctx'][b], I['x'][b]], axis=0).T) for b in range(B)]
    for l in range(DEPTH):
        def mv(row, i):
            return mod[row, l, i * D:(i + 1) * D]
        in_maps = []
        for c in range(8):
            b, g = c // 4, c % 4
            modv = {'msh_lat': mv(b, 0), 'msc_lat': mv(b, 1), 'msh_ctx': mv(2, 0), 'msc_ctx': mv(2, 1)}
            in_maps.append(ab_inputs(I, l, b, g, hT[b], modv, cstv))
        res = run_bass_kernel_spmd(get_nc('AB'), in_maps, core_ids=list(range(8)))
        oT = []
        for b in range(B):
            parts = [res.results[b * 4 + g]['oT'] for g in range(4)]
            dn = np.concatenate([p[0:256] for p in parts], axis=0)
            rw = np.concatenate([p[256:512] for p in parts], axis=0)
            oT.append(np.concatenate([dn, rw], axis=0))
        wout = lay_w(I['w_out'][l])
        wfi = lay_w(I['w_ffn_in'][l])
        wfo = lay_w(I['w_ffn_out'][l])
        in_maps = []

        def tok(a, j):
            return np.ascontiguousarray(np.concatenate([a[:, 64 * j:64 * j + 64], a[:, 256 + 1024 * j:256 + 1024 * j + 1024]], axis=1))
        for c in range(8):
            b, j = c // 4, c % 4
            vec = {'mix_post_g': I['mix_post_g'][l], 'gmix_lat': mv(b, 2), 'gmix_ctx': mv(2, 2),
                   'ffn_pre_g': I['ffn_pre_g'][l], 'fsh_lat': mv(b, 3), 'fsh_ctx': mv(2, 3),
                   'fsc_lat': mv(b, 4), 'fsc_ctx': mv(2, 4), 'ffn_post_g': I['ffn_post_g'][l],
                   'gffn_lat': mv(b, 5), 'gffn_ctx': mv(2, 5)}
            prm = np.concatenate([fm(vec[k]) for k in C_VECS], axis=1)
            in_maps.append({'hT': tok(hT[b], j), 'oT': tok(oT[b], j), 'wout': wout, 'wfi': wfi, 'wfo': wfo, 'prm': prm})
        res = run_bass_kernel_spmd(get_nc('C'), in_maps, core_ids=list(range(8)))
        for b in range(B):
            hn = np.empty_like(hT[b])
            for j in range(4):
                r = res.results[b * 4 + j]['hn']
                hn[:, 64 * j:64 * j + 64] = r[:, 0:64]
                hn[:, 256 + 1024 * j:256 + 1024 * j + 1024] = r[:, 64:]
            hT[b] = hn
    out = np.stack([np.ascontiguousarray(hT[b][:, 256:].T) for b in range(B)])
    return out.astype(np.float32)
```

```python
import numpy as np
from contextlib import ExitStack
import concourse.bass as bass
import concourse.mybir as mybir
from concourse.bass_utils import run_bass_kernel_spmd

F32 = mybir.dt.float32
BF16 = mybir.dt.bfloat16
AF = mybir.ActivationFunctionType
ALU = mybir.AluOpType
AX = mybir.AxisListType

ENGS = ['tensor', 'vector', 'scalar', 'gpsimd', 'sync']
NDS = 12


class Buf:
    __slots__ = ('w', 'r', 'name')

    def __init__(self, name=''):
        self.w = None
        self.r = []
        self.name = name


class View:
    __slots__ = ('ap', 'bufs')

    def __init__(self, ap, bufs):
        self.ap = ap
        self.bufs = bufs

    def __getitem__(self, idx):
        return View(self.ap[idx], self.bufs)

    def rr(self, s, **kw):
        return View(self.ap.rearrange(s, **kw), self.bufs)

    def bc(self, shape):
        return View(self.ap.broadcast_to(shape), self.bufs)

    def un(self, axis):
        return View(self.ap.unsqueeze(axis), self.bufs)

    def tr(self, perm):
        return View(self.ap.transpose(perm), self.bufs)

    def cast(self, dt):
        return View(self.ap.bitcast(dt), self.bufs)


class Tile:
    def __init__(self, handle, name='', keys=None):
        self.h = handle
        self.name = name
        if keys is None:
            self.bufs = {None: Buf(name)}
        else:
            self.bufs = {k: Buf(f'{name}.{k}') for k in keys}

    def __getitem__(self, idx):
        return View(self.h[idx], list(self.bufs.values()))

    def k(self, key, idx=None):
        b = [self.bufs[key]]
        if idx is None:
            return View(self.h[:], b)
        return View(self.h[idx], b)


class Sched:
    def __init__(self, nc, es):
        self.nc = nc
        self.es = es
        self.q = {e: [] for e in ENGS}
        self.sems = {}
        for e in ENGS:
            self.sems[e] = es.enter_context(nc.semaphore(f'cs_{e}'))
        self.cnt = {e: 0 for e in ENGS}
        self.waited = {e: {} for e in ENGS}
        self.dslots = {}
        self.dval = {}
        self.dnext = {}
        for e in ['sync', 'gpsimd', 'scalar']:
            self.dslots[e] = []
            for i in range(NDS):
                key = ('d', e, i)
                self.sems[key] = es.enter_context(nc.semaphore(f'ds_{e}{i}'))
                self.dval[key] = 0
                self.dslots[e].append(key)
            self.dnext[e] = 0
        self.same_engine_sync = {'tensor': False, 'vector': True, 'scalar': True,
                                 'gpsimd': True, 'sync': False}
        self.ninstr = 0

    def _uniq(self, name):
        self._u = getattr(self, '_u', 0) + 1
        return f'{name}_{self._u}'

    def sb(self, name, shape, dt=F32, keys=None):
        h = self.es.enter_context(self.nc.sbuf_tensor(self._uniq('sb_' + name), list(shape), dt))
        return Tile(h, name, keys)

    def ps(self, name, shape, dt=F32, keys=None):
        h = self.es.enter_context(self.nc.psum_tensor(self._uniq('ps_' + name), list(shape), dt))
        return Tile(h, name, keys)

    def dram(self, name, shape, dt=F32, kind="Internal", keys=None):
        h = self.nc.dram_tensor(name, list(shape), dt, kind=kind)
        return Tile(h, name, keys)

    def _waits(self, e, deps):
        need = {}
        for (key, val) in deps:
            if key == e and not self.same_engine_sync[e]:
                continue
            if self.waited[e].get(key, 0) >= val:
                continue
            if need.get(key, 0) < val:
                need[key] = val
        for key, val in need.items():
            self.waited[e][key] = val
        return list(need.items())

    def _deps(self, reads, writes):
        deps = []
        for v in reads:
            for b in v.bufs:
                if b.w is not None:
                    deps.append(b.w)
        for v in writes:
            for b in v.bufs:
                if b.w is not None:
                    deps.append(b.w)
                deps.extend(b.r)
        return deps

    def _commit(self, tok, reads, writes):
        for v in reads:
            for b in v.bufs:
                b.r.append(tok)
                if len(b.r) > 64:
                    m = {}
                    for k_, v_ in b.r:
                        if m.get(k_, 0) < v_:
                            m[k_] = v_
                    b.r = list(m.items())
        for v in writes:
            for b in v.bufs:
                b.w = tok
                b.r = []

    def op(self, e, fn, reads=(), writes=()):
        deps = self._deps(reads, writes)
        waits = self._waits(e, deps)
        self.cnt[e] += 1
        tok = (e, self.cnt[e])
        self.q[e].append((waits, fn, e, 1))
        self._commit(tok, reads, writes)
        self.ninstr += 1
        return tok

    def dma(self, e, fn, reads=(), writes=()):
        i = self.dnext[e]
        self.dnext[e] = (i + 1) % NDS
        key = self.dslots[e][i]
        deps = self._deps(reads, writes)
        if self.dval[key] > 0:
            deps.append((key, self.dval[key]))
        waits = self._waits(e, deps)
        self.dval[key] += 16
        tok = (key, self.dval[key])
        self.q[e].append((waits, fn, key, 16))
        self._commit(tok, reads, writes)
        self.ninstr += 1
        return tok

    def finish(self):
        waits = []
        for key, val in self.dval.items():
            if val > 0:
                waits.append((key, val))
        for e in ENGS:
            if e != 'sync' and self.cnt[e] > 0:
                waits.append((e, self.cnt[e]))
        self.q['sync'].append((waits, None, None, 0))

    def emit(self):
        nc = self.nc
        sems = self.sems
        q = self.q
        with nc.Block() as block:
            def run(eng, items):
                for waits, fn, skey, inc in items:
                    for key, val in waits:
                        eng.wait_ge(sems[key], val)
                    if fn is not None:
                        ins = fn(eng)
                        ins.then_inc(sems[skey], inc)

            @block.sync
            def _(eng):
                run(eng, q['sync'])

            @block.tensor
            def _(eng):
                run(eng, q['tensor'])

            @block.vector
            def _(eng):
                run(eng, q['vector'])

            @block.scalar
            def _(eng):
                run(eng, q['scalar'])

            @block.gpsimd
            def _(eng):
                run(eng, q['gpsimd'])

    def mm(self, out, lhsT, rhs, start=True, stop=True):
        rd = [lhsT, rhs] + ([] if start else [out])
        return self.op('tensor', lambda e: e.matmul(out.ap, lhsT.ap, rhs.ap, start=start, stop=stop),
                       rd, [out])

    def transpose(self, out, in_, ident):
        return self.op('tensor', lambda e: e.transpose(out.ap, in_.ap, ident.ap), [in_, ident], [out])

    def act(self, out, in_, func, bias=None, scale=None, eng='scalar'):
        rd = [in_]
        kw = {}
        if bias is not None:
            if isinstance(bias, View):
                rd.append(bias)
                kw['bias'] = bias.ap
            else:
                kw['bias'] = bias
        if scale is not None:
            if isinstance(scale, View):
                rd.append(scale)
                kw['scale'] = scale.ap
            else:
                kw['scale'] = scale
        return self.op('scalar', lambda e: e.activation(out.ap, in_.ap, func, **kw), rd, [out])

    def tt(self, out, in0, in1, op, eng='vector'):
        return self.op(eng, lambda e: e.tensor_tensor(out.ap, in0.ap, in1.ap, op), [in0, in1], [out])

    def ts(self, out, in0, s1, op0, s2=None, op1=None, eng='vector'):
        rd = [in0]
        a1 = s1
        if isinstance(s1, View):
            rd.append(s1)
            a1 = s1.ap
        a2 = s2
        if isinstance(s2, View):
            rd.append(s2)
            a2 = s2.ap
        if op1 is None:
            return self.op(eng, lambda e: e.tensor_scalar(out.ap, in0.ap, a1, None, op0), rd, [out])
        return self.op(eng, lambda e: e.tensor_scalar(out.ap, in0.ap, a1, a2, op0, op1), rd, [out])

    def stt(self, out, in0, scalar, in1, op0, op1, eng='vector'):
        rd = [in0, in1]
        a = scalar
        if isinstance(scalar, View):
            rd.append(scalar)
            a = scalar.ap
        return self.op(eng, lambda e: e.scalar_tensor_tensor(out.ap, in0.ap, a, in1.ap, op0, op1), rd, [out])

    def copy(self, out, in_, eng='vector'):
        if eng == 'scalar':
            return self.op(eng, lambda e: e.copy(out.ap, in_.ap), [in_], [out])
        return self.op(eng, lambda e: e.tensor_copy(out.ap, in_.ap), [in_], [out])

    def memset(self, out, val, eng='vector'):
        return self.op(eng, lambda e: e.memset(out.ap, val), [], [out])

    def recip(self, out, in_):
        return self.op('vector', lambda e: e.reciprocal(out.ap, in_.ap), [in_], [out])

    def scan(self, out, d0, d1, init, op0, op1):
        rd = [d0, d1]
        a = init
        if isinstance(init, View):
            rd.append(init)
            a = init.ap
        return self.op('vector', lambda e: e.tensor_tensor_scan(out.ap, d0.ap, d1.ap, a, op0, op1), rd, [out])

    def load(self, out, in_, eng='sync', **kw):
        return self.dma(eng, lambda e: e.dma_start(out=out.ap, in_=in_.ap, **kw), [in_], [out])


D = 2048
KC = 16
SEQ = 4096
CTX = 256
TT = SEQ + CTX
NTC = 1088
HALF = 544
FH = 5632
HC = 44
EPS = 1e-6


def new_nc():
    return bass.Bass("TRN2", target_bir_lowering=False)


MN = 1536


def build_M():
    nc = new_nc()
    ccT_d = nc.dram_tensor("ccT", [128, KC, 3], F32, kind="ExternalInput")
    wm_d = nc.dram_tensor("wm", [4, 128, KC, MN], F32, kind="ExternalInput")
    bm_d = nc.dram_tensor("bm", [3, 4, MN], F32, kind="ExternalInput")
    out_d = nc.dram_tensor("mo", [3, 4, MN], F32, kind="ExternalOutput")
    with ExitStack() as es:
        S = Sched(nc, es)
        ccT = Tile(ccT_d); wm = Tile(wm_d); bm = Tile(bm_d); out = Tile(out_d)
        cc = S.sb('cc', [128, KC, 3])
        sc = S.sb('sc', [128, KC, 3])
        bmt = S.sb('bmt', [3, 4, MN])
        res = S.sb('res', [3, 4, MN])
        wts = [S.sb(f'w{i}', [128, 4, 512]) for i in range(3)]
        pss = [S.ps(f'ps{i}', [128, 512]) for i in range(2)]
        S.load(cc[:], ccT[:])
        S.load(bmt[:], bm[:])
        S.act(sc[:], cc[:], AF.Silu)
        it = 0
        for l in range(4):
            for nt in range(MN // 512):
                ps = pss[(l * 3 + nt) % 2]
                for kq in range(KC // 4):
                    w = wts[it % 3]
                    it += 1
                    S.load(w[:], wm[l, :, kq * 4:(kq + 1) * 4, nt * 512:(nt + 1) * 512])
                    for k4 in range(4):
                        kc = kq * 4 + k4
                        S.mm(ps[0:3, :], sc[:, kc, :], w[:, k4, :], start=(kc == 0), stop=(kc == KC - 1))
                S.tt(res[:, l, nt * 512:(nt + 1) * 512], ps[0:3, :], bmt[:, l, nt * 512:(nt + 1) * 512], ALU.add)
        S.load(out[:], res[:])
        S.finish()
        S.emit()
    return nc


C_VECS = ['mix_post_g', 'gmix_lat', 'gmix_ctx', 'ffn_pre_g', 'fsh_lat', 'fsh_ctx', 'fsc_lat', 'fsc_ctx',
          'ffn_post_g', 'gffn_lat', 'gffn_ctx']
NPC = len(C_VECS) * KC


def build_C():
    nc = new_nc()
    hT_d = nc.dram_tensor("hT", [D, NTC], F32, kind="ExternalInput")
    oT_d = nc.dram_tensor("oT", [D, NTC], F32, kind="ExternalInput")
    wout_d = nc.dram_tensor("wout", [KC, 128, KC, 128], F32, kind="ExternalInput")
    wfi_d = nc.dram_tensor("wfi", [2 * HC, 128, KC, 128], F32, kind="ExternalInput")
    wfo_d = nc.dram_tensor("wfo", [KC, 128, HC, 128], F32, kind="ExternalInput")
    prm_d = nc.dram_tensor("prm", [128, NPC], F32, kind="ExternalInput")
    hn_d = nc.dram_tensor("hn", [D, NTC], F32, kind="ExternalOutput")
    with ExitStack() as es:
        S = Sched(nc, es)
        hT = Tile(hT_d); oT = Tile(oT_d); wout = Tile(wout_d); wfi = Tile(wfi_d); wfo = Tile(wfo_d)
        prm_dr = Tile(prm_d); hn = Tile(hn_d)
        prm = S.sb('prm', [128, NPC])
        S.load(prm[:], prm_dr[:])

        def pv(name):
            i = C_VECS.index(name)
            return prm[:, i * KC:(i + 1) * KC]
        ones = S.sb('ones', [128, 128])
        S.memset(ones[:], 1.0)
        der = S.sb('der', [128, 6, KC])
        S.tt(der[:, 0, :], pv('mix_post_g'), pv('gmix_lat'), ALU.mult)
        S.tt(der[:, 1, :], pv('mix_post_g'), pv('gmix_ctx'), ALU.mult)
        S.tt(der[:, 2, :], pv('ffn_post_g'), pv('gffn_lat'), ALU.mult)
        S.tt(der[:, 3, :], pv('ffn_post_g'), pv('gffn_ctx'), ALU.mult)
        S.stt(der[:, 4, :], pv('fsc_lat'), 1.0, pv('ffn_pre_g'), ALU.add, ALU.mult)
        S.stt(der[:, 5, :], pv('fsc_ctx'), 1.0, pv('ffn_pre_g'), ALU.add, ALU.mult)
        pg = {'lat': der[:, 0, :], 'ctx': der[:, 1, :]}
        fg = {'lat': der[:, 2, :], 'ctx': der[:, 3, :]}
        gs = {'lat': der[:, 4, :], 'ctx': der[:, 5, :]}
        fsh = {'lat': pv('fsh_lat'), 'ctx': pv('fsh_ctx')}

        H = S.sb('H', [128, KC, HALF])
        M = S.sb('M', [128, KC, HALF])
        OB = S.sb('OB', [128, KC, HALF], BF16)
        A = S.sb('A', [128, HC, HALF], BF16)
        rstd = S.sb('rstd', [128, HALF])
        sqt = [S.sb(f'sqt{i}', [128, 272]) for i in range(2)]
        t2 = [S.sb(f't2{i}', [128, HALF]) for i in range(2)]
        wA = [S.sb(f'wA{i}', [128, KC, 128], BF16) for i in range(4)]
        wB = [S.sb(f'wB{i}', [128, HC, 128], BF16) for i in range(2)]
        PS = [S.ps(f'ps{i}', [128, 512]) for i in range(8)]
        psi = [0]

        def nps():
            p = PS[psi[0] % 6]
            psi[0] += 1
            return p
        pstat = [PS[6], PS[7]]
        NT = [(0, 272), (272, 544)]
        hTv = hT[:].rr('(kc p) t -> p kc t', p=128)
        oTv = oT[:].rr('(kc p) t -> p kc t', p=128)
        hnv = hn[:].rr('(kc p) t -> p kc t', p=128)
        wai = [0]

        def stats(X, scale):
            for ni, (a, b) in enumerate(NT):
                ps = pstat[ni]
                for kc in range(KC):
                    sq = sqt[kc % 2]
                    S.act(sq[:, 0:b - a], X[:, kc, a:b], AF.Square)
                    S.mm(ps[:, 0:b - a], ones[:], sq[:, 0:b - a], start=(kc == 0), stop=(kc == KC - 1))
                S.act(rstd[:, a:b], ps[:, 0:b - a], AF.Sqrt, bias=EPS, scale=1.0 / D)
            S.recip(rstd[:], rstd[:])

        def residual(X, gvec, segs):
            for kc in range(KC):
                S.tt(X[:, kc, :], X[:, kc, :], rstd[:], ALU.mult)
                for (a, b, w) in segs:
                    S.stt(H[:, kc, a:b], X[:, kc, a:b], gvec[w][:, kc:kc + 1], H[:, kc, a:b], ALU.mult, ALU.add)

        for hf in range(2):
            t0 = hf * HALF
            segs = [(0, 64, 'ctx'), (64, HALF, 'lat')] if hf == 0 else [(0, HALF, 'lat')]
            S.load(H[:], hTv[:, :, t0:t0 + HALF])
            S.load(OB[:], oTv[:, :, t0:t0 + HALF], eng='gpsimd')
            for oc in range(KC):
                w = wA[wai[0] % 4]
                wai[0] += 1
                S.load(w[:], wout[oc], eng='gpsimd')
                for (a, b) in NT:
                    ps = nps()
                    for kc in range(KC):
                        S.mm(ps[:, 0:b - a], w[:, kc, :], OB[:, kc, a:b], start=(kc == 0), stop=(kc == KC - 1))
                    S.copy(M[:, oc, a:b], ps[:, 0:b - a], eng='scalar')
            stats(M, 1.0)
            residual(M, pg, segs)
            stats(H, 1.0)
            for kc in range(KC):
                t = t2[kc % 2]
                S.tt(t[:], H[:, kc, :], rstd[:], ALU.mult)
                for (a, b, w_) in segs:
                    S.ts(OB[:, kc, a:b], t[:, a:b], gs[w_][:, kc:kc + 1], ALU.mult, fsh[w_][:, kc:kc + 1], ALU.add)
            for hc in range(HC):
                wg = wA[wai[0] % 4]
                wai[0] += 1
                wu = wA[wai[0] % 4]
                wai[0] += 1
                S.load(wg[:], wfi[hc], eng='gpsimd')
                S.load(wu[:], wfi[HC + hc], eng='gpsimd')
                for ni, (a, b) in enumerate(NT):
                    pg_ = nps()
                    pu_ = nps()
                    for kc in range(KC):
                        S.mm(pg_[:, 0:b - a], wg[:, kc, :], OB[:, kc, a:b], start=(kc == 0), stop=(kc == KC - 1))
                    for kc in range(KC):
                        S.mm(pu_[:, 0:b - a], wu[:, kc, :], OB[:, kc, a:b], start=(kc == 0), stop=(kc == KC - 1))
                    sg = sqt[ni]
                    S.act(sg[:, 0:b - a], pg_[:, 0:b - a], AF.Silu)
                    S.tt(A[:, hc, a:b], sg[:, 0:b - a], pu_[:, 0:b - a], ALU.mult)
            for oc in range(KC):
                w = wB[oc % 2]
                S.load(w[:], wfo[oc], eng='gpsimd')
                for (a, b) in NT:
                    ps = nps()
                    for hc in range(HC):
                        S.mm(ps[:, 0:b - a], w[:, hc, :], A[:, hc, a:b], start=(hc == 0), stop=(hc == HC - 1))
                    S.copy(M[:, oc, a:b], ps[:, 0:b - a], eng='scalar')
            stats(M, 1.0)
            residual(M, fg, segs)
            S.load(hnv[:, :, t0:t0 + HALF], H[:])
        S.finish()
        S.emit()
        print('C ninstr', S.ninstr)
    return nc

NCOL = 2216
C_Q, C_K, C_V, C_Z, C_G = 0, 256, 512, 768, 1024
C_RR, C_RK, C_RV, C_WD, C_AD, C_GD = 1032, 1288, 1544, 1800, 1928, 2056
PROJ_CHUNKS = [(i * 128, 128) for i in range(8)] + [(1024, 8)] + [(1032 + i * 128, 128) for i in range(9)] + [(2184, 32)]
RW_TILES = [(C_RR, 128), (C_RR + 128, 128), (C_RK, 128), (C_RK + 128, 128), (C_RV, 128), (C_RV + 128, 128),
            (C_WD, 128), (C_AD, 128), (C_GD, 128), (C_GD + 128, 32)]

A_PRM = [('mix_pre_g', 16), ('msh_lat', 16), ('msc_lat', 16), ('msh_ctx', 16), ('msc_ctx', 16),
         ('cw', 42), ('alog', 4), ('dtb', 4), ('outg', 1), ('mu', 10), ('w0', 4), ('a0', 4),
         ('kk', 2), ('ka', 2), ('rk', 2), ('lnw', 2), ('lnb', 2)]
A_OFF = {}
_o = 0
for _n, _w in A_PRM:
    A_OFF[_n] = (_o, _w)
    _o += _w
NPA = _o
CST_OFF = {'ident': (0, 128), 'LS': (128, 64), 'US': (192, 64), 'LI': (256, 64), 'UI': (320, 64),
           'blk': (384, 128), 'reset': (512, 512), 'm4': (1024, 4), 'm2': (1028, 2)}
NCST = 1030
NEGBIG = -60000.0
RW_LN_EPS_ = 64e-5


def make_cst():
    c = np.zeros((128, NCST), np.float32)
    c[:, 0:128] = np.eye(128)
    r = np.arange(64)[:, None]
    q = np.arange(64)[None, :]
    c[0:64, 128:192] = (r > q)
    c[0:64, 192:256] = (r < q)
    c[0:64, 256:320] = (r >= q)
    c[0:64, 320:384] = (r <= q)
    blk = np.zeros((128, 128), np.float32)
    blk[0:64, 0:64] = 1
    blk[64:, 64:] = 1
    c[:, 384:512] = blk
    rs = np.ones(512, np.float32)
    rs[::64] = 0
    c[:, 512:1024] = rs[None]
    p = np.arange(128)
    for ct in range(4):
        c[:, 1024 + ct] = (p % 4 == ct)
    for ct in range(2):
        c[:, 1028 + ct] = (p % 2 == ct)
    return c


class _Stop(Exception):
    pass


class Scope:
    stopped = False

    def __enter__(self):
        self.es = ExitStack()
        return self.es

    def __exit__(self, t, v, tb):
        self.es.close()
        if t is not None and issubclass(t, _Stop):
            Scope.stopped = True
            return True
        return False


def build_AB(stages=('A', 'DN', 'RW')):
    import os as _os2
    STOP = int(_os2.environ.get('DN_STOP', '0'))

    def chk(k):
        if STOP == k:
            raise _Stop()
    nc = new_nc()
    hT_d = nc.dram_tensor("hT", [D, TT], F32, kind="ExternalInput")
    win_d = nc.dram_tensor("win", [128, KC, NCOL], F32, kind="ExternalInput")
    prm_d = nc.dram_tensor("prm", [128, NPA], F32, kind="ExternalInput")
    wup_d = nc.dram_tensor("wup", [64, 2, 256], F32, kind="ExternalInput")
    aup_d = nc.dram_tensor("aup", [64, 2, 256], F32, kind="ExternalInput")
    gup_d = nc.dram_tensor("gup", [160, 256], F32, kind="ExternalInput")
    cst_d = nc.dram_tensor("cst", [128, NCST], F32, kind="ExternalInput")
    oT_d = nc.dram_tensor("oT", [512, TT], F32, kind="ExternalOutput")
    import os as _os
    pT_d = nc.dram_tensor("pT", [NCOL, TT], F32, kind=("ExternalOutput" if _os.environ.get("DBG_PT") else "Internal"))
    with ExitStack() as es0:
        S = Sched(nc, es0)
        hT = Tile(hT_d); win = Tile(win_d); prm_dr = Tile(prm_d); wup_dr = Tile(wup_d); aup_dr = Tile(aup_d)
        gup_dr = Tile(gup_d); cst_dr = Tile(cst_d); oT = Tile(oT_d)
        pT = Tile(pT_d, 'pT', keys=list(range(len(PROJ_CHUNKS))))

        def pTrows(r0, n):
            for ci, (c0, w) in enumerate(PROJ_CHUNKS):
                if c0 <= r0 and r0 + n <= c0 + w:
                    return View(pT.h[r0:r0 + n, :], [pT.bufs[ci]])
            raise ValueError((r0, n))

        prm = S.sb('prm', [128, NPA])
        cst = S.sb('cst', [128, NCST])
        S.load(prm[:], prm_dr[:])
        S.load(cst[:], cst_dr[:])

        def pv(name, j=None, n=1, np_=128):
            o, w = A_OFF[name]
            if j is None:
                return prm[0:np_, o:o + w]
            return prm[0:np_, o + j:o + j + n]

        def cv(name, np_=64):
            o, w = CST_OFF[name]
            return cst[0:np_, o:o + w]
        ones = S.sb('ones', [128, 128])
        S.memset(ones[:], 1.0)
        RDT = mybir.dt.float32r
        identR = S.sb('identR', [128, 128], RDT)
        ident = cv('ident', 128)
        S.copy(identR[:], ident)
        id64R = identR[0:64, 0:64]
        id64 = cst[0:64, 0:64]
        PS = [S.ps(f'b{i}', [128, 512]) for i in range(8)]
        psi = [0]

        def nb():
            p = PS[psi[0] % 7]
            psi[0] += 1
            return p

        def barrier():
            toks = [(e, S.cnt[e]) for e in ENGS if S.cnt[e] > 0 and e != 'sync']
            for key, val in S.dval.items():
                if val > 0:
                    toks.append((key, val))
            for e in ['tensor', 'vector', 'scalar', 'gpsimd', 'sync']:
                w = S._waits(e, toks)
                if w:
                    S.q[e].append((w, None, None, 0))

        SEGS = [(0, 256)] + [(256 + 512 * i, 512) for i in range(8)]
        CSEGS = [(0, 4)] + [(4 + 8 * i, 8) for i in range(8)]

        if 'A' in stages:
            with ExitStack() as es:
                S.es = es
                W = S.sb('W', [128, KC, NCOL], BF16)
                for kq in range(4):
                    S.load(W[:, kq * 4:(kq + 1) * 4, :], win[:, kq * 4:(kq + 1) * 4, :], eng='gpsimd')
                der = S.sb('derA', [128, 2, KC])
                S.stt(der[:, 0, :], pv('msc_lat'), 1.0, pv('mix_pre_g'), ALU.add, ALU.mult)
                S.stt(der[:, 1, :], pv('msc_ctx'), 1.0, pv('mix_pre_g'), ALU.add, ALU.mult)
                Hs2 = [S.sb(f'Hseg{i}', [128, KC, 512]) for i in range(2)]
                U2 = [S.sb(f'Useg{i}', [128, KC, 512], BF16) for i in range(2)]
                sq = [S.sb(f'sqA{i}', [128, 512]) for i in range(2)]
                rstd2 = [S.sb(f'rstdA{i}', [128, 512]) for i in range(2)]
                stg = [S.sb(f'stg{i}', [128, 512]) for i in range(4)]
                hTv = hT[:].rr('(kc p) t -> p kc t', p=128)
                apool = [PS[0], PS[1], PS[2], PS[3], PS[4], PS[5]]
                api = [0]

                def anb():
                    p = apool[api[0] % 6]
                    api[0] += 1
                    return p

                def prep(si_):
                    (t0, n) = SEGS[si_]
                    Hs = Hs2[si_ % 2]
                    U = U2[si_ % 2]
                    rstd = rstd2[si_ % 2]
                    isctx = (t0 == 0)
                    gsv = der[:, 1, :] if isctx else der[:, 0, :]
                    shv = pv('msh_ctx') if isctx else pv('msh_lat')
                    S.load(Hs[:, :, 0:n], hTv[:, :, t0:t0 + n])
                    yield
                    ps = PS[6 + si_ % 2]
                    for kc in range(KC):
                        s_ = sq[kc % 2]
                        S.act(s_[:, 0:n], Hs[:, kc, 0:n], AF.Square)
                        S.mm(ps[:, 0:n], ones[:], s_[:, 0:n], start=(kc == 0), stop=(kc == KC - 1))
                        if kc % 2 == 1:
                            yield
                    S.act(rstd[:, 0:n], ps[:, 0:n], AF.Sqrt, bias=EPS, scale=1.0 / D)
                    S.recip(rstd[:, 0:n], rstd[:, 0:n])
                    yield
                    for kc in range(KC):
                        S.tt(Hs[:, kc, 0:n], Hs[:, kc, 0:n], rstd[:, 0:n], ALU.mult)
                        S.ts(U[:, kc, 0:n], Hs[:, kc, 0:n], gsv[:, kc:kc + 1], ALU.mult, shv[:, kc:kc + 1], ALU.add)
                        yield

                def adv(g, k):
                    if g is None:
                        return None
                    for _ in range(k):
                        try:
                            next(g)
                        except StopIteration:
                            return None
                    return g
                si = 0
                g0 = prep(0)
                while g0 is not None:
                    g0 = adv(g0, 1)
                for sx, (t0, n) in enumerate(SEGS):
                    U = U2[sx % 2]
                    gnext = prep(sx + 1) if sx + 1 < len(SEGS) else None
                    for ci, (c0, w) in enumerate(PROJ_CHUNKS):
                        ps = anb()
                        for kc in range(KC):
                            S.mm(ps[0:w, 0:n], W[:, kc, c0:c0 + w], U[:, kc, 0:n], start=(kc == 0), stop=(kc == KC - 1))
                        st = stg[si % 4]
                        si += 1
                        if si % 2 == 0:
                            S.copy(st[0:w, 0:n], ps[0:w, 0:n], eng='scalar')
                        else:
                            S.copy(st[0:w, 0:n], ps[0:w, 0:n], eng='vector')
                        S.load(View(pT.h[c0:c0 + w, t0:t0 + n], [pT.bufs[ci]]), st[0:w, 0:n])
                        gnext = adv(gnext, 2)
                    while gnext is not None:
                        gnext = adv(gnext, 1)
                barrier()
            S.es = es0

        cder = S.sb('cder', [64, 5, 64])
        S.ts(cder[:, 0, :], cv('US'), NEGBIG, ALU.mult)
        S.ts(cder[:, 1, :], cv('LS'), NEGBIG, ALU.mult)
        S.ts(cder[:, 2, :], cv('UI'), -1.0, ALU.mult)
        S.ts(cder[:, 3, :], cv('LI'), -1.0, ALU.mult)
        S.stt(cder[:, 4, :], cv('LS'), -1.0, cv('US'), ALU.mult, ALU.subtract)
        negones = S.sb('negones', [64, 64])
        S.memset(negones[:], -1.0)
        NEGM = [cder[:, 0, :], cder[:, 1, :]]
        NEGMT = [cder[:, 1, :], cder[:, 0, :]]
        TRI = [cv('UI'), cv('LI')]
        NEGTRI = [cder[:, 2, :], cder[:, 3, :]]
        NOFFD = cder[:, 4, :]
        STRICT = [cv('LS'), cv('US')]
        STRICTT = [cv('US'), cv('LS')]
        INCLT = [cv('UI'), cv('LI')]

        def b3(v, nchk, axis, w=64):
            np_ = v.ap.shape[0]
            return v.un(axis).bc([np_, nchk, w])


        class Pool:
            def __init__(self, banks):
                self.b = banks
                self.i = 0

            def nb(self):
                p = self.b[self.i % len(self.b)]
                self.i += 1
                return p

        def zipgen(*gens):
            alive = [g for g in gens if g is not None]
            while alive:
                for g in list(alive):
                    try:
                        next(g)
                    except StopIteration:
                        alive.remove(g)
                yield

        def drive(gens):
            alive = list(gens)
            while alive:
                for g in list(alive):
                    try:
                        next(g)
                    except StopIteration:
                        alive.remove(g)

        def inverse_g(pool, X, XT, Ybuf, YTbuf, TTm, nchk):
            n = nchk * 64
            S.tt(TTm[:, 0:n].rr('p (c i) -> p c i', i=64), XT[:, 0:n].rr('p (c i) -> p c i', i=64),
                 b3(id64, nchk, 1), ALU.add)
            Y, YT = X, XT
            for k in range(1, 6):
                pY = pool.nb()
                for ci in range(nchk):
                    cs = slice(ci * 64, ci * 64 + 64)
                    S.mm(pY[0:64, cs], YT[:, cs], Y[:, cs])
                if k < 5:
                    pYT = pool.nb()
                    for ci in range(nchk):
                        cs = slice(ci * 64, ci * 64 + 64)
                        S.mm(pYT[0:64, cs], Y[:, cs], YT[:, cs])
                Yn = Ybuf[k % 2]
                S.copy(Yn[:, 0:n], pY[0:64, 0:n], eng='scalar')
                if k < 5:
                    YTn = YTbuf[k % 2]
                    S.copy(YTn[:, 0:n], pYT[0:64, 0:n], eng='scalar')
                yield
                pT_ = pool.nb()
                for ci in range(nchk):
                    cs = slice(ci * 64, ci * 64 + 64)
                    S.mm(pT_[0:64, cs], Yn[:, cs], TTm[:, cs])
                S.tt(TTm[:, 0:n], TTm[:, 0:n], pT_[0:64, 0:n], ALU.add)
                Y = Yn
                if k < 5:
                    YT = YTn
                yield

        NSEG = 17
        CSEG4 = [(4 * i, 4) for i in range(NSEG)]
        SEG4 = [(256 * i, 256) for i in range(NSEG)]
        ORDER = {0: list(range(NSEG)), 1: [0] + list(range(NSEG - 1, 0, -1))}
        misc = Pool([PS[6], PS[7]])

        if 'DN' in stages:
            for hh in range(2):
                with ExitStack() as es:
                    S.es = es
                    QT = S.sb('QT', [128, 68, 64], RDT)
                    KT = S.sb('KT', [128, 68, 64], RDT)
                    VT = S.sb('VT', [128, 68, 64], RDT)
                    OT = S.sb('OT', [128, 68, 64], keys=list(range(NSEG)))
                    sq = S.sb('sqD', [128, 512])
                    rs = S.sb('rsD', [128, 512])
                    BTA = S.sb('BTA', [64, 2, 68])
                    G = S.sb('Gg', [64, 2, 68])
                    with ExitStack() as es2:
                        S.es = es2
                        RAW = S.sb('RAW', [128, TT])
                        CV = S.sb('CV', [128, 68, 64])

                        def conv(r0, dst_final, cwj):
                            dst = CV
                            cw = pv('cw', cwj * 7, 7)
                            S.load(RAW[:], pTrows(r0, 128))
                            x = RAW[:, 0:256]
                            o = dst[:, 0:4, :].rr('p c i -> p (c i)')
                            S.ts(o, x, cw[:, 3:4], ALU.mult)
                            for k in range(7):
                                off = k - 3
                                if off == 0:
                                    continue
                                lo = max(0, -off)
                                hi = 256 - max(0, off)
                                S.stt(o[:, lo:hi], x[:, lo + off:hi + off], cw[:, k:k + 1], o[:, lo:hi], ALU.mult, ALU.add)
                            xl = RAW[:, 256:TT].rr('p (r c) -> p c r', c=64)
                            ol = dst[:, 4:68, :]
                            S.ts(ol, xl, cw[:, 3:4], ALU.mult)
                            for k in range(7):
                                off = k - 3
                                if off == 0:
                                    continue
                                c_ = cw[:, k:k + 1]
                                if off > 0:
                                    S.stt(ol[:, :, 0:64 - off], xl[:, :, off:64], c_, ol[:, :, 0:64 - off], ALU.mult, ALU.add)
                                    S.stt(ol[:, 0:63, 64 - off:64], xl[:, 1:64, 0:off], c_, ol[:, 0:63, 64 - off:64], ALU.mult, ALU.add)
                                else:
                                    o_ = -off
                                    S.stt(ol[:, :, o_:64], xl[:, :, 0:64 - o_], c_, ol[:, :, o_:64], ALU.mult, ALU.add)
                                    S.stt(ol[:, 1:64, 0:o_], xl[:, 0:63, 64 - o_:64], c_, ol[:, 1:64, 0:o_], ALU.mult, ALU.add)
                            S.act(dst_final[:].rr('p c i -> p (c i)'), dst[:].rr('p c i -> p (c i)'), AF.Silu)

                        def l2norm(X, mul):
                            Xf = X[:].rr('p c i -> p (c i)')
                            for (a, n) in SEGS:
                                S.act(sq[:, 0:n], Xf[:, a:a + n], AF.Square)
                                ps = misc.nb()
                                S.mm(ps[:, 0:n], ones[:], sq[:, 0:n])
                                S.act(rs[:, 0:n], ps[:, 0:n], AF.Sqrt, bias=EPS * mul, scale=float(mul))
                                S.recip(rs[:, 0:n], rs[:, 0:n])
                                S.tt(Xf[:, a:a + n], Xf[:, a:a + n], rs[:, 0:n], ALU.mult)

                        conv(C_Q + hh * 128, QT, 0 * 2 + hh)
                        l2norm(QT, 128.0)
                        conv(C_K + hh * 128, KT, 1 * 2 + hh)
                        l2norm(KT, 1.0)
                        conv(C_V + hh * 128, VT, 2 * 2 + hh)
                        GT = S.sb('GT', [64, 4, 68])
                        for j in range(4):
                            row = C_G + hh * 4 + j
                            gr = pTrows(row, 1)
                            S.load(GT[:, j, 4:68], View(gr.ap[:, 256:TT].rearrange('o (r c) -> (o r) c', c=64), gr.bufs))
                            S.load(GT[:, j, 0:4], View(gr.ap[:, 0:256].rearrange('o (c i) -> (o i) c', i=64), gr.bufs),
                                   allow_slow_non_contiguous=True)
                        gt1 = S.sb('gt1', [64, 2, 68])
                        gt2 = S.sb('gt2', [64, 2, 68])
                        gt3 = S.sb('gt3', [64, 2, 68])
                        nea = S.sb('nea', [64, 4])
                        S.act(nea[:], pv('alog', np_=64), AF.Exp)
                        S.ts(nea[:], nea[:], -1.0, ALU.mult)
                        S.act(BTA[:], GT[:, 0:2, :], AF.Sigmoid)
                        for d in range(2):
                            S.ts(gt1[:, d, :], GT[:, 2 + d, :], pv('dtb', hh * 2 + d, np_=64), ALU.add)
                        S.stt(gt2[:], gt1[:], -1.0, gt1[:], ALU.mult, ALU.max)
                        S.act(gt2[:], gt2[:], AF.Exp, scale=-1.0)
                        S.act(gt2[:], gt2[:], AF.Ln, bias=1.0)
                        S.ts(gt3[:], gt1[:], 0.0, ALU.max)
                        S.tt(gt3[:], gt3[:], gt2[:], ALU.add)
                        for d in range(2):
                            S.ts(G[:, d, :], gt3[:, d, :], nea[:, hh * 2 + d:hh * 2 + d + 1], ALU.mult)
                        barrier()
                    S.es = es
                    S.memset(OT[:], 0.0)
                    QTf = QT[:].rr('p c i -> p (c i)')

                    def mk_dset(d):
                        t = {}
                        for nm in ('GC', 'KDS', 'EGC', 'BEG', 't68'):
                            t[nm] = S.sb(f'{nm}{d}', [64, 68])
                        t['CD'] = S.sb(f'CD{d}', [128, 68])
                        for nm in ('GU', 'NBS', 'BI', 'NBST', 'Dm', 'DTm', 'X', 'XT', 'TTm', 'GCB'):
                            t[nm] = S.sb(f'{nm}{d}', [64, 256], RDT if nm in ('X', 'XT', 'TTm') else F32)
                        t['Yb'] = [S.sb(f'Yb{d}{i}', [64, 256], RDT) for i in range(2)]
                        t['YTb'] = [S.sb(f'YTb{d}{i}', [64, 256], RDT) for i in range(2)]
                        t['EGB'] = S.sb(f'EGB{d}', [128, 256])
                        for nm in ('VB', 'KBG'):
                            t[nm] = S.sb(f'{nm}{d}', [64, 4, 128], RDT)
                        t['obs'] = []
                        for i_ in range(2):
                            t['obs'].append(dict(QD=S.sb(f'QD{d}{i_}', [128, 256], RDT), P1T=S.sb(f'P1T{d}{i_}', [128, 256], RDT),
                                                 P2=S.sb(f'P2{d}{i_}', [64, 4, 128]), MT=S.sb(f'MT{d}{i_}', [64, 256], RDT),
                                                 KDC=S.sb(f'KDC{d}{i_}', [64, 4, 128], RDT)))
                        t['Um'] = S.sb(f'Um{d}', [64, 128], RDT)
                        t['Hst'] = S.sb(f'Hst{d}', [128, 128], RDT)
                        return t

                    def dn_dir(d, t, pool):
                        GC, KDS, EGC, BEG, t68, CD = t['GC'], t['KDS'], t['EGC'], t['BEG'], t['t68'], t['CD']
                        GU, NBS, BI, NBST, Dm, DTm, X, XT, TTm, GCB = (t[k_] for k_ in ('GU', 'NBS', 'BI', 'NBST', 'Dm', 'DTm', 'X', 'XT', 'TTm', 'GCB'))
                        EGB, VB, KBG, Um, Hst = (t[k_] for k_ in ('EGB', 'VB', 'KBG', 'Um', 'Hst'))
                        gT = G[:, d, :]
                        bT = BTA[:, d, :]
                        ps = pool.nb()
                        S.mm(ps[0:64, 0:68], TRI[d], gT)
                        S.copy(GC[:], ps[0:64, 0:68])
                        ps2 = pool.nb()
                        S.mm(ps2[:, 0:68], ones[0:64, :], gT)
                        S.act(CD[:], ps2[:, 0:68], AF.Exp)
                        S.tt(t68[:], ps2[0:64, 0:68], GC[:], ALU.subtract)
                        S.act(KDS[:], t68[:], AF.Exp)
                        S.act(EGC[:], GC[:], AF.Exp)
                        S.tt(BEG[:], bT, EGC[:], ALU.mult)
                        S.memset(Hst[:].cast(F32), 0.0)
                        yield
                        nchk = 4
                        n = 256

                        def v3(tl):
                            return tl[:, 0:n].rr('p (c i) -> p c i', i=64)

                        def pre(sidx, ob):
                            QD, P1T, P2, MT, KDC = ob['QD'], ob['P1T'], ob['P2'], ob['MT'], ob['KDC']
                            c0 = 4 * sidx
                            col0 = c0 * 64
                            gTs = G[:, d, c0:c0 + nchk]
                            bTs = BTA[:, d, c0:c0 + nchk]
                            S.tt(v3(GU), b3(gTs, nchk, 2), b3(TRI[d], nchk, 1), ALU.mult)
                            S.tt(v3(NBS), b3(bTs, nchk, 2), b3(NOFFD, nchk, 1), ALU.mult)
                            S.tt(v3(BI), b3(bTs, nchk, 2), b3(id64, nchk, 1), ALU.mult)
                            yield
                            pa = pool.nb()
                            S.mm(pa[:, 0:n], ones[0:64, :], GU[:, 0:n])
                            S.act(EGB[:, 0:n], pa[:, 0:n], AF.Exp)
                            S.copy(GCB[:, 0:n], pa[0:64, 0:n], eng='scalar')
                            S.tt(QD[:, 0:n], QTf[:, col0:col0 + n], EGB[:, 0:n], ALU.mult)
                            gcs = GC[:, c0:c0 + nchk]
                            S.tt(v3(Dm), b3(gcs, nchk, 2), v3(GCB), ALU.subtract, eng='gpsimd')
                            S.tt(v3(Dm), v3(Dm), b3(NEGM[d], nchk, 1), ALU.add, eng='gpsimd')
                            S.act(Dm[:, 0:n], Dm[:, 0:n], AF.Exp)
                            yield
                            S.tt(v3(DTm), v3(GCB), b3(gcs, nchk, 2), ALU.subtract, eng='gpsimd')
                            S.tt(v3(DTm), v3(DTm), b3(NEGMT[d], nchk, 1), ALU.add, eng='gpsimd')
                            S.act(DTm[:, 0:n], DTm[:, 0:n], AF.Exp)
                            pe_ = pool.nb()
                            S.mm(pe_[0:64, 0:n], ones[0:64, 0:64], BI[:, 0:n])
                            S.tt(v3(NBST), pe_[0:64, 0:n].rr('p (c i) -> p c i', i=64), b3(NOFFD, nchk, 1), ALU.mult)
                            yield
                            pd = pool.nb()
                            for ci in range(nchk):
                                c = c0 + ci
                                S.mm(pd[0:64, ci * 64:ci * 64 + 64], KT[:, c, :], KT[:, c, :])
                            S.tt(NBS[:, 0:n], NBS[:, 0:n], Dm[:, 0:n], ALU.mult, eng='gpsimd')
                            S.tt(NBST[:, 0:n], NBST[:, 0:n], DTm[:, 0:n], ALU.mult, eng='gpsimd')
                            S.tt(X[:, 0:n], pd[0:64, 0:n], NBS[:, 0:n], ALU.mult)
                            S.tt(XT[:, 0:n], pd[0:64, 0:n], NBST[:, 0:n], ALU.mult)
                            yield
                            yield from inverse_g(pool, X, XT, t['Yb'], t['YTb'], TTm, nchk)
                            pv_ = pool.nb()
                            pk_ = pool.nb()
                            for cj in range(4):
                                c = c0 + cj
                                S.transpose(pv_[0:64, cj * 128:cj * 128 + 128].cast(RDT), VT[:, c, :], identR[:])
                                S.transpose(pk_[0:64, cj * 128:cj * 128 + 128].cast(RDT), KT[:, c, :], identR[:])
                            pv3 = pv_[0:64, :].rr('p (c k) -> p c k', k=128)
                            pk3 = pk_[0:64, :].rr('p (c k) -> p c k', k=128)
                            S.tt(VB[:], pv3, b3(BTA[:, d, c0:c0 + 4], 4, 2, 128), ALU.mult)
                            S.tt(KBG[:], pk3, b3(BEG[:, c0:c0 + 4], 4, 2, 128), ALU.mult)
                            S.tt(KDC[:], pk3, b3(KDS[:, c0:c0 + 4], 4, 2, 128), ALU.mult)
                            yield
                            pu = pool.nb()
                            for ci in range(4):
                                S.mm(pu[0:64, ci * 128:ci * 128 + 128], TTm[:, ci * 64:ci * 64 + 64], VB[:, ci, :])
                            S.copy(P2[:], pu[0:64, :].rr('p (c k) -> p c k', k=128), eng='scalar')
                            pw = pool.nb()
                            for ci in range(nchk):
                                S.mm(pw[:, ci * 64:ci * 64 + 64], KBG[:, ci, :], TTm[:, ci * 64:ci * 64 + 64])
                            S.act(P1T[:, 0:n], pw[:, 0:n], AF.Copy, scale=-1.0)
                            yield
                            pm = pool.nb()
                            for ci in range(nchk):
                                c = c0 + ci
                                S.mm(pm[0:64, ci * 64:ci * 64 + 64], KT[:, c, :], QT[:, c, :])
                            S.tt(ob['MT'][:, 0:n], pm[0:64, 0:n], DTm[:, 0:n], ALU.mult)
                            yield

                        def chain(sidx, ob):
                            QD, P1T, P2, MT, KDC = ob['QD'], ob['P1T'], ob['P2'], ob['MT'], ob['KDC']
                            c0 = 4 * sidx
                            corder = list(range(nchk)) if d == 0 else list(range(nchk - 1, -1, -1))
                            for ci in corder:
                                c = c0 + ci
                                cs = slice(ci * 64, ci * 64 + 64)
                                p1 = pool.nb()
                                S.mm(p1[0:64, 0:128], P1T[:, cs], Hst[:])
                                S.tt(Um[:], p1[0:64, 0:128], P2[:, ci, :], ALU.add)
                                yield
                                p2 = pool.nb()
                                S.mm(p2[:, 0:64], Hst[:], QD[:, cs], start=True, stop=False)
                                S.mm(p2[:, 0:64], Um[:], MT[:, cs], start=False, stop=True)
                                ov = OT.k(sidx, (slice(None), c, slice(None)))
                                S.tt(ov, p2[:, 0:64], ov, ALU.add)
                                p3 = pool.nb()
                                S.mm(p3[:, 0:128], KDC[:, ci, :], Um[:])
                                S.stt(Hst[:], Hst[:], CD[:, c:c + 1], p3[:, 0:128], ALU.mult, ALU.add)
                                yield

                        order = ORDER[d]
                        obs = t['obs']
                        yield from pre(order[0], obs[0])
                        for i_, sidx in enumerate(order):
                            gl = [chain(sidx, obs[i_ % 2])]
                            if i_ + 1 < len(order):
                                gl.append(pre(order[i_ + 1], obs[(i_ + 1) % 2]))
                            yield from zipgen(*gl)

                    dsets = [mk_dset(0), mk_dset(1)]
                    gens_ = [dn_dir(0, dsets[0], Pool(PS[0:4])), dn_dir(1, dsets[1], Pool(PS[4:8]))]
                    if hh == 0 and 'RW' in stages:
                        RAW1 = S.sb('RAW1', [128, TT])
                        OUT1 = S.sb('OUT1', [128, TT])
                        c0m = S.sb('c0m', [128, 10])
                        m4mu = S.sb('m4mu', [128, 10, 4])
                        m2mu = S.sb('m2mu', [128, 10, 2])

                        def rw1_gen():
                            S.ts(c0m[:], pv('mu'), -1.0, ALU.mult, 1.0, ALU.add)
                            S.tt(m4mu[:], pv('mu').un(2).bc([128, 10, 4]), cv('m4', 128).un(1).bc([128, 10, 4]), ALU.mult)
                            S.tt(m2mu[:], pv('mu').un(2).bc([128, 10, 2]), cv('m2', 128).un(1).bc([128, 10, 2]), ALU.mult)
                            yield
                            for ti, (r0, np_) in enumerate(RW_TILES):
                                RAW = RAW1
                                OUT = OUT1
                                S.load(RAW[0:np_, :], pTrows(r0, np_))
                                yield
                                x = RAW[0:np_, 0:256]
                                o = OUT[0:np_, 0:256]
                                S.ts(o, x, c0m[0:np_, ti:ti + 1], ALU.mult)
                                S.stt(o[:, 1:256], x[:, 0:255], m2mu[0:np_, ti, 0:1], o[:, 1:256], ALU.mult, ALU.add)
                                S.stt(o[:, 0:255], x[:, 1:256], m2mu[0:np_, ti, 1:2], o[:, 0:255], ALU.mult, ALU.add)
                                yield
                                xl = RAW[0:np_, 256:TT].rr('p (r c) -> p r c', c=64)
                                ol = OUT[0:np_, 256:TT].rr('p (r c) -> p r c', c=64)
                                S.ts(ol, xl, c0m[0:np_, ti:ti + 1], ALU.mult)
                                yield
                                S.stt(ol[:, :, 1:64], xl[:, :, 0:63], m4mu[0:np_, ti, 0:1], ol[:, :, 1:64], ALU.mult, ALU.add)
                                yield
                                S.stt(ol[:, :, 0:63], xl[:, :, 1:64], m4mu[0:np_, ti, 1:2], ol[:, :, 0:63], ALU.mult, ALU.add)
                                yield
                                S.stt(ol[:, 1:64, :], xl[:, 0:63, :], m4mu[0:np_, ti, 2:3], ol[:, 1:64, :], ALU.mult, ALU.add)
                                yield
                                S.stt(ol[:, 0:63, :], xl[:, 1:64, :], m4mu[0:np_, ti, 3:4], ol[:, 0:63, :], ALU.mult, ALU.add)
                                yield
                                if r0 == C_WD:
                                    S.act(OUT[0:np_, :], OUT[0:np_, :], AF.Tanh)
                                if r0 >= C_GD:
                                    S.act(OUT[0:np_, :], OUT[0:np_, :], AF.Sigmoid)
                                S.load(pTrows(r0, np_), OUT[0:np_, :])
                                yield
                        gens_.append(rw1_gen())
                    drive(gens_)
                    if hh == 0 and 'RW' in stages:
                        RESt, ZRt = OUT1, RAW1
                    else:
                        RESt = S.sb('RESt', [128, TT])
                        ZRt = S.sb('ZRt', [128, TT])
                    OTf = OT[:].rr('p c i -> p (c i)')
                    for (a, n) in SEGS:
                        S.act(sq[:, 0:n], OTf[:, a:a + n], AF.Square)
                        ps = misc.nb()
                        S.mm(ps[:, 0:n], ones[:], sq[:, 0:n])
                        S.act(rs[:, 0:n], ps[:, 0:n], AF.Sqrt, bias=EPS, scale=1.0 / 128)
                        S.recip(rs[:, 0:n], rs[:, 0:n])
                        S.tt(OTf[:, a:a + n], OTf[:, a:a + n], rs[:, 0:n], ALU.mult)
                    ZR = ZRt[:]
                    S.load(ZR, pTrows(C_Z + hh * 128, 128))
                    S.act(ZR, ZR, AF.Silu)
                    S.ts(OTf, OTf, pv('outg', 0), ALU.mult)
                    RES = RESt[:]
                    S.tt(RES[:, 0:256], OTf[:, 0:256], ZR[:, 0:256], ALU.mult)
                    S.tt(RES[:, 256:TT].rr('p (r c) -> p c r', c=64), OT[:, 4:68, :],
                         ZR[:, 256:TT].rr('p (r c) -> p c r', c=64), ALU.mult)
                    S.load(oT[hh * 128:(hh + 1) * 128, :], RES)
                    barrier()
                S.es = es0

        if 'RW' in stages:
            with ExitStack() as es:
                S.es = es
                RW_TILES_ = [] if 'DN' in stages else RW_TILES
                RAWs = [S.sb(f'RAWr{i}', [128, TT]) for i in range(2)]
                OUTs = [S.sb(f'OUTr{i}', [128, TT]) for i in range(2)]
                c0m = S.sb('c0m', [128, 10])
                m4mu = S.sb('m4mu', [128, 10, 4])
                m2mu = S.sb('m2mu', [128, 10, 2])
                S.ts(c0m[:], pv('mu'), -1.0, ALU.mult, 1.0, ALU.add)
                S.tt(m4mu[:], pv('mu').un(2).bc([128, 10, 4]), cv('m4', 128).un(1).bc([128, 10, 4]), ALU.mult)
                S.tt(m2mu[:], pv('mu').un(2).bc([128, 10, 2]), cv('m2', 128).un(1).bc([128, 10, 2]), ALU.mult)
                for ti, (r0, np_) in enumerate(RW_TILES_):
                    RAW = RAWs[ti % 2]
                    OUT = OUTs[ti % 2]
                    ve = 'vector'
                    S.load(RAW[0:np_, :], pTrows(r0, np_))
                    x = RAW[0:np_, 0:256]
                    o = OUT[0:np_, 0:256]
                    S.ts(o, x, c0m[0:np_, ti:ti + 1], ALU.mult, eng=ve)
                    S.stt(o[:, 1:256], x[:, 0:255], m2mu[0:np_, ti, 0:1], o[:, 1:256], ALU.mult, ALU.add, eng=ve)
                    S.stt(o[:, 0:255], x[:, 1:256], m2mu[0:np_, ti, 1:2], o[:, 0:255], ALU.mult, ALU.add, eng=ve)
                    xl = RAW[0:np_, 256:TT].rr('p (r c) -> p r c', c=64)
                    ol = OUT[0:np_, 256:TT].rr('p (r c) -> p r c', c=64)
                    S.ts(ol, xl, c0m[0:np_, ti:ti + 1], ALU.mult, eng=ve)
                    S.stt(ol[:, :, 1:64], xl[:, :, 0:63], m4mu[0:np_, ti, 0:1], ol[:, :, 1:64], ALU.mult, ALU.add, eng=ve)
                    S.stt(ol[:, :, 0:63], xl[:, :, 1:64], m4mu[0:np_, ti, 1:2], ol[:, :, 0:63], ALU.mult, ALU.add, eng=ve)
                    S.stt(ol[:, 1:64, :], xl[:, 0:63, :], m4mu[0:np_, ti, 2:3], ol[:, 1:64, :], ALU.mult, ALU.add, eng=ve)
                    S.stt(ol[:, 0:63, :], xl[:, 1:64, :], m4mu[0:np_, ti, 3:4], ol[:, 0:63, :], ALU.mult, ALU.add, eng=ve)
                    if r0 == C_WD:
                        S.act(OUT[0:np_, :], OUT[0:np_, :], AF.Tanh)
                    if r0 >= C_GD:
                        S.act(OUT[0:np_, :], OUT[0:np_, :], AF.Sigmoid)
                    S.load(pTrows(r0, np_), OUT[0:np_, :])
                barrier()
            S.es = es0
            with ExitStack() as es:
                S.es = es
                WUP = S.sb('WUP', [64, 2, 256]); AUP = S.sb('AUP', [64, 2, 256])
                GUP0 = S.sb('GUP0', [128, 256]); GUP1 = S.sb('GUP1', [32, 256])
                S.load(WUP[:], wup_dr[:]); S.load(AUP[:], aup_dr[:])
                S.load(GUP0[:], gup_dr[0:128, :]); S.load(GUP1[:], gup_dr[128:160, :])
                omka = S.sb('omka', [128, 2])
                S.ts(omka[:], pv('ka'), -1.0, ALU.mult, 1.0, ALU.add)
                YT = S.sb('YTr', [128, TT], keys=list(range(NSEG)))
                BON = S.sb('BON', [128, TT], keys=list(range(NSEG)))
                GD0 = S.sb('GD0', [128, 256]); GD1 = S.sb('GD1', [32, 256])
                RESET = cv('reset', 128)
                BLK = cv('blk', 128)
                NQ = 256

                def mk_rset(d):
                    t = {}
                    names = ('R', 'Kp', 'V', 'LD', 'IC', 'KK', 'KD', 'PRE', 'LIN', 'LEX', 'EIN', 'EEX', 'ENI', 'ETL',
                             'AT', 'KIC', 'BT_', 'KT_', 'BH', 'KH')
                    for nm in names:
                        t[nm] = S.sb(f'r{nm}{d}', [128, NQ], RDT if nm in ('AT', 'BT_', 'KT_', 'BH', 'KH') else F32)
                    t['WD'] = S.sb(f'rWD{d}', [64, NQ]); t['AD'] = S.sb(f'rAD{d}', [64, NQ])
                    t['ATK'] = S.sb(f'rATK{d}', [64, 4, 128], RDT)
                    t['obs'] = []
                    for i_ in range(2):
                        t['obs'].append(dict(
                            RT=S.sb(f'rRT{d}{i_}', [128, NQ], RDT), P1T=S.sb(f'rP1T{d}{i_}', [128, NQ], RDT),
                            P2=S.sb(f'rP2{d}{i_}', [64, 4, 128]), VTK=S.sb(f'rVTK{d}{i_}', [64, 4, 128], RDT),
                            BHK=S.sb(f'rBHK{d}{i_}', [64, 4, 128], RDT), KHK=S.sb(f'rKHK{d}{i_}', [64, 4, 128], RDT),
                            WC=S.sb(f'rWC{d}{i_}', [128, 4]),
                            MrbT=[S.sb(f'rMb{d}{i_}{e}', [64, NQ], RDT) for e in range(2)],
                            MrkT=[S.sb(f'rMk{d}{i_}{e}', [64, NQ], RDT) for e in range(2)]))
                    t['hd'] = []
                    for e in range(2):
                        t['hd'].append(dict(
                            X=S.sb(f'rX{d}{e}', [64, NQ], RDT), XT=S.sb(f'rXT{d}{e}', [64, NQ], RDT), AkT=S.sb(f'rAk{d}{e}', [64, NQ], RDT),
                            Yb=[S.sb(f'rY{d}{e}{i}', [64, NQ], RDT) for i in range(2)],
                            YTb=[S.sb(f'rYT{d}{e}{i}', [64, NQ], RDT) for i in range(2)],
                            TTm=S.sb(f'rTT{d}{e}', [64, NQ], RDT), ZS=S.sb(f'rZS{d}{e}', [64, 4, 64], RDT)))
                    t['Um'] = S.sb(f'rUm{d}', [64, 128], RDT)
                    t['Hst'] = [S.sb(f'rHst{d}{i}', [128, 128], RDT) for i in range(2)]
                    return t

                def rw_dir(hp, d, t, pool, pP1):
                    hc0 = hp * 128
                    (R, Kp, V, LD, IC, KK, KD, PRE, LIN, LEX, EIN, EEX, ENI, ETL, AT, KIC, BT_, KT_, BH, KH) = (
                        t[k_] for k_ in ('R', 'Kp', 'V', 'LD', 'IC', 'KK', 'KD', 'PRE', 'LIN', 'LEX', 'EIN', 'EEX', 'ENI', 'ETL',
                                         'AT', 'KIC', 'BT_', 'KT_', 'BH', 'KH'))
                    KX, SQ, RS, TMP, T1, T2 = PRE, LIN, LEX, EIN, EEX, ENI
                    WD, AD, ATK, hd, Um = (t[k_] for k_ in ('WD', 'AD', 'ATK', 'hd', 'Um'))
                    Hst = t['Hst'][hp]
                    S.memset(Hst[:].cast(F32), 0.0)
                    w0c = pv('w0', d * 2 + hp)
                    a0c = pv('a0', d * 2 + hp)
                    nchk = 4
                    n = 256

                    def v3(tl, np_=128):
                        return tl[0:np_, 0:n].rr('p (c i) -> p c i', i=64)

                    def pre(sidx, ob):
                        RT, P1T, P2, VTK, BHK, KHK, WC = (ob[k_] for k_ in ('RT', 'P1T', 'P2', 'VTK', 'BHK', 'KHK', 'WC'))
                        t0 = 256 * sidx
                        cols = slice(t0, t0 + n)
                        S.load(R[:, 0:n], View(pT.h[C_RR + hc0:C_RR + hc0 + 128, cols], [pT.bufs[9 + hp]]))
                        S.load(Kp[:, 0:n], View(pT.h[C_RK + hc0:C_RK + hc0 + 128, cols], [pT.bufs[11 + hp]]))
                        S.load(V[:, 0:n], View(pT.h[C_RV + hc0:C_RV + hc0 + 128, cols], [pT.bufs[13 + hp]]))
                        S.load(WD[:, 0:n], View(pT.h[C_WD + d * 64:C_WD + d * 64 + 64, cols], [pT.bufs[15]]))
                        S.load(AD[:, 0:n], View(pT.h[C_AD + d * 64:C_AD + d * 64 + 64, cols], [pT.bufs[16]]))
                        yield
                        p1 = pool.nb()
                        S.mm(p1[:, 0:n], WUP[:, d, hc0:hc0 + 128], WD[:, 0:n])
                        S.act(LD[:, 0:n], p1[:, 0:n], AF.Sigmoid, bias=w0c)
                        S.ts(LD[:, 0:n], LD[:, 0:n], -0.6065306597126334, ALU.mult, eng='gpsimd')
                        p2 = pool.nb()
                        S.mm(p2[:, 0:n], AUP[:, d, hc0:hc0 + 128], AD[:, 0:n])
                        S.act(IC[:, 0:n], p2[:, 0:n], AF.Sigmoid, bias=a0c)
                        S.ts(KX[:, 0:n], Kp[:, 0:n], pv('kk', hp), ALU.mult)
                        S.act(SQ[:, 0:n], KX[:, 0:n], AF.Square)
                        yield
                        p3 = pool.nb()
                        S.mm(p3[:, 0:n], BLK, SQ[:, 0:n])
                        S.act(RS[:, 0:n], p3[:, 0:n], AF.Sqrt, bias=EPS, scale=1.0)
                        S.recip(RS[:, 0:n], RS[:, 0:n])
                        S.tt(KK[:, 0:n], KX[:, 0:n], RS[:, 0:n], ALU.mult)
                        S.ts(TMP[:, 0:n], IC[:, 0:n], pv('ka', hp), ALU.mult, omka[:, hp:hp + 1], ALU.add)
                        S.tt(KD[:, 0:n], Kp[:, 0:n], TMP[:, 0:n], ALU.mult)
                        S.tt(T1[:, 0:n], R[:, 0:n], KD[:, 0:n], ALU.mult)
                        S.ts(T1[:, 0:n], T1[:, 0:n], pv('rk', hp), ALU.mult)
                        yield
                        p4 = pool.nb()
                        S.mm(p4[:, 0:n], BLK, T1[:, 0:n])
                        S.tt(T2[:, 0:n], p4[:, 0:n], V[:, 0:n], ALU.mult)
                        bv = BON.k(sidx, (slice(None), cols))
                        S.tt(bv, bv, T2[:, 0:n], ALU.add)
                        S.scan(PRE[:, 0:n], RESET[:, 0:n], LD[:, 0:n], 0.0, ALU.mult, ALU.add)
                        TOT = v3(PRE)[:, :, 63:64]
                        if d == 0:
                            LINv = PRE
                        else:
                            S.tt(v3(LIN), TOT.bc([128, nchk, 64]), v3(PRE), ALU.subtract)
                            S.tt(LIN[:, 0:n], LIN[:, 0:n], LD[:, 0:n], ALU.add)
                            LINv = LIN
                        S.tt(LEX[:, 0:n], LINv[:, 0:n], LD[:, 0:n], ALU.subtract, eng='gpsimd')
                        S.act(WC[:, 0:nchk], PRE[:, 0:n].rr('p (c i) -> p c i', i=64)[:, :, 63], AF.Exp)
                        S.tt(v3(ETL), TOT.bc([128, nchk, 64]), v3(LINv), ALU.subtract)
                        yield
                        S.act(ETL[:, 0:n], ETL[:, 0:n], AF.Exp)
                        S.act(EEX[:, 0:n], LEX[:, 0:n], AF.Exp)
                        S.act(ENI[:, 0:n], LINv[:, 0:n], AF.Exp, scale=-1.0)
                        S.act(EIN[:, 0:n], LINv[:, 0:n], AF.Exp)
                        S.tt(RT[:, 0:n], R[:, 0:n], EIN[:, 0:n], ALU.mult, eng='gpsimd')
                        S.stt(AT[:, 0:n], KK[:, 0:n], -1.0, EEX[:, 0:n], ALU.mult, ALU.mult)
                        S.tt(KIC[:, 0:n], KK[:, 0:n], IC[:, 0:n], ALU.mult, eng='gpsimd')
                        yield
                        S.tt(BT_[:, 0:n], KIC[:, 0:n], ENI[:, 0:n], ALU.mult)
                        S.tt(KT_[:, 0:n], KD[:, 0:n], ENI[:, 0:n], ALU.mult, eng='gpsimd')
                        S.tt(BH[:, 0:n], KIC[:, 0:n], ETL[:, 0:n], ALU.mult)
                        S.tt(KH[:, 0:n], KD[:, 0:n], ETL[:, 0:n], ALU.mult, eng='gpsimd')
                        yield
                        for qi, (src, dst) in enumerate(((AT, ATK), (BH, BHK), (KH, KHK), (V, VTK))):
                            pt_ = pool.nb()
                            for cj in range(4):
                                if src is V:
                                    S.transpose(pt_[0:64, cj * 128:cj * 128 + 128], src[:, cj * 64:cj * 64 + 64], ident)
                                else:
                                    S.transpose(pt_[0:64, cj * 128:cj * 128 + 128].cast(RDT), src[:, cj * 64:cj * 64 + 64], identR[:])
                            S.copy(dst[:], pt_[0:64, :].rr('p (c k) -> p c k', k=128),
                                   eng='scalar' if qi % 2 == 0 else 'vector')
                            if qi % 2 == 1:
                                yield

                        def head(e):
                            h_ = hd[e]
                            pe = slice(64 * e, 64 * e + 64)
                            for (lh, rh, dst, msk) in ((AT, BT_, h_['X'], STRICT[d]), (BT_, AT, h_['XT'], STRICTT[d]),
                                                      (KT_, AT, h_['AkT'], STRICTT[d]), (BT_, RT, ob['MrbT'][e], INCLT[d]),
                                                      (KT_, RT, ob['MrkT'][e], INCLT[d])):
                                pq = pool.nb()
                                for ci in range(nchk):
                                    cs = slice(ci * 64, ci * 64 + 64)
                                    S.mm(pq[0:64, cs], lh[pe, cs], rh[pe, cs])
                                S.tt(v3(dst, 64), pq[0:64, 0:n].rr('p (c i) -> p c i', i=64), b3(msk, nchk, 1), ALU.mult)
                                yield
                            yield from inverse_g(pool, h_['X'], h_['XT'], h_['Yb'], h_['YTb'], h_['TTm'], nchk)
                            TTm = h_['TTm']
                            for ci in range(nchk):
                                cs = slice(ci * 64, ci * 64 + 64)
                                if e == 0:
                                    S.mm(pP1[pe, cs], ATK[:, ci, pe], TTm[:, cs])
                                else:
                                    S.mm(pP1[pe, cs], ATK[:, ci, pe].cast(F32), TTm[:, cs].cast(F32))
                            pz = pool.nb()
                            for ci in range(nchk):
                                cs = slice(ci * 64, ci * 64 + 64)
                                S.mm(pz[0:64, cs], h_['AkT'][:, cs], VTK[:, ci, pe])
                            S.copy(h_['ZS'][:], pz[0:64, 0:n].rr('p (c v) -> p c v', v=64), eng='scalar')
                            yield
                            pp2 = pool.nb()
                            for ci in range(nchk):
                                cs = slice(ci * 64, ci * 64 + 64)
                                S.mm(pp2[0:64, cs], TTm[:, cs], h_['ZS'][:, ci, :])
                            S.copy(P2[:, :, pe], pp2[0:64, 0:n].rr('p (c v) -> p c v', v=64), eng='vector')
                            yield
                        yield from zipgen(head(0), head(1))
                        S.copy(P1T[:, 0:n], pP1[:, 0:n], eng='scalar')
                        yield

                    def chain(sidx, ob):
                        RT, P1T, P2, VTK, BHK, KHK, WC = (ob[k_] for k_ in ('RT', 'P1T', 'P2', 'VTK', 'BHK', 'KHK', 'WC'))
                        t0 = 256 * sidx
                        corder = list(range(nchk)) if d == 0 else list(range(nchk - 1, -1, -1))
                        for ci in corder:
                            cs = slice(ci * 64, ci * 64 + 64)
                            gcol = slice(t0 + ci * 64, t0 + ci * 64 + 64)
                            q1 = pool.nb()
                            S.mm(q1[0:64, 0:128], P1T[:, cs], Hst[:])
                            S.tt(Um[:], q1[0:64, 0:128], P2[:, ci, :], ALU.add)
                            yield
                            q2 = pool.nb()
                            S.mm(q2[:, 0:64], Hst[:], RT[:, cs], start=True, stop=False)
                            for e in range(2):
                                pe = slice(64 * e, 64 * e + 64)
                                if e == 0:
                                    S.mm(q2[pe, 0:64], Um[:, pe], ob['MrbT'][e][:, cs], start=False, stop=False)
                                    S.mm(q2[pe, 0:64], VTK[:, ci, pe], ob['MrkT'][e][:, cs], start=False, stop=True)
                                else:
                                    S.mm(q2[pe, 0:64], Um[:, pe].cast(F32), ob['MrbT'][e][:, cs].cast(F32), start=False, stop=False)
                                    S.mm(q2[pe, 0:64], VTK[:, ci, pe].cast(F32), ob['MrkT'][e][:, cs].cast(F32), start=False, stop=True)
                            yv = YT.k(sidx, (slice(None), gcol))
                            S.tt(yv, q2[:, 0:64], yv, ALU.add)
                            q3 = pool.nb()
                            S.mm(q3[:, 0:128], BHK[:, ci, :], Um[:], start=True, stop=False)
                            S.mm(q3[:, 0:128], KHK[:, ci, :], VTK[:, ci, :], start=False, stop=True)
                            for e in range(2):
                                pe = slice(64 * e, 64 * e + 64)
                                S.stt(Hst[pe, pe], Hst[pe, pe], WC[pe, ci:ci + 1], q3[pe, pe], ALU.mult, ALU.add)
                            yield

                    order = ORDER[d]
                    obs = t['obs']
                    yield from pre(order[0], obs[0])
                    for i_, sidx in enumerate(order):
                        gl = [chain(sidx, obs[i_ % 2])]
                        if i_ + 1 < len(order):
                            gl.append(pre(order[i_ + 1], obs[(i_ + 1) % 2]))
                        yield from zipgen(*gl)
                        done_[d].add(sidx)

                done_ = {0: set(), 1: set()}
                FR = S.sb('FRf', [128, 256]); FK = S.sb('FKf', [128, 256]); FV = S.sb('FVf', [128, 256])

                def fin_gen(hp):
                    hc0 = hp * 128
                    finpool = Pool([PS[2], PS[5]])
                    finished = set()
                    n = 256
                    while len(finished) < NSEG:
                        ready = [s_ for s_ in range(NSEG) if s_ not in finished and s_ in done_[0] and s_ in done_[1]]
                        if not ready:
                            yield
                            continue
                        sidx = ready[0]
                        t0 = 256 * sidx
                        cols = slice(t0, t0 + n)
                        S.load(GD0[:, 0:n], View(pT.h[C_GD:C_GD + 128, cols], [pT.bufs[17]]))
                        S.load(GD1[:, 0:n], View(pT.h[C_GD + 128:C_GD + 160, cols], [pT.bufs[18]]))
                        yv = YT.k(sidx, (slice(None), cols))
                        f1 = finpool.nb()
                        S.mm(f1[:, 0:n], BLK, yv)
                        S.stt(FR[:, 0:n], f1[:, 0:n], -1.0 / 64, yv, ALU.mult, ALU.add)
                        S.act(FK[:, 0:n], FR[:, 0:n], AF.Square)
                        yield
                        f2 = finpool.nb()
                        S.mm(f2[:, 0:n], BLK, FK[:, 0:n])
                        S.act(FV[:, 0:n], f2[:, 0:n], AF.Sqrt, bias=RW_LN_EPS_, scale=1.0 / 64)
                        S.recip(FV[:, 0:n], FV[:, 0:n])
                        S.tt(FR[:, 0:n], FR[:, 0:n], FV[:, 0:n], ALU.mult)
                        S.ts(FR[:, 0:n], FR[:, 0:n], pv('lnw', hp), ALU.mult, pv('lnb', hp), ALU.add)
                        S.tt(FR[:, 0:n], FR[:, 0:n], BON.k(sidx, (slice(None), cols)), ALU.add)
                        yield
                        f3 = finpool.nb()
                        S.mm(f3[:, 0:n], GUP0[:, hc0:hc0 + 128], GD0[:, 0:n], start=True, stop=False)
                        S.mm(f3[:, 0:n], GUP1[:, hc0:hc0 + 128], GD1[:, 0:n], start=False, stop=True)
                        S.tt(yv, FR[:, 0:n], f3[:, 0:n], ALU.mult)
                        finished.add(sidx)
                        yield

                rsets = [mk_rset(0), mk_rset(1)]
                R_ = rsets[0]['R']; Kp_ = rsets[0]['Kp']; V_ = rsets[0]['V']
                fpool = Pool(PS[0:6])
                for hp in range(2):
                    hc0 = hp * 128
                    S.memset(YT[:], 0.0)
                    S.memset(BON[:], 0.0)
                    done_[0].clear()
                    done_[1].clear()
                    drive([rw_dir(hp, 0, rsets[0], Pool(PS[0:3]), PS[6]), rw_dir(hp, 1, rsets[1], Pool(PS[3:6]), PS[7]), fin_gen(hp)])
                    S.load(oT[256 + hc0:256 + hc0 + 128, :], YT[:])
                barrier()
            S.es = es0
        S.finish()
        S.emit()
        print('AB ninstr', S.ninstr, {e: S.cnt[e] for e in ENGS})
    return nc

DEPTH = 4
P_DN_ = 4128


def fm(v):
    return np.ascontiguousarray(np.asarray(v, np.float32).reshape(KC, 128).T)


def lay_w(w):
    K, N = w.shape
    return np.ascontiguousarray(w.reshape(K // 128, 128, N // 128, 128).transpose(2, 1, 0, 3))


def ab_cols(g):
    cols = []
    for t in range(4):
        cols += list(range(t * 1024 + 256 * g, t * 1024 + 256 * g + 256))
    for hh in range(2):
        for j in range(4):
            cols.append(4096 + j * 8 + 2 * g + hh)
    for t in range(3):
        cols += list(range(P_DN_ + t * 1024 + 256 * g, P_DN_ + t * 1024 + 256 * g + 256))
    cols += list(range(P_DN_ + 3072, P_DN_ + 3488))
    return np.array(cols)


def rep(x):
    return np.full((128,), x, np.float32)


def ab_prm(I, l, g, modv):
    P = np.zeros((128, NPA), np.float32)

    def put(name, j, col):
        o, w = A_OFF[name]
        P[:len(col), o + j] = col
    for name in ('mix_pre_g',):
        o, w = A_OFF[name]
        P[:, o:o + 16] = fm(I['mix_pre_g'][l])
    for name in ('msh_lat', 'msc_lat', 'msh_ctx', 'msc_ctx'):
        o, w = A_OFF[name]
        P[:, o:o + 16] = fm(modv[name])
    for t in range(3):
        for hh in range(2):
            ch0 = t * 1024 + (2 * g + hh) * 128
            for k in range(7):
                put('cw', (t * 2 + hh) * 7 + k, I['dn_conv'][l][k, ch0:ch0 + 128])
    for hh in range(2):
        for d in range(2):
            put('alog', hh * 2 + d, rep(I['dn_a_log'][l][d, 2 * g + hh]))
            put('dtb', hh * 2 + d, rep(I['dn_dt_bias'][l][d, 2 * g + hh]))
    put('outg', 0, I['dn_out_g'][l])
    mu = I['rw_mu'][l]
    ch0s = [256 * g, 256 * g + 128, 1024 + 256 * g, 1024 + 256 * g + 128, 2048 + 256 * g, 2048 + 256 * g + 128,
            3072, 3200, 3328, 3456]
    for ti, c0 in enumerate(ch0s):
        n = 32 if ti == 9 else 128
        put('mu', ti, mu[c0:c0 + n])
    for hp in range(2):
        c0 = 256 * g + 128 * hp
        for d in range(2):
            put('w0', d * 2 + hp, I['rw_w0'][l][d, c0:c0 + 128])
            put('a0', d * 2 + hp, I['rw_a0'][l][d, c0:c0 + 128])
        put('kk', hp, I['rw_k_k'][l][c0:c0 + 128])
        put('ka', hp, I['rw_k_a'][l][c0:c0 + 128])
        put('rk', hp, I['rw_r_k'][l].reshape(-1)[c0:c0 + 128])
        put('lnw', hp, I['rw_ln_w'][l][c0:c0 + 128])
        put('lnb', hp, I['rw_ln_b'][l][c0:c0 + 128])
    return P


def ab_inputs(I, l, b, g, hT_b, modv, cstv):
    cols = ab_cols(g)
    w = I['w_in'][l][:, cols]
    win = np.ascontiguousarray(w.reshape(KC, 128, NCOL).transpose(1, 0, 2))
    c0 = 256 * g
    return {
        'hT': hT_b, 'win': win, 'prm': ab_prm(I, l, g, modv),
        'wup': np.ascontiguousarray(I['rw_w_up'][l][:, :, c0:c0 + 256].transpose(1, 0, 2)),
        'aup': np.ascontiguousarray(I['rw_a_up'][l][:, :, c0:c0 + 256].transpose(1, 0, 2)),
        'gup': np.ascontiguousarray(I['rw_g_up'][l][:, c0:c0 + 256]),
        'cst': cstv,
    }


_NC_CACHE = {}


def get_nc(name):
    if name not in _NC_CACHE:
        _NC_CACHE[name] = {'M': build_M, 'AB': build_AB, 'C': build_C}[name]()
    return _NC_CACHE[name]


def kernel(**I):
    I = {k: np.asarray(v, np.float32) for k, v in I.items()}
    B = 2
    cc = np.stack([I['c'][0], I['c'][1], I['c_ctx']])
    ccT = np.ascontiguousarray(cc.reshape(3, KC, 128).transpose(2, 1, 0))
    in_maps = []
    for c in range(8):
        wm = I['w_mod'][:, :, c * MN:(c + 1) * MN]
        in_maps.append({'ccT': ccT,
                        'wm': np.ascontiguousarray(wm.reshape(DEPTH, KC, 128, MN).transpose(0, 2, 1, 3)),
                        'bm': np.ascontiguousarray(np.broadcast_to(I['b_mod'][None, :, c * MN:(c + 1) * MN], (3, DEPTH, MN)))})
    res = run_bass_kernel_spmd(get_nc('M'), in_maps, core_ids=list(range(8)))
    mod = np.concatenate([r['mo'] for r in res.results], axis=2)
    cstv = make_cst()
    hT = [np.ascontiguousarray(np.concatenate([I['ctx'][b], I['x'][b]], axis=0).T) for b in range(B)]
    for l in range(DEPTH):
        def mv(row, i):
            return mod[row, l, i * D:(i + 1) * D]
        in_maps = []
        for c in range(8):
            b, g = c // 4, c % 4
            modv = {'msh_lat': mv(b, 0), 'msc_lat': mv(b, 1), 'msh_ctx': mv(2, 0), 'msc_ctx': mv(2, 1)}
            in_maps.append(ab_inputs(I, l, b, g, hT[b], modv, cstv))
        res = run_bass_kernel_spmd(get_nc('AB'), in_maps, core_ids=list(range(8)))
        oT = []
        for b in range(B):
            parts = [res.results[b * 4 + g]['oT'] for g in range(4)]
            dn = np.concatenate([p[0:256] for p in parts], axis=0)
            rw = np.concatenate([p[256:512] for p in parts], axis=0)
            oT.append(np.concatenate([dn, rw], axis=0))
        wout = lay_w(I['w_out'][l])
        wfi = lay_w(I['w_ffn_in'][l])
        wfo = lay_w(I['w_ffn_out'][l])
        in_maps = []

        def tok(a, j):
            return np.ascontiguousarray(np.concatenate([a[:, 64 * j:64 * j + 64], a[:, 256 + 1024 * j:256 + 1024 * j + 1024]], axis=1))
        for c in range(8):
            b, j = c // 4, c % 4
            vec = {'mix_post_g': I['mix_post_g'][l], 'gmix_lat': mv(b, 2), 'gmix_ctx': mv(2, 2),
                   'ffn_pre_g': I['ffn_pre_g'][l], 'fsh_lat': mv(b, 3), 'fsh_ctx': mv(2, 3),
                   'fsc_lat': mv(b, 4), 'fsc_ctx': mv(2, 4), 'ffn_post_g': I['ffn_post_g'][l],
                   'gffn_lat': mv(b, 5), 'gffn_ctx': mv(2, 5)}
            prm = np.concatenate([fm(vec[k]) for k in C_VECS], axis=1)
            in_maps.append({'hT': tok(hT[b], j), 'oT': tok(oT[b], j), 'wout': wout, 'wfi': wfi, 'wfo': wfo, 'prm': prm})
        res = run_bass_kernel_spmd(get_nc('C'), in_maps, core_ids=list(range(8)))
        for b in range(B):
            hn = np.empty_like(hT[b])
            for j in range(4):
                r = res.results[b * 4 + j]['hn']
                hn[:, 64 * j:64 * j + 64] = r[:, 0:64]
                hn[:, 256 + 1024 * j:256 + 1024 * j + 1024] = r[:, 64:]
            hT[b] = hn
    out = np.stack([np.ascontiguousarray(hT[b][:, 256:].T) for b in range(B)])
    return out.astype(np.float32)
```
